# Optimizing a Trainium2 kernel written in Bass

```python
import math, functools
import jax, jax.numpy as jnp
from jax import lax
import numpy as np

D_MODEL = 1024
BATCH = 16
SEQ = 2048
DEPTH = 1
DEC_BATCH = 128
DEC_SEQ = 1
PAST_LEN = 8192
PAGE_SIZE = 128

GLA_HEADS = 4
GLA_DK = 64
GLA_DV = 128
GLA_GATE_RANK = 16
GLA_GATE_NORM = 16.0
GLA_CHUNK = 64
MLA_HEADS = 4
MLA_NOPE = 128
MLA_ROPE = 64
MLA_V = 128
MLA_Q_RANK = 256
MLA_KV_RANK = 128
ROPE_THETA = 10000.0
Q_BLOCK = 128
MLA_SCALE = 1.0 / math.sqrt(MLA_NOPE + MLA_ROPE)
D_FF = 2816
EPS = 1e-6

GLA_WIDTH = GLA_HEADS * GLA_DV
MLA_WIDTH = MLA_HEADS * MLA_V
MIX_WIDTH = GLA_WIDTH + MLA_WIDTH
IN_SPLITS = (GLA_HEADS * GLA_DK, GLA_HEADS * GLA_DK, GLA_WIDTH, GLA_WIDTH, GLA_GATE_RANK,
             MLA_Q_RANK, MLA_KV_RANK, MLA_ROPE)
IN_WIDTH = sum(IN_SPLITS)

kernel_name = "hymba_gla_mla_macaron_step"


def rms_norm(x, w):
    xf = x.astype(jnp.float32)
    y = xf * lax.rsqrt(jnp.mean(xf * xf, axis=-1, keepdims=True) + EPS)
    return (y * w.astype(jnp.float32)).astype(x.dtype)


def swiglu(x, w_gate, w_up, w_down):
    return (jax.nn.silu(x @ w_gate) * (x @ w_up)) @ w_down


def rope(x, pos):
    half = MLA_ROPE // 2
    inv_freq = jnp.power(jnp.float32(ROPE_THETA), -jnp.arange(half, dtype=jnp.float32) / half)
    ang = pos.astype(jnp.float32)[:, None] * inv_freq[None, :]
    cos = jnp.cos(ang)[None, :, None, :]
    sin = jnp.sin(ang)[None, :, None, :]
    xf = x.astype(jnp.float32)
    x1, x2 = xf[..., :half], xf[..., half:]
    return jnp.concatenate([x1 * cos - x2 * sin, x1 * sin + x2 * cos], axis=-1).astype(x.dtype)


def gla_inputs(q, k, v, a_low, w_a_up, b_a):
    B, T = q.shape[:2]
    def to_heads(t, d):
        return t.reshape(B, T, GLA_HEADS, d).transpose(0, 2, 1, 3)
    log_a = jax.nn.log_sigmoid((a_low @ w_a_up + b_a).astype(jnp.float32)) / GLA_GATE_NORM
    return (to_heads(q, GLA_DK) * (GLA_DK ** -0.5), to_heads(k, GLA_DK),
            to_heads(v, GLA_DV), to_heads(log_a, GLA_DK))


def gla_chunk(S, q, k, v, log_a):
    C = q.shape[2]
    b = jnp.cumsum(log_a, axis=2)
    causal = jnp.tril(jnp.ones((C, C), dtype=bool))
    diff = b[:, :, :, None, :] - b[:, :, None, :, :]
    decay = jnp.exp(jnp.where(causal[:, :, None], diff, -jnp.inf))
    attn = jnp.einsum('bhid,bhjd,bhijd->bhij', q, k, decay)
    o = (jnp.einsum('bhij,bhjv->bhiv', attn, v)
         + jnp.einsum('bhid,bhdv->bhiv', q * jnp.exp(b), S))
    b_last = b[:, :, -1:, :]
    S_new = (jnp.exp(b_last[:, :, 0, :])[..., None] * S
             + jnp.einsum('bhjd,bhjv->bhdv', k * jnp.exp(b_last - b), v))
    return S_new.astype(jnp.float32), o.astype(jnp.float32)


def gla_prompt(q, k, v, log_a):
    B, H, T, _ = q.shape
    n = T // GLA_CHUNK
    def chunks(t):
        return t.reshape(B, H, n, GLA_CHUNK, t.shape[-1]).transpose(2, 0, 1, 3, 4)
    S0 = jnp.zeros((B, H, GLA_DK, GLA_DV), jnp.float32)
    S, o = lax.scan(lambda s, c: gla_chunk(s, *c), S0, (chunks(q), chunks(k), chunks(v), chunks(log_a)))
    o = o.transpose(1, 2, 0, 3, 4).reshape(B, H, T, GLA_DV)
    return o, S


def gla_sample(S0, q, k, v, log_a):
    S, o = gla_chunk(S0.astype(jnp.float32), q, k, v, log_a)
    return o, S


def gla_output(o, g, norm_w):
    B, H, T, _ = o.shape
    o = rms_norm(o.transpose(0, 2, 1, 3), norm_w)
    return (o * jax.nn.silu(g.astype(jnp.float32)).reshape(B, T, H, GLA_DV)).reshape(B, T, GLA_WIDTH).astype(g.dtype)


def mla_project(c_q, c_kv, k_pe_raw, pos, q_norm_w, kv_norm_w, w_uq):
    B, T = c_q.shape[:2]
    q = (rms_norm(c_q, q_norm_w) @ w_uq).reshape(B, T, MLA_HEADS, MLA_NOPE + MLA_ROPE)
    q_nope = q[..., :MLA_NOPE]
    q_pe = rope(q[..., MLA_NOPE:], pos)
    lat = rms_norm(c_kv, kv_norm_w)
    k_pe = rope(k_pe_raw[:, :, None, :], pos)[:, :, 0, :]
    return q_nope, q_pe, lat, k_pe


def mla_prompt_attention(q_nope, q_pe, lat, k_pe, w_uk, w_uv):
    B, T = q_nope.shape[:2]
    k_nope = jnp.einsum('btc,chd->bthd', lat, w_uk)
    v = jnp.einsum('btc,chd->bthd', lat, w_uv)
    nb = T // Q_BLOCK
    qn_b = q_nope.reshape(B, nb, Q_BLOCK, MLA_HEADS, MLA_NOPE).transpose(1, 0, 2, 3, 4)
    qp_b = q_pe.reshape(B, nb, Q_BLOCK, MLA_HEADS, MLA_ROPE).transpose(1, 0, 2, 3, 4)
    starts = jnp.arange(nb, dtype=jnp.int32) * Q_BLOCK
    kpos = jnp.arange(T, dtype=jnp.int32)

    def block(args):
        qn, qp, start = args
        s = (jnp.einsum('bqhd,bkhd->bhqk', qn, k_nope)
             + jnp.einsum('bqhr,bkr->bhqk', qp, k_pe)).astype(jnp.float32) * MLA_SCALE
        qpos = start + jnp.arange(Q_BLOCK, dtype=jnp.int32)
        s = jnp.where(kpos[None, :] <= qpos[:, None], s, -jnp.inf)
        p = jax.nn.softmax(s, axis=-1)
        return jnp.einsum('bhqk,bkhd->bqhd', p.astype(v.dtype), v)

    o = lax.map(block, (qn_b, qp_b, starts))
    return o.transpose(1, 0, 2, 3, 4).reshape(B, T, MLA_WIDTH)


def mla_sample_attention(past_lat, past_pe, q_nope, q_pe, lat, k_pe, w_uk, w_uv):
    B, T = q_nope.shape[:2]
    past_len = past_lat.shape[1]
    lat_all = jnp.concatenate([past_lat.astype(lat.dtype), lat], axis=1)
    pe_all = jnp.concatenate([past_pe.astype(k_pe.dtype), k_pe], axis=1)
    q_lat = jnp.einsum('bqhd,chd->bqhc', q_nope, w_uk)
    s = (jnp.einsum('bqhc,bkc->bhqk', q_lat, lat_all)
         + jnp.einsum('bqhr,bkr->bhqk', q_pe, pe_all)).astype(jnp.float32) * MLA_SCALE
    kpos = jnp.arange(lat_all.shape[1], dtype=jnp.int32)
    qpos = past_len + jnp.arange(T, dtype=jnp.int32)
    s = jnp.where(kpos[None, :] <= qpos[:, None], s, -jnp.inf)
    p = jax.nn.softmax(s, axis=-1)
    o_lat = jnp.einsum('bhqk,bkc->bqhc', p.astype(lat_all.dtype), lat_all)
    return jnp.einsum('bqhc,chd->bqhd', o_lat, w_uv).reshape(B, T, MLA_WIDTH)


def hybrid_layer(x, pos, lw, gla_mix, mla_attend):
    (f1n, f1g, f1u, f1d, mix_n, w_in, w_a_up, b_a, g_norm, q_norm, w_uq, kv_norm,
     w_uk, w_uv, w_out, f2n, f2g, f2u, f2d) = lw
    x = x + 0.5 * swiglu(rms_norm(x, f1n), f1g, f1u, f1d)
    h = rms_norm(x, mix_n)
    offsets = [int(o) for o in np.cumsum(IN_SPLITS)[:-1]]
    q, k, v, g, a_low, c_q, c_kv, k_pe_raw = jnp.split(h @ w_in, offsets, axis=-1)
    o_gla, S = gla_mix(*gla_inputs(q, k, v, a_low, w_a_up, b_a))
    y_gla = gla_output(o_gla, g, g_norm)
    q_nope, q_pe, lat, k_pe = mla_project(c_q, c_kv, k_pe_raw, pos, q_norm, kv_norm, w_uq)
    y_mla = mla_attend(q_nope, q_pe, lat, k_pe, w_uk, w_uv)
    x = x + jnp.concatenate([y_gla, y_mla], axis=-1) @ w_out
    x = x + 0.5 * swiglu(rms_norm(x, f2n), f2g, f2u, f2d)
    return x, lat, k_pe, S


def setup_inputs(seed: int = 0) -> dict:
    key = jax.random.key(seed)
    ks = iter(jax.random.split(key, 40))
    N_PAGES = PAST_LEN // PAGE_SIZE
    N_POOL = (DEC_BATCH * N_PAGES * 5) // 4
    f32 = jnp.float32

    def nrm(shape, scale):
        return jax.random.normal(next(ks), shape, f32) * scale

    def gain(shape):
        return 1.0 + 0.01 * jax.random.normal(next(ks), shape, f32)

    x_prompt = jax.random.normal(next(ks), (BATCH, SEQ, D_MODEL), f32)
    x_sample = jax.random.normal(next(ks), (DEC_BATCH, DEC_SEQ, D_MODEL), f32)
    cache_kv = jax.random.normal(next(ks), (DEPTH, N_POOL, PAGE_SIZE, MLA_KV_RANK), f32)
    cache_pe = jax.random.normal(next(ks), (DEPTH, N_POOL, PAGE_SIZE, MLA_ROPE), f32)
    state_gla = nrm((DEPTH, DEC_BATCH, GLA_HEADS, GLA_DK, GLA_DV), 0.5)
    page_table = jax.random.permutation(next(ks), N_POOL)[:DEC_BATCH * N_PAGES].reshape(
        DEC_BATCH, N_PAGES).astype(jnp.int32)
    return {
        "x_prompt": x_prompt,
        "x_sample": x_sample,
        "cache_kv": cache_kv,
        "cache_pe": cache_pe,
        "state_gla": state_gla,
        "page_table": page_table,
        "ffn1_norm_w": gain((DEPTH, D_MODEL)),
        "ffn1_w_gate": nrm((DEPTH, D_MODEL, D_FF), D_MODEL ** -0.5),
        "ffn1_w_up": nrm((DEPTH, D_MODEL, D_FF), D_MODEL ** -0.5),
        "ffn1_w_down": nrm((DEPTH, D_FF, D_MODEL), D_FF ** -0.5),
        "mix_norm_w": gain((DEPTH, D_MODEL)),
        "w_in": nrm((DEPTH, D_MODEL, IN_WIDTH), D_MODEL ** -0.5),
        "gla_w_a_up": nrm((DEPTH, GLA_GATE_RANK, GLA_HEADS * GLA_DK), GLA_GATE_RANK ** -0.5),
        "gla_b_a": nrm((DEPTH, GLA_HEADS * GLA_DK), 0.1),
        "gla_norm_w": gain((DEPTH, GLA_DV)),
        "mla_q_norm_w": gain((DEPTH, MLA_Q_RANK)),
        "mla_w_uq": nrm((DEPTH, MLA_Q_RANK, MLA_HEADS * (MLA_NOPE + MLA_ROPE)), MLA_Q_RANK ** -0.5),
        "mla_kv_norm_w": gain((DEPTH, MLA_KV_RANK)),
        "mla_w_uk": nrm((DEPTH, MLA_KV_RANK, MLA_HEADS, MLA_NOPE), MLA_KV_RANK ** -0.5),
        "mla_w_uv": nrm((DEPTH, MLA_KV_RANK, MLA_HEADS, MLA_V), MLA_KV_RANK ** -0.5),
        "w_out": nrm((DEPTH, MIX_WIDTH, D_MODEL), MIX_WIDTH ** -0.5),
        "ffn2_norm_w": gain((DEPTH, D_MODEL)),
        "ffn2_w_gate": nrm((DEPTH, D_MODEL, D_FF), D_MODEL ** -0.5),
        "ffn2_w_up": nrm((DEPTH, D_MODEL, D_FF), D_MODEL ** -0.5),
        "ffn2_w_down": nrm((DEPTH, D_FF, D_MODEL), D_FF ** -0.5),
        "final_norm_w": gain((D_MODEL,)),
    }


def reference(x_prompt, x_sample, cache_kv, cache_pe, state_gla, page_table,
              ffn1_norm_w, ffn1_w_gate, ffn1_w_up, ffn1_w_down, mix_norm_w, w_in,
              gla_w_a_up, gla_b_a, gla_norm_w, mla_q_norm_w, mla_w_uq, mla_kv_norm_w,
              mla_w_uk, mla_w_uv, w_out, ffn2_norm_w, ffn2_w_gate, ffn2_w_up, ffn2_w_down,
              final_norm_w):
    B_s, T_s = x_sample.shape[:2]
    pos_prompt = jnp.arange(x_prompt.shape[1], dtype=jnp.int32)
    pos_sample = PAST_LEN + jnp.arange(T_s, dtype=jnp.int32)
    layer_weights = (ffn1_norm_w, ffn1_w_gate, ffn1_w_up, ffn1_w_down, mix_norm_w, w_in,
                     gla_w_a_up, gla_b_a, gla_norm_w, mla_q_norm_w, mla_w_uq, mla_kv_norm_w,
                     mla_w_uk, mla_w_uv, w_out, ffn2_norm_w, ffn2_w_gate, ffn2_w_up, ffn2_w_down)
    xp, xs = x_prompt, x_sample
    lat_p, pe_p, gla_p, lat_s, pe_s, gla_s = [], [], [], [], [], []
    for l in range(DEPTH):
        lw = tuple(w[l] for w in layer_weights)
        xp, lat, kpe, S = hybrid_layer(xp, pos_prompt, lw, gla_prompt, mla_prompt_attention)
        lat_p.append(lat); pe_p.append(kpe); gla_p.append(S)
        past_lat = cache_kv[l, page_table].reshape(B_s, -1, MLA_KV_RANK)
        past_pe = cache_pe[l, page_table].reshape(B_s, -1, MLA_ROPE)
        xs, lat, kpe, S = hybrid_layer(
            xs, pos_sample, lw,
            functools.partial(gla_sample, state_gla[l]),
            functools.partial(mla_sample_attention, past_lat, past_pe))
        lat_s.append(lat); pe_s.append(kpe); gla_s.append(S)
    y_prompt = rms_norm(xp, final_norm_w)
    y_sample = rms_norm(xs, final_norm_w)
    new_kv_prompt = jnp.stack(lat_p)
    new_pe_prompt = jnp.stack(pe_p)
    new_gla_prompt = jnp.stack(gla_p)
    new_kv_sample = jnp.stack(lat_s)
    new_pe_sample = jnp.stack(pe_s)
    new_gla_sample = jnp.stack(gla_s)
    return (y_prompt, y_sample, new_kv_prompt, new_pe_prompt, new_gla_prompt,
            new_kv_sample, new_pe_sample, new_gla_sample)
```

```python
import math
import numpy as np
from contextlib import ExitStack
import concourse.bass as bass
import concourse.mybir as mybir
from concourse.bass_utils import run_bass_kernel_spmd

F32 = mybir.dt.float32
BF16 = mybir.dt.bfloat16
I32 = mybir.dt.int32
AF = mybir.ActivationFunctionType
ALU = mybir.AluOpType

NCORES = 8
D = 1024
KC = 8
DFF = 2816
SEQ = 2048
NT = 512
NPOOL = 10240
NSEQ_S = 16
EPS = 1e-6
MLA_SCALE = 1.0 / math.sqrt(192.0)
NUNITS = 80
SAME_ENGINE_SYNC = True

C_Q, C_K, C_V, C_G, C_A, C_CQ, C_KV = 0, 256, 512, 1024, 1536, 1552, 1808


class Buf:
    __slots__ = ("name", "last_w", "readers", "war", "sem_in", "n_in", "sem_out", "n_out")

    def __init__(self, name):
        self.name = name
        self.last_w = None
        self.readers = []
        self.war = []
        self.sem_in = None
        self.n_in = 0
        self.sem_out = None
        self.n_out = 0


class Op:
    __slots__ = ("eng", "fn", "deps", "dma", "sem", "count", "needs_inc", "seq")

    def __init__(self, eng, fn, dma):
        self.eng = eng
        self.fn = fn
        self.deps = set()
        self.dma = dma
        self.sem = None
        self.count = 0
        self.needs_inc = False
        self.seq = 0


class Prog:
    ENGS = ("pe", "act", "dve", "pool", "sp")

    def __init__(self, nc, stack):
        self.nc = nc
        self.stack = stack
        self.ops = []
        self.nsem = 0
        self.stores = []

    def new_sem(self, name):
        self.nsem += 1
        return self.stack.enter_context(self.nc.semaphore(f"s{self.nsem}_{name}"))

    def sb(self, name, shape, dtype):
        return self.stack.enter_context(self.nc.sbuf_tensor(name, list(shape), dtype))

    def ps(self, name, shape, dtype=F32):
        return self.stack.enter_context(self.nc.psum_tensor(name, list(shape), dtype))

    def op(self, eng, fn, reads=(), writes=(), dma=False, no_waw=False):
        o = Op(eng, fn, dma)
        for b in reads:
            if b.last_w is not None:
                o.deps.add(b.last_w)
        for b in writes:
            for r in b.readers:
                o.deps.add(r)
            if no_waw:
                for r in b.war:
                    o.deps.add(r)
            elif b.last_w is not None:
                o.deps.add(b.last_w)
        o.deps.discard(o)
        if dma:
            if writes:
                b = writes[0]
                if b.sem_in is None:
                    b.sem_in = self.new_sem("i_" + b.name)
                b.n_in += 16
                o.sem, o.count = b.sem_in, b.n_in
            else:
                b = reads[0]
                if b.sem_out is None:
                    b.sem_out = self.new_sem("o_" + b.name)
                b.n_out += 16
                o.sem, o.count = b.sem_out, b.n_out
                self.stores.append(o)
        for b in reads:
            b.readers.append(o)
        for b in writes:
            if b.readers or not no_waw:
                b.war = [r for r in b.readers if r is not o]
            b.readers = []
            b.last_w = o
        for d in o.deps:
            if not d.dma and (d.eng != eng or dma or (SAME_ENGINE_SYNC and eng != "pe")):
                d.needs_inc = True
        self.ops.append(o)
        return o

    def pe(self, fn, reads=(), writes=()):
        return self.op("pe", fn, reads, writes)

    def act(self, fn, reads=(), writes=()):
        return self.op("act", fn, reads, writes)

    def dve(self, fn, reads=(), writes=()):
        return self.op("dve", fn, reads, writes)

    def pool(self, fn, reads=(), writes=()):
        return self.op("pool", fn, reads, writes)

    def dma(self, fn, reads=(), writes=(), q="sp", no_waw=True):
        return self.op(q, fn, reads, writes, dma=True, no_waw=no_waw)

    def emit(self):
        nc = self.nc
        eng_sem = {e: self.new_sem("eng_" + e) for e in self.ENGS}
        cnt = {e: 0 for e in self.ENGS}
        for o in self.ops:
            if not o.dma and o.needs_inc:
                cnt[o.eng] += 1
                o.seq = cnt[o.eng]
        per = {e: [o for o in self.ops if o.eng == e] for e in self.ENGS}
        final_waits = {}
        for o in self.stores:
            k = id(o.sem)
            if k not in final_waits or final_waits[k][1] < o.count:
                final_waits[k] = (o.sem, o.count)

        def run(engname, engobj):
            waited = {}
            for o in per[engname]:
                need = {}
                for d in o.deps:
                    if d.dma:
                        s, c = d.sem, d.count
                    else:
                        if d.eng == engname and not o.dma and (engname == "pe" or not SAME_ENGINE_SYNC):
                            continue
                        s, c = eng_sem[d.eng], d.seq
                    k = id(s)
                    if k not in need or need[k][1] < c:
                        need[k] = (s, c)
                for k, (s, c) in need.items():
                    if waited.get(k, 0) >= c:
                        continue
                    engobj.wait_ge(s, c)
                    waited[k] = c
                inst = o.fn(engobj)
                if o.dma:
                    inst.then_inc(o.sem, 16)
                elif o.needs_inc:
                    inst.then_inc(eng_sem[engname], 1)
            if engname == "sp":
                for k, (s, c) in final_waits.items():
                    if waited.get(k, 0) < c:
                        engobj.wait_ge(s, c)

        with nc.Block() as block:
            @block.tensor
            def _(e):
                run("pe", e)

            @block.scalar
            def _(e):
                run("act", e)

            @block.vector
            def _(e):
                run("dve", e)

            @block.gpsimd
            def _(e):
                run("pool", e)

            @block.sync
            def _(e):
                run("sp", e)


class T:
    def __init__(self, P, name, shape, dtype, psum=False):
        self.t = P.ps("P_" + name, shape, dtype) if psum else P.sb("S_" + name, shape, dtype)
        self.b = Buf(name)


class View:
    def __init__(self, ap, b):
        self.t = ap
        self.b = b


class Rot:
    def __init__(self, items):
        self.items = items
        self.i = 0

    def next(self):
        x = self.items[self.i % len(self.items)]
        self.i += 1
        return x


class Builder:
    def __init__(self, nc, P, do_sample=True, n_tiles=8):
        self.nc = nc
        self.P = P
        self.do_sample = do_sample
        self.n_tiles = n_tiles
        dt = nc.dram_tensor
        I, O = "ExternalInput", "ExternalOutput"
        self.xp = dt("xp", [2, SEQ, D], F32, kind=I)
        self.xs = dt("xs", [NSEQ_S, D], F32, kind=I)
        if do_sample:
            self.ckv = dt("ckv", [NPOOL, 128, 128], F32, kind=I)
            self.cpe = dt("cpe", [NPOOL, 128, 64], F32, kind=I)
            self.sgla = dt("sgla", [NSEQ_S, 4, 64, 128], F32, kind=I)
            self.ptl = dt("ptl", [128, NSEQ_S * 4], I32, kind=I)
        self.w = {}
        for name, shp in (("f1g", [D, DFF]), ("f1u", [D, DFF]), ("f1d", [DFF, D]), ("win", [D, 2000]),
                          ("waup", [17, 256]), ("wuq", [256, 768]), ("wuk", [128, 512]), ("wuv", [128, 512]),
                          ("wout", [D, D]), ("f2g", [D, DFF]), ("f2u", [D, DFF]), ("f2d", [DFF, D]),
                          ("smallw", [128, 36]), ("consts", [128, 448]), ("rope", [2, 64, SEQ + 16])):
            self.w[name] = dt(name, shp, F32, kind=I)
        self.scr = dt("wscr", [NUNITS, 128, 11 * 256], BF16, kind="Internal")
        self.scrb = Buf("wscr")
        self.units = {}
        self.first_pass = True
        self.yp = dt("yp", [2, SEQ, D], F32, kind=O)
        self.ys = dt("ys", [NSEQ_S, D], F32, kind=O)
        self.kvp = dt("kvp", [2, SEQ, 128], F32, kind=O)
        self.pep = dt("pep", [2, SEQ, 64], F32, kind=O)
        self.glap = dt("glap", [2, 4, 64, 128], F32, kind=O)
        self.kvs = dt("kvs", [NSEQ_S, 128], F32, kind=O)
        self.pes = dt("pes", [NSEQ_S, 64], F32, kind=O)
        self.glas = dt("glas", [NSEQ_S, 4, 64, 128], F32, kind=O)

    def alloc(self):
        P = self.P
        mk = lambda n, s, d=F32: T(P, n, s, d)
        self.cst = mk("cst", [128, 448])
        self.smw = mk("smw", [128, 36])
        self.identb = mk("identb", [128, 128], BF16)
        self.trib = mk("trib", [128, 128], BF16)
        self.on1024 = mk("on1024", [128, 128], BF16)
        self.on256 = mk("on256", [128, 128], BF16)
        self.on128 = mk("on128", [128, 128], BF16)
        self.on1 = mk("on1", [128, 128], BF16)
        self.onf = mk("onf", [128, 128])
        self.waup = mk("waup", [17, 256], BF16)
        self.wuq = mk("wuq", [128, 2, 768], BF16)
        self.wuqr = mk("wuqr", [128, 2, 4, 64], BF16)
        self.wuk = mk("wuk", [128, 512], BF16)
        self.wuv = mk("wuv", [128, 512], BF16)
        self.wukT = mk("wukT", [128, 4, 128], BF16)
        self.xio = Rot([mk(f"xio{i}", [128, D]) for i in range(2)])
        self.xT = mk("xT", [128, KC, NT])
        self.xn = mk("xn", [128, KC, NT], BF16)
        self.h = mk("h", [128, 11, NT], BF16)
        self.rstd = mk("rstd", [128, NT])
        self.sg = Rot([mk(f"sg{i}", [128, NT]) for i in range(2)])
        self.wbuf = Rot([mk(f"wb{i}", [128, 11, 256], BF16) for i in range(4)])
        self.dummy = mk("dummy", [128, 8])
        self.arena = mk("arena", [128, 6144], BF16)
        ar = self.arena.t
        self.latT_t = ar[:, 0:2048]
        self.latM_t = ar[:, 2048:4096].rearrange("p (a c) -> p a c", c=128)
        self.kpe_t = ar[0:64, 4096:6144]
        self.bA, self.bB, self.bC = Buf("arA"), Buf("arB"), Buf("arC")
        self.alow = mk("alow", [17, NT], BF16)
        self.Lt = Rot([mk(f"Lt{i}", [128, 256]) for i in range(2)])
        self.enb = Rot([mk(f"enb{i}", [128, 256]) for i in range(2)])
        self.bTs = mk("bTs", [64, 4, NT])
        self.eb = Rot([mk(f"eb{i}", [64, NT]) for i in range(2)])
        self.elast = mk("elast", [64, 4, 4])
        self.qt = [mk(f"qt{h}", [64, NT], BF16) for h in range(4)]
        self.kt = [mk(f"kt{h}", [64, NT], BF16) for h in range(4)]
        self.ktm = mk("ktm", [128, 4, 256], BF16)
        self.vtm = mk("vtm", [128, 4, 512], BF16)
        self.sgT = mk("sgT", [128, 4, NT], BF16)
        self.cq = mk("cq", [128, 2, NT])
        self.cqn = mk("cqn", [128, 2, NT], BF16)
        self.qn = Rot([mk(f"qn{i}", [128, NT], BF16) for i in range(1)])
        self.ql = Rot([mk(f"ql{i}", [128, NT], BF16) for i in range(4)])
        self.qpe = Rot([mk(f"qpe{i}", [64, NT], BF16) for i in range(4)])
        self.ckvf = mk("ckvf", [128, NT])
        self.latf = mk("latf", [128, NT])
        self.kpef = mk("kpef", [64, NT])
        self.cs = mk("cs", [64, 2, NT])
        self.rt = Rot([mk(f"rt{i}", [64, NT]) for i in range(2)])
        self.wkvr = mk("wkvr", [128, KC, 64], BF16)
        self.S = [mk(f"S{h}", [64, 128]) for h in range(4)]
        self.Sb = [mk(f"Sb{h}", [64, 128], BF16) for h in range(4)]
        self.ym = mk("ym", [128, 8, NT], BF16)
        self.pt = Rot([mk(f"pt{i}", [128, NT], BF16) for i in range(3)])
        self.atm = Rot([mk(f"atm{i}", [128, 128], BF16) for i in range(3)])
        self.rl = mk("rl", [128, NT])
        self.ol = mk("ol", [128, NT], BF16)
        self.otok = Rot([mk(f"otok{i}", [128, 4, 128]) for i in range(2)])
        self.otok2 = Rot([mk(f"otk2{i}", [128, 4, 64]) for i in range(2)])
        self.G = Rot([T(P, f"pg{i}", [128, NT], F32, psum=True) for i in range(3)])
        self.O = Rot([T(P, f"po{i}", [128, NT], F32, psum=True) for i in range(2)])
        self.XS = T(P, "pxs", [128, NT], F32, psum=True)
        xbs = []
        for i in range(2):
            full = T(P, f"pxb{i}", [128, 1024], BF16, psum=True)
            for j in range(2):
                v = View(full.t[:, j * 512:(j + 1) * 512], full.b)
                xbs.append(v)
        self.XB = Rot(xbs)
        if self.do_sample:
            self.idx = mk("idx", [128, NSEQ_S * 4], I32)
            self.ptls = mk("ptls", [128, NSEQ_S * 4], I32)
            self.lt4 = Rot([mk(f"lt4_{i}", [128, 512], BF16) for i in range(2)])
            self.pt4 = Rot([mk(f"pt4_{i}", [128, 256], BF16) for i in range(2)])
            self.pts = Rot([mk(f"pts{i}", [128, 16, 4], BF16) for i in range(2)])
            self.ptsum = Rot([mk(f"ptsum{i}", [128, 4]) for i in range(2)])
            self.QL = mk("QL", [128, NSEQ_S, 4], BF16)
            self.QP = mk("QP", [128, NSEQ_S, 4], BF16)
            self.s0 = Rot([mk(f"s0_{i}", [64, 4, 128]) for i in range(2)])
            self.s1 = mk("s1", [64, 4, 128])
            self.kmask = Rot([mk(f"kmask{i}", [16, 256], BF16) for i in range(2)])
            self.qsf = mk("qsf", [64, 4, 18])
            self.esf = mk("esf", [64, 4, 16])
            self.num = mk("num", [128, NSEQ_S, 4])
            self.den = mk("den", [128, NSEQ_S, 4])
            self.pnew = mk("pnew", [128, 16])
            self.prod = mk("prod", [128, 16])
            self.prod2 = mk("prod2", [64, 16])
            self.tmp16 = mk("tmp16", [128, 16])
            self.gsb = mk("gsb", [128, 64])

    def mm(self, out, ob, lhsT, lb, rhs, rb, start=True, stop=True):
        rd = [lb] if rb is lb else [lb, rb]
        self.P.pe(lambda e: e.matmul(out, lhsT=lhsT, rhs=rhs, start=start, stop=stop), reads=rd, writes=[ob])

    def tr(self, out, ob, in_, ib, ident, idb):
        self.P.pe(lambda e: e.transpose(out, in_, ident), reads=[ib, idb], writes=[ob])

    def actf(self, out, ob, in_, ib, func, scale=1.0, bias=0.0):
        self.P.act(lambda e: e.activation(out=out, in_=in_, func=func, bias=bias, scale=scale), reads=[ib], writes=[ob])

    def acp(self, out, ob, in_, ib):
        self.P.act(lambda e: e.activation(out=out, in_=in_, func=AF.Copy), reads=[ib], writes=[ob])

    def tt(self, out, ob, a, ab, b, bb, op):
        self.P.dve(lambda e: e.tensor_tensor(out=out, in0=a, in1=b, op=op), reads=[ab, bb], writes=[ob])

    def stt(self, out, ob, a, ab, scalar, b, bb, op0, op1, sreads=()):
        self.P.dve(lambda e: e.scalar_tensor_tensor(out=out, in0=a, scalar=scalar, in1=b, op0=op0, op1=op1),
                   reads=[ab, bb] + list(sreads), writes=[ob])

    def ts(self, out, ob, a, ab, s1, op0, s2=None, op1=None, sreads=()):
        if op1 is None:
            self.P.dve(lambda e: e.tensor_scalar(out=out, in0=a, scalar1=s1, scalar2=None, op0=op0),
                       reads=[ab] + list(sreads), writes=[ob])
        else:
            self.P.dve(lambda e: e.tensor_scalar(out=out, in0=a, scalar1=s1, scalar2=s2, op0=op0, op1=op1),
                       reads=[ab] + list(sreads), writes=[ob])

    def cp(self, out, ob, in_, ib):
        self.P.dve(lambda e: e.tensor_copy(out=out, in_=in_), reads=[ib], writes=[ob])

    def memset(self, ap, b, v):
        self.P.dve(lambda e: e.memset(ap, v), writes=[b])

    def recip(self, out, ob, in_, ib):
        self.P.dve(lambda e: e.reciprocal(out=out, in_=in_), reads=[ib], writes=[ob])

    def dma_in(self, out, ob, src, q="sp"):
        self.P.dma(lambda e: e.dma_start(out=out, in_=src), writes=[ob], q=q)

    def dma_out(self, dst, in_, ib, q="sp"):
        self.P.dma(lambda e: e.dma_start(out=dst, in_=in_), reads=[ib], q=q)

    def rstd_from(self, sq_aps, sqb, ones, C, npart=128):
        ps = self.XS
        n = len(sq_aps)
        for i, ap in enumerate(sq_aps):
            self.mm(ps.t[:, 0:C], ps.b, ones.t[0:npart, :], ones.b, ap, sqb, start=(i == 0), stop=(i == n - 1))
        self.actf(self.rstd.t[:, 0:C], self.rstd.b, ps.t[:, 0:C], ps.b, AF.Ln, bias=EPS)
        self.actf(self.rstd.t[:, 0:C], self.rstd.b, self.rstd.t[:, 0:C], self.rstd.b, AF.Exp, scale=-0.5)

    def unit_idx(self, key):
        if key not in self.units:
            assert self.first_pass, key
            self.units[key] = len(self.units)
            assert len(self.units) <= NUNITS
        return self.units[key]

    def wstream(self, key, wt, dst, src_f32, scr_view):
        if self.first_pass:
            self.dma_in(dst, wt.b, src_f32, q="pool")
            self.P.dma(lambda e: e.dma_start(out=scr_view, in_=dst), reads=[wt.b], q="sp")
        else:
            self.P.dma(lambda e: e.dma_start(out=dst, in_=scr_view), writes=[wt.b], q="pool")

    def end_first_pass(self):
        self.first_pass = False
        bufs = [w.b for w in self.wbuf.items]
        self.P.pool(lambda e: e.memset(self.dummy.t[:], 0.0), writes=bufs + [self.dummy.b])

    def wload8(self, wd, c0, ncols):
        wt = self.wbuf.next()
        u = self.unit_idx((wd.name, c0, ncols))
        src = wd.ap()[:, c0:c0 + ncols].rearrange("(kc p) c -> p kc c", p=128)
        scr_view = self.scr.ap()[u, :, 0:KC * ncols].rearrange("p (k c) -> p k c", c=ncols)
        self.wstream(u, wt, wt.t[:, 0:KC, 0:ncols], src, scr_view)
        return wt

    def setup(self):
        w = self.w
        self.dma_in(self.cst.t[:], self.cst.b, w["consts"].ap())
        self.dma_in(self.smw.t[:], self.smw.b, w["smallw"].ap())
        self.ident = self.cst.t[:, 0:128]
        self.trif = self.cst.t[:, 256:384]
        self.negI = self.cst.t[:, 401:417]
        self.cp(self.identb.t[:], self.identb.b, self.cst.t[:, 0:128], self.cst.b)
        self.cp(self.trib.t[:], self.trib.b, self.cst.t[:, 128:256], self.cst.b)
        for tl, v in ((self.on1024, 1.0 / 1024), (self.on256, 1.0 / 256), (self.on128, 1.0 / 128), (self.on1, 1.0),
                      (self.onf, 1.0), (self.alow, 1.0)):
            self.memset(tl.t[:], tl.b, v)
        self.memset(self.latf.t[:, 0:128], self.latf.b, 0.0)
        self.memset(self.kpef.t[:, 0:128], self.kpef.b, 0.0)
        self.memset(self.xT.t[:, :, 0:128], self.xT.b, 0.0)
        if self.do_sample:
            self.memset(self.qsf.t[:], self.qsf.b, 0.0)
        self.dma_in(self.waup.t[:], self.waup.b, w["waup"].ap(), q="pool")
        self.dma_in(self.wuq.t[:], self.wuq.b, w["wuq"].ap().rearrange("(kc p) c -> p kc c", p=128), q="pool")
        self.dma_in(self.wuk.t[:], self.wuk.b, w["wuk"].ap(), q="pool")
        self.dma_in(self.wuv.t[:], self.wuv.b, w["wuv"].ap(), q="pool")
        for h in range(4):
            b0 = h * 192 + 128
            self.ts(self.wuqr.t[:, :, h, 0:32], self.wuqr.b, self.wuq.t[:, :, b0 + 32:b0 + 64], self.wuq.b, -1.0, ALU.mult)
            self.cp(self.wuqr.t[:, :, h, 32:64], self.wuqr.b, self.wuq.t[:, :, b0:b0 + 32], self.wuq.b)
            xb = self.XB.next()
            self.tr(xb.t[:, 0:128], xb.b, self.wuk.t[:, h * 128:(h + 1) * 128], self.wuk.b, self.identb.t[:], self.identb.b)
            self.cp(self.wukT.t[:, h, :], self.wukT.b, xb.t[:, 0:128], xb.b)

    def norm_x(self, col0, C, inplace=False):
        sq = self.h
        self.actf(sq.t[:, 0:KC, 0:C], sq.b, self.xT.t[:, :, 0:C], self.xT.b, AF.Square)
        self.rstd_from([sq.t[:, kc, 0:C] for kc in range(KC)], sq.b, self.on1024, C)
        for kc in range(KC):
            o, ob = (self.xT.t[:, kc, 0:C], self.xT.b) if inplace else (self.xn.t[:, kc, 0:C], self.xn.b)
            self.stt(o, ob, self.xT.t[:, kc, 0:C], self.xT.b, self.smw.t[:, col0 + kc:col0 + kc + 1],
                     self.rstd.t[:, 0:C], self.rstd.b, ALU.mult, ALU.mult, sreads=[self.smw.b])

    def ffn(self, wg, wu, wdn, col0, C):
        xn = self.xn
        self.norm_x(col0, C)
        for half in range(2):
            f0 = half * 11
            for u in range(6):
                fs = [f for f in (f0 + 2 * u, f0 + 2 * u + 1) if f < f0 + 11]
                wgt = self.wload8(wg, fs[0] * 128, len(fs) * 128)
                wut = self.wload8(wu, fs[0] * 128, len(fs) * 128)
                for i, f in enumerate(fs):
                    pg = self.G.next()
                    pu = self.G.next()
                    for kc in range(KC):
                        self.mm(pg.t[:, 0:C], pg.b, wgt.t[:, kc, i * 128:(i + 1) * 128], wgt.b, xn.t[:, kc, 0:C], xn.b, kc == 0, kc == KC - 1)
                    for kc in range(KC):
                        self.mm(pu.t[:, 0:C], pu.b, wut.t[:, kc, i * 128:(i + 1) * 128], wut.b, xn.t[:, kc, 0:C], xn.b, kc == 0, kc == KC - 1)
                    sg = self.sg.next()
                    self.actf(sg.t[:, 0:C], sg.b, pg.t[:, 0:C], pg.b, AF.Silu)
                    self.tt(self.h.t[:, f - f0, 0:C], self.h.b, pu.t[:, 0:C], pu.b, sg.t[:, 0:C], sg.b, ALU.mult)
            for dcp in range(4):
                wdt = self.wbuf.next()
                src = wdn.ap()[f0 * 128:(f0 + 11) * 128, dcp * 256:(dcp + 1) * 256].rearrange("(f p) c -> p f c", p=128)
                u = self.unit_idx((wdn.name, "down", f0, dcp))
                self.wstream(u, wdt, wdt.t[:], src, self.scr.ap()[u].rearrange("p (f c) -> p f c", c=256))
                for i in range(2):
                    dc = dcp * 2 + i
                    po = self.O.next()
                    for f in range(11):
                        self.mm(po.t[:, 0:C], po.b, wdt.t[:, f, i * 128:(i + 1) * 128], wdt.b, self.h.t[:, f, 0:C], self.h.b, f == 0, f == 10)
                    self.stt(self.xT.t[:, dc, 0:C], self.xT.b, po.t[:, 0:C], po.b, 0.5, self.xT.t[:, dc, 0:C], self.xT.b, ALU.mult, ALU.add)

    def load_x(self, src_rows, nblk, npart):
        for a in range(nblk):
            xi = self.xio.next()
            self.dma_in(xi.t[0:npart, :], xi.b, src_rows(a))
            for g in range(2):
                pg = self.G.next()
                for k in range(4):
                    kc = g * 4 + k
                    self.tr(pg.t[:, k * 128:k * 128 + npart], pg.b, xi.t[0:npart, kc * 128:(kc + 1) * 128], xi.b,
                            self.ident[0:npart, 0:npart], self.cst.b)
                src_v = pg.t[:].rearrange("p (k t) -> p k t", k=4)[:, :, 0:npart]
                dst_v = self.xT.t[:, g * 4:(g + 1) * 4, a * 128:a * 128 + npart]
                if g == 0:
                    self.acp(dst_v, self.xT.b, src_v, pg.b)
                else:
                    self.cp(dst_v, self.xT.b, src_v, pg.b)

    def store_y(self, dst_rows, nblk, npart):
        yT = self.xT
        for a in range(nblk):
            yo = self.xio.next()
            for g in range(2):
                pg = self.G.next()
                for k in range(4):
                    kc = g * 4 + k
                    self.tr(pg.t[:, k * 128:(k + 1) * 128], pg.b, yT.t[:, kc, a * 128:(a + 1) * 128], yT.b, self.ident, self.cst.b)
                if g == 0:
                    self.acp(yo.t[0:npart, 0:512], yo.b, pg.t[0:npart, :], pg.b)
                else:
                    self.cp(yo.t[0:npart, 512:1024], yo.b, pg.t[0:npart, :], pg.b)
            self.dma_out(dst_rows(a), yo.t[0:npart, :], yo.b)

    def mixer_proj(self, C, A, npart, pos0, sample):
        win = self.w["win"]
        xn = self.xn
        self.norm_x(8, C)
        self.dma_in(self.cs.t[:, :, 0:C], self.cs.b, self.w["rope"].ap()[:, :, pos0:pos0 + C].rearrange("s r c -> r s c"))
        cum = self.negI[0:npart, 0:npart] if sample else self.trif
        wa = self.wload8(win, C_A, 16)
        pa = self.G.next()
        for kc in range(KC):
            self.mm(pa.t[0:16, 0:C], pa.b, wa.t[:, kc, 0:16], wa.b, xn.t[:, kc, 0:C], xn.b, kc == 0, kc == KC - 1)
        self.cp(self.alow.t[0:16, 0:C], self.alow.b, pa.t[0:16, 0:C], pa.b)
        wk = self.wload8(win, C_K, 256)
        wv0 = self.wload8(win, C_V, 256)
        wv1 = self.wload8(win, C_V + 256, 256)
        for a in range(A):
            ca = slice(a * 128, a * 128 + npart)
            pz = self.XS
            self.mm(pz.t[0:npart, 0:256], pz.b, self.alow.t[:, ca], self.alow.b, self.waup.t[:], self.waup.b)
            L = self.Lt.next()
            self.actf(L.t[0:npart, :], L.b, pz.t[0:npart, 0:256], pz.b, AF.Exp, scale=-1.0)
            self.actf(L.t[0:npart, :], L.b, L.t[0:npart, :], L.b, AF.Ln, bias=1.0)
            if not sample:
                pb = self.XS
                self.mm(pb.t[0:npart, 0:256], pb.b, cum, self.cst.b, L.t[0:npart, :], L.b)
                enb = self.enb.next()
                self.actf(enb.t[0:npart, :], enb.b, pb.t[0:npart, 0:256], pb.b, AF.Exp, scale=-1.0)
            pbt = self.G.next()
            for h in range(4):
                self.mm(pbt.t[0:64, h * 128:h * 128 + npart], pbt.b, L.t[0:npart, h * 64:(h + 1) * 64], L.b, cum, self.cst.b)
            self.cp(self.bTs.t[:, :, ca], self.bTs.b, pbt.t[0:64, :].rearrange("p (h t) -> p h t", h=4)[:, :, 0:npart], pbt.b)
            pk = self.G.next()
            for kc in range(KC):
                self.mm(pk.t[0:npart, 0:256], pk.b, xn.t[:, kc, ca], xn.b, wk.t[:, kc, 0:256], wk.b, kc == 0, kc == KC - 1)
            if sample:
                self.cp(self.ktm.t[0:npart, a, :], self.ktm.b, pk.t[0:npart, 0:256], pk.b)
            else:
                self.tt(self.ktm.t[0:npart, a, :], self.ktm.b, pk.t[0:npart, 0:256], pk.b, enb.t[0:npart, :], enb.b, ALU.mult)
            pv = self.G.next()
            for kc in range(KC):
                self.mm(pv.t[0:npart, 0:256], pv.b, xn.t[:, kc, ca], xn.b, wv0.t[:, kc, 0:256], wv0.b, kc == 0, kc == KC - 1)
            for kc in range(KC):
                self.mm(pv.t[0:npart, 256:512], pv.b, xn.t[:, kc, ca], xn.b, wv1.t[:, kc, 0:256], wv1.b, kc == 0, kc == KC - 1)
            self.acp(self.vtm.t[0:npart, a, :], self.vtm.b, pv.t[0:npart, :], pv.b)
        wq = self.wload8(win, C_Q, 256)
        for h in range(4):
            ebq = self.eb.next()
            self.actf(ebq.t[:, 0:C], ebq.b, self.bTs.t[:, h, 0:C], self.bTs.b, AF.Exp, scale=1.0)
            if sample:
                self.cp(self.esf.t[:, h, 0:C], self.esf.b, ebq.t[:, 0:C], ebq.b)
            else:
                for a in range(A):
                    self.cp(self.elast.t[:, h, a:a + 1], self.elast.b, ebq.t[:, a * 128 + 127:a * 128 + 128], ebq.b)
            pq = self.G.next()
            for kc in range(KC):
                self.mm(pq.t[0:64, 0:C], pq.b, wq.t[:, kc, h * 64:(h + 1) * 64], wq.b, xn.t[:, kc, 0:C], xn.b, kc == 0, kc == KC - 1)
            if sample:
                self.ts(self.qsf.t[:, h, 0:C], self.qsf.b, pq.t[0:64, 0:C], pq.b, 0.125, ALU.mult)
            else:
                self.stt(self.qt[h].t[:, 0:C], self.qt[h].b, pq.t[0:64, 0:C], pq.b, 0.125, ebq.t[:, 0:C], ebq.b, ALU.mult, ALU.mult)
                ebk = self.eb.next()
                self.actf(ebk.t[:, 0:C], ebk.b, self.bTs.t[:, h, 0:C], self.bTs.b, AF.Exp, scale=-1.0)
                pk = self.G.next()
                for kc in range(KC):
                    self.mm(pk.t[0:64, 0:C], pk.b, wk.t[:, kc, h * 64:(h + 1) * 64], wk.b, xn.t[:, kc, 0:C], xn.b, kc == 0, kc == KC - 1)
                self.tt(self.kt[h].t[:, 0:C], self.kt[h].b, pk.t[0:64, 0:C], pk.b, ebk.t[:, 0:C], ebk.b, ALU.mult)
        for u in range(2):
            wg = self.wload8(win, C_G + u * 256, 256)
            for i in range(2):
                pg = self.G.next()
                for kc in range(KC):
                    self.mm(pg.t[:, 0:C], pg.b, wg.t[:, kc, i * 128:(i + 1) * 128], wg.b, xn.t[:, kc, 0:C], xn.b, kc == 0, kc == KC - 1)
                self.actf(self.sgT.t[:, u * 2 + i, 0:C], self.sgT.b, pg.t[:, 0:C], pg.b, AF.Silu)
        wc = self.wload8(win, C_CQ, 256)
        for i in range(2):
            pc = self.G.next()
            for kc in range(KC):
                self.mm(pc.t[:, 0:C], pc.b, wc.t[:, kc, i * 128:(i + 1) * 128], wc.b, xn.t[:, kc, 0:C], xn.b, kc == 0, kc == KC - 1)
            self.cp(self.cq.t[:, i, 0:C], self.cq.b, pc.t[:, 0:C], pc.b)
        sq = self.h
        self.actf(sq.t[:, 0:2, 0:C], sq.b, self.cq.t[:, :, 0:C], self.cq.b, AF.Square)
        self.rstd_from([sq.t[:, i, 0:C] for i in range(2)], sq.b, self.on256, C)
        for i in range(2):
            self.stt(self.cqn.t[:, i, 0:C], self.cqn.b, self.cq.t[:, i, 0:C], self.cq.b, self.smw.t[:, 33 + i:34 + i],
                     self.rstd.t[:, 0:C], self.rstd.b, ALU.mult, ALU.mult, sreads=[self.smw.b])
        wkv = self.wload8(win, C_KV, 192)
        self.ts(self.wkvr.t[:, :, 0:32], self.wkvr.b, wkv.t[:, 0:KC, 160:192], wkv.b, -1.0, ALU.mult)
        self.cp(self.wkvr.t[:, :, 32:64], self.wkvr.b, wkv.t[:, 0:KC, 128:160], wkv.b)
        pc = self.G.next()
        for kc in range(KC):
            self.mm(pc.t[:, 0:C], pc.b, wkv.t[:, kc, 0:128], wkv.b, xn.t[:, kc, 0:C], xn.b, kc == 0, kc == KC - 1)
        self.cp(self.ckvf.t[:, 0:C], self.ckvf.b, pc.t[:, 0:C], pc.b)
        self.actf(sq.t[:, 0, 0:C], sq.b, self.ckvf.t[:, 0:C], self.ckvf.b, AF.Square)
        self.rstd_from([sq.t[:, 0, 0:C]], sq.b, self.on128, C)
        self.stt(self.latf.t[:, 0:C], self.latf.b, self.ckvf.t[:, 0:C], self.ckvf.b, self.smw.t[:, 35:36],
                 self.rstd.t[:, 0:C], self.rstd.b, ALU.mult, ALU.mult, sreads=[self.smw.b])
        p1 = self.G.next()
        for kc in range(KC):
            self.mm(p1.t[0:64, 0:C], p1.b, wkv.t[:, kc, 128:192], wkv.b, xn.t[:, kc, 0:C], xn.b, kc == 0, kc == KC - 1)
        r1 = self.rt.next()
        self.tt(r1.t[:, 0:C], r1.b, p1.t[0:64, 0:C], p1.b, self.cs.t[:, 0, 0:C], self.cs.b, ALU.mult)
        p2 = self.G.next()
        for kc in range(KC):
            self.mm(p2.t[0:64, 0:C], p2.b, self.wkvr.t[:, kc, :], self.wkvr.b, xn.t[:, kc, 0:C], xn.b, kc == 0, kc == KC - 1)
        r2 = self.rt.next()
        self.tt(r2.t[:, 0:C], r2.b, p2.t[0:64, 0:C], p2.b, self.cs.t[:, 1, 0:C], self.cs.b, ALU.mult)
        self.tt(self.kpef.t[:, 0:C], self.kpef.b, r1.t[:, 0:C], r1.b, r2.t[:, 0:C], r2.b, ALU.add)

    def mla_q(self, h, C):
        cqn = self.cqn
        pn = self.G.next()
        for i in range(2):
            self.mm(pn.t[:, 0:C], pn.b, self.wuq.t[:, i, h * 192:h * 192 + 128], self.wuq.b, cqn.t[:, i, 0:C], cqn.b, i == 0, i == 1)
        qn = self.qn.next()
        self.acp(qn.t[:, 0:C], qn.b, pn.t[:, 0:C], pn.b)
        pl = self.G.next()
        self.mm(pl.t[:, 0:C], pl.b, self.wukT.t[:, h, :], self.wukT.b, qn.t[:, 0:C], qn.b)
        ql = self.ql.next()
        self.acp(ql.t[:, 0:C], ql.b, pl.t[:, 0:C], pl.b)
        p1 = self.G.next()
        for i in range(2):
            self.mm(p1.t[0:64, 0:C], p1.b, self.wuq.t[:, i, h * 192 + 128:h * 192 + 192], self.wuq.b, cqn.t[:, i, 0:C], cqn.b, i == 0, i == 1)
        r1 = self.rt.next()
        self.tt(r1.t[:, 0:C], r1.b, p1.t[0:64, 0:C], p1.b, self.cs.t[:, 0, 0:C], self.cs.b, ALU.mult)
        p2 = self.G.next()
        for i in range(2):
            self.mm(p2.t[0:64, 0:C], p2.b, self.wuqr.t[:, i, h, :], self.wuqr.b, cqn.t[:, i, 0:C], cqn.b, i == 0, i == 1)
        r2 = self.rt.next()
        self.tt(r2.t[:, 0:C], r2.b, p2.t[0:64, 0:C], p2.b, self.cs.t[:, 1, 0:C], self.cs.b, ALU.mult)
        qp = self.qpe.next()
        self.tt(qp.t[:, 0:C], qp.b, r1.t[:, 0:C], r1.b, r2.t[:, 0:C], r2.b, ALU.add)
        return ql, qp

    def gla_out(self, h, src, srcb, C):
        sq = self.h
        self.actf(sq.t[:, 0, 0:C], sq.b, src, srcb, AF.Square)
        self.rstd_from([sq.t[:, 0, 0:C]], sq.b, self.on128, C)
        r = self.rl
        self.stt(r.t[:, 0:C], r.b, src, srcb, self.smw.t[:, 32:33], self.rstd.t[:, 0:C], self.rstd.b,
                 ALU.mult, ALU.mult, sreads=[self.smw.b])
        self.tt(self.ym.t[:, h, 0:C], self.ym.b, r.t[:, 0:C], r.b, self.sgT.t[:, h, 0:C], self.sgT.b, ALU.mult)

    def out_proj(self, C):
        wout = self.w["wout"]
        for u in range(4):
            wo = self.wload8(wout, u * 256, 256)
            for i in range(2):
                dc = u * 2 + i
                po = self.O.next()
                for kc in range(KC):
                    self.mm(po.t[:, 0:C], po.b, wo.t[:, kc, i * 128:(i + 1) * 128], wo.b, self.ym.t[:, kc, 0:C], self.ym.b, kc == 0, kc == KC - 1)
                self.tt(self.xT.t[:, dc, 0:C], self.xT.b, po.t[:, 0:C], po.b, self.xT.t[:, dc, 0:C], self.xT.b, ALU.add)

    def prompt_tile(self, ti):
        s, t = ti // 4, ti % 4
        C = NT
        tok0 = t * NT
        w = self.w
        self.load_x(lambda a: self.xp.ap()[s, tok0 + a * 128:tok0 + (a + 1) * 128, :], 4, 128)
        self.ffn(w["f1g"], w["f1u"], w["f1d"], 0, C)
        if t == 0:
            for h in range(4):
                self.memset(self.S[h].t[:], self.S[h].b, 0.0)
                self.memset(self.Sb[h].t[:], self.Sb[h].b, 0.0)
        self.mixer_proj(C, 4, 128, tok0, sample=False)
        self.cp(self.latT_t[:, tok0:tok0 + C], self.bA, self.latf.t[:, 0:C], self.latf.b)
        self.cp(self.kpe_t[:, tok0:tok0 + C], self.bC, self.kpef.t[:, 0:C], self.kpef.b)
        pt_ = self.G.next()
        for a in range(4):
            self.tr(pt_.t[:, a * 128:(a + 1) * 128], pt_.b, self.latf.t[:, a * 128:(a + 1) * 128], self.latf.b, self.ident, self.cst.b)
        ot = self.otok.next()
        self.acp(ot.t[:].rearrange("p a c -> p (a c)"), ot.b, pt_.t[:], pt_.b)
        self.cp(self.latM_t[:, t * 4:(t + 1) * 4, :], self.bB, ot.t[:], ot.b)
        self.dma_out(self.kvp.ap()[s, tok0:tok0 + C, :].rearrange("(a p) c -> p a c", p=128), ot.t[:], ot.b)
        pt2 = self.G.next()
        for a in range(4):
            self.tr(pt2.t[:, a * 64:(a + 1) * 64], pt2.b, self.kpef.t[:, a * 128:(a + 1) * 128], self.kpef.b, self.ident[0:64, 0:64], self.cst.b)
        ot2 = self.otok2.next()
        self.acp(ot2.t[:].rearrange("p a c -> p (a c)"), ot2.b, pt2.t[:, 0:256], pt2.b)
        self.dma_out(self.pep.ap()[s, tok0:tok0 + C, :].rearrange("(a p) c -> p a c", p=128), ot2.t[:], ot2.b)
        gitems = [(a, h) for a in range(4) for h in range(4)]
        pos = {}

        def gla_A(a, h):
            ca = slice(a * 128, (a + 1) * 128)
            pa = self.G.next()
            self.mm(pa.t[:, 0:128], pa.b, self.kt[h].t[:, ca], self.kt[h].b, self.qt[h].t[:, ca], self.qt[h].b)
            am = self.atm.next()
            self.tt(am.t[:], am.b, pa.t[:, 0:128], pa.b, self.trib.t[:], self.trib.b, ALU.mult)
            return am

        def gla_rest(a, h, am):
            ca = slice(a * 128, (a + 1) * 128)
            hc = slice(h * 128, (h + 1) * 128)
            if h == 0:
                pos[a] = self.O.next()
            po = pos[a]
            self.mm(po.t[:, hc], po.b, self.vtm.t[:, a, hc], self.vtm.b, am.t[:], am.b, True, False)
            self.mm(po.t[:, hc], po.b, self.Sb[h].t[:], self.Sb[h].b, self.qt[h].t[:, ca], self.qt[h].b, False, True)
            pS = self.G.next()
            self.mm(pS.t[0:64, 0:128], pS.b, self.ktm.t[:, a, h * 64:(h + 1) * 64], self.ktm.b, self.vtm.t[:, a, hc], self.vtm.b)
            el = self.elast.t[:, h, a:a + 1]
            self.ts(self.S[h].t[:], self.S[h].b, self.S[h].t[:], self.S[h].b, el, ALU.mult, sreads=[self.elast.b])
            self.stt(self.S[h].t[:], self.S[h].b, pS.t[0:64, 0:128], pS.b, el, self.S[h].t[:], self.S[h].b, ALU.mult, ALU.add,
                     sreads=[self.elast.b])
            self.acp(self.Sb[h].t[:], self.Sb[h].b, self.S[h].t[:], self.S[h].b)
            if h == 3:
                sq = self.h
                self.actf(sq.t[:, 0, 0:C], sq.b, po.t[:, 0:C], po.b, AF.Square)
                self.rstd_from([sq.t[:, 0, 0:C]], sq.b, self.on128, C)
                r = self.rl
                self.stt(r.t[:, 0:C], r.b, po.t[:, 0:C], po.b, self.smw.t[:, 32:33], self.rstd.t[:, 0:C], self.rstd.b,
                         ALU.mult, ALU.mult, sreads=[self.smw.b])
                self.tt(self.ym.t[:, 0:4, ca], self.ym.b, r.t[:, 0:C].rearrange("p (h t) -> p h t", h=4), r.b,
                        self.sgT.t[:, 0:4, ca], self.sgT.b, ALU.mult)

        qs = []
        am_next = gla_A(*gitems[0])
        for i, (a, h) in enumerate(gitems):
            am_cur = am_next
            if i + 1 < len(gitems):
                am_next = gla_A(*gitems[i + 1])
            gla_rest(a, h, am_cur)
            if h == 1:
                qs.append(self.mla_q(a, C))
        if t == 3:
            for h in range(4):
                self.dma_out(self.glap.ap()[s, h], self.S[h].t[:], self.S[h].b)
        nkb = 4 * t + 4
        items = [(h, kb) for h in range(4) for kb in range(nkb)]
        acc = {}

        def emit_scores(h, kb):
            ql, qp = qs[h]
            r = kb - 4 * t
            c0 = 128 * r if r > 0 else 0
            N = C - c0
            ps = self.G.next()
            ks = slice(kb * 128, (kb + 1) * 128)
            self.mm(ps.t[:, 0:N], ps.b, self.latT_t[:, ks], self.bA, ql.t[:, c0:C], ql.b, True, False)
            self.mm(ps.t[:, 0:N], ps.b, self.kpe_t[:, ks], self.bC, qp.t[:, c0:C], qp.b, False, True)
            return ps, r, c0, N

        def emit_pv(h, kb, info):
            ps, r, c0, N = info
            if kb == 0:
                acc[h] = (self.O.next(), self.O.next())
            po, pl = acc[h]
            pt = self.pt.next()
            self.actf(pt.t[:, 0:N], pt.b, ps.t[:, 0:N], ps.b, AF.Exp, scale=MLA_SCALE)
            if r >= 0:
                self.tt(pt.t[:, 0:128], pt.b, pt.t[:, 0:128], pt.b, self.trib.t[:], self.trib.b, ALU.mult)
            self.mm(po.t[:, c0:C], po.b, self.latM_t[:, kb, :], self.bB, pt.t[:, 0:N], pt.b, kb == 0, kb == nkb - 1)
            self.mm(pl.t[:, c0:C], pl.b, self.on1.t[:], self.on1.b, pt.t[:, 0:N], pt.b, kb == 0, kb == nkb - 1)
            if kb == nkb - 1:
                self.actf(self.rl.t[:, 0:C], self.rl.b, pl.t[:, 0:C], pl.b, AF.Ln)
                self.actf(self.rl.t[:, 0:C], self.rl.b, self.rl.t[:, 0:C], self.rl.b, AF.Exp, scale=-1.0)
                self.tt(self.ol.t[:, 0:C], self.ol.b, po.t[:, 0:C], po.b, self.rl.t[:, 0:C], self.rl.b, ALU.mult)
                py = self.G.next()
                self.mm(py.t[:, 0:C], py.b, self.wuv.t[:, h * 128:(h + 1) * 128], self.wuv.b, self.ol.t[:, 0:C], self.ol.b)
                self.acp(self.ym.t[:, 4 + h, 0:C], self.ym.b, py.t[:, 0:C], py.b)

        info = emit_scores(*items[0])
        for i, (h, kb) in enumerate(items):
            nxt = emit_scores(*items[i + 1]) if i + 1 < len(items) else None
            emit_pv(h, kb, info)
            info = nxt
        self.out_proj(C)
        self.ffn(w["f2g"], w["f2u"], w["f2d"], 16, C)
        self.norm_x(24, C, inplace=True)
        self.store_y(lambda a: self.yp.ap()[s, tok0 + a * 128:tok0 + (a + 1) * 128, :], 4, 128)

    def sample_tile(self):
        C = NSEQ_S
        w = self.w
        self.load_x(lambda a: self.xs.ap(), 1, 16)
        self.ffn(w["f1g"], w["f1u"], w["f1d"], 0, C)
        self.mixer_proj(C, 1, 16, SEQ, sample=True)
        pt_ = self.G.next()
        self.tr(pt_.t[:, 0:128], pt_.b, self.latf.t[:, 0:128], self.latf.b, self.ident, self.cst.b)
        ot = self.otok.next()
        self.acp(ot.t[0:16, 0, :], ot.b, pt_.t[0:16, 0:128], pt_.b)
        self.dma_out(self.kvs.ap(), ot.t[0:16, 0, :], ot.b)
        pt2 = self.G.next()
        self.tr(pt2.t[:, 0:64], pt2.b, self.kpef.t[:, 0:128], self.kpef.b, self.ident[0:64, 0:64], self.cst.b)
        ot2 = self.otok2.next()
        self.acp(ot2.t[0:16, 0, :], ot2.b, pt2.t[0:16, 0:64], pt2.b)
        self.dma_out(self.pes.ap(), ot2.t[0:16, 0, :], ot2.b)
        self.decode_attention()
        self.out_proj(C)
        self.ffn(w["f2g"], w["f2u"], w["f2d"], 16, C)
        self.norm_x(24, C, inplace=True)
        self.store_y(lambda a: self.ys.ap(), 1, 16)

    def gla_sample_seq(self, b):
        pgl = self.XS
        s0 = self.s0.next()
        self.dma_in(s0.t[:], s0.b, self.sgla.ap()[b].rearrange("h d v -> d h v"))
        km = self.kmask.next()
        self.ts(km.t[:], km.b, self.ktm.t[0:16, 0, :], self.ktm.b, self.cst.t[0:16, b:b + 1], ALU.mult, sreads=[self.cst.b])
        pd = self.G.next()
        for h in range(4):
            self.mm(pd.t[0:64, h * 128:(h + 1) * 128], pd.b, km.t[:, h * 64:(h + 1) * 64], km.b,
                    self.vtm.t[0:16, 0, h * 128:(h + 1) * 128], self.vtm.b)
        s1 = self.s1
        for h in range(4):
            self.stt(s1.t[:, h, :], s1.b, s0.t[:, h, :], s0.b, self.esf.t[:, h, b:b + 1], pd.t[0:64, h * 128:(h + 1) * 128], pd.b,
                     ALU.mult, ALU.add, sreads=[self.esf.b])
        self.dma_out(self.glas.ap()[b].rearrange("h d v -> d h v"), s1.t[:], s1.b)
        for h in range(4):
            self.mm(pgl.t[:, h * 18 + b:h * 18 + b + 2], pgl.b, s1.t[:, h, :], s1.b, self.qsf.t[:, h, b:b + 2], self.qsf.b)

    def gla_sample_finish(self):
        pgl = self.XS
        self.cp(self.gsb.t[:].rearrange("p (h t) -> p h t", h=4), self.gsb.b,
                pgl.t[:, 0:72].rearrange("p (h t) -> p h t", h=4)[:, :, 0:16], pgl.b)
        for h in range(4):
            self.gla_out(h, self.gsb.t[:, h * 16:(h + 1) * 16], self.gsb.b, NSEQ_S)

    def bcreg(self, e):
        if getattr(self, "_bcreg", None) is None:
            self._bcreg = e.alloc_register("gather_bound")
            e.reg_mov(self._bcreg, NPOOL * 8 - 1)
        return self._bcreg

    def decode_attention(self):
        P = self.P
        C = NSEQ_S
        ar = self.arena.t
        gl = [(ar[:, 0:2048].rearrange("p (t c) -> p t c", c=128), ar[:, 0:2048], self.bA),
              (ar[:, 2048:4096].rearrange("p (t c) -> p t c", c=128), ar[:, 2048:4096], self.bB)]
        gp = [(ar[:, 4096:5120].rearrange("p (t c) -> p t c", c=64), ar[:, 4096:5120], self.bC),
              (ar[:, 5120:6144].rearrange("p (t c) -> p t c", c=64), ar[:, 5120:6144], self.bC)]
        self.dma_in(self.ptls.t[:], self.ptls.b, self.ptl.ap())
        self.ts(self.idx.t[:], self.idx.b, self.ptls.t[:], self.ptls.b, 8.0, ALU.mult, self.cst.t[:, 384:385], ALU.add, sreads=[self.cst.b])
        ckv_v = self.ckv.ap().rearrange("n (g t) c -> (n g) (t c)", t=16)
        cpe_v = self.cpe.ap().rearrange("n (g t) c -> (n g) (t c)", t=16)
        for h in range(4):
            ql, qp = self.mla_q(h, C)
            self.cp(self.QL.t[:, :, h], self.QL.b, ql.t[:, 0:C], ql.b)
            self.cp(self.QP.t[0:64, :, h], self.QP.b, qp.t[:, 0:C], qp.b)
        P.dma(lambda e: e.dma_start(out=self.QP.t[64:128, :, :], in_=self.QP.t[0:64, :, :]), writes=[self.QP.b], q="sp", no_waw=False)
        pnum, pden = self.O.next(), self.O.next()
        NB_ = 2
        NG = NSEQ_S * 4

        def gather(gi):
            glv, glf, glb = gl[gi % NB_]
            gpv, gpf, gpb = gp[gi % NB_]
            P.dma(lambda e: e.indirect_dma_start(
                out=glf, out_offset=None, in_=ckv_v,
                in_offset=bass.IndirectOffsetOnAxis(ap=self.idx.t[:, gi:gi + 1], axis=0),
                bounds_check=self.bcreg(e), oob_is_err=False),
                reads=[self.idx.b], writes=[glb], q="pool", no_waw=False)
            P.dma(lambda e: e.indirect_dma_start(
                out=gpf, out_offset=None, in_=cpe_v,
                in_offset=bass.IndirectOffsetOnAxis(ap=self.idx.t[:, gi:gi + 1], axis=0),
                bounds_check=self.bcreg(e), oob_is_err=False),
                reads=[self.idx.b], writes=[gpb], q="pool", no_waw=False)

        stage = {}
        psts = {}
        ptss = {}

        def emit_T(qi):
            gi, quad = qi // 4, qi % 4
            glv, glf, glb = gl[gi % NB_]
            gpv, gpf, gpb = gp[gi % NB_]
            xa, xb = self.XB.next(), self.XB.next()
            for k in range(4):
                self.tr(xa.t[:, k * 128:(k + 1) * 128], xa.b, glv[:, quad * 4 + k, :], glb, self.identb.t[:], self.identb.b)
            for j in range(2):
                c0_ = (quad * 4 + 2 * j) * 64
                self.tr(xb.t[:, j * 128:(j + 1) * 128], xb.b, gpf[:, c0_:c0_ + 128], gpb, self.identb.t[:], self.identb.b)
            lt4 = self.lt4.next()
            self.acp(lt4.t[:], lt4.b, xa.t[:], xa.b)
            pt4 = self.pt4.next()
            self.P.dve(lambda e: e.tensor_copy(out=pt4.t[:], in_=xb.t[:, 0:256]), reads=[xb.b, lt4.b], writes=[pt4.b])
            stage[qi] = (lt4, pt4)

        def emit_S(qi):
            gi, quad = qi // 4, qi % 4
            b, g = gi // 4, gi % 4
            if b not in psts:
                psts[b] = self.G.next()
            pst = psts[b]
            lt4, pt4 = stage.pop(qi)
            for k in range(4):
                kbi = g * 16 + quad * 4 + k
                self.mm(pst.t[:, kbi * 4:(kbi + 1) * 4], pst.b, lt4.t[:, k * 128:(k + 1) * 128], lt4.b, self.QL.t[:, b, :], self.QL.b, True, False)
                hp = slice((k % 2) * 64, (k % 2) * 64 + 64)
                self.mm(pst.t[:, kbi * 4:(kbi + 1) * 4], pst.b, pt4.t[hp, (k // 2) * 128:(k // 2 + 1) * 128], pt4.b,
                        self.QP.t[hp, b, :], self.QP.b, False, True)
            if quad == 3:
                pts = self.pts.next()
                self.actf(pts.t[:].rearrange("p t h -> p (t h)"), pts.b, pst.t[:, g * 64:(g + 1) * 64], pst.b, AF.Exp, scale=MLA_SCALE)
                psm = self.ptsum.next()
                P.dve(lambda e: e.tensor_reduce(out=psm.t[:], in_=pts.t[:].rearrange("p t h -> p h t"),
                                                axis=mybir.AxisListType.X, op=ALU.add),
                      reads=[pts.b], writes=[psm.b])
                ptss[gi] = (pts, psm)

        def emit_PV(gi):
            b, g = gi // 4, gi % 4
            glv, glf, glb = gl[gi % NB_]
            pts, psm = ptss.pop(gi)
            for tt_ in range(16):
                self.mm(pnum.t[:, b * 4:(b + 1) * 4], pnum.b, glv[:, tt_, :], glb, pts.t[:, tt_, :], pts.b,
                        g == 0 and tt_ == 0, g == 3 and tt_ == 15)
            self.mm(pden.t[:, b * 4:(b + 1) * 4], pden.b, self.onf.t[:], self.onf.b, psm.t[:], psm.b, g == 0, g == 3)

        NQ = NG * 4
        gather(0)
        for qi in range(NQ + 1):
            if qi < NQ:
                if qi % 4 == 0 and qi // 4 + 1 < NG:
                    pass
                emit_T(qi)
            if qi >= 1:
                emit_S(qi - 1)
                if (qi - 1) % 4 == 3:
                    gdone = (qi - 1) // 4
                    emit_PV(gdone)
                    if gdone + 2 < NG:
                        gather(gdone + 2)
            if qi == 0 and NG > 1:
                gather(1)
            if qi % 16 == 6 and qi < NQ:
                self.gla_sample_seq(qi // 16)
        self.gla_sample_finish()
        self.cp(self.num.t[:].rearrange("p b h -> p (b h)"), self.num.b, pnum.t[:, 0:64], pnum.b)
        self.cp(self.den.t[:].rearrange("p b h -> p (b h)"), self.den.b, pden.t[:, 0:64], pden.b)
        for h in range(4):
            self.tt(self.prod.t[:], self.prod.b, self.QL.t[:, :, h], self.QL.b, self.latf.t[:, 0:C], self.latf.b, ALU.mult)
            self.tt(self.prod2.t[:], self.prod2.b, self.QP.t[0:64, :, h], self.QP.b, self.kpef.t[:, 0:C], self.kpef.b, ALU.mult)
            psn = self.G.next()
            self.mm(psn.t[:, 0:C], psn.b, self.onf.t[:], self.onf.b, self.prod.t[:], self.prod.b, True, False)
            self.mm(psn.t[:, 0:C], psn.b, self.onf.t[0:64, :], self.onf.b, self.prod2.t[:], self.prod2.b, False, True)
            self.actf(self.pnew.t[:], self.pnew.b, psn.t[:, 0:C], psn.b, AF.Exp, scale=MLA_SCALE)
            self.tt(self.tmp16.t[:], self.tmp16.b, self.pnew.t[:], self.pnew.b, self.latf.t[:, 0:C], self.latf.b, ALU.mult)
            self.tt(self.num.t[:, :, h], self.num.b, self.num.t[:, :, h], self.num.b, self.tmp16.t[:], self.tmp16.b, ALU.add)
            self.tt(self.den.t[:, :, h], self.den.b, self.den.t[:, :, h], self.den.b, self.pnew.t[:], self.pnew.b, ALU.add)
            self.recip(self.tmp16.t[:], self.tmp16.b, self.den.t[:, :, h], self.den.b)
            self.tt(self.ol.t[:, 0:C], self.ol.b, self.num.t[:, :, h], self.num.b, self.tmp16.t[:], self.tmp16.b, ALU.mult)
            py = self.G.next()
            self.mm(py.t[:, 0:C], py.b, self.wuv.t[:, h * 128:(h + 1) * 128], self.wuv.b, self.ol.t[:, 0:C], self.ol.b)
            self.acp(self.ym.t[:, 4 + h, 0:C], self.ym.b, py.t[:, 0:C], py.b)


def build_program(do_sample=True, n_tiles=8):
    nc = bass.Bass("TRN2", target_bir_lowering=False)
    with ExitStack() as st:
        P = Prog(nc, st)
        B = Builder(nc, P, do_sample=do_sample, n_tiles=n_tiles)
        B.alloc()
        build_program.sbuf_left = nc.sbuf_bytes_remaining
        B.setup()
        for ti in range(n_tiles):
            B.prompt_tile(ti)
            if B.first_pass:
                B.end_first_pass()
        if do_sample:
            B.sample_tile()
        build_program.n_ops = len(P.ops)
        P.emit()
    return nc


def _consts():
    c = np.zeros((128, 448), np.float32)
    p = np.arange(128)
    c[:, 0:128] = np.eye(128, dtype=np.float32)
    tri = (p[:, None] <= p[None, :]).astype(np.float32)
    c[:, 128:256] = tri
    c[:, 256:384] = tri * np.float32(-1.0 / 16.0)
    c[:, 384] = (p % 8).astype(np.float32)
    c[0:16, 401:417] = np.eye(16, dtype=np.float32) * np.float32(-1.0 / 16.0)
    return c


def _rope_tables():
    half = 32
    inv_freq = np.power(np.float32(10000.0), -np.arange(half, dtype=np.float32) / np.float32(half)).astype(np.float32)
    pos = np.concatenate([np.arange(SEQ, dtype=np.float32), np.full(16, 8192.0, np.float32)])
    ang = (pos[None, :] * inv_freq[:, None]).astype(np.float32)
    cos = np.cos(ang).astype(np.float32)
    sin = np.sin(ang).astype(np.float32)
    return np.stack([np.concatenate([cos, cos], 0), np.concatenate([sin, sin], 0)], 0)


_NC_CACHE = {}


def kernel(x_prompt, x_sample, cache_kv, cache_pe, state_gla, page_table,
           ffn1_norm_w, ffn1_w_gate, ffn1_w_up, ffn1_w_down, mix_norm_w, w_in,
           gla_w_a_up, gla_b_a, gla_norm_w, mla_q_norm_w, mla_w_uq, mla_kv_norm_w,
           mla_w_uk, mla_w_uv, w_out, ffn2_norm_w, ffn2_w_gate, ffn2_w_up, ffn2_w_down,
           final_norm_w, _do_sample=True, _n_tiles=8, _trace=False):
    f = lambda a: np.ascontiguousarray(np.asarray(a, dtype=np.float32))
    key = (_do_sample, _n_tiles)
    if key not in _NC_CACHE:
        _NC_CACHE[key] = build_program(_do_sample, _n_tiles)
    nc = _NC_CACHE[key]
    col = lambda v: np.asarray(v, np.float32).reshape(-1, 128).T
    smallw = np.zeros((128, 36), np.float32)
    smallw[:, 0:8] = col(ffn1_norm_w[0])
    smallw[:, 8:16] = col(mix_norm_w[0])
    smallw[:, 16:24] = col(ffn2_norm_w[0])
    smallw[:, 24:32] = col(final_norm_w)
    smallw[:, 32:33] = col(gla_norm_w[0])
    smallw[:, 33:35] = col(mla_q_norm_w[0])
    smallw[:, 35:36] = col(mla_kv_norm_w[0])
    shared = {
        "f1g": f(ffn1_w_gate[0]), "f1u": f(ffn1_w_up[0]), "f1d": f(ffn1_w_down[0]), "win": f(w_in[0]),
        "waup": f(np.concatenate([np.asarray(gla_w_a_up[0]), np.asarray(gla_b_a[0])[None, :]], 0)),
        "wuq": f(mla_w_uq[0]), "wuk": f(np.asarray(mla_w_uk[0]).reshape(128, 512)),
        "wuv": f(np.asarray(mla_w_uv[0]).reshape(128, 512)), "wout": f(w_out[0]),
        "f2g": f(ffn2_w_gate[0]), "f2u": f(ffn2_w_up[0]), "f2d": f(ffn2_w_down[0]),
        "smallw": smallw, "consts": _consts(), "rope": _rope_tables(),
    }
    xp = np.asarray(x_prompt, np.float32)
    xs = np.asarray(x_sample, np.float32)
    sg = np.asarray(state_gla, np.float32)
    ptab = np.asarray(page_table, np.int32)
    pidx = np.arange(128) // 8
    ckv_full = f(cache_kv[0]) if _do_sample else None
    cpe_full = f(cache_pe[0]) if _do_sample else None
    in_maps = []
    for c in range(NCORES):
        m = dict(shared)
        m["xp"] = np.ascontiguousarray(xp[2 * c:2 * c + 2])
        m["xs"] = np.ascontiguousarray(xs[16 * c:16 * c + 16, 0, :])
        if not _do_sample:
            in_maps.append(m)
            continue
        m["ckv"] = ckv_full
        m["cpe"] = cpe_full
        m["sgla"] = np.ascontiguousarray(sg[0, 16 * c:16 * c + 16])
        pt_c = ptab[16 * c:16 * c + 16].reshape(16, 4, 16)
        m["ptl"] = np.ascontiguousarray(pt_c[:, :, pidx].transpose(2, 0, 1).reshape(128, 64)).astype(np.int32)
        in_maps.append(m)
    res = run_bass_kernel_spmd(nc, in_maps, core_ids=list(range(NCORES)), trace=_trace)
    R = res.results
    cat = lambda k: np.concatenate([np.asarray(r[k]) for r in R], axis=0)
    y_prompt = cat("yp")
    y_sample = cat("ys")[:, None, :]
    outs = (y_prompt, y_sample, cat("kvp")[None], cat("pep")[None], cat("glap")[None],
            cat("kvs")[None, :, None, :], cat("pes")[None, :, None, :], cat("glas")[None])
    outs = tuple(np.ascontiguousarray(o, dtype=np.float32) for o in outs)
    if _trace:
        kernel.last_exec_ns = res.exec_time_ns
    return outs
```

```python
import math
import numpy as np
from contextlib import ExitStack
import concourse.bass as bass
import concourse.mybir as mybir
from concourse.bass_utils import run_bass_kernel_spmd

F32 = mybir.dt.float32
BF16 = mybir.dt.bfloat16
I32 = mybir.dt.int32
AF = mybir.ActivationFunctionType
ALU = mybir.AluOpType

NCORES = 8
D = 1024
KC = 8
DFF = 2816
SEQ = 2048
NT = 512
NPOOL = 10240
NSEQ_S = 16
EPS = 1e-6
MLA_SCALE = 1.0 / math.sqrt(192.0)
NUNITS = 80
SAME_ENGINE_SYNC = True

C_Q, C_K, C_V, C_G, C_A, C_CQ, C_KV = 0, 256, 512, 1024, 1536, 1552, 1808


class Buf:
    __slots__ = ("name", "last_w", "readers", "war", "sem_in", "n_in", "sem_out", "n_out")

    def __init__(self, name):
        self.name = name
        self.last_w = None
        self.readers = []
        self.war = []
        self.sem_in = None
        self.n_in = 0
        self.sem_out = None
        self.n_out = 0


class Op:
    __slots__ = ("eng", "fn", "deps", "dma", "sem", "count", "needs_inc", "seq")

    def __init__(self, eng, fn, dma):
        self.eng = eng
        self.fn = fn
        self.deps = set()
        self.dma = dma
        self.sem = None
        self.count = 0
        self.needs_inc = False
        self.seq = 0


class Prog:
    ENGS = ("pe", "act", "dve", "pool", "sp")

    def __init__(self, nc, stack):
        self.nc = nc
        self.stack = stack
        self.ops = []
        self.nsem = 0
        self.stores = []

    def new_sem(self, name):
        self.nsem += 1
        return self.stack.enter_context(self.nc.semaphore(f"s{self.nsem}_{name}"))

    def sb(self, name, shape, dtype):
        return self.stack.enter_context(self.nc.sbuf_tensor(name, list(shape), dtype))

    def ps(self, name, shape, dtype=F32):
        return self.stack.enter_context(self.nc.psum_tensor(name, list(shape), dtype))

    def op(self, eng, fn, reads=(), writes=(), dma=False, no_waw=False):
        o = Op(eng, fn, dma)
        for b in reads:
            if b.last_w is not None:
                o.deps.add(b.last_w)
        for b in writes:
            for r in b.readers:
                o.deps.add(r)
            if no_waw:
                for r in b.war:
                    o.deps.add(r)
            elif b.last_w is not None:
                o.deps.add(b.last_w)
        o.deps.discard(o)
        if dma:
            if writes:
                b = writes[0]
                if b.sem_in is None:
                    b.sem_in = self.new_sem("i_" + b.name)
                b.n_in += 16
                o.sem, o.count = b.sem_in, b.n_in
            else:
                b = reads[0]
                if b.sem_out is None:
                    b.sem_out = self.new_sem("o_" + b.name)
                b.n_out += 16
                o.sem, o.count = b.sem_out, b.n_out
                self.stores.append(o)
        for b in reads:
            b.readers.append(o)
        for b in writes:
            if b.readers or not no_waw:
                b.war = [r for r in b.readers if r is not o]
            b.readers = []
            b.last_w = o
        for d in o.deps:
            if not d.dma and (d.eng != eng or dma or (SAME_ENGINE_SYNC and eng != "pe")):
                d.needs_inc = True
        self.ops.append(o)
        return o

    def pe(self, fn, reads=(), writes=()):
        return self.op("pe", fn, reads, writes)

    def act(self, fn, reads=(), writes=()):
        return self.op("act", fn, reads, writes)

    def dve(self, fn, reads=(), writes=()):
        return self.op("dve", fn, reads, writes)

    def pool(self, fn, reads=(), writes=()):
        return self.op("pool", fn, reads, writes)

    def dma(self, fn, reads=(), writes=(), q="sp", no_waw=True):
        return self.op(q, fn, reads, writes, dma=True, no_waw=no_waw)

    def emit(self):
        nc = self.nc
        eng_sem = {e: self.new_sem("eng_" + e) for e in self.ENGS}
        cnt = {e: 0 for e in self.ENGS}
        for o in self.ops:
            if not o.dma and o.needs_inc:
                cnt[o.eng] += 1
                o.seq = cnt[o.eng]
        per = {e: [o for o in self.ops if o.eng == e] for e in self.ENGS}
        final_waits = {}
        for o in self.stores:
            k = id(o.sem)
            if k not in final_waits or final_waits[k][1] < o.count:
                final_waits[k] = (o.sem, o.count)

        def run(engname, engobj):
            waited = {}
            for o in per[engname]:
                need = {}
                for d in o.deps:
                    if d.dma:
                        s, c = d.sem, d.count
                    else:
                        if d.eng == engname and not o.dma and (engname == "pe" or not SAME_ENGINE_SYNC):
                            continue
                        s, c = eng_sem[d.eng], d.seq
                    k = id(s)
                    if k not in need or need[k][1] < c:
                        need[k] = (s, c)
                for k, (s, c) in need.items():
                    if waited.get(k, 0) >= c:
                        continue
                    engobj.wait_ge(s, c)
                    waited[k] = c
                inst = o.fn(engobj)
                if o.dma:
                    inst.then_inc(o.sem, 16)
                elif o.needs_inc:
                    inst.then_inc(eng_sem[engname], 1)
            if engname == "sp":
                for k, (s, c) in final_waits.items():
                    if waited.get(k, 0) < c:
                        engobj.wait_ge(s, c)

        with nc.Block() as block:
            @block.tensor
            def _(e):
                run("pe", e)

            @block.scalar
            def _(e):
                run("act", e)

            @block.vector
            def _(e):
                run("dve", e)

            @block.gpsimd
            def _(e):
                run("pool", e)

            @block.sync
            def _(e):
                run("sp", e)


class T:
    def __init__(self, P, name, shape, dtype, psum=False):
        self.t = P.ps("P_" + name, shape, dtype) if psum else P.sb("S_" + name, shape, dtype)
        self.b = Buf(name)


class View:
    def __init__(self, ap, b):
        self.t = ap
        self.b = b


class Rot:
    def __init__(self, items):
        self.items = items
        self.i = 0

    def next(self):
        x = self.items[self.i % len(self.items)]
        self.i += 1
        return x


class Builder:
    def __init__(self, nc, P, do_sample=True, n_tiles=8):
        self.nc = nc
        self.P = P
        self.do_sample = do_sample
        self.n_tiles = n_tiles
        dt = nc.dram_tensor
        I, O = "ExternalInput", "ExternalOutput"
        self.xp = dt("xp", [2, SEQ, D], F32, kind=I)
        self.xs = dt("xs", [NSEQ_S, D], F32, kind=I)
        if do_sample:
            self.ckv = dt("ckv", [NPOOL, 128, 128], F32, kind=I)
            self.cpe = dt("cpe", [NPOOL, 128, 64], F32, kind=I)
            self.sgla = dt("sgla", [NSEQ_S, 4, 64, 128], F32, kind=I)
            self.ptl = dt("ptl", [128, NSEQ_S * 4], I32, kind=I)
        self.w = {}
        for name, shp in (("f1g", [D, DFF]), ("f1u", [D, DFF]), ("f1d", [DFF, D]), ("win", [D, 2000]),
                          ("waup", [17, 256]), ("wuq", [256, 768]), ("wuk", [128, 512]), ("wuv", [128, 512]),
                          ("wout", [D, D]), ("f2g", [D, DFF]), ("f2u", [D, DFF]), ("f2d", [DFF, D]),
                          ("smallw", [128, 36]), ("consts", [128, 448]), ("rope", [2, 64, SEQ + 16])):
            self.w[name] = dt(name, shp, F32, kind=I)
        self.scr = dt("wscr", [NUNITS, 128, 11 * 256], BF16, kind="Internal")
        self.scrb = Buf("wscr")
        self.units = {}
        self.first_pass = True
        self.yp = dt("yp", [2, SEQ, D], F32, kind=O)
        self.ys = dt("ys", [NSEQ_S, D], F32, kind=O)
        self.kvp = dt("kvp", [2, SEQ, 128], F32, kind=O)
        self.pep = dt("pep", [2, SEQ, 64], F32, kind=O)
        self.glap = dt("glap", [2, 4, 64, 128], F32, kind=O)
        self.kvs = dt("kvs", [NSEQ_S, 128], F32, kind=O)
        self.pes = dt("pes", [NSEQ_S, 64], F32, kind=O)
        self.glas = dt("glas", [NSEQ_S, 4, 64, 128], F32, kind=O)

    def alloc(self):
        P = self.P
        mk = lambda n, s, d=F32: T(P, n, s, d)
        self.cst = mk("cst", [128, 448])
        self.smw = mk("smw", [128, 36])
        self.identb = mk("identb", [128, 128], BF16)
        self.trib = mk("trib", [128, 128], BF16)
        self.on1024 = mk("on1024", [128, 128], BF16)
        self.on256 = mk("on256", [128, 128], BF16)
        self.on128 = mk("on128", [128, 128], BF16)
        self.on1 = mk("on1", [128, 128], BF16)
        self.onf = mk("onf", [128, 128])
        self.waup = mk("waup", [17, 256], BF16)
        self.wuq = mk("wuq", [128, 2, 768], BF16)
        self.wuqr = mk("wuqr", [128, 2, 4, 64], BF16)
        self.wuk = mk("wuk", [128, 512], BF16)
        self.wuv = mk("wuv", [128, 512], BF16)
        self.wukT = mk("wukT", [128, 4, 128], BF16)
        self.xio = Rot([mk(f"xio{i}", [128, D]) for i in range(2)])
        self.xT = mk("xT", [128, KC, NT])
        self.xn = mk("xn", [128, KC, NT], BF16)
        self.h = mk("h", [128, 11, NT], BF16)
        self.rstd = mk("rstd", [128, NT])
        self.sg = Rot([mk(f"sg{i}", [128, NT]) for i in range(2)])
        self.wbuf = Rot([mk(f"wb{i}", [128, 11, 256], BF16) for i in range(4)])
        self.dummy = mk("dummy", [128, 8])
        self.arena = mk("arena", [128, 6144], BF16)
        ar = self.arena.t
        self.latT_t = ar[:, 0:2048]
        self.latM_t = ar[:, 2048:4096].rearrange("p (a c) -> p a c", c=128)
        self.kpe_t = ar[0:64, 4096:6144]
        self.bA, self.bB, self.bC = Buf("arA"), Buf("arB"), Buf("arC")
        self.alow = mk("alow", [17, NT], BF16)
        self.Lt = Rot([mk(f"Lt{i}", [128, 256]) for i in range(2)])
        self.enb = Rot([mk(f"enb{i}", [128, 256]) for i in range(2)])
        self.bTs = mk("bTs", [64, 4, NT])
        self.eb = Rot([mk(f"eb{i}", [64, NT]) for i in range(2)])
        self.elast = mk("elast", [64, 4, 4])
        self.qt = [mk(f"qt{h}", [64, NT], BF16) for h in range(4)]
        self.kt = [mk(f"kt{h}", [64, NT], BF16) for h in range(4)]
        self.ktm = mk("ktm", [128, 4, 256], BF16)
        self.vtm = mk("vtm", [128, 4, 512], BF16)
        self.sgT = mk("sgT", [128, 4, NT], BF16)
        self.cq = mk("cq", [128, 2, NT])
        self.cqn = mk("cqn", [128, 2, NT], BF16)
        self.qn = Rot([mk(f"qn{i}", [128, NT], BF16) for i in range(1)])
        self.ql = Rot([mk(f"ql{i}", [128, NT], BF16) for i in range(4)])
        self.qpe = Rot([mk(f"qpe{i}", [64, NT], BF16) for i in range(4)])
        self.ckvf = mk("ckvf", [128, NT])
        self.latf = mk("latf", [128, NT])
        self.kpef = mk("kpef", [64, NT])
        self.cs = mk("cs", [64, 2, NT])
        self.rt = Rot([mk(f"rt{i}", [64, NT]) for i in range(2)])
        self.wkvr = mk("wkvr", [128, KC, 64], BF16)
        self.S = [mk(f"S{h}", [64, 128]) for h in range(4)]
        self.Sb = [mk(f"Sb{h}", [64, 128], BF16) for h in range(4)]
        self.ym = mk("ym", [128, 8, NT], BF16)
        self.pt = Rot([mk(f"pt{i}", [128, NT], BF16) for i in range(3)])
        self.atm = Rot([mk(f"atm{i}", [128, 128], BF16) for i in range(3)])
        self.rl = mk("rl", [128, NT])
        self.ol = mk("ol", [128, NT], BF16)
        self.otok = Rot([mk(f"otok{i}", [128, 4, 128]) for i in range(2)])
        self.otok2 = Rot([mk(f"otk2{i}", [128, 4, 64]) for i in range(2)])
        self.G = Rot([T(P, f"pg{i}", [128, NT], F32, psum=True) for i in range(3)])
        self.O = Rot([T(P, f"po{i}", [128, NT], F32, psum=True) for i in range(2)])
        self.XS = T(P, "pxs", [128, NT], F32, psum=True)
        self.F = Rot(self.G.items + [self.XS])
        xbs = []
        for i in range(2):
            full = T(P, f"pxb{i}", [128, 1024], BF16, psum=True)
            for j in range(2):
                v = View(full.t[:, j * 512:(j + 1) * 512], full.b)
                xbs.append(v)
        self.XB = Rot(xbs)
        if self.do_sample:
            self.idx = mk("idx", [128, NSEQ_S * 4], I32)
            self.ptls = mk("ptls", [128, NSEQ_S * 4], I32)
            self.lt4 = Rot([mk(f"lt4_{i}", [128, 512], BF16) for i in range(2)])
            self.pt4 = Rot([mk(f"pt4_{i}", [128, 256], BF16) for i in range(2)])
            self.pts = Rot([mk(f"pts{i}", [128, 16, 4], BF16) for i in range(2)])
            self.ptsum = Rot([mk(f"ptsum{i}", [128, 4]) for i in range(2)])
            self.QL = mk("QL", [128, NSEQ_S, 4], BF16)
            self.QP = mk("QP", [128, NSEQ_S, 4], BF16)
            self.s0 = Rot([mk(f"s0_{i}", [64, 4, 128]) for i in range(2)])
            self.s1 = mk("s1", [64, 4, 128])
            self.kmask = Rot([mk(f"kmask{i}", [16, 256], BF16) for i in range(2)])
            self.qsf = mk("qsf", [64, 4, 18])
            self.esf = mk("esf", [64, 4, 16])
            self.num = mk("num", [128, NSEQ_S, 4])
            self.den = mk("den", [128, NSEQ_S, 4])
            self.pnew = mk("pnew", [128, 16])
            self.prod = mk("prod", [128, 16])
            self.prod2 = mk("prod2", [64, 16])
            self.tmp16 = mk("tmp16", [128, 16])
            self.gsb = mk("gsb", [128, 64])

    def mm(self, out, ob, lhsT, lb, rhs, rb, start=True, stop=True):
        rd = [lb] if rb is lb else [lb, rb]
        self.P.pe(lambda e: e.matmul(out, lhsT=lhsT, rhs=rhs, start=start, stop=stop), reads=rd, writes=[ob])

    def tr(self, out, ob, in_, ib, ident, idb):
        self.P.pe(lambda e: e.transpose(out, in_, ident), reads=[ib, idb], writes=[ob])

    def actf(self, out, ob, in_, ib, func, scale=1.0, bias=0.0):
        self.P.act(lambda e: e.activation(out=out, in_=in_, func=func, bias=bias, scale=scale), reads=[ib], writes=[ob])

    def acp(self, out, ob, in_, ib):
        self.P.act(lambda e: e.activation(out=out, in_=in_, func=AF.Copy), reads=[ib], writes=[ob])

    def tt(self, out, ob, a, ab, b, bb, op):
        self.P.dve(lambda e: e.tensor_tensor(out=out, in0=a, in1=b, op=op), reads=[ab, bb], writes=[ob])

    def stt(self, out, ob, a, ab, scalar, b, bb, op0, op1, sreads=()):
        self.P.dve(lambda e: e.scalar_tensor_tensor(out=out, in0=a, scalar=scalar, in1=b, op0=op0, op1=op1),
                   reads=[ab, bb] + list(sreads), writes=[ob])

    def ts(self, out, ob, a, ab, s1, op0, s2=None, op1=None, sreads=()):
        if op1 is None:
            self.P.dve(lambda e: e.tensor_scalar(out=out, in0=a, scalar1=s1, scalar2=None, op0=op0),
                       reads=[ab] + list(sreads), writes=[ob])
        else:
            self.P.dve(lambda e: e.tensor_scalar(out=out, in0=a, scalar1=s1, scalar2=s2, op0=op0, op1=op1),
                       reads=[ab] + list(sreads), writes=[ob])

    def cp(self, out, ob, in_, ib):
        self.P.dve(lambda e: e.tensor_copy(out=out, in_=in_), reads=[ib], writes=[ob])

    def memset(self, ap, b, v):
        self.P.dve(lambda e: e.memset(ap, v), writes=[b])

    def recip(self, out, ob, in_, ib):
        self.P.dve(lambda e: e.reciprocal(out=out, in_=in_), reads=[ib], writes=[ob])

    def dma_in(self, out, ob, src, q="sp"):
        self.P.dma(lambda e: e.dma_start(out=out, in_=src), writes=[ob], q=q)

    def dma_out(self, dst, in_, ib, q="sp"):
        self.P.dma(lambda e: e.dma_start(out=dst, in_=in_), reads=[ib], q=q)

    def rstd_from(self, sq_aps, sqb, ones, C, npart=128):
        ps = self.XS
        n = len(sq_aps)
        for i, ap in enumerate(sq_aps):
            self.mm(ps.t[:, 0:C], ps.b, ones.t[0:npart, :], ones.b, ap, sqb, start=(i == 0), stop=(i == n - 1))
        self.actf(self.rstd.t[:, 0:C], self.rstd.b, ps.t[:, 0:C], ps.b, AF.Ln, bias=EPS)
        self.actf(self.rstd.t[:, 0:C], self.rstd.b, self.rstd.t[:, 0:C], self.rstd.b, AF.Exp, scale=-0.5)

    def unit_idx(self, key):
        if key not in self.units:
            assert self.first_pass, key
            self.units[key] = len(self.units)
            assert len(self.units) <= NUNITS
        return self.units[key]

    def wstream(self, key, wt, dst, src_f32, scr_view):
        if self.first_pass:
            self.dma_in(dst, wt.b, src_f32, q="pool")
            self.P.dma(lambda e: e.dma_start(out=scr_view, in_=dst), reads=[wt.b], q="sp")
        else:
            self.P.dma(lambda e: e.dma_start(out=dst, in_=scr_view), writes=[wt.b], q="pool")

    def end_first_pass(self):
        self.first_pass = False
        bufs = [w.b for w in self.wbuf.items]
        self.P.pool(lambda e: e.memset(self.dummy.t[:], 0.0), writes=bufs + [self.dummy.b])

    def wload8(self, wd, c0, ncols):
        wt = self.wbuf.next()
        u = self.unit_idx((wd.name, c0, ncols))
        src = wd.ap()[:, c0:c0 + ncols].rearrange("(kc p) c -> p kc c", p=128)
        scr_view = self.scr.ap()[u, :, 0:KC * ncols].rearrange("p (k c) -> p k c", c=ncols)
        self.wstream(u, wt, wt.t[:, 0:KC, 0:ncols], src, scr_view)
        return wt

    def setup(self):
        w = self.w
        self.dma_in(self.cst.t[:], self.cst.b, w["consts"].ap())
        self.dma_in(self.smw.t[:], self.smw.b, w["smallw"].ap())
        self.ident = self.cst.t[:, 0:128]
        self.trif = self.cst.t[:, 256:384]
        self.negI = self.cst.t[:, 401:417]
        self.cp(self.identb.t[:], self.identb.b, self.cst.t[:, 0:128], self.cst.b)
        self.cp(self.trib.t[:], self.trib.b, self.cst.t[:, 128:256], self.cst.b)
        for tl, v in ((self.on1024, 1.0 / 1024), (self.on256, 1.0 / 256), (self.on128, 1.0 / 128), (self.on1, 1.0),
                      (self.onf, 1.0), (self.alow, 1.0)):
            self.memset(tl.t[:], tl.b, v)
        self.memset(self.latf.t[:, 0:128], self.latf.b, 0.0)
        self.memset(self.kpef.t[:, 0:128], self.kpef.b, 0.0)
        self.memset(self.xT.t[:, :, 0:128], self.xT.b, 0.0)
        if self.do_sample:
            self.memset(self.qsf.t[:], self.qsf.b, 0.0)
        self.dma_in(self.waup.t[:], self.waup.b, w["waup"].ap(), q="pool")
        self.dma_in(self.wuq.t[:], self.wuq.b, w["wuq"].ap().rearrange("(kc p) c -> p kc c", p=128), q="pool")
        self.dma_in(self.wuk.t[:], self.wuk.b, w["wuk"].ap(), q="pool")
        self.dma_in(self.wuv.t[:], self.wuv.b, w["wuv"].ap(), q="pool")
        for h in range(4):
            b0 = h * 192 + 128
            self.ts(self.wuqr.t[:, :, h, 0:32], self.wuqr.b, self.wuq.t[:, :, b0 + 32:b0 + 64], self.wuq.b, -1.0, ALU.mult)
            self.cp(self.wuqr.t[:, :, h, 32:64], self.wuqr.b, self.wuq.t[:, :, b0:b0 + 32], self.wuq.b)
            xb = self.XB.next()
            self.tr(xb.t[:, 0:128], xb.b, self.wuk.t[:, h * 128:(h + 1) * 128], self.wuk.b, self.identb.t[:], self.identb.b)
            self.cp(self.wukT.t[:, h, :], self.wukT.b, xb.t[:, 0:128], xb.b)

    def norm_x(self, col0, C, inplace=False):
        sq = self.h
        self.actf(sq.t[:, 0:KC, 0:C], sq.b, self.xT.t[:, :, 0:C], self.xT.b, AF.Square)
        self.rstd_from([sq.t[:, kc, 0:C] for kc in range(KC)], sq.b, self.on1024, C)
        for kc in range(KC):
            o, ob = (self.xT.t[:, kc, 0:C], self.xT.b) if inplace else (self.xn.t[:, kc, 0:C], self.xn.b)
            self.stt(o, ob, self.xT.t[:, kc, 0:C], self.xT.b, self.smw.t[:, col0 + kc:col0 + kc + 1],
                     self.rstd.t[:, 0:C], self.rstd.b, ALU.mult, ALU.mult, sreads=[self.smw.b])

    def ffn(self, wg, wu, wdn, col0, C):
        xn = self.xn
        self.norm_x(col0, C)
        for half in range(2):
            f0 = half * 11
            for u in range(6):
                fs = [f for f in (f0 + 2 * u, f0 + 2 * u + 1) if f < f0 + 11]
                wgt = self.wload8(wg, fs[0] * 128, len(fs) * 128)
                wut = self.wload8(wu, fs[0] * 128, len(fs) * 128)
                for i, f in enumerate(fs):
                    pg = self.F.next()
                    pu = self.F.next()
                    for kc in range(KC):
                        self.mm(pg.t[:, 0:C], pg.b, wgt.t[:, kc, i * 128:(i + 1) * 128], wgt.b, xn.t[:, kc, 0:C], xn.b, kc == 0, kc == KC - 1)
                    for kc in range(KC):
                        self.mm(pu.t[:, 0:C], pu.b, wut.t[:, kc, i * 128:(i + 1) * 128], wut.b, xn.t[:, kc, 0:C], xn.b, kc == 0, kc == KC - 1)
                    sg = self.sg.next()
                    self.actf(sg.t[:, 0:C], sg.b, pg.t[:, 0:C], pg.b, AF.Silu)
                    self.tt(self.h.t[:, f - f0, 0:C], self.h.b, pu.t[:, 0:C], pu.b, sg.t[:, 0:C], sg.b, ALU.mult)
            for dcp in range(4):
                wdt = self.wbuf.next()
                src = wdn.ap()[f0 * 128:(f0 + 11) * 128, dcp * 256:(dcp + 1) * 256].rearrange("(f p) c -> p f c", p=128)
                u = self.unit_idx((wdn.name, "down", f0, dcp))
                self.wstream(u, wdt, wdt.t[:], src, self.scr.ap()[u].rearrange("p (f c) -> p f c", c=256))
                for i in range(2):
                    dc = dcp * 2 + i
                    po = self.O.next()
                    for f in range(11):
                        self.mm(po.t[:, 0:C], po.b, wdt.t[:, f, i * 128:(i + 1) * 128], wdt.b, self.h.t[:, f, 0:C], self.h.b, f == 0, f == 10)
                    self.stt(self.xT.t[:, dc, 0:C], self.xT.b, po.t[:, 0:C], po.b, 0.5, self.xT.t[:, dc, 0:C], self.xT.b, ALU.mult, ALU.add)

    def load_x(self, src_rows, nblk, npart):
        for a in range(nblk):
            xi = self.xio.next()
            self.dma_in(xi.t[0:npart, :], xi.b, src_rows(a))
            for g in range(2):
                pg = self.G.next()
                for k in range(4):
                    kc = g * 4 + k
                    self.tr(pg.t[:, k * 128:k * 128 + npart], pg.b, xi.t[0:npart, kc * 128:(kc + 1) * 128], xi.b,
                            self.ident[0:npart, 0:npart], self.cst.b)
                src_v = pg.t[:].rearrange("p (k t) -> p k t", k=4)[:, :, 0:npart]
                dst_v = self.xT.t[:, g * 4:(g + 1) * 4, a * 128:a * 128 + npart]
                if g == 0:
                    self.acp(dst_v, self.xT.b, src_v, pg.b)
                else:
                    self.cp(dst_v, self.xT.b, src_v, pg.b)

    def store_y(self, dst_rows, nblk, npart):
        yT = self.xT
        for a in range(nblk):
            yo = self.xio.next()
            for g in range(2):
                pg = self.G.next()
                for k in range(4):
                    kc = g * 4 + k
                    self.tr(pg.t[:, k * 128:(k + 1) * 128], pg.b, yT.t[:, kc, a * 128:(a + 1) * 128], yT.b, self.ident, self.cst.b)
                if g == 0:
                    self.acp(yo.t[0:npart, 0:512], yo.b, pg.t[0:npart, :], pg.b)
                else:
                    self.cp(yo.t[0:npart, 512:1024], yo.b, pg.t[0:npart, :], pg.b)
            self.dma_out(dst_rows(a), yo.t[0:npart, :], yo.b)

    def mixer_proj(self, C, A, npart, pos0, sample):
        win = self.w["win"]
        xn = self.xn
        self.norm_x(8, C)
        self.dma_in(self.cs.t[:, :, 0:C], self.cs.b, self.w["rope"].ap()[:, :, pos0:pos0 + C].rearrange("s r c -> r s c"))
        cum = self.negI[0:npart, 0:npart] if sample else self.trif
        wa = self.wload8(win, C_A, 16)
        pa = self.G.next()
        for kc in range(KC):
            self.mm(pa.t[0:16, 0:C], pa.b, wa.t[:, kc, 0:16], wa.b, xn.t[:, kc, 0:C], xn.b, kc == 0, kc == KC - 1)
        self.cp(self.alow.t[0:16, 0:C], self.alow.b, pa.t[0:16, 0:C], pa.b)
        wk = self.wload8(win, C_K, 256)
        wv0 = self.wload8(win, C_V, 256)
        wv1 = self.wload8(win, C_V + 256, 256)
        for a in range(A):
            ca = slice(a * 128, a * 128 + npart)
            pz = self.XS
            self.mm(pz.t[0:npart, 0:256], pz.b, self.alow.t[:, ca], self.alow.b, self.waup.t[:], self.waup.b)
            L = self.Lt.next()
            self.actf(L.t[0:npart, :], L.b, pz.t[0:npart, 0:256], pz.b, AF.Exp, scale=-1.0)
            self.actf(L.t[0:npart, :], L.b, L.t[0:npart, :], L.b, AF.Ln, bias=1.0)
            if not sample:
                pb = self.XS
                self.mm(pb.t[0:npart, 0:256], pb.b, cum, self.cst.b, L.t[0:npart, :], L.b)
                enb = self.enb.next()
                self.actf(enb.t[0:npart, :], enb.b, pb.t[0:npart, 0:256], pb.b, AF.Exp, scale=-1.0)
            pbt = self.G.next()
            for h in range(4):
                self.mm(pbt.t[0:64, h * 128:h * 128 + npart], pbt.b, L.t[0:npart, h * 64:(h + 1) * 64], L.b, cum, self.cst.b)
            self.cp(self.bTs.t[:, :, ca], self.bTs.b, pbt.t[0:64, :].rearrange("p (h t) -> p h t", h=4)[:, :, 0:npart], pbt.b)
            pk = self.G.next()
            for kc in range(KC):
                self.mm(pk.t[0:npart, 0:256], pk.b, xn.t[:, kc, ca], xn.b, wk.t[:, kc, 0:256], wk.b, kc == 0, kc == KC - 1)
            if sample:
                self.cp(self.ktm.t[0:npart, a, :], self.ktm.b, pk.t[0:npart, 0:256], pk.b)
            else:
                self.tt(self.ktm.t[0:npart, a, :], self.ktm.b, pk.t[0:npart, 0:256], pk.b, enb.t[0:npart, :], enb.b, ALU.mult)
            pv = self.G.next()
            for kc in range(KC):
                self.mm(pv.t[0:npart, 0:256], pv.b, xn.t[:, kc, ca], xn.b, wv0.t[:, kc, 0:256], wv0.b, kc == 0, kc == KC - 1)
            for kc in range(KC):
                self.mm(pv.t[0:npart, 256:512], pv.b, xn.t[:, kc, ca], xn.b, wv1.t[:, kc, 0:256], wv1.b, kc == 0, kc == KC - 1)
            self.acp(self.vtm.t[0:npart, a, :], self.vtm.b, pv.t[0:npart, :], pv.b)
        wq = self.wload8(win, C_Q, 256)
        for h in range(4):
            ebq = self.eb.next()
            self.actf(ebq.t[:, 0:C], ebq.b, self.bTs.t[:, h, 0:C], self.bTs.b, AF.Exp, scale=1.0)
            if sample:
                self.cp(self.esf.t[:, h, 0:C], self.esf.b, ebq.t[:, 0:C], ebq.b)
            else:
                for a in range(A):
                    self.cp(self.elast.t[:, h, a:a + 1], self.elast.b, ebq.t[:, a * 128 + 127:a * 128 + 128], ebq.b)
            pq = self.G.next()
            for kc in range(KC):
                self.mm(pq.t[0:64, 0:C], pq.b, wq.t[:, kc, h * 64:(h + 1) * 64], wq.b, xn.t[:, kc, 0:C], xn.b, kc == 0, kc == KC - 1)
            if sample:
                self.ts(self.qsf.t[:, h, 0:C], self.qsf.b, pq.t[0:64, 0:C], pq.b, 0.125, ALU.mult)
            else:
                self.stt(self.qt[h].t[:, 0:C], self.qt[h].b, pq.t[0:64, 0:C], pq.b, 0.125, ebq.t[:, 0:C], ebq.b, ALU.mult, ALU.mult)
                ebk = self.eb.next()
                self.actf(ebk.t[:, 0:C], ebk.b, self.bTs.t[:, h, 0:C], self.bTs.b, AF.Exp, scale=-1.0)
                pk = self.G.next()
                for kc in range(KC):
                    self.mm(pk.t[0:64, 0:C], pk.b, wk.t[:, kc, h * 64:(h + 1) * 64], wk.b, xn.t[:, kc, 0:C], xn.b, kc == 0, kc == KC - 1)
                self.tt(self.kt[h].t[:, 0:C], self.kt[h].b, pk.t[0:64, 0:C], pk.b, ebk.t[:, 0:C], ebk.b, ALU.mult)
        for u in range(2):
            wg = self.wload8(win, C_G + u * 256, 256)
            for i in range(2):
                pg = self.G.next()
                for kc in range(KC):
                    self.mm(pg.t[:, 0:C], pg.b, wg.t[:, kc, i * 128:(i + 1) * 128], wg.b, xn.t[:, kc, 0:C], xn.b, kc == 0, kc == KC - 1)
                self.actf(self.sgT.t[:, u * 2 + i, 0:C], self.sgT.b, pg.t[:, 0:C], pg.b, AF.Silu)
        wc = self.wload8(win, C_CQ, 256)
        for i in range(2):
            pc = self.G.next()
            for kc in range(KC):
                self.mm(pc.t[:, 0:C], pc.b, wc.t[:, kc, i * 128:(i + 1) * 128], wc.b, xn.t[:, kc, 0:C], xn.b, kc == 0, kc == KC - 1)
            self.cp(self.cq.t[:, i, 0:C], self.cq.b, pc.t[:, 0:C], pc.b)
        sq = self.h
        self.actf(sq.t[:, 0:2, 0:C], sq.b, self.cq.t[:, :, 0:C], self.cq.b, AF.Square)
        self.rstd_from([sq.t[:, i, 0:C] for i in range(2)], sq.b, self.on256, C)
        for i in range(2):
            self.stt(self.cqn.t[:, i, 0:C], self.cqn.b, self.cq.t[:, i, 0:C], self.cq.b, self.smw.t[:, 33 + i:34 + i],
                     self.rstd.t[:, 0:C], self.rstd.b, ALU.mult, ALU.mult, sreads=[self.smw.b])
        wkv = self.wload8(win, C_KV, 192)
        self.ts(self.wkvr.t[:, :, 0:32], self.wkvr.b, wkv.t[:, 0:KC, 160:192], wkv.b, -1.0, ALU.mult)
        self.cp(self.wkvr.t[:, :, 32:64], self.wkvr.b, wkv.t[:, 0:KC, 128:160], wkv.b)
        pc = self.G.next()
        for kc in range(KC):
            self.mm(pc.t[:, 0:C], pc.b, wkv.t[:, kc, 0:128], wkv.b, xn.t[:, kc, 0:C], xn.b, kc == 0, kc == KC - 1)
        self.cp(self.ckvf.t[:, 0:C], self.ckvf.b, pc.t[:, 0:C], pc.b)
        self.actf(sq.t[:, 0, 0:C], sq.b, self.ckvf.t[:, 0:C], self.ckvf.b, AF.Square)
        self.rstd_from([sq.t[:, 0, 0:C]], sq.b, self.on128, C)
        self.stt(self.latf.t[:, 0:C], self.latf.b, self.ckvf.t[:, 0:C], self.ckvf.b, self.smw.t[:, 35:36],
                 self.rstd.t[:, 0:C], self.rstd.b, ALU.mult, ALU.mult, sreads=[self.smw.b])
        p1 = self.G.next()
        for kc in range(KC):
            self.mm(p1.t[0:64, 0:C], p1.b, wkv.t[:, kc, 128:192], wkv.b, xn.t[:, kc, 0:C], xn.b, kc == 0, kc == KC - 1)
        r1 = self.rt.next()
        self.tt(r1.t[:, 0:C], r1.b, p1.t[0:64, 0:C], p1.b, self.cs.t[:, 0, 0:C], self.cs.b, ALU.mult)
        p2 = self.G.next()
        for kc in range(KC):
            self.mm(p2.t[0:64, 0:C], p2.b, self.wkvr.t[:, kc, :], self.wkvr.b, xn.t[:, kc, 0:C], xn.b, kc == 0, kc == KC - 1)
        r2 = self.rt.next()
        self.tt(r2.t[:, 0:C], r2.b, p2.t[0:64, 0:C], p2.b, self.cs.t[:, 1, 0:C], self.cs.b, ALU.mult)
        self.tt(self.kpef.t[:, 0:C], self.kpef.b, r1.t[:, 0:C], r1.b, r2.t[:, 0:C], r2.b, ALU.add)

    def mla_q(self, h, C):
        cqn = self.cqn
        pn = self.G.next()
        for i in range(2):
            self.mm(pn.t[:, 0:C], pn.b, self.wuq.t[:, i, h * 192:h * 192 + 128], self.wuq.b, cqn.t[:, i, 0:C], cqn.b, i == 0, i == 1)
        qn = self.qn.next()
        self.acp(qn.t[:, 0:C], qn.b, pn.t[:, 0:C], pn.b)
        pl = self.G.next()
        self.mm(pl.t[:, 0:C], pl.b, self.wukT.t[:, h, :], self.wukT.b, qn.t[:, 0:C], qn.b)
        ql = self.ql.next()
        self.acp(ql.t[:, 0:C], ql.b, pl.t[:, 0:C], pl.b)
        p1 = self.G.next()
        for i in range(2):
            self.mm(p1.t[0:64, 0:C], p1.b, self.wuq.t[:, i, h * 192 + 128:h * 192 + 192], self.wuq.b, cqn.t[:, i, 0:C], cqn.b, i == 0, i == 1)
        r1 = self.rt.next()
        self.tt(r1.t[:, 0:C], r1.b, p1.t[0:64, 0:C], p1.b, self.cs.t[:, 0, 0:C], self.cs.b, ALU.mult)
        p2 = self.G.next()
        for i in range(2):
            self.mm(p2.t[0:64, 0:C], p2.b, self.wuqr.t[:, i, h, :], self.wuqr.b, cqn.t[:, i, 0:C], cqn.b, i == 0, i == 1)
        r2 = self.rt.next()
        self.tt(r2.t[:, 0:C], r2.b, p2.t[0:64, 0:C], p2.b, self.cs.t[:, 1, 0:C], self.cs.b, ALU.mult)
        qp = self.qpe.next()
        self.tt(qp.t[:, 0:C], qp.b, r1.t[:, 0:C], r1.b, r2.t[:, 0:C], r2.b, ALU.add)
        return ql, qp

    def gla_out(self, h, src, srcb, C):
        sq = self.h
        self.actf(sq.t[:, 0, 0:C], sq.b, src, srcb, AF.Square)
        self.rstd_from([sq.t[:, 0, 0:C]], sq.b, self.on128, C)
        r = self.rl
        self.stt(r.t[:, 0:C], r.b, src, srcb, self.smw.t[:, 32:33], self.rstd.t[:, 0:C], self.rstd.b,
                 ALU.mult, ALU.mult, sreads=[self.smw.b])
        self.tt(self.ym.t[:, h, 0:C], self.ym.b, r.t[:, 0:C], r.b, self.sgT.t[:, h, 0:C], self.sgT.b, ALU.mult)

    def out_proj(self, C):
        wout = self.w["wout"]
        for u in range(4):
            wo = self.wload8(wout, u * 256, 256)
            for i in range(2):
                dc = u * 2 + i
                po = self.O.next()
                for kc in range(KC):
                    self.mm(po.t[:, 0:C], po.b, wo.t[:, kc, i * 128:(i + 1) * 128], wo.b, self.ym.t[:, kc, 0:C], self.ym.b, kc == 0, kc == KC - 1)
                self.tt(self.xT.t[:, dc, 0:C], self.xT.b, po.t[:, 0:C], po.b, self.xT.t[:, dc, 0:C], self.xT.b, ALU.add)

    def prompt_tile(self, ti):
        s, t = ti // 4, ti % 4
        C = NT
        tok0 = t * NT
        w = self.w
        self.load_x(lambda a: self.xp.ap()[s, tok0 + a * 128:tok0 + (a + 1) * 128, :], 4, 128)
        self.ffn(w["f1g"], w["f1u"], w["f1d"], 0, C)
        if t == 0:
            for h in range(4):
                self.memset(self.S[h].t[:], self.S[h].b, 0.0)
                self.memset(self.Sb[h].t[:], self.Sb[h].b, 0.0)
        self.mixer_proj(C, 4, 128, tok0, sample=False)
        self.cp(self.latT_t[:, tok0:tok0 + C], self.bA, self.latf.t[:, 0:C], self.latf.b)
        self.cp(self.kpe_t[:, tok0:tok0 + C], self.bC, self.kpef.t[:, 0:C], self.kpef.b)
        pt_ = self.G.next()
        for a in range(4):
            self.tr(pt_.t[:, a * 128:(a + 1) * 128], pt_.b, self.latf.t[:, a * 128:(a + 1) * 128], self.latf.b, self.ident, self.cst.b)
        ot = self.otok.next()
        self.acp(ot.t[:].rearrange("p a c -> p (a c)"), ot.b, pt_.t[:], pt_.b)
        self.cp(self.latM_t[:, t * 4:(t + 1) * 4, :], self.bB, ot.t[:], ot.b)
        self.dma_out(self.kvp.ap()[s, tok0:tok0 + C, :].rearrange("(a p) c -> p a c", p=128), ot.t[:], ot.b)
        pt2 = self.G.next()
        for a in range(4):
            self.tr(pt2.t[:, a * 64:(a + 1) * 64], pt2.b, self.kpef.t[:, a * 128:(a + 1) * 128], self.kpef.b, self.ident[0:64, 0:64], self.cst.b)
        ot2 = self.otok2.next()
        self.acp(ot2.t[:].rearrange("p a c -> p (a c)"), ot2.b, pt2.t[:, 0:256], pt2.b)
        self.dma_out(self.pep.ap()[s, tok0:tok0 + C, :].rearrange("(a p) c -> p a c", p=128), ot2.t[:], ot2.b)
        gitems = [(a, h) for a in range(4) for h in range(4)]
        pos = {}

        def gla_A(a, h):
            ca = slice(a * 128, (a + 1) * 128)
            pa = self.G.next()
            self.mm(pa.t[:, 0:128], pa.b, self.kt[h].t[:, ca], self.kt[h].b, self.qt[h].t[:, ca], self.qt[h].b)
            am = self.atm.next()
            self.tt(am.t[:], am.b, pa.t[:, 0:128], pa.b, self.trib.t[:], self.trib.b, ALU.mult)
            return am

        def gla_rest(a, h, am):
            ca = slice(a * 128, (a + 1) * 128)
            hc = slice(h * 128, (h + 1) * 128)
            if h == 0:
                pos[a] = self.O.next()
            po = pos[a]
            self.mm(po.t[:, hc], po.b, self.vtm.t[:, a, hc], self.vtm.b, am.t[:], am.b, True, False)
            self.mm(po.t[:, hc], po.b, self.Sb[h].t[:], self.Sb[h].b, self.qt[h].t[:, ca], self.qt[h].b, False, True)
            pS = self.G.next()
            self.mm(pS.t[0:64, 0:128], pS.b, self.ktm.t[:, a, h * 64:(h + 1) * 64], self.ktm.b, self.vtm.t[:, a, hc], self.vtm.b)
            el = self.elast.t[:, h, a:a + 1]
            self.ts(self.S[h].t[:], self.S[h].b, self.S[h].t[:], self.S[h].b, el, ALU.mult, sreads=[self.elast.b])
            self.stt(self.S[h].t[:], self.S[h].b, pS.t[0:64, 0:128], pS.b, el, self.S[h].t[:], self.S[h].b, ALU.mult, ALU.add,
                     sreads=[self.elast.b])
            self.acp(self.Sb[h].t[:], self.Sb[h].b, self.S[h].t[:], self.S[h].b)
            if h == 3:
                sq = self.h
                self.actf(sq.t[:, 0, 0:C], sq.b, po.t[:, 0:C], po.b, AF.Square)
                self.rstd_from([sq.t[:, 0, 0:C]], sq.b, self.on128, C)
                r = self.rl
                self.stt(r.t[:, 0:C], r.b, po.t[:, 0:C], po.b, self.smw.t[:, 32:33], self.rstd.t[:, 0:C], self.rstd.b,
                         ALU.mult, ALU.mult, sreads=[self.smw.b])
                self.tt(self.ym.t[:, 0:4, ca], self.ym.b, r.t[:, 0:C].rearrange("p (h t) -> p h t", h=4), r.b,
                        self.sgT.t[:, 0:4, ca], self.sgT.b, ALU.mult)

        am_next = gla_A(*gitems[0])
        for i, (a, h) in enumerate(gitems):
            am_cur = am_next
            if i + 1 < len(gitems):
                am_next = gla_A(*gitems[i + 1])
            gla_rest(a, h, am_cur)
        if t == 3:
            for h in range(4):
                self.dma_out(self.glap.ap()[s, h], self.S[h].t[:], self.S[h].b)
        qs = [self.mla_q(h, C) for h in range(4)]
        nkb = 4 * t + 4
        items = [(h, kb) for h in range(4) for kb in range(nkb)]
        acc = {}

        def emit_scores(h, kb):
            ql, qp = qs[h]
            r = kb - 4 * t
            c0 = 128 * r if r > 0 else 0
            N = C - c0
            ps = self.G.next()
            ks = slice(kb * 128, (kb + 1) * 128)
            self.mm(ps.t[:, 0:N], ps.b, self.latT_t[:, ks], self.bA, ql.t[:, c0:C], ql.b, True, False)
            self.mm(ps.t[:, 0:N], ps.b, self.kpe_t[:, ks], self.bC, qp.t[:, c0:C], qp.b, False, True)
            return ps, r, c0, N

        def emit_pv(h, kb, info):
            ps, r, c0, N = info
            if kb == 0:
                acc[h] = (self.O.next(), self.O.next())
            po, pl = acc[h]
            pt = self.pt.next()
            self.actf(pt.t[:, 0:N], pt.b, ps.t[:, 0:N], ps.b, AF.Exp, scale=MLA_SCALE)
            if r >= 0:
                self.tt(pt.t[:, 0:128], pt.b, pt.t[:, 0:128], pt.b, self.trib.t[:], self.trib.b, ALU.mult)
            self.mm(po.t[:, c0:C], po.b, self.latM_t[:, kb, :], self.bB, pt.t[:, 0:N], pt.b, kb == 0, kb == nkb - 1)
            self.mm(pl.t[:, c0:C], pl.b, self.on1.t[:], self.on1.b, pt.t[:, 0:N], pt.b, kb == 0, kb == nkb - 1)
            if kb == nkb - 1:
                self.actf(self.rl.t[:, 0:C], self.rl.b, pl.t[:, 0:C], pl.b, AF.Ln)
                self.actf(self.rl.t[:, 0:C], self.rl.b, self.rl.t[:, 0:C], self.rl.b, AF.Exp, scale=-1.0)
                self.tt(self.ol.t[:, 0:C], self.ol.b, po.t[:, 0:C], po.b, self.rl.t[:, 0:C], self.rl.b, ALU.mult)
                py = self.G.next()
                self.mm(py.t[:, 0:C], py.b, self.wuv.t[:, h * 128:(h + 1) * 128], self.wuv.b, self.ol.t[:, 0:C], self.ol.b)
                self.acp(self.ym.t[:, 4 + h, 0:C], self.ym.b, py.t[:, 0:C], py.b)

        info = emit_scores(*items[0])
        for i, (h, kb) in enumerate(items):
            nxt = emit_scores(*items[i + 1]) if i + 1 < len(items) else None
            emit_pv(h, kb, info)
            info = nxt
        self.out_proj(C)
        self.ffn(w["f2g"], w["f2u"], w["f2d"], 16, C)
        self.norm_x(24, C, inplace=True)
        self.store_y(lambda a: self.yp.ap()[s, tok0 + a * 128:tok0 + (a + 1) * 128, :], 4, 128)

    def sample_tile(self):
        C = NSEQ_S
        w = self.w
        self.load_x(lambda a: self.xs.ap(), 1, 16)
        self.ffn(w["f1g"], w["f1u"], w["f1d"], 0, C)
        self.mixer_proj(C, 1, 16, SEQ, sample=True)
        pt_ = self.G.next()
        self.tr(pt_.t[:, 0:128], pt_.b, self.latf.t[:, 0:128], self.latf.b, self.ident, self.cst.b)
        ot = self.otok.next()
        self.acp(ot.t[0:16, 0, :], ot.b, pt_.t[0:16, 0:128], pt_.b)
        self.dma_out(self.kvs.ap(), ot.t[0:16, 0, :], ot.b)
        pt2 = self.G.next()
        self.tr(pt2.t[:, 0:64], pt2.b, self.kpef.t[:, 0:128], self.kpef.b, self.ident[0:64, 0:64], self.cst.b)
        ot2 = self.otok2.next()
        self.acp(ot2.t[0:16, 0, :], ot2.b, pt2.t[0:16, 0:64], pt2.b)
        self.dma_out(self.pes.ap(), ot2.t[0:16, 0, :], ot2.b)
        self.decode_attention()
        self.out_proj(C)
        self.ffn(w["f2g"], w["f2u"], w["f2d"], 16, C)
        self.norm_x(24, C, inplace=True)
        self.store_y(lambda a: self.ys.ap(), 1, 16)

    def gla_sample_seq(self, b):
        pgl = self.XS
        s0 = self.s0.next()
        self.dma_in(s0.t[:], s0.b, self.sgla.ap()[b].rearrange("h d v -> d h v"))
        km = self.kmask.next()
        self.ts(km.t[:], km.b, self.ktm.t[0:16, 0, :], self.ktm.b, self.cst.t[0:16, b:b + 1], ALU.mult, sreads=[self.cst.b])
        pd = self.G.next()
        for h in range(4):
            self.mm(pd.t[0:64, h * 128:(h + 1) * 128], pd.b, km.t[:, h * 64:(h + 1) * 64], km.b,
                    self.vtm.t[0:16, 0, h * 128:(h + 1) * 128], self.vtm.b)
        s1 = self.s1
        for h in range(4):
            self.stt(s1.t[:, h, :], s1.b, s0.t[:, h, :], s0.b, self.esf.t[:, h, b:b + 1], pd.t[0:64, h * 128:(h + 1) * 128], pd.b,
                     ALU.mult, ALU.add, sreads=[self.esf.b])
        self.dma_out(self.glas.ap()[b].rearrange("h d v -> d h v"), s1.t[:], s1.b)
        for h in range(4):
            self.mm(pgl.t[:, h * 18 + b:h * 18 + b + 2], pgl.b, s1.t[:, h, :], s1.b, self.qsf.t[:, h, b:b + 2], self.qsf.b)

    def gla_sample_finish(self):
        pgl = self.XS
        self.cp(self.gsb.t[:].rearrange("p (h t) -> p h t", h=4), self.gsb.b,
                pgl.t[:, 0:72].rearrange("p (h t) -> p h t", h=4)[:, :, 0:16], pgl.b)
        for h in range(4):
            self.gla_out(h, self.gsb.t[:, h * 16:(h + 1) * 16], self.gsb.b, NSEQ_S)

    def bcreg(self, e):
        if getattr(self, "_bcreg", None) is None:
            self._bcreg = e.alloc_register("gather_bound")
            e.reg_mov(self._bcreg, NPOOL * 8 - 1)
        return self._bcreg

    def decode_attention(self):
        P = self.P
        C = NSEQ_S
        ar = self.arena.t
        gl = [(ar[:, 0:2048].rearrange("p (t c) -> p t c", c=128), ar[:, 0:2048], self.bA),
              (ar[:, 2048:4096].rearrange("p (t c) -> p t c", c=128), ar[:, 2048:4096], self.bB)]
        gp = [(ar[:, 4096:5120].rearrange("p (t c) -> p t c", c=64), ar[:, 4096:5120], self.bC),
              (ar[:, 5120:6144].rearrange("p (t c) -> p t c", c=64), ar[:, 5120:6144], self.bC)]
        self.dma_in(self.ptls.t[:], self.ptls.b, self.ptl.ap())
        self.ts(self.idx.t[:], self.idx.b, self.ptls.t[:], self.ptls.b, 8.0, ALU.mult, self.cst.t[:, 384:385], ALU.add, sreads=[self.cst.b])
        ckv_v = self.ckv.ap().rearrange("n (g t) c -> (n g) (t c)", t=16)
        cpe_v = self.cpe.ap().rearrange("n (g t) c -> (n g) (t c)", t=16)
        for h in range(4):
            ql, qp = self.mla_q(h, C)
            self.cp(self.QL.t[:, :, h], self.QL.b, ql.t[:, 0:C], ql.b)
            self.cp(self.QP.t[0:64, :, h], self.QP.b, qp.t[:, 0:C], qp.b)
        P.dma(lambda e: e.dma_start(out=self.QP.t[64:128, :, :], in_=self.QP.t[0:64, :, :]), writes=[self.QP.b], q="sp", no_waw=False)
        pnum, pden = self.O.next(), self.O.next()
        NB_ = 2
        NG = NSEQ_S * 4

        def gather(gi):
            glv, glf, glb = gl[gi % NB_]
            gpv, gpf, gpb = gp[gi % NB_]
            P.dma(lambda e: e.indirect_dma_start(
                out=glf, out_offset=None, in_=ckv_v,
                in_offset=bass.IndirectOffsetOnAxis(ap=self.idx.t[:, gi:gi + 1], axis=0),
                bounds_check=self.bcreg(e), oob_is_err=False),
                reads=[self.idx.b], writes=[glb], q="pool", no_waw=False)
            P.dma(lambda e: e.indirect_dma_start(
                out=gpf, out_offset=None, in_=cpe_v,
                in_offset=bass.IndirectOffsetOnAxis(ap=self.idx.t[:, gi:gi + 1], axis=0),
                bounds_check=self.bcreg(e), oob_is_err=False),
                reads=[self.idx.b], writes=[gpb], q="pool", no_waw=False)

        stage = {}
        psts = {}
        ptss = {}

        def emit_T(qi):
            gi, quad = qi // 4, qi % 4
            glv, glf, glb = gl[gi % NB_]
            gpv, gpf, gpb = gp[gi % NB_]
            xa, xb = self.XB.next(), self.XB.next()
            for k in range(4):
                self.tr(xa.t[:, k * 128:(k + 1) * 128], xa.b, glv[:, quad * 4 + k, :], glb, self.identb.t[:], self.identb.b)
            for j in range(2):
                c0_ = (quad * 4 + 2 * j) * 64
                self.tr(xb.t[:, j * 128:(j + 1) * 128], xb.b, gpf[:, c0_:c0_ + 128], gpb, self.identb.t[:], self.identb.b)
            lt4 = self.lt4.next()
            self.acp(lt4.t[:], lt4.b, xa.t[:], xa.b)
            pt4 = self.pt4.next()
            self.P.dve(lambda e: e.tensor_copy(out=pt4.t[:], in_=xb.t[:, 0:256]), reads=[xb.b, lt4.b], writes=[pt4.b])
            stage[qi] = (lt4, pt4)

        def emit_S(qi):
            gi, quad = qi // 4, qi % 4
            b, g = gi // 4, gi % 4
            if b not in psts:
                psts[b] = self.G.next()
            pst = psts[b]
            lt4, pt4 = stage.pop(qi)
            for k in range(4):
                kbi = g * 16 + quad * 4 + k
                self.mm(pst.t[:, kbi * 4:(kbi + 1) * 4], pst.b, lt4.t[:, k * 128:(k + 1) * 128], lt4.b, self.QL.t[:, b, :], self.QL.b, True, False)
                hp = slice((k % 2) * 64, (k % 2) * 64 + 64)
                self.mm(pst.t[:, kbi * 4:(kbi + 1) * 4], pst.b, pt4.t[hp, (k // 2) * 128:(k // 2 + 1) * 128], pt4.b,
                        self.QP.t[hp, b, :], self.QP.b, False, True)
            if quad == 3:
                pts = self.pts.next()
                self.actf(pts.t[:].rearrange("p t h -> p (t h)"), pts.b, pst.t[:, g * 64:(g + 1) * 64], pst.b, AF.Exp, scale=MLA_SCALE)
                psm = self.ptsum.next()
                P.dve(lambda e: e.tensor_reduce(out=psm.t[:], in_=pts.t[:].rearrange("p t h -> p h t"),
                                                axis=mybir.AxisListType.X, op=ALU.add),
                      reads=[pts.b], writes=[psm.b])
                ptss[gi] = (pts, psm)

        def emit_PV(gi):
            b, g = gi // 4, gi % 4
            glv, glf, glb = gl[gi % NB_]
            pts, psm = ptss.pop(gi)
            for tt_ in range(16):
                self.mm(pnum.t[:, b * 4:(b + 1) * 4], pnum.b, glv[:, tt_, :], glb, pts.t[:, tt_, :], pts.b,
                        g == 0 and tt_ == 0, g == 3 and tt_ == 15)
            self.mm(pden.t[:, b * 4:(b + 1) * 4], pden.b, self.onf.t[:], self.onf.b, psm.t[:], psm.b, g == 0, g == 3)

        NQ = NG * 4
        gather(0)
        for qi in range(NQ + 1):
            if qi < NQ:
                if qi % 4 == 0 and qi // 4 + 1 < NG:
                    pass
                emit_T(qi)
            if qi >= 1:
                emit_S(qi - 1)
                if (qi - 1) % 4 == 3:
                    gdone = (qi - 1) // 4
                    emit_PV(gdone)
                    if gdone + 2 < NG:
                        gather(gdone + 2)
            if qi == 0 and NG > 1:
                gather(1)
            if qi % 16 == 6 and qi < NQ:
                self.gla_sample_seq(qi // 16)
        self.gla_sample_finish()
        self.cp(self.num.t[:].rearrange("p b h -> p (b h)"), self.num.b, pnum.t[:, 0:64], pnum.b)
        self.cp(self.den.t[:].rearrange("p b h -> p (b h)"), self.den.b, pden.t[:, 0:64], pden.b)
        for h in range(4):
            self.tt(self.prod.t[:], self.prod.b, self.QL.t[:, :, h], self.QL.b, self.latf.t[:, 0:C], self.latf.b, ALU.mult)
            self.tt(self.prod2.t[:], self.prod2.b, self.QP.t[0:64, :, h], self.QP.b, self.kpef.t[:, 0:C], self.kpef.b, ALU.mult)
            psn = self.G.next()
            self.mm(psn.t[:, 0:C], psn.b, self.onf.t[:], self.onf.b, self.prod.t[:], self.prod.b, True, False)
            self.mm(psn.t[:, 0:C], psn.b, self.onf.t[0:64, :], self.onf.b, self.prod2.t[:], self.prod2.b, False, True)
            self.actf(self.pnew.t[:], self.pnew.b, psn.t[:, 0:C], psn.b, AF.Exp, scale=MLA_SCALE)
            self.tt(self.tmp16.t[:], self.tmp16.b, self.pnew.t[:], self.pnew.b, self.latf.t[:, 0:C], self.latf.b, ALU.mult)
            self.tt(self.num.t[:, :, h], self.num.b, self.num.t[:, :, h], self.num.b, self.tmp16.t[:], self.tmp16.b, ALU.add)
            self.tt(self.den.t[:, :, h], self.den.b, self.den.t[:, :, h], self.den.b, self.pnew.t[:], self.pnew.b, ALU.add)
            self.recip(self.tmp16.t[:], self.tmp16.b, self.den.t[:, :, h], self.den.b)
            self.tt(self.ol.t[:, 0:C], self.ol.b, self.num.t[:, :, h], self.num.b, self.tmp16.t[:], self.tmp16.b, ALU.mult)
            py = self.G.next()
            self.mm(py.t[:, 0:C], py.b, self.wuv.t[:, h * 128:(h + 1) * 128], self.wuv.b, self.ol.t[:, 0:C], self.ol.b)
            self.acp(self.ym.t[:, 4 + h, 0:C], self.ym.b, py.t[:, 0:C], py.b)


def build_program(do_sample=True, n_tiles=8):
    nc = bass.Bass("TRN2", target_bir_lowering=False)
    with ExitStack() as st:
        P = Prog(nc, st)
        B = Builder(nc, P, do_sample=do_sample, n_tiles=n_tiles)
        B.alloc()
        build_program.sbuf_left = nc.sbuf_bytes_remaining
        B.setup()
        for ti in range(n_tiles):
            B.prompt_tile(ti)
            if B.first_pass:
                B.end_first_pass()
        if do_sample:
            B.sample_tile()
        build_program.n_ops = len(P.ops)
        P.emit()
    return nc


def _consts():
    c = np.zeros((128, 448), np.float32)
    p = np.arange(128)
    c[:, 0:128] = np.eye(128, dtype=np.float32)
    tri = (p[:, None] <= p[None, :]).astype(np.float32)
    c[:, 128:256] = tri
    c[:, 256:384] = tri * np.float32(-1.0 / 16.0)
    c[:, 384] = (p % 8).astype(np.float32)
    c[0:16, 401:417] = np.eye(16, dtype=np.float32) * np.float32(-1.0 / 16.0)
    return c


def _rope_tables():
    half = 32
    inv_freq = np.power(np.float32(10000.0), -np.arange(half, dtype=np.float32) / np.float32(half)).astype(np.float32)
    pos = np.concatenate([np.arange(SEQ, dtype=np.float32), np.full(16, 8192.0, np.float32)])
    ang = (pos[None, :] * inv_freq[:, None]).astype(np.float32)
    cos = np.cos(ang).astype(np.float32)
    sin = np.sin(ang).astype(np.float32)
    return np.stack([np.concatenate([cos, cos], 0), np.concatenate([sin, sin], 0)], 0)


_NC_CACHE = {}


def kernel(x_prompt, x_sample, cache_kv, cache_pe, state_gla, page_table,
           ffn1_norm_w, ffn1_w_gate, ffn1_w_up, ffn1_w_down, mix_norm_w, w_in,
           gla_w_a_up, gla_b_a, gla_norm_w, mla_q_norm_w, mla_w_uq, mla_kv_norm_w,
           mla_w_uk, mla_w_uv, w_out, ffn2_norm_w, ffn2_w_gate, ffn2_w_up, ffn2_w_down,
           final_norm_w, _do_sample=True, _n_tiles=8, _trace=False):
    f = lambda a: np.ascontiguousarray(np.asarray(a, dtype=np.float32))
    key = (_do_sample, _n_tiles)
    if key not in _NC_CACHE:
        _NC_CACHE[key] = build_program(_do_sample, _n_tiles)
    nc = _NC_CACHE[key]
    col = lambda v: np.asarray(v, np.float32).reshape(-1, 128).T
    smallw = np.zeros((128, 36), np.float32)
    smallw[:, 0:8] = col(ffn1_norm_w[0])
    smallw[:, 8:16] = col(mix_norm_w[0])
    smallw[:, 16:24] = col(ffn2_norm_w[0])
    smallw[:, 24:32] = col(final_norm_w)
    smallw[:, 32:33] = col(gla_norm_w[0])
    smallw[:, 33:35] = col(mla_q_norm_w[0])
    smallw[:, 35:36] = col(mla_kv_norm_w[0])
    shared = {
        "f1g": f(ffn1_w_gate[0]), "f1u": f(ffn1_w_up[0]), "f1d": f(ffn1_w_down[0]), "win": f(w_in[0]),
        "waup": f(np.concatenate([np.asarray(gla_w_a_up[0]), np.asarray(gla_b_a[0])[None, :]], 0)),
        "wuq": f(mla_w_uq[0]), "wuk": f(np.asarray(mla_w_uk[0]).reshape(128, 512)),
        "wuv": f(np.asarray(mla_w_uv[0]).reshape(128, 512)), "wout": f(w_out[0]),
        "f2g": f(ffn2_w_gate[0]), "f2u": f(ffn2_w_up[0]), "f2d": f(ffn2_w_down[0]),
        "smallw": smallw, "consts": _consts(), "rope": _rope_tables(),
    }
    xp = np.asarray(x_prompt, np.float32)
    xs = np.asarray(x_sample, np.float32)
    sg = np.asarray(state_gla, np.float32)
    ptab = np.asarray(page_table, np.int32)
    pidx = np.arange(128) // 8
    ckv_full = f(cache_kv[0]) if _do_sample else None
    cpe_full = f(cache_pe[0]) if _do_sample else None
    in_maps = []
    for c in range(NCORES):
        m = dict(shared)
        m["xp"] = np.ascontiguousarray(xp[2 * c:2 * c + 2])
        m["xs"] = np.ascontiguousarray(xs[16 * c:16 * c + 16, 0, :])
        if not _do_sample:
            in_maps.append(m)
            continue
        m["ckv"] = ckv_full
        m["cpe"] = cpe_full
        m["sgla"] = np.ascontiguousarray(sg[0, 16 * c:16 * c + 16])
        pt_c = ptab[16 * c:16 * c + 16].reshape(16, 4, 16)
        m["ptl"] = np.ascontiguousarray(pt_c[:, :, pidx].transpose(2, 0, 1).reshape(128, 64)).astype(np.int32)
        in_maps.append(m)
    res = run_bass_kernel_spmd(nc, in_maps, core_ids=list(range(NCORES)), trace=_trace)
    R = res.results
    cat = lambda k: np.concatenate([np.asarray(r[k]) for r in R], axis=0)
    y_prompt = cat("yp")
    y_sample = cat("ys")[:, None, :]
    outs = (y_prompt, y_sample, cat("kvp")[None], cat("pep")[None], cat("glap")[None],
            cat("kvs")[None, :, None, :], cat("pes")[None, :, None, :], cat("glas")[None])
    outs = tuple(np.ascontiguousarray(o, dtype=np.float32) for o in outs)
    if _trace:
        kernel.last_exec_ns = res.exec_time_ns
    return outs
```

```python
import math
import numpy as np
from contextlib import ExitStack
import concourse.bass as bass
import concourse.mybir as mybir
from concourse.bass_utils import run_bass_kernel_spmd

F32 = mybir.dt.float32
BF16 = mybir.dt.bfloat16
I32 = mybir.dt.int32
AF = mybir.ActivationFunctionType
ALU = mybir.AluOpType

NCORES = 8
D = 1024
KC = 8
DFF = 2816
SEQ = 2048
NT = 512
NPOOL = 10240
NSEQ_S = 16
EPS = 1e-6
MLA_SCALE = 1.0 / math.sqrt(192.0)
NUNITS = 80
SAME_ENGINE_SYNC = True

C_Q, C_K, C_V, C_G, C_A, C_CQ, C_KV = 0, 256, 512, 1024, 1536, 1552, 1808


class Buf:
    __slots__ = ("name", "last_w", "readers", "war", "sem_in", "n_in", "sem_out", "n_out")

    def __init__(self, name):
        self.name = name
        self.last_w = None
        self.readers = []
        self.war = []
        self.sem_in = None
        self.n_in = 0
        self.sem_out = None
        self.n_out = 0


class Op:
    __slots__ = ("eng", "fn", "deps", "dma", "sem", "count", "needs_inc", "seq")

    def __init__(self, eng, fn, dma):
        self.eng = eng
        self.fn = fn
        self.deps = set()
        self.dma = dma
        self.sem = None
        self.count = 0
        self.needs_inc = False
        self.seq = 0


class Prog:
    ENGS = ("pe", "act", "dve", "pool", "sp")

    def __init__(self, nc, stack):
        self.nc = nc
        self.stack = stack
        self.ops = []
        self.nsem = 0
        self.stores = []

    def new_sem(self, name):
        self.nsem += 1
        return self.stack.enter_context(self.nc.semaphore(f"s{self.nsem}_{name}"))

    def sb(self, name, shape, dtype):
        return self.stack.enter_context(self.nc.sbuf_tensor(name, list(shape), dtype))

    def ps(self, name, shape, dtype=F32):
        return self.stack.enter_context(self.nc.psum_tensor(name, list(shape), dtype))

    def op(self, eng, fn, reads=(), writes=(), dma=False, no_waw=False):
        o = Op(eng, fn, dma)
        for b in reads:
            if b.last_w is not None:
                o.deps.add(b.last_w)
        for b in writes:
            for r in b.readers:
                o.deps.add(r)
            if no_waw:
                for r in b.war:
                    o.deps.add(r)
            elif b.last_w is not None:
                o.deps.add(b.last_w)
        o.deps.discard(o)
        if dma:
            if writes:
                b = writes[0]
                if b.sem_in is None:
                    b.sem_in = self.new_sem("i_" + b.name)
                b.n_in += 16
                o.sem, o.count = b.sem_in, b.n_in
            else:
                b = reads[0]
                if b.sem_out is None:
                    b.sem_out = self.new_sem("o_" + b.name)
                b.n_out += 16
                o.sem, o.count = b.sem_out, b.n_out
                self.stores.append(o)
        for b in reads:
            b.readers.append(o)
        for b in writes:
            if b.readers or not no_waw:
                b.war = [r for r in b.readers if r is not o]
            b.readers = []
            b.last_w = o
        for d in o.deps:
            if not d.dma and (d.eng != eng or dma or (SAME_ENGINE_SYNC and eng != "pe")):
                d.needs_inc = True
        self.ops.append(o)
        return o

    def pe(self, fn, reads=(), writes=()):
        return self.op("pe", fn, reads, writes)

    def act(self, fn, reads=(), writes=()):
        return self.op("act", fn, reads, writes)

    def dve(self, fn, reads=(), writes=()):
        return self.op("dve", fn, reads, writes)

    def pool(self, fn, reads=(), writes=()):
        return self.op("pool", fn, reads, writes)

    def dma(self, fn, reads=(), writes=(), q="sp", no_waw=True):
        return self.op(q, fn, reads, writes, dma=True, no_waw=no_waw)

    def emit(self):
        nc = self.nc
        eng_sem = {e: self.new_sem("eng_" + e) for e in self.ENGS}
        cnt = {e: 0 for e in self.ENGS}
        for o in self.ops:
            if not o.dma and o.needs_inc:
                cnt[o.eng] += 1
                o.seq = cnt[o.eng]
        per = {e: [o for o in self.ops if o.eng == e] for e in self.ENGS}
        final_waits = {}
        for o in self.stores:
            k = id(o.sem)
            if k not in final_waits or final_waits[k][1] < o.count:
                final_waits[k] = (o.sem, o.count)

        def run(engname, engobj):
            waited = {}
            for o in per[engname]:
                need = {}
                for d in o.deps:
                    if d.dma:
                        s, c = d.sem, d.count
                    else:
                        if d.eng == engname and not o.dma and (engname == "pe" or not SAME_ENGINE_SYNC):
                            continue
                        s, c = eng_sem[d.eng], d.seq
                    k = id(s)
                    if k not in need or need[k][1] < c:
                        need[k] = (s, c)
                for k, (s, c) in need.items():
                    if waited.get(k, 0) >= c:
                        continue
                    engobj.wait_ge(s, c)
                    waited[k] = c
                inst = o.fn(engobj)
                if o.dma:
                    inst.then_inc(o.sem, 16)
                elif o.needs_inc:
                    inst.then_inc(eng_sem[engname], 1)
            if engname == "sp":
                for k, (s, c) in final_waits.items():
                    if waited.get(k, 0) < c:
                        engobj.wait_ge(s, c)

        with nc.Block() as block:
            @block.tensor
            def _(e):
                run("pe", e)

            @block.scalar
            def _(e):
                run("act", e)

            @block.vector
            def _(e):
                run("dve", e)

            @block.gpsimd
            def _(e):
                run("pool", e)

            @block.sync
            def _(e):
                run("sp", e)


class T:
    def __init__(self, P, name, shape, dtype, psum=False):
        self.t = P.ps("P_" + name, shape, dtype) if psum else P.sb("S_" + name, shape, dtype)
        self.b = Buf(name)


class View:
    def __init__(self, ap, b):
        self.t = ap
        self.b = b


class Rot:
    def __init__(self, items):
        self.items = items
        self.i = 0

    def next(self):
        x = self.items[self.i % len(self.items)]
        self.i += 1
        return x


class Builder:
    def __init__(self, nc, P, do_sample=True, n_tiles=8):
        self.nc = nc
        self.P = P
        self.do_sample = do_sample
        self.n_tiles = n_tiles
        dt = nc.dram_tensor
        I, O = "ExternalInput", "ExternalOutput"
        self.xp = dt("xp", [2, SEQ, D], F32, kind=I)
        self.xs = dt("xs", [NSEQ_S, D], F32, kind=I)
        if do_sample:
            self.ckv = dt("ckv", [NPOOL, 128, 128], F32, kind=I)
            self.cpe = dt("cpe", [NPOOL, 128, 64], F32, kind=I)
            self.sgla = dt("sgla", [NSEQ_S, 4, 64, 128], F32, kind=I)
            self.ptl = dt("ptl", [128, NSEQ_S * 4], I32, kind=I)
        self.w = {}
        for name, shp in (("f1g", [D, DFF]), ("f1u", [D, DFF]), ("f1d", [DFF, D]), ("win", [D, 2000]),
                          ("waup", [17, 256]), ("wuq", [256, 768]), ("wuk", [128, 512]), ("wuv", [128, 512]),
                          ("wout", [D, D]), ("f2g", [D, DFF]), ("f2u", [D, DFF]), ("f2d", [DFF, D]),
                          ("smallw", [128, 36]), ("consts", [128, 448]), ("rope", [2, 64, SEQ + 16])):
            self.w[name] = dt(name, shp, F32, kind=I)
        self.scr = dt("wscr", [NUNITS, 128, 11 * 256], BF16, kind="Internal")
        self.scrb = Buf("wscr")
        self.units = {}
        self.first_pass = True
        self.yp = dt("yp", [2, SEQ, D], F32, kind=O)
        self.ys = dt("ys", [NSEQ_S, D], F32, kind=O)
        self.kvp = dt("kvp", [2, SEQ, 128], F32, kind=O)
        self.pep = dt("pep", [2, SEQ, 64], F32, kind=O)
        self.glap = dt("glap", [2, 4, 64, 128], F32, kind=O)
        self.kvs = dt("kvs", [NSEQ_S, 128], F32, kind=O)
        self.pes = dt("pes", [NSEQ_S, 64], F32, kind=O)
        self.glas = dt("glas", [NSEQ_S, 4, 64, 128], F32, kind=O)

    def alloc(self):
        P = self.P
        mk = lambda n, s, d=F32: T(P, n, s, d)
        self.cst = mk("cst", [128, 448])
        self.smw = mk("smw", [128, 36])
        self.identb = mk("identb", [128, 128], BF16)
        self.trib = mk("trib", [128, 128], BF16)
        self.on1024 = mk("on1024", [128, 128], BF16)
        self.on256 = mk("on256", [128, 128], BF16)
        self.on128 = mk("on128", [128, 128], BF16)
        self.on1 = mk("on1", [128, 128], BF16)
        self.onf = mk("onf", [128, 128])
        self.waup = mk("waup", [17, 256], BF16)
        self.wuq = mk("wuq", [128, 2, 768], BF16)
        self.wuqr = mk("wuqr", [128, 2, 4, 64], BF16)
        self.wuk = mk("wuk", [128, 512], BF16)
        self.wuv = mk("wuv", [128, 512], BF16)
        self.wukT = mk("wukT", [128, 4, 128], BF16)
        self.xio = Rot([mk(f"xio{i}", [128, D]) for i in range(2)])
        self.xT = mk("xT", [128, KC, NT])
        self.xn = mk("xn", [128, KC, NT], BF16)
        self.h = mk("h", [128, 11, NT], BF16)
        self.rstd = mk("rstd", [128, NT])
        self.sg = Rot([mk(f"sg{i}", [128, NT]) for i in range(2)])
        self.su = Rot([mk(f"su{i}", [128, NT]) for i in range(2)])
        self.wbuf = Rot([mk(f"wb{i}", [128, 11, 256], BF16) for i in range(4)])
        self.dummy = mk("dummy", [128, 8])
        self.arena = mk("arena", [128, 6144], BF16)
        ar = self.arena.t
        self.latT_t = ar[:, 0:2048]
        self.latM_t = ar[:, 2048:4096].rearrange("p (a c) -> p a c", c=128)
        self.kpe_t = ar[0:64, 4096:6144]
        self.bA, self.bB, self.bC = Buf("arA"), Buf("arB"), Buf("arC")
        self.alow = mk("alow", [17, NT], BF16)
        self.Lt = Rot([mk(f"Lt{i}", [128, 256]) for i in range(2)])
        self.enb = Rot([mk(f"enb{i}", [128, 256]) for i in range(2)])
        self.bTs = mk("bTs", [64, 4, NT])
        self.eb = Rot([mk(f"eb{i}", [64, NT]) for i in range(2)])
        self.elast = mk("elast", [64, 4, 4])
        self.qt = [mk(f"qt{h}", [64, NT], BF16) for h in range(4)]
        self.kt = [mk(f"kt{h}", [64, NT], BF16) for h in range(4)]
        self.ktm = mk("ktm", [128, 4, 256], BF16)
        self.vtm = mk("vtm", [128, 4, 512], BF16)
        self.sgT = mk("sgT", [128, 4, NT], BF16)
        self.cq = mk("cq", [128, 2, NT])
        self.cqn = mk("cqn", [128, 2, NT], BF16)
        self.qn = Rot([mk(f"qn{i}", [128, NT], BF16) for i in range(1)])
        self.ql = Rot([mk(f"ql{i}", [128, NT], BF16) for i in range(4)])
        self.qpe = Rot([mk(f"qpe{i}", [64, NT], BF16) for i in range(4)])
        self.ckvf = mk("ckvf", [128, NT])
        self.latf = mk("latf", [128, NT])
        self.kpef = mk("kpef", [64, NT])
        self.cs = mk("cs", [64, 2, NT])
        self.rt = Rot([mk(f"rt{i}", [64, NT]) for i in range(2)])
        self.wkvr = mk("wkvr", [128, KC, 64], BF16)
        self.S = [mk(f"S{h}", [64, 128]) for h in range(4)]
        self.Sb = [mk(f"Sb{h}", [64, 128], BF16) for h in range(4)]
        self.ym = mk("ym", [128, 8, NT], BF16)
        self.pt = Rot([mk(f"pt{i}", [128, NT], BF16) for i in range(3)])
        self.atm = Rot([mk(f"atm{i}", [128, 128], BF16) for i in range(3)])
        self.rl = mk("rl", [128, NT])
        self.ol = mk("ol", [128, NT], BF16)
        self.otok = Rot([mk(f"otok{i}", [128, 4, 128]) for i in range(2)])
        self.otok2 = Rot([mk(f"otk2{i}", [128, 4, 64]) for i in range(2)])
        self.G = Rot([T(P, f"pg{i}", [128, NT], F32, psum=True) for i in range(3)])
        self.O = Rot([T(P, f"po{i}", [128, NT], F32, psum=True) for i in range(2)])
        self.XS = T(P, "pxs", [128, NT], F32, psum=True)
        self.F = Rot(self.G.items + [self.XS])
        xbs = []
        for i in range(2):
            full = T(P, f"pxb{i}", [128, 1024], BF16, psum=True)
            for j in range(2):
                v = View(full.t[:, j * 512:(j + 1) * 512], full.b)
                xbs.append(v)
        self.XB = Rot(xbs)
        if self.do_sample:
            self.idx = mk("idx", [128, NSEQ_S * 4], I32)
            self.ptls = mk("ptls", [128, NSEQ_S * 4], I32)
            self.lt4 = Rot([mk(f"lt4_{i}", [128, 512], BF16) for i in range(2)])
            self.pt4 = Rot([mk(f"pt4_{i}", [128, 256], BF16) for i in range(2)])
            self.pts = Rot([mk(f"pts{i}", [128, 16, 4], BF16) for i in range(2)])
            self.ptsum = Rot([mk(f"ptsum{i}", [128, 4]) for i in range(2)])
            self.QL = mk("QL", [128, NSEQ_S, 4], BF16)
            self.QP = mk("QP", [128, NSEQ_S, 4], BF16)
            self.s0 = Rot([mk(f"s0_{i}", [64, 4, 128]) for i in range(2)])
            self.s1 = mk("s1", [64, 4, 128])
            self.kmask = Rot([mk(f"kmask{i}", [16, 256], BF16) for i in range(2)])
            self.qsf = mk("qsf", [64, 4, 18])
            self.esf = mk("esf", [64, 4, 16])
            self.num = mk("num", [128, NSEQ_S, 4])
            self.den = mk("den", [128, NSEQ_S, 4])
            self.pnew = mk("pnew", [128, 16])
            self.prod = mk("prod", [128, 16])
            self.prod2 = mk("prod2", [64, 16])
            self.tmp16 = mk("tmp16", [128, 16])
            self.gsb = mk("gsb", [128, 64])

    def mm(self, out, ob, lhsT, lb, rhs, rb, start=True, stop=True):
        rd = [lb] if rb is lb else [lb, rb]
        self.P.pe(lambda e: e.matmul(out, lhsT=lhsT, rhs=rhs, start=start, stop=stop), reads=rd, writes=[ob])

    def tr(self, out, ob, in_, ib, ident, idb):
        self.P.pe(lambda e: e.transpose(out, in_, ident), reads=[ib, idb], writes=[ob])

    def actf(self, out, ob, in_, ib, func, scale=1.0, bias=0.0):
        self.P.act(lambda e: e.activation(out=out, in_=in_, func=func, bias=bias, scale=scale), reads=[ib], writes=[ob])

    def acp(self, out, ob, in_, ib):
        self.P.act(lambda e: e.activation(out=out, in_=in_, func=AF.Copy), reads=[ib], writes=[ob])

    def tt(self, out, ob, a, ab, b, bb, op):
        self.P.dve(lambda e: e.tensor_tensor(out=out, in0=a, in1=b, op=op), reads=[ab, bb], writes=[ob])

    def stt(self, out, ob, a, ab, scalar, b, bb, op0, op1, sreads=()):
        self.P.dve(lambda e: e.scalar_tensor_tensor(out=out, in0=a, scalar=scalar, in1=b, op0=op0, op1=op1),
                   reads=[ab, bb] + list(sreads), writes=[ob])

    def ts(self, out, ob, a, ab, s1, op0, s2=None, op1=None, sreads=()):
        if op1 is None:
            self.P.dve(lambda e: e.tensor_scalar(out=out, in0=a, scalar1=s1, scalar2=None, op0=op0),
                       reads=[ab] + list(sreads), writes=[ob])
        else:
            self.P.dve(lambda e: e.tensor_scalar(out=out, in0=a, scalar1=s1, scalar2=s2, op0=op0, op1=op1),
                       reads=[ab] + list(sreads), writes=[ob])

    def cp(self, out, ob, in_, ib):
        self.P.dve(lambda e: e.tensor_copy(out=out, in_=in_), reads=[ib], writes=[ob])

    def memset(self, ap, b, v):
        self.P.dve(lambda e: e.memset(ap, v), writes=[b])

    def recip(self, out, ob, in_, ib):
        self.P.dve(lambda e: e.reciprocal(out=out, in_=in_), reads=[ib], writes=[ob])

    def dma_in(self, out, ob, src, q="sp"):
        self.P.dma(lambda e: e.dma_start(out=out, in_=src), writes=[ob], q=q)

    def dma_out(self, dst, in_, ib, q="sp"):
        self.P.dma(lambda e: e.dma_start(out=dst, in_=in_), reads=[ib], q=q)

    def rstd_from(self, sq_aps, sqb, ones, C, npart=128):
        ps = self.XS
        n = len(sq_aps)
        for i, ap in enumerate(sq_aps):
            self.mm(ps.t[:, 0:C], ps.b, ones.t[0:npart, :], ones.b, ap, sqb, start=(i == 0), stop=(i == n - 1))
        self.actf(self.rstd.t[:, 0:C], self.rstd.b, ps.t[:, 0:C], ps.b, AF.Ln, bias=EPS)
        self.actf(self.rstd.t[:, 0:C], self.rstd.b, self.rstd.t[:, 0:C], self.rstd.b, AF.Exp, scale=-0.5)

    def unit_idx(self, key):
        if key not in self.units:
            assert self.first_pass, key
            self.units[key] = len(self.units)
            assert len(self.units) <= NUNITS
        return self.units[key]

    def wstream(self, key, wt, dst, src_f32, scr_view):
        if self.first_pass:
            self.dma_in(dst, wt.b, src_f32, q="pool")
            self.P.dma(lambda e: e.dma_start(out=scr_view, in_=dst), reads=[wt.b], q="sp")
        else:
            self.P.dma(lambda e: e.dma_start(out=dst, in_=scr_view), writes=[wt.b], q="pool")

    def end_first_pass(self):
        self.first_pass = False
        bufs = [w.b for w in self.wbuf.items]
        self.P.pool(lambda e: e.memset(self.dummy.t[:], 0.0), writes=bufs + [self.dummy.b])

    def wload8(self, wd, c0, ncols):
        wt = self.wbuf.next()
        u = self.unit_idx((wd.name, c0, ncols))
        src = wd.ap()[:, c0:c0 + ncols].rearrange("(kc p) c -> p kc c", p=128)
        scr_view = self.scr.ap()[u, :, 0:KC * ncols].rearrange("p (k c) -> p k c", c=ncols)
        self.wstream(u, wt, wt.t[:, 0:KC, 0:ncols], src, scr_view)
        return wt

    def setup(self):
        w = self.w
        self.dma_in(self.cst.t[:], self.cst.b, w["consts"].ap())
        self.dma_in(self.smw.t[:], self.smw.b, w["smallw"].ap())
        self.ident = self.cst.t[:, 0:128]
        self.trif = self.cst.t[:, 256:384]
        self.negI = self.cst.t[:, 401:417]
        self.cp(self.identb.t[:], self.identb.b, self.cst.t[:, 0:128], self.cst.b)
        self.cp(self.trib.t[:], self.trib.b, self.cst.t[:, 128:256], self.cst.b)
        for tl, v in ((self.on1024, 1.0 / 1024), (self.on256, 1.0 / 256), (self.on128, 1.0 / 128), (self.on1, 1.0),
                      (self.onf, 1.0), (self.alow, 1.0)):
            self.memset(tl.t[:], tl.b, v)
        self.memset(self.latf.t[:, 0:128], self.latf.b, 0.0)
        self.memset(self.kpef.t[:, 0:128], self.kpef.b, 0.0)
        self.memset(self.xT.t[:, :, 0:128], self.xT.b, 0.0)
        if self.do_sample:
            self.memset(self.qsf.t[:], self.qsf.b, 0.0)
        self.dma_in(self.waup.t[:], self.waup.b, w["waup"].ap(), q="pool")
        self.dma_in(self.wuq.t[:], self.wuq.b, w["wuq"].ap().rearrange("(kc p) c -> p kc c", p=128), q="pool")
        self.dma_in(self.wuk.t[:], self.wuk.b, w["wuk"].ap(), q="pool")
        self.dma_in(self.wuv.t[:], self.wuv.b, w["wuv"].ap(), q="pool")
        for h in range(4):
            b0 = h * 192 + 128
            self.ts(self.wuqr.t[:, :, h, 0:32], self.wuqr.b, self.wuq.t[:, :, b0 + 32:b0 + 64], self.wuq.b, -1.0, ALU.mult)
            self.cp(self.wuqr.t[:, :, h, 32:64], self.wuqr.b, self.wuq.t[:, :, b0:b0 + 32], self.wuq.b)
            xb = self.XB.next()
            self.tr(xb.t[:, 0:128], xb.b, self.wuk.t[:, h * 128:(h + 1) * 128], self.wuk.b, self.identb.t[:], self.identb.b)
            self.cp(self.wukT.t[:, h, :], self.wukT.b, xb.t[:, 0:128], xb.b)

    def norm_x(self, col0, C, inplace=False):
        sq = self.h
        self.actf(sq.t[:, 0:KC, 0:C], sq.b, self.xT.t[:, :, 0:C], self.xT.b, AF.Square)
        self.rstd_from([sq.t[:, kc, 0:C] for kc in range(KC)], sq.b, self.on1024, C)
        for kc in range(KC):
            o, ob = (self.xT.t[:, kc, 0:C], self.xT.b) if inplace else (self.xn.t[:, kc, 0:C], self.xn.b)
            self.stt(o, ob, self.xT.t[:, kc, 0:C], self.xT.b, self.smw.t[:, col0 + kc:col0 + kc + 1],
                     self.rstd.t[:, 0:C], self.rstd.b, ALU.mult, ALU.mult, sreads=[self.smw.b])

    def ffn(self, wg, wu, wdn, col0, C):
        xn = self.xn
        sq = self.h
        self.actf(sq.t[:, 0:KC, 0:C], sq.b, self.xT.t[:, :, 0:C], self.xT.b, AF.Square)
        for kc in range(KC):
            self.ts(xn.t[:, kc, 0:C], xn.b, self.xT.t[:, kc, 0:C], self.xT.b, self.smw.t[:, col0 + kc:col0 + kc + 1], ALU.mult,
                    sreads=[self.smw.b])
        stats_done = False
        for half in range(2):
            f0 = half * 11
            for u in range(6):
                fs = [f for f in (f0 + 2 * u, f0 + 2 * u + 1) if f < f0 + 11]
                wgt = self.wload8(wg, fs[0] * 128, len(fs) * 128)
                wut = self.wload8(wu, fs[0] * 128, len(fs) * 128)
                for i, f in enumerate(fs):
                    pg = self.F.next() if stats_done else self.G.next()
                    pu = self.F.next() if stats_done else self.G.next()
                    for kc in range(KC):
                        self.mm(pg.t[:, 0:C], pg.b, wgt.t[:, kc, i * 128:(i + 1) * 128], wgt.b, xn.t[:, kc, 0:C], xn.b, kc == 0, kc == KC - 1)
                    for kc in range(KC):
                        self.mm(pu.t[:, 0:C], pu.b, wut.t[:, kc, i * 128:(i + 1) * 128], wut.b, xn.t[:, kc, 0:C], xn.b, kc == 0, kc == KC - 1)
                    if not stats_done:
                        self.rstd_from([sq.t[:, kc, 0:C] for kc in range(KC)], sq.b, self.on1024, C)
                        stats_done = True
                    sg = self.sg.next()
                    su = self.su.next()
                    self.tt(sg.t[:, 0:C], sg.b, pg.t[:, 0:C], pg.b, self.rstd.t[:, 0:C], self.rstd.b, ALU.mult)
                    self.actf(sg.t[:, 0:C], sg.b, sg.t[:, 0:C], sg.b, AF.Silu)
                    self.tt(su.t[:, 0:C], su.b, pu.t[:, 0:C], pu.b, self.rstd.t[:, 0:C], self.rstd.b, ALU.mult)
                    self.tt(self.h.t[:, f - f0, 0:C], self.h.b, su.t[:, 0:C], su.b, sg.t[:, 0:C], sg.b, ALU.mult)
            for dcp in range(4):
                wdt = self.wbuf.next()
                src = wdn.ap()[f0 * 128:(f0 + 11) * 128, dcp * 256:(dcp + 1) * 256].rearrange("(f p) c -> p f c", p=128)
                u = self.unit_idx((wdn.name, "down", f0, dcp))
                self.wstream(u, wdt, wdt.t[:], src, self.scr.ap()[u].rearrange("p (f c) -> p f c", c=256))
                for i in range(2):
                    dc = dcp * 2 + i
                    po = self.O.next()
                    for f in range(11):
                        self.mm(po.t[:, 0:C], po.b, wdt.t[:, f, i * 128:(i + 1) * 128], wdt.b, self.h.t[:, f, 0:C], self.h.b, f == 0, f == 10)
                    self.stt(self.xT.t[:, dc, 0:C], self.xT.b, po.t[:, 0:C], po.b, 0.5, self.xT.t[:, dc, 0:C], self.xT.b, ALU.mult, ALU.add)

    def load_x(self, src_rows, nblk, npart):
        for a in range(nblk):
            xi = self.xio.next()
            self.dma_in(xi.t[0:npart, :], xi.b, src_rows(a))
            for g in range(2):
                pg = self.G.next()
                for k in range(4):
                    kc = g * 4 + k
                    self.tr(pg.t[:, k * 128:k * 128 + npart], pg.b, xi.t[0:npart, kc * 128:(kc + 1) * 128], xi.b,
                            self.ident[0:npart, 0:npart], self.cst.b)
                src_v = pg.t[:].rearrange("p (k t) -> p k t", k=4)[:, :, 0:npart]
                dst_v = self.xT.t[:, g * 4:(g + 1) * 4, a * 128:a * 128 + npart]
                if g == 0:
                    self.acp(dst_v, self.xT.b, src_v, pg.b)
                else:
                    self.cp(dst_v, self.xT.b, src_v, pg.b)

    def store_y(self, dst_rows, nblk, npart):
        yT = self.xT
        for a in range(nblk):
            yo = self.xio.next()
            for g in range(2):
                pg = self.G.next()
                for k in range(4):
                    kc = g * 4 + k
                    self.tr(pg.t[:, k * 128:(k + 1) * 128], pg.b, yT.t[:, kc, a * 128:(a + 1) * 128], yT.b, self.ident, self.cst.b)
                if g == 0:
                    self.acp(yo.t[0:npart, 0:512], yo.b, pg.t[0:npart, :], pg.b)
                else:
                    self.cp(yo.t[0:npart, 512:1024], yo.b, pg.t[0:npart, :], pg.b)
            self.dma_out(dst_rows(a), yo.t[0:npart, :], yo.b)

    def mixer_proj(self, C, A, npart, pos0, sample):
        win = self.w["win"]
        xn = self.xn
        self.norm_x(8, C)
        self.dma_in(self.cs.t[:, :, 0:C], self.cs.b, self.w["rope"].ap()[:, :, pos0:pos0 + C].rearrange("s r c -> r s c"))
        cum = self.negI[0:npart, 0:npart] if sample else self.trif
        wa = self.wload8(win, C_A, 16)
        pa = self.G.next()
        for kc in range(KC):
            self.mm(pa.t[0:16, 0:C], pa.b, wa.t[:, kc, 0:16], wa.b, xn.t[:, kc, 0:C], xn.b, kc == 0, kc == KC - 1)
        self.cp(self.alow.t[0:16, 0:C], self.alow.b, pa.t[0:16, 0:C], pa.b)
        wk = self.wload8(win, C_K, 256)
        wv0 = self.wload8(win, C_V, 256)
        wv1 = self.wload8(win, C_V + 256, 256)
        for a in range(A):
            ca = slice(a * 128, a * 128 + npart)
            pz = self.XS
            self.mm(pz.t[0:npart, 0:256], pz.b, self.alow.t[:, ca], self.alow.b, self.waup.t[:], self.waup.b)
            L = self.Lt.next()
            self.actf(L.t[0:npart, :], L.b, pz.t[0:npart, 0:256], pz.b, AF.Exp, scale=-1.0)
            self.actf(L.t[0:npart, :], L.b, L.t[0:npart, :], L.b, AF.Ln, bias=1.0)
            if not sample:
                pb = self.XS
                self.mm(pb.t[0:npart, 0:256], pb.b, cum, self.cst.b, L.t[0:npart, :], L.b)
                enb = self.enb.next()
                self.actf(enb.t[0:npart, :], enb.b, pb.t[0:npart, 0:256], pb.b, AF.Exp, scale=-1.0)
            pbt = self.G.next()
            for h in range(4):
                self.mm(pbt.t[0:64, h * 128:h * 128 + npart], pbt.b, L.t[0:npart, h * 64:(h + 1) * 64], L.b, cum, self.cst.b)
            self.cp(self.bTs.t[:, :, ca], self.bTs.b, pbt.t[0:64, :].rearrange("p (h t) -> p h t", h=4)[:, :, 0:npart], pbt.b)
            pk = self.G.next()
            for kc in range(KC):
                self.mm(pk.t[0:npart, 0:256], pk.b, xn.t[:, kc, ca], xn.b, wk.t[:, kc, 0:256], wk.b, kc == 0, kc == KC - 1)
            if sample:
                self.cp(self.ktm.t[0:npart, a, :], self.ktm.b, pk.t[0:npart, 0:256], pk.b)
            else:
                self.tt(self.ktm.t[0:npart, a, :], self.ktm.b, pk.t[0:npart, 0:256], pk.b, enb.t[0:npart, :], enb.b, ALU.mult)
            pv = self.G.next()
            for kc in range(KC):
                self.mm(pv.t[0:npart, 0:256], pv.b, xn.t[:, kc, ca], xn.b, wv0.t[:, kc, 0:256], wv0.b, kc == 0, kc == KC - 1)
            for kc in range(KC):
                self.mm(pv.t[0:npart, 256:512], pv.b, xn.t[:, kc, ca], xn.b, wv1.t[:, kc, 0:256], wv1.b, kc == 0, kc == KC - 1)
            self.acp(self.vtm.t[0:npart, a, :], self.vtm.b, pv.t[0:npart, :], pv.b)
        wq = self.wload8(win, C_Q, 256)
        for h in range(4):
            ebq = self.eb.next()
            self.actf(ebq.t[:, 0:C], ebq.b, self.bTs.t[:, h, 0:C], self.bTs.b, AF.Exp, scale=1.0)
            if sample:
                self.cp(self.esf.t[:, h, 0:C], self.esf.b, ebq.t[:, 0:C], ebq.b)
            else:
                for a in range(A):
                    self.cp(self.elast.t[:, h, a:a + 1], self.elast.b, ebq.t[:, a * 128 + 127:a * 128 + 128], ebq.b)
            pq = self.G.next()
            for kc in range(KC):
                self.mm(pq.t[0:64, 0:C], pq.b, wq.t[:, kc, h * 64:(h + 1) * 64], wq.b, xn.t[:, kc, 0:C], xn.b, kc == 0, kc == KC - 1)
            if sample:
                self.ts(self.qsf.t[:, h, 0:C], self.qsf.b, pq.t[0:64, 0:C], pq.b, 0.125, ALU.mult)
            else:
                self.stt(self.qt[h].t[:, 0:C], self.qt[h].b, pq.t[0:64, 0:C], pq.b, 0.125, ebq.t[:, 0:C], ebq.b, ALU.mult, ALU.mult)
                ebk = self.eb.next()
                self.actf(ebk.t[:, 0:C], ebk.b, self.bTs.t[:, h, 0:C], self.bTs.b, AF.Exp, scale=-1.0)
                pk = self.G.next()
                for kc in range(KC):
                    self.mm(pk.t[0:64, 0:C], pk.b, wk.t[:, kc, h * 64:(h + 1) * 64], wk.b, xn.t[:, kc, 0:C], xn.b, kc == 0, kc == KC - 1)
                self.tt(self.kt[h].t[:, 0:C], self.kt[h].b, pk.t[0:64, 0:C], pk.b, ebk.t[:, 0:C], ebk.b, ALU.mult)
        for u in range(2):
            wg = self.wload8(win, C_G + u * 256, 256)
            for i in range(2):
                pg = self.G.next()
                for kc in range(KC):
                    self.mm(pg.t[:, 0:C], pg.b, wg.t[:, kc, i * 128:(i + 1) * 128], wg.b, xn.t[:, kc, 0:C], xn.b, kc == 0, kc == KC - 1)
                self.actf(self.sgT.t[:, u * 2 + i, 0:C], self.sgT.b, pg.t[:, 0:C], pg.b, AF.Silu)
        wc = self.wload8(win, C_CQ, 256)
        for i in range(2):
            pc = self.G.next()
            for kc in range(KC):
                self.mm(pc.t[:, 0:C], pc.b, wc.t[:, kc, i * 128:(i + 1) * 128], wc.b, xn.t[:, kc, 0:C], xn.b, kc == 0, kc == KC - 1)
            self.cp(self.cq.t[:, i, 0:C], self.cq.b, pc.t[:, 0:C], pc.b)
        sq = self.h
        self.actf(sq.t[:, 0:2, 0:C], sq.b, self.cq.t[:, :, 0:C], self.cq.b, AF.Square)
        self.rstd_from([sq.t[:, i, 0:C] for i in range(2)], sq.b, self.on256, C)
        for i in range(2):
            self.stt(self.cqn.t[:, i, 0:C], self.cqn.b, self.cq.t[:, i, 0:C], self.cq.b, self.smw.t[:, 33 + i:34 + i],
                     self.rstd.t[:, 0:C], self.rstd.b, ALU.mult, ALU.mult, sreads=[self.smw.b])
        wkv = self.wload8(win, C_KV, 192)
        self.ts(self.wkvr.t[:, :, 0:32], self.wkvr.b, wkv.t[:, 0:KC, 160:192], wkv.b, -1.0, ALU.mult)
        self.cp(self.wkvr.t[:, :, 32:64], self.wkvr.b, wkv.t[:, 0:KC, 128:160], wkv.b)
        pc = self.G.next()
        for kc in range(KC):
            self.mm(pc.t[:, 0:C], pc.b, wkv.t[:, kc, 0:128], wkv.b, xn.t[:, kc, 0:C], xn.b, kc == 0, kc == KC - 1)
        self.cp(self.ckvf.t[:, 0:C], self.ckvf.b, pc.t[:, 0:C], pc.b)
        self.actf(sq.t[:, 0, 0:C], sq.b, self.ckvf.t[:, 0:C], self.ckvf.b, AF.Square)
        self.rstd_from([sq.t[:, 0, 0:C]], sq.b, self.on128, C)
        self.stt(self.latf.t[:, 0:C], self.latf.b, self.ckvf.t[:, 0:C], self.ckvf.b, self.smw.t[:, 35:36],
                 self.rstd.t[:, 0:C], self.rstd.b, ALU.mult, ALU.mult, sreads=[self.smw.b])
        p1 = self.G.next()
        for kc in range(KC):
            self.mm(p1.t[0:64, 0:C], p1.b, wkv.t[:, kc, 128:192], wkv.b, xn.t[:, kc, 0:C], xn.b, kc == 0, kc == KC - 1)
        r1 = self.rt.next()
        self.tt(r1.t[:, 0:C], r1.b, p1.t[0:64, 0:C], p1.b, self.cs.t[:, 0, 0:C], self.cs.b, ALU.mult)
        p2 = self.G.next()
        for kc in range(KC):
            self.mm(p2.t[0:64, 0:C], p2.b, self.wkvr.t[:, kc, :], self.wkvr.b, xn.t[:, kc, 0:C], xn.b, kc == 0, kc == KC - 1)
        r2 = self.rt.next()
        self.tt(r2.t[:, 0:C], r2.b, p2.t[0:64, 0:C], p2.b, self.cs.t[:, 1, 0:C], self.cs.b, ALU.mult)
        self.tt(self.kpef.t[:, 0:C], self.kpef.b, r1.t[:, 0:C], r1.b, r2.t[:, 0:C], r2.b, ALU.add)

    def mla_q(self, h, C):
        cqn = self.cqn
        pn = self.G.next()
        for i in range(2):
            self.mm(pn.t[:, 0:C], pn.b, self.wuq.t[:, i, h * 192:h * 192 + 128], self.wuq.b, cqn.t[:, i, 0:C], cqn.b, i == 0, i == 1)
        qn = self.qn.next()
        self.acp(qn.t[:, 0:C], qn.b, pn.t[:, 0:C], pn.b)
        pl = self.G.next()
        self.mm(pl.t[:, 0:C], pl.b, self.wukT.t[:, h, :], self.wukT.b, qn.t[:, 0:C], qn.b)
        ql = self.ql.next()
        self.acp(ql.t[:, 0:C], ql.b, pl.t[:, 0:C], pl.b)
        p1 = self.G.next()
        for i in range(2):
            self.mm(p1.t[0:64, 0:C], p1.b, self.wuq.t[:, i, h * 192 + 128:h * 192 + 192], self.wuq.b, cqn.t[:, i, 0:C], cqn.b, i == 0, i == 1)
        r1 = self.rt.next()
        self.tt(r1.t[:, 0:C], r1.b, p1.t[0:64, 0:C], p1.b, self.cs.t[:, 0, 0:C], self.cs.b, ALU.mult)
        p2 = self.G.next()
        for i in range(2):
            self.mm(p2.t[0:64, 0:C], p2.b, self.wuqr.t[:, i, h, :], self.wuqr.b, cqn.t[:, i, 0:C], cqn.b, i == 0, i == 1)
        r2 = self.rt.next()
        self.tt(r2.t[:, 0:C], r2.b, p2.t[0:64, 0:C], p2.b, self.cs.t[:, 1, 0:C], self.cs.b, ALU.mult)
        qp = self.qpe.next()
        self.tt(qp.t[:, 0:C], qp.b, r1.t[:, 0:C], r1.b, r2.t[:, 0:C], r2.b, ALU.add)
        return ql, qp

    def gla_out(self, h, src, srcb, C):
        sq = self.h
        self.actf(sq.t[:, 0, 0:C], sq.b, src, srcb, AF.Square)
        self.rstd_from([sq.t[:, 0, 0:C]], sq.b, self.on128, C)
        r = self.rl
        self.stt(r.t[:, 0:C], r.b, src, srcb, self.smw.t[:, 32:33], self.rstd.t[:, 0:C], self.rstd.b,
                 ALU.mult, ALU.mult, sreads=[self.smw.b])
        self.tt(self.ym.t[:, h, 0:C], self.ym.b, r.t[:, 0:C], r.b, self.sgT.t[:, h, 0:C], self.sgT.b, ALU.mult)

    def out_proj(self, C):
        wout = self.w["wout"]
        for u in range(4):
            wo = self.wload8(wout, u * 256, 256)
            for i in range(2):
                dc = u * 2 + i
                po = self.O.next()
                for kc in range(KC):
                    self.mm(po.t[:, 0:C], po.b, wo.t[:, kc, i * 128:(i + 1) * 128], wo.b, self.ym.t[:, kc, 0:C], self.ym.b, kc == 0, kc == KC - 1)
                self.tt(self.xT.t[:, dc, 0:C], self.xT.b, po.t[:, 0:C], po.b, self.xT.t[:, dc, 0:C], self.xT.b, ALU.add)

    def prompt_tile(self, ti):
        s, t = ti // 4, ti % 4
        C = NT
        tok0 = t * NT
        w = self.w
        self.load_x(lambda a: self.xp.ap()[s, tok0 + a * 128:tok0 + (a + 1) * 128, :], 4, 128)
        self.ffn(w["f1g"], w["f1u"], w["f1d"], 0, C)
        if t == 0:
            for h in range(4):
                self.memset(self.S[h].t[:], self.S[h].b, 0.0)
                self.memset(self.Sb[h].t[:], self.Sb[h].b, 0.0)
        self.mixer_proj(C, 4, 128, tok0, sample=False)
        self.cp(self.latT_t[:, tok0:tok0 + C], self.bA, self.latf.t[:, 0:C], self.latf.b)
        self.cp(self.kpe_t[:, tok0:tok0 + C], self.bC, self.kpef.t[:, 0:C], self.kpef.b)
        pt_ = self.G.next()
        for a in range(4):
            self.tr(pt_.t[:, a * 128:(a + 1) * 128], pt_.b, self.latf.t[:, a * 128:(a + 1) * 128], self.latf.b, self.ident, self.cst.b)
        ot = self.otok.next()
        self.acp(ot.t[:].rearrange("p a c -> p (a c)"), ot.b, pt_.t[:], pt_.b)
        self.cp(self.latM_t[:, t * 4:(t + 1) * 4, :], self.bB, ot.t[:], ot.b)
        self.dma_out(self.kvp.ap()[s, tok0:tok0 + C, :].rearrange("(a p) c -> p a c", p=128), ot.t[:], ot.b)
        pt2 = self.G.next()
        for a in range(4):
            self.tr(pt2.t[:, a * 64:(a + 1) * 64], pt2.b, self.kpef.t[:, a * 128:(a + 1) * 128], self.kpef.b, self.ident[0:64, 0:64], self.cst.b)
        ot2 = self.otok2.next()
        self.acp(ot2.t[:].rearrange("p a c -> p (a c)"), ot2.b, pt2.t[:, 0:256], pt2.b)
        self.dma_out(self.pep.ap()[s, tok0:tok0 + C, :].rearrange("(a p) c -> p a c", p=128), ot2.t[:], ot2.b)
        gitems = [(a, h) for a in range(4) for h in range(4)]
        pos = {}

        def gla_A(a, h):
            ca = slice(a * 128, (a + 1) * 128)
            pa = self.G.next()
            self.mm(pa.t[:, 0:128], pa.b, self.kt[h].t[:, ca], self.kt[h].b, self.qt[h].t[:, ca], self.qt[h].b)
            am = self.atm.next()
            self.tt(am.t[:], am.b, pa.t[:, 0:128], pa.b, self.trib.t[:], self.trib.b, ALU.mult)
            return am

        def gla_rest(a, h, am):
            ca = slice(a * 128, (a + 1) * 128)
            hc = slice(h * 128, (h + 1) * 128)
            if h == 0:
                pos[a] = self.O.next()
            po = pos[a]
            self.mm(po.t[:, hc], po.b, self.vtm.t[:, a, hc], self.vtm.b, am.t[:], am.b, True, False)
            self.mm(po.t[:, hc], po.b, self.Sb[h].t[:], self.Sb[h].b, self.qt[h].t[:, ca], self.qt[h].b, False, True)
            pS = self.G.next()
            self.mm(pS.t[0:64, 0:128], pS.b, self.ktm.t[:, a, h * 64:(h + 1) * 64], self.ktm.b, self.vtm.t[:, a, hc], self.vtm.b)
            el = self.elast.t[:, h, a:a + 1]
            self.ts(self.S[h].t[:], self.S[h].b, self.S[h].t[:], self.S[h].b, el, ALU.mult, sreads=[self.elast.b])
            self.stt(self.S[h].t[:], self.S[h].b, pS.t[0:64, 0:128], pS.b, el, self.S[h].t[:], self.S[h].b, ALU.mult, ALU.add,
                     sreads=[self.elast.b])
            self.acp(self.Sb[h].t[:], self.Sb[h].b, self.S[h].t[:], self.S[h].b)
            if h == 3:
                sq = self.h
                self.actf(sq.t[:, 0, 0:C], sq.b, po.t[:, 0:C], po.b, AF.Square)
                self.rstd_from([sq.t[:, 0, 0:C]], sq.b, self.on128, C)
                r = self.rl
                self.stt(r.t[:, 0:C], r.b, po.t[:, 0:C], po.b, self.smw.t[:, 32:33], self.rstd.t[:, 0:C], self.rstd.b,
                         ALU.mult, ALU.mult, sreads=[self.smw.b])
                self.tt(self.ym.t[:, 0:4, ca], self.ym.b, r.t[:, 0:C].rearrange("p (h t) -> p h t", h=4), r.b,
                        self.sgT.t[:, 0:4, ca], self.sgT.b, ALU.mult)

        am_next = gla_A(*gitems[0])
        for i, (a, h) in enumerate(gitems):
            am_cur = am_next
            if i + 1 < len(gitems):
                am_next = gla_A(*gitems[i + 1])
            gla_rest(a, h, am_cur)
        if t == 3:
            for h in range(4):
                self.dma_out(self.glap.ap()[s, h], self.S[h].t[:], self.S[h].b)
        qs = [self.mla_q(h, C) for h in range(4)]
        nkb = 4 * t + 4
        items = [(h, kb) for h in range(4) for kb in range(nkb)]
        acc = {}

        def emit_scores(h, kb):
            ql, qp = qs[h]
            r = kb - 4 * t
            c0 = 128 * r if r > 0 else 0
            N = C - c0
            ps = self.G.next()
            ks = slice(kb * 128, (kb + 1) * 128)
            self.mm(ps.t[:, 0:N], ps.b, self.latT_t[:, ks], self.bA, ql.t[:, c0:C], ql.b, True, False)
            self.mm(ps.t[:, 0:N], ps.b, self.kpe_t[:, ks], self.bC, qp.t[:, c0:C], qp.b, False, True)
            return ps, r, c0, N

        def emit_pv(h, kb, info):
            ps, r, c0, N = info
            if kb == 0:
                acc[h] = (self.O.next(), self.O.next())
            po, pl = acc[h]
            pt = self.pt.next()
            self.actf(pt.t[:, 0:N], pt.b, ps.t[:, 0:N], ps.b, AF.Exp, scale=MLA_SCALE)
            if r >= 0:
                self.tt(pt.t[:, 0:128], pt.b, pt.t[:, 0:128], pt.b, self.trib.t[:], self.trib.b, ALU.mult)
            self.mm(po.t[:, c0:C], po.b, self.latM_t[:, kb, :], self.bB, pt.t[:, 0:N], pt.b, kb == 0, kb == nkb - 1)
            self.mm(pl.t[:, c0:C], pl.b, self.on1.t[:], self.on1.b, pt.t[:, 0:N], pt.b, kb == 0, kb == nkb - 1)
            if kb == nkb - 1:
                self.actf(self.rl.t[:, 0:C], self.rl.b, pl.t[:, 0:C], pl.b, AF.Ln)
                self.actf(self.rl.t[:, 0:C], self.rl.b, self.rl.t[:, 0:C], self.rl.b, AF.Exp, scale=-1.0)
                self.tt(self.ol.t[:, 0:C], self.ol.b, po.t[:, 0:C], po.b, self.rl.t[:, 0:C], self.rl.b, ALU.mult)
                py = self.G.next()
                self.mm(py.t[:, 0:C], py.b, self.wuv.t[:, h * 128:(h + 1) * 128], self.wuv.b, self.ol.t[:, 0:C], self.ol.b)
                self.acp(self.ym.t[:, 4 + h, 0:C], self.ym.b, py.t[:, 0:C], py.b)

        info = emit_scores(*items[0])
        for i, (h, kb) in enumerate(items):
            nxt = emit_scores(*items[i + 1]) if i + 1 < len(items) else None
            emit_pv(h, kb, info)
            info = nxt
        self.out_proj(C)
        self.ffn(w["f2g"], w["f2u"], w["f2d"], 16, C)
        self.norm_x(24, C, inplace=True)
        self.store_y(lambda a: self.yp.ap()[s, tok0 + a * 128:tok0 + (a + 1) * 128, :], 4, 128)

    def sample_tile(self):
        C = NSEQ_S
        w = self.w
        self.load_x(lambda a: self.xs.ap(), 1, 16)
        self.ffn(w["f1g"], w["f1u"], w["f1d"], 0, C)
        self.mixer_proj(C, 1, 16, SEQ, sample=True)
        pt_ = self.G.next()
        self.tr(pt_.t[:, 0:128], pt_.b, self.latf.t[:, 0:128], self.latf.b, self.ident, self.cst.b)
        ot = self.otok.next()
        self.acp(ot.t[0:16, 0, :], ot.b, pt_.t[0:16, 0:128], pt_.b)
        self.dma_out(self.kvs.ap(), ot.t[0:16, 0, :], ot.b)
        pt2 = self.G.next()
        self.tr(pt2.t[:, 0:64], pt2.b, self.kpef.t[:, 0:128], self.kpef.b, self.ident[0:64, 0:64], self.cst.b)
        ot2 = self.otok2.next()
        self.acp(ot2.t[0:16, 0, :], ot2.b, pt2.t[0:16, 0:64], pt2.b)
        self.dma_out(self.pes.ap(), ot2.t[0:16, 0, :], ot2.b)
        self.decode_attention()
        self.out_proj(C)
        self.ffn(w["f2g"], w["f2u"], w["f2d"], 16, C)
        self.norm_x(24, C, inplace=True)
        self.store_y(lambda a: self.ys.ap(), 1, 16)

    def gla_sample_seq(self, b):
        pgl = self.XS
        s0 = self.s0.next()
        self.dma_in(s0.t[:], s0.b, self.sgla.ap()[b].rearrange("h d v -> d h v"))
        km = self.kmask.next()
        self.ts(km.t[:], km.b, self.ktm.t[0:16, 0, :], self.ktm.b, self.cst.t[0:16, b:b + 1], ALU.mult, sreads=[self.cst.b])
        pd = self.G.next()
        for h in range(4):
            self.mm(pd.t[0:64, h * 128:(h + 1) * 128], pd.b, km.t[:, h * 64:(h + 1) * 64], km.b,
                    self.vtm.t[0:16, 0, h * 128:(h + 1) * 128], self.vtm.b)
        s1 = self.s1
        for h in range(4):
            self.stt(s1.t[:, h, :], s1.b, s0.t[:, h, :], s0.b, self.esf.t[:, h, b:b + 1], pd.t[0:64, h * 128:(h + 1) * 128], pd.b,
                     ALU.mult, ALU.add, sreads=[self.esf.b])
        self.dma_out(self.glas.ap()[b].rearrange("h d v -> d h v"), s1.t[:], s1.b)
        for h in range(4):
            self.mm(pgl.t[:, h * 18 + b:h * 18 + b + 2], pgl.b, s1.t[:, h, :], s1.b, self.qsf.t[:, h, b:b + 2], self.qsf.b)

    def gla_sample_finish(self):
        pgl = self.XS
        self.cp(self.gsb.t[:].rearrange("p (h t) -> p h t", h=4), self.gsb.b,
                pgl.t[:, 0:72].rearrange("p (h t) -> p h t", h=4)[:, :, 0:16], pgl.b)
        for h in range(4):
            self.gla_out(h, self.gsb.t[:, h * 16:(h + 1) * 16], self.gsb.b, NSEQ_S)

    def bcreg(self, e):
        if getattr(self, "_bcreg", None) is None:
            self._bcreg = e.alloc_register("gather_bound")
            e.reg_mov(self._bcreg, NPOOL * 8 - 1)
        return self._bcreg

    def decode_attention(self):
        P = self.P
        C = NSEQ_S
        ar = self.arena.t
        gl = [(ar[:, 0:2048].rearrange("p (t c) -> p t c", c=128), ar[:, 0:2048], self.bA),
              (ar[:, 2048:4096].rearrange("p (t c) -> p t c", c=128), ar[:, 2048:4096], self.bB)]
        gp = [(ar[:, 4096:5120].rearrange("p (t c) -> p t c", c=64), ar[:, 4096:5120], self.bC),
              (ar[:, 5120:6144].rearrange("p (t c) -> p t c", c=64), ar[:, 5120:6144], self.bC)]
        self.dma_in(self.ptls.t[:], self.ptls.b, self.ptl.ap())
        self.ts(self.idx.t[:], self.idx.b, self.ptls.t[:], self.ptls.b, 8.0, ALU.mult, self.cst.t[:, 384:385], ALU.add, sreads=[self.cst.b])
        ckv_v = self.ckv.ap().rearrange("n (g t) c -> (n g) (t c)", t=16)
        cpe_v = self.cpe.ap().rearrange("n (g t) c -> (n g) (t c)", t=16)
        for h in range(4):
            ql, qp = self.mla_q(h, C)
            self.cp(self.QL.t[:, :, h], self.QL.b, ql.t[:, 0:C], ql.b)
            self.cp(self.QP.t[0:64, :, h], self.QP.b, qp.t[:, 0:C], qp.b)
        P.dma(lambda e: e.dma_start(out=self.QP.t[64:128, :, :], in_=self.QP.t[0:64, :, :]), writes=[self.QP.b], q="sp", no_waw=False)
        pnum, pden = self.O.next(), self.O.next()
        NB_ = 2
        NG = NSEQ_S * 4

        def gather(gi):
            glv, glf, glb = gl[gi % NB_]
            gpv, gpf, gpb = gp[gi % NB_]
            P.dma(lambda e: e.indirect_dma_start(
                out=glf, out_offset=None, in_=ckv_v,
                in_offset=bass.IndirectOffsetOnAxis(ap=self.idx.t[:, gi:gi + 1], axis=0),
                bounds_check=self.bcreg(e), oob_is_err=False),
                reads=[self.idx.b], writes=[glb], q="pool", no_waw=False)
            P.dma(lambda e: e.indirect_dma_start(
                out=gpf, out_offset=None, in_=cpe_v,
                in_offset=bass.IndirectOffsetOnAxis(ap=self.idx.t[:, gi:gi + 1], axis=0),
                bounds_check=self.bcreg(e), oob_is_err=False),
                reads=[self.idx.b], writes=[gpb], q="pool", no_waw=False)

        stage = {}
        psts = {}
        ptss = {}

        def emit_T(qi):
            gi, quad = qi // 4, qi % 4
            glv, glf, glb = gl[gi % NB_]
            gpv, gpf, gpb = gp[gi % NB_]
            xa, xb = self.XB.next(), self.XB.next()
            for k in range(4):
                self.tr(xa.t[:, k * 128:(k + 1) * 128], xa.b, glv[:, quad * 4 + k, :], glb, self.identb.t[:], self.identb.b)
            for j in range(2):
                c0_ = (quad * 4 + 2 * j) * 64
                self.tr(xb.t[:, j * 128:(j + 1) * 128], xb.b, gpf[:, c0_:c0_ + 128], gpb, self.identb.t[:], self.identb.b)
            lt4 = self.lt4.next()
            self.acp(lt4.t[:], lt4.b, xa.t[:], xa.b)
            pt4 = self.pt4.next()
            self.P.dve(lambda e: e.tensor_copy(out=pt4.t[:], in_=xb.t[:, 0:256]), reads=[xb.b, lt4.b], writes=[pt4.b])
            stage[qi] = (lt4, pt4)

        def emit_S(qi):
            gi, quad = qi // 4, qi % 4
            b, g = gi // 4, gi % 4
            if b not in psts:
                psts[b] = self.G.next()
            pst = psts[b]
            lt4, pt4 = stage.pop(qi)
            for k in range(4):
                kbi = g * 16 + quad * 4 + k
                self.mm(pst.t[:, kbi * 4:(kbi + 1) * 4], pst.b, lt4.t[:, k * 128:(k + 1) * 128], lt4.b, self.QL.t[:, b, :], self.QL.b, True, False)
                hp = slice((k % 2) * 64, (k % 2) * 64 + 64)
                self.mm(pst.t[:, kbi * 4:(kbi + 1) * 4], pst.b, pt4.t[hp, (k // 2) * 128:(k // 2 + 1) * 128], pt4.b,
                        self.QP.t[hp, b, :], self.QP.b, False, True)
            if quad == 3:
                pts = self.pts.next()
                self.actf(pts.t[:].rearrange("p t h -> p (t h)"), pts.b, pst.t[:, g * 64:(g + 1) * 64], pst.b, AF.Exp, scale=MLA_SCALE)
                psm = self.ptsum.next()
                P.dve(lambda e: e.tensor_reduce(out=psm.t[:], in_=pts.t[:].rearrange("p t h -> p h t"),
                                                axis=mybir.AxisListType.X, op=ALU.add),
                      reads=[pts.b], writes=[psm.b])
                ptss[gi] = (pts, psm)

        def emit_PV(gi):
            b, g = gi // 4, gi % 4
            glv, glf, glb = gl[gi % NB_]
            pts, psm = ptss.pop(gi)
            for tt_ in range(16):
                self.mm(pnum.t[:, b * 4:(b + 1) * 4], pnum.b, glv[:, tt_, :], glb, pts.t[:, tt_, :], pts.b,
                        g == 0 and tt_ == 0, g == 3 and tt_ == 15)
            self.mm(pden.t[:, b * 4:(b + 1) * 4], pden.b, self.onf.t[:], self.onf.b, psm.t[:], psm.b, g == 0, g == 3)

        NQ = NG * 4
        gather(0)
        for qi in range(NQ + 1):
            if qi < NQ:
                if qi % 4 == 0 and qi // 4 + 1 < NG:
                    pass
                emit_T(qi)
            if qi >= 1:
                emit_S(qi - 1)
                if (qi - 1) % 4 == 3:
                    gdone = (qi - 1) // 4
                    emit_PV(gdone)
                    if gdone + 2 < NG:
                        gather(gdone + 2)
            if qi == 0 and NG > 1:
                gather(1)
            if qi % 16 == 6 and qi < NQ:
                self.gla_sample_seq(qi // 16)
        self.gla_sample_finish()
        self.cp(self.num.t[:].rearrange("p b h -> p (b h)"), self.num.b, pnum.t[:, 0:64], pnum.b)
        self.cp(self.den.t[:].rearrange("p b h -> p (b h)"), self.den.b, pden.t[:, 0:64], pden.b)
        for h in range(4):
            self.tt(self.prod.t[:], self.prod.b, self.QL.t[:, :, h], self.QL.b, self.latf.t[:, 0:C], self.latf.b, ALU.mult)
            self.tt(self.prod2.t[:], self.prod2.b, self.QP.t[0:64, :, h], self.QP.b, self.kpef.t[:, 0:C], self.kpef.b, ALU.mult)
            psn = self.G.next()
            self.mm(psn.t[:, 0:C], psn.b, self.onf.t[:], self.onf.b, self.prod.t[:], self.prod.b, True, False)
            self.mm(psn.t[:, 0:C], psn.b, self.onf.t[0:64, :], self.onf.b, self.prod2.t[:], self.prod2.b, False, True)
            self.actf(self.pnew.t[:], self.pnew.b, psn.t[:, 0:C], psn.b, AF.Exp, scale=MLA_SCALE)
            self.tt(self.tmp16.t[:], self.tmp16.b, self.pnew.t[:], self.pnew.b, self.latf.t[:, 0:C], self.latf.b, ALU.mult)
            self.tt(self.num.t[:, :, h], self.num.b, self.num.t[:, :, h], self.num.b, self.tmp16.t[:], self.tmp16.b, ALU.add)
            self.tt(self.den.t[:, :, h], self.den.b, self.den.t[:, :, h], self.den.b, self.pnew.t[:], self.pnew.b, ALU.add)
            self.recip(self.tmp16.t[:], self.tmp16.b, self.den.t[:, :, h], self.den.b)
            self.tt(self.ol.t[:, 0:C], self.ol.b, self.num.t[:, :, h], self.num.b, self.tmp16.t[:], self.tmp16.b, ALU.mult)
            py = self.G.next()
            self.mm(py.t[:, 0:C], py.b, self.wuv.t[:, h * 128:(h + 1) * 128], self.wuv.b, self.ol.t[:, 0:C], self.ol.b)
            self.acp(self.ym.t[:, 4 + h, 0:C], self.ym.b, py.t[:, 0:C], py.b)


def build_program(do_sample=True, n_tiles=8):
    nc = bass.Bass("TRN2", target_bir_lowering=False)
    with ExitStack() as st:
        P = Prog(nc, st)
        B = Builder(nc, P, do_sample=do_sample, n_tiles=n_tiles)
        B.alloc()
        build_program.sbuf_left = nc.sbuf_bytes_remaining
        B.setup()
        for ti in range(n_tiles):
            B.prompt_tile(ti)
            if B.first_pass:
                B.end_first_pass()
        if do_sample:
            B.sample_tile()
        build_program.n_ops = len(P.ops)
        P.emit()
    return nc


def _consts():
    c = np.zeros((128, 448), np.float32)
    p = np.arange(128)
    c[:, 0:128] = np.eye(128, dtype=np.float32)
    tri = (p[:, None] <= p[None, :]).astype(np.float32)
    c[:, 128:256] = tri
    c[:, 256:384] = tri * np.float32(-1.0 / 16.0)
    c[:, 384] = (p % 8).astype(np.float32)
    c[0:16, 401:417] = np.eye(16, dtype=np.float32) * np.float32(-1.0 / 16.0)
    return c


def _rope_tables():
    half = 32
    inv_freq = np.power(np.float32(10000.0), -np.arange(half, dtype=np.float32) / np.float32(half)).astype(np.float32)
    pos = np.concatenate([np.arange(SEQ, dtype=np.float32), np.full(16, 8192.0, np.float32)])
    ang = (pos[None, :] * inv_freq[:, None]).astype(np.float32)
    cos = np.cos(ang).astype(np.float32)
    sin = np.sin(ang).astype(np.float32)
    return np.stack([np.concatenate([cos, cos], 0), np.concatenate([sin, sin], 0)], 0)


_NC_CACHE = {}


def kernel(x_prompt, x_sample, cache_kv, cache_pe, state_gla, page_table,
           ffn1_norm_w, ffn1_w_gate, ffn1_w_up, ffn1_w_down, mix_norm_w, w_in,
           gla_w_a_up, gla_b_a, gla_norm_w, mla_q_norm_w, mla_w_uq, mla_kv_norm_w,
           mla_w_uk, mla_w_uv, w_out, ffn2_norm_w, ffn2_w_gate, ffn2_w_up, ffn2_w_down,
           final_norm_w, _do_sample=True, _n_tiles=8, _trace=False):
    f = lambda a: np.ascontiguousarray(np.asarray(a, dtype=np.float32))
    key = (_do_sample, _n_tiles)
    if key not in _NC_CACHE:
        _NC_CACHE[key] = build_program(_do_sample, _n_tiles)
    nc = _NC_CACHE[key]
    col = lambda v: np.asarray(v, np.float32).reshape(-1, 128).T
    smallw = np.zeros((128, 36), np.float32)
    smallw[:, 0:8] = col(ffn1_norm_w[0])
    smallw[:, 8:16] = col(mix_norm_w[0])
    smallw[:, 16:24] = col(ffn2_norm_w[0])
    smallw[:, 24:32] = col(final_norm_w)
    smallw[:, 32:33] = col(gla_norm_w[0])
    smallw[:, 33:35] = col(mla_q_norm_w[0])
    smallw[:, 35:36] = col(mla_kv_norm_w[0])
    shared = {
        "f1g": f(ffn1_w_gate[0]), "f1u": f(ffn1_w_up[0]), "f1d": f(ffn1_w_down[0]), "win": f(w_in[0]),
        "waup": f(np.concatenate([np.asarray(gla_w_a_up[0]), np.asarray(gla_b_a[0])[None, :]], 0)),
        "wuq": f(mla_w_uq[0]), "wuk": f(np.asarray(mla_w_uk[0]).reshape(128, 512)),
        "wuv": f(np.asarray(mla_w_uv[0]).reshape(128, 512)), "wout": f(w_out[0]),
        "f2g": f(ffn2_w_gate[0]), "f2u": f(ffn2_w_up[0]), "f2d": f(ffn2_w_down[0]),
        "smallw": smallw, "consts": _consts(), "rope": _rope_tables(),
    }
    xp = np.asarray(x_prompt, np.float32)
    xs = np.asarray(x_sample, np.float32)
    sg = np.asarray(state_gla, np.float32)
    ptab = np.asarray(page_table, np.int32)
    pidx = np.arange(128) // 8
    ckv_full = f(cache_kv[0]) if _do_sample else None
    cpe_full = f(cache_pe[0]) if _do_sample else None
    in_maps = []
    for c in range(NCORES):
        m = dict(shared)
        m["xp"] = np.ascontiguousarray(xp[2 * c:2 * c + 2])
        m["xs"] = np.ascontiguousarray(xs[16 * c:16 * c + 16, 0, :])
        if not _do_sample:
            in_maps.append(m)
            continue
        m["ckv"] = ckv_full
        m["cpe"] = cpe_full
        m["sgla"] = np.ascontiguousarray(sg[0, 16 * c:16 * c + 16])
        pt_c = ptab[16 * c:16 * c + 16].reshape(16, 4, 16)
        m["ptl"] = np.ascontiguousarray(pt_c[:, :, pidx].transpose(2, 0, 1).reshape(128, 64)).astype(np.int32)
        in_maps.append(m)
    res = run_bass_kernel_spmd(nc, in_maps, core_ids=list(range(NCORES)), trace=_trace)
    R = res.results
    cat = lambda k: np.concatenate([np.asarray(r[k]) for r in R], axis=0)
    y_prompt = cat("yp")
    y_sample = cat("ys")[:, None, :]
    outs = (y_prompt, y_sample, cat("kvp")[None], cat("pep")[None], cat("glap")[None],
            cat("kvs")[None, :, None, :], cat("pes")[None, :, None, :], cat("glas")[None])
    outs = tuple(np.ascontiguousarray(o, dtype=np.float32) for o in outs)
    if _trace:
        kernel.last_exec_ns = res.exec_time_ns
    return outs
```

```python
import math
import numpy as np
from contextlib import ExitStack
import concourse.bass as bass
import concourse.mybir as mybir
from concourse.bass_utils import run_bass_kernel_spmd

F32 = mybir.dt.float32
BF16 = mybir.dt.bfloat16
I32 = mybir.dt.int32
AF = mybir.ActivationFunctionType
ALU = mybir.AluOpType

NCORES = 8
D = 1024
KC = 8
DFF = 2816
SEQ = 2048
NT = 512
NPOOL = 10240
NSEQ_S = 16
EPS = 1e-6
MLA_SCALE = 1.0 / math.sqrt(192.0)
NUNITS = 80
SAME_ENGINE_SYNC = True

C_Q, C_K, C_V, C_G, C_A, C_CQ, C_KV = 0, 256, 512, 1024, 1536, 1552, 1808


class Buf:
    __slots__ = ("name", "last_w", "readers", "war", "sem_in", "n_in", "sem_out", "n_out")

    def __init__(self, name):
        self.name = name
        self.last_w = None
        self.readers = []
        self.war = []
        self.sem_in = None
        self.n_in = 0
        self.sem_out = None
        self.n_out = 0


class Op:
    __slots__ = ("eng", "fn", "deps", "dma", "sem", "count", "needs_inc", "seq")

    def __init__(self, eng, fn, dma):
        self.eng = eng
        self.fn = fn
        self.deps = set()
        self.dma = dma
        self.sem = None
        self.count = 0
        self.needs_inc = False
        self.seq = 0


class Prog:
    ENGS = ("pe", "act", "dve", "pool", "sp")

    def __init__(self, nc, stack):
        self.nc = nc
        self.stack = stack
        self.ops = []
        self.nsem = 0
        self.stores = []

    def new_sem(self, name):
        self.nsem += 1
        return self.stack.enter_context(self.nc.semaphore(f"s{self.nsem}_{name}"))

    def sb(self, name, shape, dtype):
        return self.stack.enter_context(self.nc.sbuf_tensor(name, list(shape), dtype))

    def ps(self, name, shape, dtype=F32):
        return self.stack.enter_context(self.nc.psum_tensor(name, list(shape), dtype))

    def op(self, eng, fn, reads=(), writes=(), dma=False, no_waw=False):
        o = Op(eng, fn, dma)
        for b in reads:
            if b.last_w is not None:
                o.deps.add(b.last_w)
        for b in writes:
            for r in b.readers:
                o.deps.add(r)
            if no_waw:
                for r in b.war:
                    o.deps.add(r)
            elif b.last_w is not None:
                o.deps.add(b.last_w)
        o.deps.discard(o)
        if dma:
            if writes:
                b = writes[0]
                if b.sem_in is None:
                    b.sem_in = self.new_sem("i_" + b.name)
                b.n_in += 16
                o.sem, o.count = b.sem_in, b.n_in
            else:
                b = reads[0]
                if b.sem_out is None:
                    b.sem_out = self.new_sem("o_" + b.name)
                b.n_out += 16
                o.sem, o.count = b.sem_out, b.n_out
                self.stores.append(o)
        for b in reads:
            b.readers.append(o)
        for b in writes:
            if b.readers or not no_waw:
                b.war = [r for r in b.readers if r is not o]
            b.readers = []
            b.last_w = o
        for d in o.deps:
            if not d.dma and (d.eng != eng or dma or (SAME_ENGINE_SYNC and eng != "pe")):
                d.needs_inc = True
        self.ops.append(o)
        return o

    def pe(self, fn, reads=(), writes=()):
        return self.op("pe", fn, reads, writes)

    def act(self, fn, reads=(), writes=()):
        return self.op("act", fn, reads, writes)

    def dve(self, fn, reads=(), writes=()):
        return self.op("dve", fn, reads, writes)

    def pool(self, fn, reads=(), writes=()):
        return self.op("pool", fn, reads, writes)

    def dma(self, fn, reads=(), writes=(), q="sp", no_waw=True):
        return self.op(q, fn, reads, writes, dma=True, no_waw=no_waw)

    def emit(self):
        nc = self.nc
        eng_sem = {e: self.new_sem("eng_" + e) for e in self.ENGS}
        cnt = {e: 0 for e in self.ENGS}
        for o in self.ops:
            if not o.dma and o.needs_inc:
                cnt[o.eng] += 1
                o.seq = cnt[o.eng]
        per = {e: [o for o in self.ops if o.eng == e] for e in self.ENGS}
        final_waits = {}
        for o in self.stores:
            k = id(o.sem)
            if k not in final_waits or final_waits[k][1] < o.count:
                final_waits[k] = (o.sem, o.count)

        def run(engname, engobj):
            waited = {}
            for o in per[engname]:
                need = {}
                for d in o.deps:
                    if d.dma:
                        s, c = d.sem, d.count
                    else:
                        if d.eng == engname and not o.dma and (engname == "pe" or not SAME_ENGINE_SYNC):
                            continue
                        s, c = eng_sem[d.eng], d.seq
                    k = id(s)
                    if k not in need or need[k][1] < c:
                        need[k] = (s, c)
                for k, (s, c) in need.items():
                    if waited.get(k, 0) >= c:
                        continue
                    engobj.wait_ge(s, c)
                    waited[k] = c
                inst = o.fn(engobj)
                if o.dma:
                    inst.then_inc(o.sem, 16)
                elif o.needs_inc:
                    inst.then_inc(eng_sem[engname], 1)
            if engname == "sp":
                for k, (s, c) in final_waits.items():
                    if waited.get(k, 0) < c:
                        engobj.wait_ge(s, c)

        with nc.Block() as block:
            @block.tensor
            def _(e):
                run("pe", e)

            @block.scalar
            def _(e):
                run("act", e)

            @block.vector
            def _(e):
                run("dve", e)

            @block.gpsimd
            def _(e):
                run("pool", e)

            @block.sync
            def _(e):
                run("sp", e)


class T:
    def __init__(self, P, name, shape, dtype, psum=False):
        self.t = P.ps("P_" + name, shape, dtype) if psum else P.sb("S_" + name, shape, dtype)
        self.b = Buf(name)


class View:
    def __init__(self, ap, b):
        self.t = ap
        self.b = b


class Rot:
    def __init__(self, items):
        self.items = items
        self.i = 0

    def next(self):
        x = self.items[self.i % len(self.items)]
        self.i += 1
        return x


class Builder:
    def __init__(self, nc, P, do_sample=True, n_tiles=8):
        self.nc = nc
        self.P = P
        self.do_sample = do_sample
        self.n_tiles = n_tiles
        dt = nc.dram_tensor
        I, O = "ExternalInput", "ExternalOutput"
        self.xp = dt("xp", [2, SEQ, D], F32, kind=I)
        self.xs = dt("xs", [NSEQ_S, D], F32, kind=I)
        if do_sample:
            self.ckv = dt("ckv", [NPOOL, 128, 128], F32, kind=I)
            self.cpe = dt("cpe", [NPOOL, 128, 64], F32, kind=I)
            self.sgla = dt("sgla", [NSEQ_S, 4, 64, 128], F32, kind=I)
            self.ptl = dt("ptl", [128, NSEQ_S * 4], I32, kind=I)
        self.w = {}
        for name, shp in (("f1g", [D, DFF]), ("f1u", [D, DFF]), ("f1d", [DFF, D]), ("win", [D, 2000]),
                          ("waup", [17, 256]), ("wuq", [256, 768]), ("wuk", [128, 512]), ("wuv", [128, 512]),
                          ("wout", [D, D]), ("f2g", [D, DFF]), ("f2u", [D, DFF]), ("f2d", [DFF, D]),
                          ("smallw", [128, 36]), ("consts", [128, 448]), ("rope", [2, 64, SEQ + 16])):
            self.w[name] = dt(name, shp, F32, kind=I)
        self.scr = dt("wscr", [NUNITS, 128, 11 * 256], BF16, kind="Internal")
        self.scrb = Buf("wscr")
        self.units = {}
        self.first_pass = True
        self.yp = dt("yp", [2, SEQ, D], F32, kind=O)
        self.ys = dt("ys", [NSEQ_S, D], F32, kind=O)
        self.kvp = dt("kvp", [2, SEQ, 128], F32, kind=O)
        self.pep = dt("pep", [2, SEQ, 64], F32, kind=O)
        self.glap = dt("glap", [2, 4, 64, 128], F32, kind=O)
        self.kvs = dt("kvs", [NSEQ_S, 128], F32, kind=O)
        self.pes = dt("pes", [NSEQ_S, 64], F32, kind=O)
        self.glas = dt("glas", [NSEQ_S, 4, 64, 128], F32, kind=O)

    def alloc(self):
        P = self.P
        mk = lambda n, s, d=F32: T(P, n, s, d)
        self.cst = mk("cst", [128, 448])
        self.smw = mk("smw", [128, 36])
        self.identb = mk("identb", [128, 128], BF16)
        self.trib = mk("trib", [128, 128], BF16)
        self.on1024 = mk("on1024", [128, 128], BF16)
        self.on256 = mk("on256", [128, 128], BF16)
        self.on128 = mk("on128", [128, 128], BF16)
        self.on1 = mk("on1", [128, 128], BF16)
        self.onf = mk("onf", [128, 128])
        self.waup = mk("waup", [17, 256], BF16)
        self.wuq = mk("wuq", [128, 2, 768], BF16)
        self.wuqr = mk("wuqr", [128, 2, 4, 64], BF16)
        self.wuk = mk("wuk", [128, 512], BF16)
        self.wuv = mk("wuv", [128, 512], BF16)
        self.wukT = mk("wukT", [128, 4, 128], BF16)
        self.xio = Rot([mk(f"xio{i}", [128, D]) for i in range(2)])
        self.xT = mk("xT", [128, KC, NT])
        self.xn = mk("xn", [128, KC, NT], BF16)
        self.h = mk("h", [128, 11, NT], BF16)
        self.rstd = mk("rstd", [128, NT])
        self.sg = Rot([mk(f"sg{i}", [128, NT]) for i in range(2)])
        self.su = Rot([mk(f"su{i}", [128, NT]) for i in range(2)])
        self.wbuf = Rot([mk(f"wb{i}", [128, 11, 256], BF16) for i in range(4)])
        self.dummy = mk("dummy", [128, 8])
        self.arena = mk("arena", [128, 6144], BF16)
        ar = self.arena.t
        self.latT_t = ar[:, 0:2048]
        self.latM_t = ar[:, 2048:4096].rearrange("p (a c) -> p a c", c=128)
        self.kpe_t = ar[0:64, 4096:6144]
        self.bA, self.bB, self.bC = Buf("arA"), Buf("arB"), Buf("arC")
        self.alow = mk("alow", [17, NT], BF16)
        self.Lt = Rot([mk(f"Lt{i}", [128, 256]) for i in range(2)])
        self.enb = Rot([mk(f"enb{i}", [128, 256]) for i in range(2)])
        self.bTs = mk("bTs", [64, 4, NT])
        self.eb = Rot([mk(f"eb{i}", [64, NT]) for i in range(2)])
        self.elast = mk("elast", [64, 4, 4])
        self.qt = [mk(f"qt{h}", [64, NT], BF16) for h in range(4)]
        self.kt = [mk(f"kt{h}", [64, NT], BF16) for h in range(4)]
        self.ktm = mk("ktm", [128, 4, 256], BF16)
        self.vtm = mk("vtm", [128, 4, 512], BF16)
        self.sgT = mk("sgT", [128, 4, NT], BF16)
        self.cq = mk("cq", [128, 2, NT])
        self.cqn = mk("cqn", [128, 2, NT], BF16)
        self.qn = Rot([mk(f"qn{i}", [128, NT], BF16) for i in range(1)])
        self.ql = Rot([mk(f"ql{i}", [128, NT], BF16) for i in range(4)])
        self.qpe = Rot([mk(f"qpe{i}", [64, NT], BF16) for i in range(4)])
        self.ckvf = mk("ckvf", [128, NT])
        self.latf = mk("latf", [128, NT])
        self.kpef = mk("kpef", [64, NT])
        self.cs = mk("cs", [64, 2, NT])
        self.rt = Rot([mk(f"rt{i}", [64, NT]) for i in range(2)])
        self.wkvr = mk("wkvr", [128, KC, 64], BF16)
        self.S = [mk(f"S{h}", [64, 128]) for h in range(4)]
        self.Sb = [mk(f"Sb{h}", [64, 128], BF16) for h in range(4)]
        self.ym = mk("ym", [128, 8, NT], BF16)
        self.pt = Rot([mk(f"pt{i}", [128, NT], BF16) for i in range(3)])
        self.atm = Rot([mk(f"atm{i}", [128, 128], BF16) for i in range(3)])
        self.rl = mk("rl", [128, NT])
        self.ol = mk("ol", [128, NT], BF16)
        self.otok = Rot([mk(f"otok{i}", [128, 4, 128]) for i in range(2)])
        self.otok2 = Rot([mk(f"otk2{i}", [128, 4, 64]) for i in range(2)])
        self.G = Rot([T(P, f"pg{i}", [128, NT], F32, psum=True) for i in range(3)])
        self.O = Rot([T(P, f"po{i}", [128, NT], F32, psum=True) for i in range(2)])
        self.XS = T(P, "pxs", [128, NT], F32, psum=True)
        self.F = Rot(self.G.items + [self.XS])
        xbs = []
        for i in range(2):
            full = T(P, f"pxb{i}", [128, 1024], BF16, psum=True)
            for j in range(2):
                v = View(full.t[:, j * 512:(j + 1) * 512], full.b)
                xbs.append(v)
        self.XB = Rot(xbs)
        if self.do_sample:
            self.idx = mk("idx", [128, NSEQ_S * 4], I32)
            self.ptls = mk("ptls", [128, NSEQ_S * 4], I32)
            self.lt4 = Rot([mk(f"lt4_{i}", [128, 512], BF16) for i in range(2)])
            self.pt4 = Rot([mk(f"pt4_{i}", [128, 256], BF16) for i in range(2)])
            self.pts = Rot([mk(f"pts{i}", [128, 16, 4], BF16) for i in range(2)])
            self.ptsum = Rot([mk(f"ptsum{i}", [128, 4]) for i in range(2)])
            self.QL = mk("QL", [128, NSEQ_S, 4], BF16)
            self.QP = mk("QP", [128, NSEQ_S, 4], BF16)
            self.s0 = Rot([mk(f"s0_{i}", [64, 4, 128]) for i in range(2)])
            self.s1 = mk("s1", [64, 4, 128])
            self.kmask = Rot([mk(f"kmask{i}", [16, 256], BF16) for i in range(2)])
            self.qsf = mk("qsf", [64, 4, 18])
            self.esf = mk("esf", [64, 4, 16])
            self.num = mk("num", [128, NSEQ_S, 4])
            self.den = mk("den", [128, NSEQ_S, 4])
            self.pnew = mk("pnew", [128, 16])
            self.prod = mk("prod", [128, 16])
            self.prod2 = mk("prod2", [64, 16])
            self.tmp16 = mk("tmp16", [128, 16])
            self.gsb = mk("gsb", [128, 64])

    def mm(self, out, ob, lhsT, lb, rhs, rb, start=True, stop=True):
        rd = [lb] if rb is lb else [lb, rb]
        self.P.pe(lambda e: e.matmul(out, lhsT=lhsT, rhs=rhs, start=start, stop=stop), reads=rd, writes=[ob])

    def tr(self, out, ob, in_, ib, ident, idb):
        self.P.pe(lambda e: e.transpose(out, in_, ident), reads=[ib, idb], writes=[ob])

    def actf(self, out, ob, in_, ib, func, scale=1.0, bias=0.0):
        self.P.act(lambda e: e.activation(out=out, in_=in_, func=func, bias=bias, scale=scale), reads=[ib], writes=[ob])

    def acp(self, out, ob, in_, ib):
        self.P.act(lambda e: e.activation(out=out, in_=in_, func=AF.Copy), reads=[ib], writes=[ob])

    def tt(self, out, ob, a, ab, b, bb, op):
        self.P.dve(lambda e: e.tensor_tensor(out=out, in0=a, in1=b, op=op), reads=[ab, bb], writes=[ob])

    def stt(self, out, ob, a, ab, scalar, b, bb, op0, op1, sreads=()):
        self.P.dve(lambda e: e.scalar_tensor_tensor(out=out, in0=a, scalar=scalar, in1=b, op0=op0, op1=op1),
                   reads=[ab, bb] + list(sreads), writes=[ob])

    def ts(self, out, ob, a, ab, s1, op0, s2=None, op1=None, sreads=()):
        if op1 is None:
            self.P.dve(lambda e: e.tensor_scalar(out=out, in0=a, scalar1=s1, scalar2=None, op0=op0),
                       reads=[ab] + list(sreads), writes=[ob])
        else:
            self.P.dve(lambda e: e.tensor_scalar(out=out, in0=a, scalar1=s1, scalar2=s2, op0=op0, op1=op1),
                       reads=[ab] + list(sreads), writes=[ob])

    def cp(self, out, ob, in_, ib):
        self.P.dve(lambda e: e.tensor_copy(out=out, in_=in_), reads=[ib], writes=[ob])

    def memset(self, ap, b, v):
        self.P.dve(lambda e: e.memset(ap, v), writes=[b])

    def recip(self, out, ob, in_, ib):
        self.P.dve(lambda e: e.reciprocal(out=out, in_=in_), reads=[ib], writes=[ob])

    def dma_in(self, out, ob, src, q="sp"):
        self.P.dma(lambda e: e.dma_start(out=out, in_=src), writes=[ob], q=q)

    def dma_out(self, dst, in_, ib, q="sp"):
        self.P.dma(lambda e: e.dma_start(out=dst, in_=in_), reads=[ib], q=q)

    def rstd_from(self, sq_aps, sqb, ones, C, npart=128):
        ps = self.XS
        n = len(sq_aps)
        for i, ap in enumerate(sq_aps):
            self.mm(ps.t[:, 0:C], ps.b, ones.t[0:npart, :], ones.b, ap, sqb, start=(i == 0), stop=(i == n - 1))
        self.actf(self.rstd.t[:, 0:C], self.rstd.b, ps.t[:, 0:C], ps.b, AF.Ln, bias=EPS)
        self.actf(self.rstd.t[:, 0:C], self.rstd.b, self.rstd.t[:, 0:C], self.rstd.b, AF.Exp, scale=-0.5)

    def unit_idx(self, key):
        if key not in self.units:
            assert self.first_pass, key
            self.units[key] = len(self.units)
            assert len(self.units) <= NUNITS
        return self.units[key]

    def wstream(self, key, wt, dst, src_f32, scr_view):
        if self.first_pass:
            self.dma_in(dst, wt.b, src_f32, q="pool")
            self.P.dma(lambda e: e.dma_start(out=scr_view, in_=dst), reads=[wt.b], q="sp")
        else:
            self.P.dma(lambda e: e.dma_start(out=dst, in_=scr_view), writes=[wt.b], q="pool")

    def end_first_pass(self):
        self.first_pass = False
        bufs = [w.b for w in self.wbuf.items]
        self.P.pool(lambda e: e.memset(self.dummy.t[:], 0.0), writes=bufs + [self.dummy.b])

    def wload8(self, wd, c0, ncols):
        wt = self.wbuf.next()
        u = self.unit_idx((wd.name, c0, ncols))
        src = wd.ap()[:, c0:c0 + ncols].rearrange("(kc p) c -> p kc c", p=128)
        scr_view = self.scr.ap()[u, :, 0:KC * ncols].rearrange("p (k c) -> p k c", c=ncols)
        self.wstream(u, wt, wt.t[:, 0:KC, 0:ncols], src, scr_view)
        return wt

    def setup(self):
        w = self.w
        self.dma_in(self.cst.t[:], self.cst.b, w["consts"].ap())
        self.dma_in(self.smw.t[:], self.smw.b, w["smallw"].ap())
        self.ident = self.cst.t[:, 0:128]
        self.trif = self.cst.t[:, 256:384]
        self.negI = self.cst.t[:, 401:417]
        self.cp(self.identb.t[:], self.identb.b, self.cst.t[:, 0:128], self.cst.b)
        self.cp(self.trib.t[:], self.trib.b, self.cst.t[:, 128:256], self.cst.b)
        for tl, v in ((self.on1024, 1.0 / 1024), (self.on256, 1.0 / 256), (self.on128, 1.0 / 128), (self.on1, 1.0),
                      (self.onf, 1.0), (self.alow, 1.0)):
            self.memset(tl.t[:], tl.b, v)
        self.memset(self.latf.t[:, 0:128], self.latf.b, 0.0)
        self.memset(self.kpef.t[:, 0:128], self.kpef.b, 0.0)
        self.memset(self.xT.t[:, :, 0:128], self.xT.b, 0.0)
        if self.do_sample:
            self.memset(self.qsf.t[:], self.qsf.b, 0.0)
        self.dma_in(self.waup.t[:], self.waup.b, w["waup"].ap(), q="pool")
        self.dma_in(self.wuq.t[:], self.wuq.b, w["wuq"].ap().rearrange("(kc p) c -> p kc c", p=128), q="pool")
        self.dma_in(self.wuk.t[:], self.wuk.b, w["wuk"].ap(), q="pool")
        self.dma_in(self.wuv.t[:], self.wuv.b, w["wuv"].ap(), q="pool")
        for h in range(4):
            b0 = h * 192 + 128
            self.ts(self.wuqr.t[:, :, h, 0:32], self.wuqr.b, self.wuq.t[:, :, b0 + 32:b0 + 64], self.wuq.b, -1.0, ALU.mult)
            self.cp(self.wuqr.t[:, :, h, 32:64], self.wuqr.b, self.wuq.t[:, :, b0:b0 + 32], self.wuq.b)
            xb = self.XB.next()
            self.tr(xb.t[:, 0:128], xb.b, self.wuk.t[:, h * 128:(h + 1) * 128], self.wuk.b, self.identb.t[:], self.identb.b)
            self.cp(self.wukT.t[:, h, :], self.wukT.b, xb.t[:, 0:128], xb.b)

    def norm_x(self, col0, C, inplace=False):
        sq = self.h
        self.actf(sq.t[:, 0:KC, 0:C], sq.b, self.xT.t[:, :, 0:C], self.xT.b, AF.Square)
        self.rstd_from([sq.t[:, kc, 0:C] for kc in range(KC)], sq.b, self.on1024, C)
        for kc in range(KC):
            o, ob = (self.xT.t[:, kc, 0:C], self.xT.b) if inplace else (self.xn.t[:, kc, 0:C], self.xn.b)
            self.stt(o, ob, self.xT.t[:, kc, 0:C], self.xT.b, self.smw.t[:, col0 + kc:col0 + kc + 1],
                     self.rstd.t[:, 0:C], self.rstd.b, ALU.mult, ALU.mult, sreads=[self.smw.b])

    def ffn(self, wg, wu, wdn, col0, C):
        xn = self.xn
        sq = self.h
        self.actf(sq.t[:, 0:KC, 0:C], sq.b, self.xT.t[:, :, 0:C], self.xT.b, AF.Square)
        for kc in range(KC):
            self.ts(xn.t[:, kc, 0:C], xn.b, self.xT.t[:, kc, 0:C], self.xT.b, self.smw.t[:, col0 + kc:col0 + kc + 1], ALU.mult,
                    sreads=[self.smw.b])
        stats_done = False
        for half in range(2):
            f0 = half * 11
            for u in range(6):
                fs = [f for f in (f0 + 2 * u, f0 + 2 * u + 1) if f < f0 + 11]
                wgt = self.wload8(wg, fs[0] * 128, len(fs) * 128)
                wut = self.wload8(wu, fs[0] * 128, len(fs) * 128)
                for i, f in enumerate(fs):
                    pg = self.F.next() if stats_done else self.G.next()
                    pu = self.F.next() if stats_done else self.G.next()
                    for kc in range(KC):
                        self.mm(pg.t[:, 0:C], pg.b, wgt.t[:, kc, i * 128:(i + 1) * 128], wgt.b, xn.t[:, kc, 0:C], xn.b, kc == 0, kc == KC - 1)
                    for kc in range(KC):
                        self.mm(pu.t[:, 0:C], pu.b, wut.t[:, kc, i * 128:(i + 1) * 128], wut.b, xn.t[:, kc, 0:C], xn.b, kc == 0, kc == KC - 1)
                    if not stats_done:
                        self.rstd_from([sq.t[:, kc, 0:C] for kc in range(KC)], sq.b, self.on1024, C)
                        stats_done = True
                    sg = self.sg.next()
                    su = self.su.next()
                    self.tt(sg.t[:, 0:C], sg.b, pg.t[:, 0:C], pg.b, self.rstd.t[:, 0:C], self.rstd.b, ALU.mult)
                    self.actf(sg.t[:, 0:C], sg.b, sg.t[:, 0:C], sg.b, AF.Silu)
                    self.tt(su.t[:, 0:C], su.b, pu.t[:, 0:C], pu.b, self.rstd.t[:, 0:C], self.rstd.b, ALU.mult)
                    self.tt(self.h.t[:, f - f0, 0:C], self.h.b, su.t[:, 0:C], su.b, sg.t[:, 0:C], sg.b, ALU.mult)
            for dcp in range(4):
                wdt = self.wbuf.next()
                src = wdn.ap()[f0 * 128:(f0 + 11) * 128, dcp * 256:(dcp + 1) * 256].rearrange("(f p) c -> p f c", p=128)
                u = self.unit_idx((wdn.name, "down", f0, dcp))
                self.wstream(u, wdt, wdt.t[:], src, self.scr.ap()[u].rearrange("p (f c) -> p f c", c=256))
                for i in range(2):
                    dc = dcp * 2 + i
                    po = self.O.next()
                    for f in range(11):
                        self.mm(po.t[:, 0:C], po.b, wdt.t[:, f, i * 128:(i + 1) * 128], wdt.b, self.h.t[:, f, 0:C], self.h.b, f == 0, f == 10)
                    self.stt(self.xT.t[:, dc, 0:C], self.xT.b, po.t[:, 0:C], po.b, 0.5, self.xT.t[:, dc, 0:C], self.xT.b, ALU.mult, ALU.add)

    def load_x(self, src_rows, nblk, npart):
        for a in range(nblk):
            xi = self.xio.next()
            self.dma_in(xi.t[0:npart, :], xi.b, src_rows(a))
            for g in range(2):
                pg = self.G.next()
                for k in range(4):
                    kc = g * 4 + k
                    self.tr(pg.t[:, k * 128:k * 128 + npart], pg.b, xi.t[0:npart, kc * 128:(kc + 1) * 128], xi.b,
                            self.ident[0:npart, 0:npart], self.cst.b)
                src_v = pg.t[:].rearrange("p (k t) -> p k t", k=4)[:, :, 0:npart]
                dst_v = self.xT.t[:, g * 4:(g + 1) * 4, a * 128:a * 128 + npart]
                if g == 0:
                    self.acp(dst_v, self.xT.b, src_v, pg.b)
                else:
                    self.cp(dst_v, self.xT.b, src_v, pg.b)

    def store_y(self, dst_rows, nblk, npart):
        yT = self.xT
        for a in range(nblk):
            yo = self.xio.next()
            for g in range(2):
                pg = self.G.next()
                for k in range(4):
                    kc = g * 4 + k
                    self.tr(pg.t[:, k * 128:(k + 1) * 128], pg.b, yT.t[:, kc, a * 128:(a + 1) * 128], yT.b, self.ident, self.cst.b)
                if g == 0:
                    self.acp(yo.t[0:npart, 0:512], yo.b, pg.t[0:npart, :], pg.b)
                else:
                    self.cp(yo.t[0:npart, 512:1024], yo.b, pg.t[0:npart, :], pg.b)
            self.dma_out(dst_rows(a), yo.t[0:npart, :], yo.b)

    def mixer_proj(self, C, A, npart, pos0, sample):
        win = self.w["win"]
        xn = self.xn
        self.norm_x(8, C)
        self.dma_in(self.cs.t[:, :, 0:C], self.cs.b, self.w["rope"].ap()[:, :, pos0:pos0 + C].rearrange("s r c -> r s c"))
        cum = self.negI[0:npart, 0:npart] if sample else self.trif
        wa = self.wload8(win, C_A, 16)
        pa = self.G.next()
        for kc in range(KC):
            self.mm(pa.t[0:16, 0:C], pa.b, wa.t[:, kc, 0:16], wa.b, xn.t[:, kc, 0:C], xn.b, kc == 0, kc == KC - 1)
        self.cp(self.alow.t[0:16, 0:C], self.alow.b, pa.t[0:16, 0:C], pa.b)
        for u in range(2):
            wg = self.wload8(win, C_G + u * 256, 256)
            for i in range(2):
                pg = self.G.next()
                for kc in range(KC):
                    self.mm(pg.t[:, 0:C], pg.b, wg.t[:, kc, i * 128:(i + 1) * 128], wg.b, xn.t[:, kc, 0:C], xn.b, kc == 0, kc == KC - 1)
                self.actf(self.sgT.t[:, u * 2 + i, 0:C], self.sgT.b, pg.t[:, 0:C], pg.b, AF.Silu)
        wc = self.wload8(win, C_CQ, 256)
        for i in range(2):
            pc = self.G.next()
            for kc in range(KC):
                self.mm(pc.t[:, 0:C], pc.b, wc.t[:, kc, i * 128:(i + 1) * 128], wc.b, xn.t[:, kc, 0:C], xn.b, kc == 0, kc == KC - 1)
            self.cp(self.cq.t[:, i, 0:C], self.cq.b, pc.t[:, 0:C], pc.b)
        sq = self.h
        self.actf(sq.t[:, 0:2, 0:C], sq.b, self.cq.t[:, :, 0:C], self.cq.b, AF.Square)
        self.rstd_from([sq.t[:, i, 0:C] for i in range(2)], sq.b, self.on256, C)
        for i in range(2):
            self.stt(self.cqn.t[:, i, 0:C], self.cqn.b, self.cq.t[:, i, 0:C], self.cq.b, self.smw.t[:, 33 + i:34 + i],
                     self.rstd.t[:, 0:C], self.rstd.b, ALU.mult, ALU.mult, sreads=[self.smw.b])
        wkv = self.wload8(win, C_KV, 192)
        self.ts(self.wkvr.t[:, :, 0:32], self.wkvr.b, wkv.t[:, 0:KC, 160:192], wkv.b, -1.0, ALU.mult)
        self.cp(self.wkvr.t[:, :, 32:64], self.wkvr.b, wkv.t[:, 0:KC, 128:160], wkv.b)
        pc = self.G.next()
        for kc in range(KC):
            self.mm(pc.t[:, 0:C], pc.b, wkv.t[:, kc, 0:128], wkv.b, xn.t[:, kc, 0:C], xn.b, kc == 0, kc == KC - 1)
        self.cp(self.ckvf.t[:, 0:C], self.ckvf.b, pc.t[:, 0:C], pc.b)
        self.actf(sq.t[:, 0, 0:C], sq.b, self.ckvf.t[:, 0:C], self.ckvf.b, AF.Square)
        self.rstd_from([sq.t[:, 0, 0:C]], sq.b, self.on128, C)
        self.stt(self.latf.t[:, 0:C], self.latf.b, self.ckvf.t[:, 0:C], self.ckvf.b, self.smw.t[:, 35:36],
                 self.rstd.t[:, 0:C], self.rstd.b, ALU.mult, ALU.mult, sreads=[self.smw.b])
        p1 = self.G.next()
        for kc in range(KC):
            self.mm(p1.t[0:64, 0:C], p1.b, wkv.t[:, kc, 128:192], wkv.b, xn.t[:, kc, 0:C], xn.b, kc == 0, kc == KC - 1)
        r1 = self.rt.next()
        self.tt(r1.t[:, 0:C], r1.b, p1.t[0:64, 0:C], p1.b, self.cs.t[:, 0, 0:C], self.cs.b, ALU.mult)
        p2 = self.G.next()
        for kc in range(KC):
            self.mm(p2.t[0:64, 0:C], p2.b, self.wkvr.t[:, kc, :], self.wkvr.b, xn.t[:, kc, 0:C], xn.b, kc == 0, kc == KC - 1)
        r2 = self.rt.next()
        self.tt(r2.t[:, 0:C], r2.b, p2.t[0:64, 0:C], p2.b, self.cs.t[:, 1, 0:C], self.cs.b, ALU.mult)
        self.tt(self.kpef.t[:, 0:C], self.kpef.b, r1.t[:, 0:C], r1.b, r2.t[:, 0:C], r2.b, ALU.add)

        wk = self.wload8(win, C_K, 256)
        wv0 = self.wload8(win, C_V, 256)
        wv1 = self.wload8(win, C_V + 256, 256)
        for a in range(A):
            ca = slice(a * 128, a * 128 + npart)
            pz = self.XS
            self.mm(pz.t[0:npart, 0:256], pz.b, self.alow.t[:, ca], self.alow.b, self.waup.t[:], self.waup.b)
            L = self.Lt.next()
            self.actf(L.t[0:npart, :], L.b, pz.t[0:npart, 0:256], pz.b, AF.Exp, scale=-1.0)
            self.actf(L.t[0:npart, :], L.b, L.t[0:npart, :], L.b, AF.Ln, bias=1.0)
            pv = self.G.next()
            for kc in range(KC):
                self.mm(pv.t[0:npart, 0:256], pv.b, xn.t[:, kc, ca], xn.b, wv0.t[:, kc, 0:256], wv0.b, kc == 0, kc == KC - 1)
            for kc in range(KC):
                self.mm(pv.t[0:npart, 256:512], pv.b, xn.t[:, kc, ca], xn.b, wv1.t[:, kc, 0:256], wv1.b, kc == 0, kc == KC - 1)
            self.acp(self.vtm.t[0:npart, a, :], self.vtm.b, pv.t[0:npart, :], pv.b)
            pk = self.G.next()
            for kc in range(KC):
                self.mm(pk.t[0:npart, 0:256], pk.b, xn.t[:, kc, ca], xn.b, wk.t[:, kc, 0:256], wk.b, kc == 0, kc == KC - 1)
            if sample:
                self.cp(self.ktm.t[0:npart, a, :], self.ktm.b, pk.t[0:npart, 0:256], pk.b)
            else:
                pb = self.XS
                self.mm(pb.t[0:npart, 0:256], pb.b, cum, self.cst.b, L.t[0:npart, :], L.b)
                enb = self.enb.next()
                self.actf(enb.t[0:npart, :], enb.b, pb.t[0:npart, 0:256], pb.b, AF.Exp, scale=-1.0)
            pbt = self.G.next()
            for h in range(4):
                self.mm(pbt.t[0:64, h * 128:h * 128 + npart], pbt.b, L.t[0:npart, h * 64:(h + 1) * 64], L.b, cum, self.cst.b)
            self.cp(self.bTs.t[:, :, ca], self.bTs.b, pbt.t[0:64, :].rearrange("p (h t) -> p h t", h=4)[:, :, 0:npart], pbt.b)
            if not sample:
                self.tt(self.ktm.t[0:npart, a, :], self.ktm.b, pk.t[0:npart, 0:256], pk.b, enb.t[0:npart, :], enb.b, ALU.mult)
        wq = self.wload8(win, C_Q, 256)
        for h in range(4):
            ebq = self.eb.next()
            self.actf(ebq.t[:, 0:C], ebq.b, self.bTs.t[:, h, 0:C], self.bTs.b, AF.Exp, scale=1.0)
            if sample:
                self.cp(self.esf.t[:, h, 0:C], self.esf.b, ebq.t[:, 0:C], ebq.b)
            else:
                for a in range(A):
                    self.cp(self.elast.t[:, h, a:a + 1], self.elast.b, ebq.t[:, a * 128 + 127:a * 128 + 128], ebq.b)
            pq = self.G.next()
            for kc in range(KC):
                self.mm(pq.t[0:64, 0:C], pq.b, wq.t[:, kc, h * 64:(h + 1) * 64], wq.b, xn.t[:, kc, 0:C], xn.b, kc == 0, kc == KC - 1)
            if sample:
                self.ts(self.qsf.t[:, h, 0:C], self.qsf.b, pq.t[0:64, 0:C], pq.b, 0.125, ALU.mult)
            else:
                self.stt(self.qt[h].t[:, 0:C], self.qt[h].b, pq.t[0:64, 0:C], pq.b, 0.125, ebq.t[:, 0:C], ebq.b, ALU.mult, ALU.mult)
                ebk = self.eb.next()
                self.actf(ebk.t[:, 0:C], ebk.b, self.bTs.t[:, h, 0:C], self.bTs.b, AF.Exp, scale=-1.0)
                pk = self.G.next()
                for kc in range(KC):
                    self.mm(pk.t[0:64, 0:C], pk.b, wk.t[:, kc, h * 64:(h + 1) * 64], wk.b, xn.t[:, kc, 0:C], xn.b, kc == 0, kc == KC - 1)
                self.tt(self.kt[h].t[:, 0:C], self.kt[h].b, pk.t[0:64, 0:C], pk.b, ebk.t[:, 0:C], ebk.b, ALU.mult)
    def mla_q(self, h, C):
        cqn = self.cqn
        pn = self.G.next()
        for i in range(2):
            self.mm(pn.t[:, 0:C], pn.b, self.wuq.t[:, i, h * 192:h * 192 + 128], self.wuq.b, cqn.t[:, i, 0:C], cqn.b, i == 0, i == 1)
        qn = self.qn.next()
        self.acp(qn.t[:, 0:C], qn.b, pn.t[:, 0:C], pn.b)
        pl = self.G.next()
        self.mm(pl.t[:, 0:C], pl.b, self.wukT.t[:, h, :], self.wukT.b, qn.t[:, 0:C], qn.b)
        ql = self.ql.next()
        self.acp(ql.t[:, 0:C], ql.b, pl.t[:, 0:C], pl.b)
        p1 = self.G.next()
        for i in range(2):
            self.mm(p1.t[0:64, 0:C], p1.b, self.wuq.t[:, i, h * 192 + 128:h * 192 + 192], self.wuq.b, cqn.t[:, i, 0:C], cqn.b, i == 0, i == 1)
        r1 = self.rt.next()
        self.tt(r1.t[:, 0:C], r1.b, p1.t[0:64, 0:C], p1.b, self.cs.t[:, 0, 0:C], self.cs.b, ALU.mult)
        p2 = self.G.next()
        for i in range(2):
            self.mm(p2.t[0:64, 0:C], p2.b, self.wuqr.t[:, i, h, :], self.wuqr.b, cqn.t[:, i, 0:C], cqn.b, i == 0, i == 1)
        r2 = self.rt.next()
        self.tt(r2.t[:, 0:C], r2.b, p2.t[0:64, 0:C], p2.b, self.cs.t[:, 1, 0:C], self.cs.b, ALU.mult)
        qp = self.qpe.next()
        self.tt(qp.t[:, 0:C], qp.b, r1.t[:, 0:C], r1.b, r2.t[:, 0:C], r2.b, ALU.add)
        return ql, qp

    def gla_out(self, h, src, srcb, C):
        sq = self.h
        self.actf(sq.t[:, 0, 0:C], sq.b, src, srcb, AF.Square)
        self.rstd_from([sq.t[:, 0, 0:C]], sq.b, self.on128, C)
        r = self.rl
        self.stt(r.t[:, 0:C], r.b, src, srcb, self.smw.t[:, 32:33], self.rstd.t[:, 0:C], self.rstd.b,
                 ALU.mult, ALU.mult, sreads=[self.smw.b])
        self.tt(self.ym.t[:, h, 0:C], self.ym.b, r.t[:, 0:C], r.b, self.sgT.t[:, h, 0:C], self.sgT.b, ALU.mult)

    def out_proj(self, C):
        wout = self.w["wout"]
        for u in range(4):
            wo = self.wload8(wout, u * 256, 256)
            for i in range(2):
                dc = u * 2 + i
                po = self.O.next()
                for kc in range(KC):
                    self.mm(po.t[:, 0:C], po.b, wo.t[:, kc, i * 128:(i + 1) * 128], wo.b, self.ym.t[:, kc, 0:C], self.ym.b, kc == 0, kc == KC - 1)
                self.tt(self.xT.t[:, dc, 0:C], self.xT.b, po.t[:, 0:C], po.b, self.xT.t[:, dc, 0:C], self.xT.b, ALU.add)

    def prompt_tile(self, ti):
        s, t = ti // 4, ti % 4
        C = NT
        tok0 = t * NT
        w = self.w
        self.load_x(lambda a: self.xp.ap()[s, tok0 + a * 128:tok0 + (a + 1) * 128, :], 4, 128)
        self.ffn(w["f1g"], w["f1u"], w["f1d"], 0, C)
        if t == 0:
            for h in range(4):
                self.memset(self.S[h].t[:], self.S[h].b, 0.0)
                self.memset(self.Sb[h].t[:], self.Sb[h].b, 0.0)
        self.mixer_proj(C, 4, 128, tok0, sample=False)
        self.cp(self.latT_t[:, tok0:tok0 + C], self.bA, self.latf.t[:, 0:C], self.latf.b)
        self.cp(self.kpe_t[:, tok0:tok0 + C], self.bC, self.kpef.t[:, 0:C], self.kpef.b)
        pt_ = self.G.next()
        for a in range(4):
            self.tr(pt_.t[:, a * 128:(a + 1) * 128], pt_.b, self.latf.t[:, a * 128:(a + 1) * 128], self.latf.b, self.ident, self.cst.b)
        ot = self.otok.next()
        self.acp(ot.t[:].rearrange("p a c -> p (a c)"), ot.b, pt_.t[:], pt_.b)
        self.cp(self.latM_t[:, t * 4:(t + 1) * 4, :], self.bB, ot.t[:], ot.b)
        self.dma_out(self.kvp.ap()[s, tok0:tok0 + C, :].rearrange("(a p) c -> p a c", p=128), ot.t[:], ot.b)
        pt2 = self.G.next()
        for a in range(4):
            self.tr(pt2.t[:, a * 64:(a + 1) * 64], pt2.b, self.kpef.t[:, a * 128:(a + 1) * 128], self.kpef.b, self.ident[0:64, 0:64], self.cst.b)
        ot2 = self.otok2.next()
        self.acp(ot2.t[:].rearrange("p a c -> p (a c)"), ot2.b, pt2.t[:, 0:256], pt2.b)
        self.dma_out(self.pep.ap()[s, tok0:tok0 + C, :].rearrange("(a p) c -> p a c", p=128), ot2.t[:], ot2.b)
        gitems = [(a, h) for a in range(4) for h in range(4)]
        pos = {}

        def gla_A(a, h):
            ca = slice(a * 128, (a + 1) * 128)
            pa = self.G.next()
            self.mm(pa.t[:, 0:128], pa.b, self.kt[h].t[:, ca], self.kt[h].b, self.qt[h].t[:, ca], self.qt[h].b)
            am = self.atm.next()
            self.tt(am.t[:], am.b, pa.t[:, 0:128], pa.b, self.trib.t[:], self.trib.b, ALU.mult)
            return am

        def gla_rest(a, h, am):
            ca = slice(a * 128, (a + 1) * 128)
            hc = slice(h * 128, (h + 1) * 128)
            if h == 0:
                pos[a] = self.O.next()
            po = pos[a]
            self.mm(po.t[:, hc], po.b, self.vtm.t[:, a, hc], self.vtm.b, am.t[:], am.b, True, False)
            self.mm(po.t[:, hc], po.b, self.Sb[h].t[:], self.Sb[h].b, self.qt[h].t[:, ca], self.qt[h].b, False, True)
            pS = self.G.next()
            self.mm(pS.t[0:64, 0:128], pS.b, self.ktm.t[:, a, h * 64:(h + 1) * 64], self.ktm.b, self.vtm.t[:, a, hc], self.vtm.b)
            el = self.elast.t[:, h, a:a + 1]
            self.ts(self.S[h].t[:], self.S[h].b, self.S[h].t[:], self.S[h].b, el, ALU.mult, sreads=[self.elast.b])
            self.stt(self.S[h].t[:], self.S[h].b, pS.t[0:64, 0:128], pS.b, el, self.S[h].t[:], self.S[h].b, ALU.mult, ALU.add,
                     sreads=[self.elast.b])
            self.acp(self.Sb[h].t[:], self.Sb[h].b, self.S[h].t[:], self.S[h].b)
            if h == 3:
                sq = self.h
                self.actf(sq.t[:, 0, 0:C], sq.b, po.t[:, 0:C], po.b, AF.Square)
                self.rstd_from([sq.t[:, 0, 0:C]], sq.b, self.on128, C)
                r = self.rl
                self.stt(r.t[:, 0:C], r.b, po.t[:, 0:C], po.b, self.smw.t[:, 32:33], self.rstd.t[:, 0:C], self.rstd.b,
                         ALU.mult, ALU.mult, sreads=[self.smw.b])
                self.tt(self.ym.t[:, 0:4, ca], self.ym.b, r.t[:, 0:C].rearrange("p (h t) -> p h t", h=4), r.b,
                        self.sgT.t[:, 0:4, ca], self.sgT.b, ALU.mult)

        am_next = gla_A(*gitems[0])
        for i, (a, h) in enumerate(gitems):
            am_cur = am_next
            if i + 1 < len(gitems):
                am_next = gla_A(*gitems[i + 1])
            gla_rest(a, h, am_cur)
        if t == 3:
            for h in range(4):
                self.dma_out(self.glap.ap()[s, h], self.S[h].t[:], self.S[h].b)
        qs = [self.mla_q(h, C) for h in range(4)]
        nkb = 4 * t + 4
        items = [(h, kb) for h in range(4) for kb in range(nkb)]
        acc = {}

        def emit_scores(h, kb):
            ql, qp = qs[h]
            r = kb - 4 * t
            c0 = 128 * r if r > 0 else 0
            N = C - c0
            ps = self.G.next()
            ks = slice(kb * 128, (kb + 1) * 128)
            self.mm(ps.t[:, 0:N], ps.b, self.latT_t[:, ks], self.bA, ql.t[:, c0:C], ql.b, True, False)
            self.mm(ps.t[:, 0:N], ps.b, self.kpe_t[:, ks], self.bC, qp.t[:, c0:C], qp.b, False, True)
            return ps, r, c0, N

        def emit_pv(h, kb, info):
            ps, r, c0, N = info
            if kb == 0:
                acc[h] = (self.O.next(), self.O.next())
            po, pl = acc[h]
            pt = self.pt.next()
            self.actf(pt.t[:, 0:N], pt.b, ps.t[:, 0:N], ps.b, AF.Exp, scale=MLA_SCALE)
            if r >= 0:
                self.tt(pt.t[:, 0:128], pt.b, pt.t[:, 0:128], pt.b, self.trib.t[:], self.trib.b, ALU.mult)
            self.mm(po.t[:, c0:C], po.b, self.latM_t[:, kb, :], self.bB, pt.t[:, 0:N], pt.b, kb == 0, kb == nkb - 1)
            self.mm(pl.t[:, c0:C], pl.b, self.on1.t[:], self.on1.b, pt.t[:, 0:N], pt.b, kb == 0, kb == nkb - 1)
            if kb == nkb - 1:
                self.actf(self.rl.t[:, 0:C], self.rl.b, pl.t[:, 0:C], pl.b, AF.Ln)
                self.actf(self.rl.t[:, 0:C], self.rl.b, self.rl.t[:, 0:C], self.rl.b, AF.Exp, scale=-1.0)
                self.tt(self.ol.t[:, 0:C], self.ol.b, po.t[:, 0:C], po.b, self.rl.t[:, 0:C], self.rl.b, ALU.mult)
                py = self.G.next()
                self.mm(py.t[:, 0:C], py.b, self.wuv.t[:, h * 128:(h + 1) * 128], self.wuv.b, self.ol.t[:, 0:C], self.ol.b)
                self.acp(self.ym.t[:, 4 + h, 0:C], self.ym.b, py.t[:, 0:C], py.b)

        info = emit_scores(*items[0])
        for i, (h, kb) in enumerate(items):
            nxt = emit_scores(*items[i + 1]) if i + 1 < len(items) else None
            emit_pv(h, kb, info)
            info = nxt
        self.out_proj(C)
        self.ffn(w["f2g"], w["f2u"], w["f2d"], 16, C)
        self.norm_x(24, C, inplace=True)
        self.store_y(lambda a: self.yp.ap()[s, tok0 + a * 128:tok0 + (a + 1) * 128, :], 4, 128)

    def sample_tile(self):
        C = NSEQ_S
        w = self.w
        self.load_x(lambda a: self.xs.ap(), 1, 16)
        self.ffn(w["f1g"], w["f1u"], w["f1d"], 0, C)
        self.mixer_proj(C, 1, 16, SEQ, sample=True)
        pt_ = self.G.next()
        self.tr(pt_.t[:, 0:128], pt_.b, self.latf.t[:, 0:128], self.latf.b, self.ident, self.cst.b)
        ot = self.otok.next()
        self.acp(ot.t[0:16, 0, :], ot.b, pt_.t[0:16, 0:128], pt_.b)
        self.dma_out(self.kvs.ap(), ot.t[0:16, 0, :], ot.b)
        pt2 = self.G.next()
        self.tr(pt2.t[:, 0:64], pt2.b, self.kpef.t[:, 0:128], self.kpef.b, self.ident[0:64, 0:64], self.cst.b)
        ot2 = self.otok2.next()
        self.acp(ot2.t[0:16, 0, :], ot2.b, pt2.t[0:16, 0:64], pt2.b)
        self.dma_out(self.pes.ap(), ot2.t[0:16, 0, :], ot2.b)
        self.decode_attention()
        self.out_proj(C)
        self.ffn(w["f2g"], w["f2u"], w["f2d"], 16, C)
        self.norm_x(24, C, inplace=True)
        self.store_y(lambda a: self.ys.ap(), 1, 16)

    def gla_sample_seq(self, b):
        pgl = self.XS
        s0 = self.s0.next()
        self.dma_in(s0.t[:], s0.b, self.sgla.ap()[b].rearrange("h d v -> d h v"))
        km = self.kmask.next()
        self.ts(km.t[:], km.b, self.ktm.t[0:16, 0, :], self.ktm.b, self.cst.t[0:16, b:b + 1], ALU.mult, sreads=[self.cst.b])
        pd = self.G.next()
        for h in range(4):
            self.mm(pd.t[0:64, h * 128:(h + 1) * 128], pd.b, km.t[:, h * 64:(h + 1) * 64], km.b,
                    self.vtm.t[0:16, 0, h * 128:(h + 1) * 128], self.vtm.b)
        s1 = self.s1
        for h in range(4):
            self.stt(s1.t[:, h, :], s1.b, s0.t[:, h, :], s0.b, self.esf.t[:, h, b:b + 1], pd.t[0:64, h * 128:(h + 1) * 128], pd.b,
                     ALU.mult, ALU.add, sreads=[self.esf.b])
        self.dma_out(self.glas.ap()[b].rearrange("h d v -> d h v"), s1.t[:], s1.b)
        for h in range(4):
            self.mm(pgl.t[:, h * 18 + b:h * 18 + b + 2], pgl.b, s1.t[:, h, :], s1.b, self.qsf.t[:, h, b:b + 2], self.qsf.b)

    def gla_sample_finish(self):
        pgl = self.XS
        self.cp(self.gsb.t[:].rearrange("p (h t) -> p h t", h=4), self.gsb.b,
                pgl.t[:, 0:72].rearrange("p (h t) -> p h t", h=4)[:, :, 0:16], pgl.b)
        for h in range(4):
            self.gla_out(h, self.gsb.t[:, h * 16:(h + 1) * 16], self.gsb.b, NSEQ_S)

    def bcreg(self, e):
        if getattr(self, "_bcreg", None) is None:
            self._bcreg = e.alloc_register("gather_bound")
            e.reg_mov(self._bcreg, NPOOL * 8 - 1)
        return self._bcreg

    def decode_attention(self):
        P = self.P
        C = NSEQ_S
        ar = self.arena.t
        gl = [(ar[:, 0:2048].rearrange("p (t c) -> p t c", c=128), ar[:, 0:2048], self.bA),
              (ar[:, 2048:4096].rearrange("p (t c) -> p t c", c=128), ar[:, 2048:4096], self.bB)]
        gp = [(ar[:, 4096:5120].rearrange("p (t c) -> p t c", c=64), ar[:, 4096:5120], self.bC),
              (ar[:, 5120:6144].rearrange("p (t c) -> p t c", c=64), ar[:, 5120:6144], self.bC)]
        self.dma_in(self.ptls.t[:], self.ptls.b, self.ptl.ap())
        self.ts(self.idx.t[:], self.idx.b, self.ptls.t[:], self.ptls.b, 8.0, ALU.mult, self.cst.t[:, 384:385], ALU.add, sreads=[self.cst.b])
        ckv_v = self.ckv.ap().rearrange("n (g t) c -> (n g) (t c)", t=16)
        cpe_v = self.cpe.ap().rearrange("n (g t) c -> (n g) (t c)", t=16)
        for h in range(4):
            ql, qp = self.mla_q(h, C)
            self.cp(self.QL.t[:, :, h], self.QL.b, ql.t[:, 0:C], ql.b)
            self.cp(self.QP.t[0:64, :, h], self.QP.b, qp.t[:, 0:C], qp.b)
        P.dma(lambda e: e.dma_start(out=self.QP.t[64:128, :, :], in_=self.QP.t[0:64, :, :]), writes=[self.QP.b], q="sp", no_waw=False)
        pnum, pden = self.O.next(), self.O.next()
        NB_ = 2
        NG = NSEQ_S * 4

        def gather(gi):
            glv, glf, glb = gl[gi % NB_]
            gpv, gpf, gpb = gp[gi % NB_]
            P.dma(lambda e: e.indirect_dma_start(
                out=glf, out_offset=None, in_=ckv_v,
                in_offset=bass.IndirectOffsetOnAxis(ap=self.idx.t[:, gi:gi + 1], axis=0),
                bounds_check=self.bcreg(e), oob_is_err=False),
                reads=[self.idx.b], writes=[glb], q="pool", no_waw=False)
            P.dma(lambda e: e.indirect_dma_start(
                out=gpf, out_offset=None, in_=cpe_v,
                in_offset=bass.IndirectOffsetOnAxis(ap=self.idx.t[:, gi:gi + 1], axis=0),
                bounds_check=self.bcreg(e), oob_is_err=False),
                reads=[self.idx.b], writes=[gpb], q="pool", no_waw=False)

        stage = {}
        psts = {}
        ptss = {}

        def emit_T(qi):
            gi, quad = qi // 4, qi % 4
            glv, glf, glb = gl[gi % NB_]
            gpv, gpf, gpb = gp[gi % NB_]
            xa, xb = self.XB.next(), self.XB.next()
            for k in range(4):
                self.tr(xa.t[:, k * 128:(k + 1) * 128], xa.b, glv[:, quad * 4 + k, :], glb, self.identb.t[:], self.identb.b)
            for j in range(2):
                c0_ = (quad * 4 + 2 * j) * 64
                self.tr(xb.t[:, j * 128:(j + 1) * 128], xb.b, gpf[:, c0_:c0_ + 128], gpb, self.identb.t[:], self.identb.b)
            lt4 = self.lt4.next()
            self.acp(lt4.t[:], lt4.b, xa.t[:], xa.b)
            pt4 = self.pt4.next()
            self.P.dve(lambda e: e.tensor_copy(out=pt4.t[:], in_=xb.t[:, 0:256]), reads=[xb.b, lt4.b], writes=[pt4.b])
            stage[qi] = (lt4, pt4)

        def emit_S(qi):
            gi, quad = qi // 4, qi % 4
            b, g = gi // 4, gi % 4
            if b not in psts:
                psts[b] = self.G.next()
            pst = psts[b]
            lt4, pt4 = stage.pop(qi)
            for k in range(4):
                kbi = g * 16 + quad * 4 + k
                self.mm(pst.t[:, kbi * 4:(kbi + 1) * 4], pst.b, lt4.t[:, k * 128:(k + 1) * 128], lt4.b, self.QL.t[:, b, :], self.QL.b, True, False)
                hp = slice((k % 2) * 64, (k % 2) * 64 + 64)
                self.mm(pst.t[:, kbi * 4:(kbi + 1) * 4], pst.b, pt4.t[hp, (k // 2) * 128:(k // 2 + 1) * 128], pt4.b,
                        self.QP.t[hp, b, :], self.QP.b, False, True)
            if quad == 3:
                pts = self.pts.next()
                self.actf(pts.t[:].rearrange("p t h -> p (t h)"), pts.b, pst.t[:, g * 64:(g + 1) * 64], pst.b, AF.Exp, scale=MLA_SCALE)
                psm = self.ptsum.next()
                P.dve(lambda e: e.tensor_reduce(out=psm.t[:], in_=pts.t[:].rearrange("p t h -> p h t"),
                                                axis=mybir.AxisListType.X, op=ALU.add),
                      reads=[pts.b], writes=[psm.b])
                ptss[gi] = (pts, psm)

        def emit_PV(gi):
            b, g = gi // 4, gi % 4
            glv, glf, glb = gl[gi % NB_]
            pts, psm = ptss.pop(gi)
            for tt_ in range(16):
                self.mm(pnum.t[:, b * 4:(b + 1) * 4], pnum.b, glv[:, tt_, :], glb, pts.t[:, tt_, :], pts.b,
                        g == 0 and tt_ == 0, g == 3 and tt_ == 15)
            self.mm(pden.t[:, b * 4:(b + 1) * 4], pden.b, self.onf.t[:], self.onf.b, psm.t[:], psm.b, g == 0, g == 3)

        NQ = NG * 4
        gather(0)
        for qi in range(NQ + 1):
            if qi < NQ:
                if qi % 4 == 0 and qi // 4 + 1 < NG:
                    pass
                emit_T(qi)
            if qi >= 1:
                emit_S(qi - 1)
                if (qi - 1) % 4 == 3:
                    gdone = (qi - 1) // 4
                    emit_PV(gdone)
                    if gdone + 2 < NG:
                        gather(gdone + 2)
            if qi == 0 and NG > 1:
                gather(1)
            if qi % 16 == 6 and qi < NQ:
                self.gla_sample_seq(qi // 16)
        self.gla_sample_finish()
        self.cp(self.num.t[:].rearrange("p b h -> p (b h)"), self.num.b, pnum.t[:, 0:64], pnum.b)
        self.cp(self.den.t[:].rearrange("p b h -> p (b h)"), self.den.b, pden.t[:, 0:64], pden.b)
        for h in range(4):
            self.tt(self.prod.t[:], self.prod.b, self.QL.t[:, :, h], self.QL.b, self.latf.t[:, 0:C], self.latf.b, ALU.mult)
            self.tt(self.prod2.t[:], self.prod2.b, self.QP.t[0:64, :, h], self.QP.b, self.kpef.t[:, 0:C], self.kpef.b, ALU.mult)
            psn = self.G.next()
            self.mm(psn.t[:, 0:C], psn.b, self.onf.t[:], self.onf.b, self.prod.t[:], self.prod.b, True, False)
            self.mm(psn.t[:, 0:C], psn.b, self.onf.t[0:64, :], self.onf.b, self.prod2.t[:], self.prod2.b, False, True)
            self.actf(self.pnew.t[:], self.pnew.b, psn.t[:, 0:C], psn.b, AF.Exp, scale=MLA_SCALE)
            self.tt(self.tmp16.t[:], self.tmp16.b, self.pnew.t[:], self.pnew.b, self.latf.t[:, 0:C], self.latf.b, ALU.mult)
            self.tt(self.num.t[:, :, h], self.num.b, self.num.t[:, :, h], self.num.b, self.tmp16.t[:], self.tmp16.b, ALU.add)
            self.tt(self.den.t[:, :, h], self.den.b, self.den.t[:, :, h], self.den.b, self.pnew.t[:], self.pnew.b, ALU.add)
            self.recip(self.tmp16.t[:], self.tmp16.b, self.den.t[:, :, h], self.den.b)
            self.tt(self.ol.t[:, 0:C], self.ol.b, self.num.t[:, :, h], self.num.b, self.tmp16.t[:], self.tmp16.b, ALU.mult)
            py = self.G.next()
            self.mm(py.t[:, 0:C], py.b, self.wuv.t[:, h * 128:(h + 1) * 128], self.wuv.b, self.ol.t[:, 0:C], self.ol.b)
            self.acp(self.ym.t[:, 4 + h, 0:C], self.ym.b, py.t[:, 0:C], py.b)


def build_program(do_sample=True, n_tiles=8):
    nc = bass.Bass("TRN2", target_bir_lowering=False)
    with ExitStack() as st:
        P = Prog(nc, st)
        B = Builder(nc, P, do_sample=do_sample, n_tiles=n_tiles)
        B.alloc()
        build_program.sbuf_left = nc.sbuf_bytes_remaining
        B.setup()
        for ti in range(n_tiles):
            B.prompt_tile(ti)
            if B.first_pass:
                B.end_first_pass()
        if do_sample:
            B.sample_tile()
        build_program.n_ops = len(P.ops)
        P.emit()
    return nc


def _consts():
    c = np.zeros((128, 448), np.float32)
    p = np.arange(128)
    c[:, 0:128] = np.eye(128, dtype=np.float32)
    tri = (p[:, None] <= p[None, :]).astype(np.float32)
    c[:, 128:256] = tri
    c[:, 256:384] = tri * np.float32(-1.0 / 16.0)
    c[:, 384] = (p % 8).astype(np.float32)
    c[0:16, 401:417] = np.eye(16, dtype=np.float32) * np.float32(-1.0 / 16.0)
    return c


def _rope_tables():
    half = 32
    inv_freq = np.power(np.float32(10000.0), -np.arange(half, dtype=np.float32) / np.float32(half)).astype(np.float32)
    pos = np.concatenate([np.arange(SEQ, dtype=np.float32), np.full(16, 8192.0, np.float32)])
    ang = (pos[None, :] * inv_freq[:, None]).astype(np.float32)
    cos = np.cos(ang).astype(np.float32)
    sin = np.sin(ang).astype(np.float32)
    return np.stack([np.concatenate([cos, cos], 0), np.concatenate([sin, sin], 0)], 0)


_NC_CACHE = {}


def kernel(x_prompt, x_sample, cache_kv, cache_pe, state_gla, page_table,
           ffn1_norm_w, ffn1_w_gate, ffn1_w_up, ffn1_w_down, mix_norm_w, w_in,
           gla_w_a_up, gla_b_a, gla_norm_w, mla_q_norm_w, mla_w_uq, mla_kv_norm_w,
           mla_w_uk, mla_w_uv, w_out, ffn2_norm_w, ffn2_w_gate, ffn2_w_up, ffn2_w_down,
           final_norm_w, _do_sample=True, _n_tiles=8, _trace=False):
    f = lambda a: np.ascontiguousarray(np.asarray(a, dtype=np.float32))
    key = (_do_sample, _n_tiles)
    if key not in _NC_CACHE:
        _NC_CACHE[key] = build_program(_do_sample, _n_tiles)
    nc = _NC_CACHE[key]
    col = lambda v: np.asarray(v, np.float32).reshape(-1, 128).T
    smallw = np.zeros((128, 36), np.float32)
    smallw[:, 0:8] = col(ffn1_norm_w[0])
    smallw[:, 8:16] = col(mix_norm_w[0])
    smallw[:, 16:24] = col(ffn2_norm_w[0])
    smallw[:, 24:32] = col(final_norm_w)
    smallw[:, 32:33] = col(gla_norm_w[0])
    smallw[:, 33:35] = col(mla_q_norm_w[0])
    smallw[:, 35:36] = col(mla_kv_norm_w[0])
    shared = {
        "f1g": f(ffn1_w_gate[0]), "f1u": f(ffn1_w_up[0]), "f1d": f(ffn1_w_down[0]), "win": f(w_in[0]),
        "waup": f(np.concatenate([np.asarray(gla_w_a_up[0]), np.asarray(gla_b_a[0])[None, :]], 0)),
        "wuq": f(mla_w_uq[0]), "wuk": f(np.asarray(mla_w_uk[0]).reshape(128, 512)),
        "wuv": f(np.asarray(mla_w_uv[0]).reshape(128, 512)), "wout": f(w_out[0]),
        "f2g": f(ffn2_w_gate[0]), "f2u": f(ffn2_w_up[0]), "f2d": f(ffn2_w_down[0]),
        "smallw": smallw, "consts": _consts(), "rope": _rope_tables(),
    }
    xp = np.asarray(x_prompt, np.float32)
    xs = np.asarray(x_sample, np.float32)
    sg = np.asarray(state_gla, np.float32)
    ptab = np.asarray(page_table, np.int32)
    pidx = np.arange(128) // 8
    ckv_full = f(cache_kv[0]) if _do_sample else None
    cpe_full = f(cache_pe[0]) if _do_sample else None
    in_maps = []
    for c in range(NCORES):
        m = dict(shared)
        m["xp"] = np.ascontiguousarray(xp[2 * c:2 * c + 2])
        m["xs"] = np.ascontiguousarray(xs[16 * c:16 * c + 16, 0, :])
        if not _do_sample:
            in_maps.append(m)
            continue
        m["ckv"] = ckv_full
        m["cpe"] = cpe_full
        m["sgla"] = np.ascontiguousarray(sg[0, 16 * c:16 * c + 16])
        pt_c = ptab[16 * c:16 * c + 16].reshape(16, 4, 16)
        m["ptl"] = np.ascontiguousarray(pt_c[:, :, pidx].transpose(2, 0, 1).reshape(128, 64)).astype(np.int32)
        in_maps.append(m)
    res = run_bass_kernel_spmd(nc, in_maps, core_ids=list(range(NCORES)), trace=_trace)
    R = res.results
    cat = lambda k: np.concatenate([np.asarray(r[k]) for r in R], axis=0)
    y_prompt = cat("yp")
    y_sample = cat("ys")[:, None, :]
    outs = (y_prompt, y_sample, cat("kvp")[None], cat("pep")[None], cat("glap")[None],
            cat("kvs")[None, :, None, :], cat("pes")[None, :, None, :], cat("glas")[None])
    outs = tuple(np.ascontiguousarray(o, dtype=np.float32) for o in outs)
    if _trace:
        kernel.last_exec_ns = res.exec_time_ns
    return outs
```

```python
import math
import numpy as np
from contextlib import ExitStack
import concourse.bass as bass
import concourse.mybir as mybir
from concourse.bass_utils import run_bass_kernel_spmd

F32 = mybir.dt.float32
BF16 = mybir.dt.bfloat16
I32 = mybir.dt.int32
AF = mybir.ActivationFunctionType
ALU = mybir.AluOpType

NCORES = 8
D = 1024
KC = 8
DFF = 2816
SEQ = 2048
NT = 512
NPOOL = 10240
NSEQ_S = 16
EPS = 1e-6
MLA_SCALE = 1.0 / math.sqrt(192.0)
NUNITS = 80
SAME_ENGINE_SYNC = True

C_Q, C_K, C_V, C_G, C_A, C_CQ, C_KV = 0, 256, 512, 1024, 1536, 1552, 1808


class Buf:
    __slots__ = ("name", "last_w", "readers", "war", "sem_in", "n_in", "sem_out", "n_out")

    def __init__(self, name):
        self.name = name
        self.last_w = None
        self.readers = []
        self.war = []
        self.sem_in = None
        self.n_in = 0
        self.sem_out = None
        self.n_out = 0


class Op:
    __slots__ = ("eng", "fn", "deps", "dma", "sem", "count", "needs_inc", "seq")

    def __init__(self, eng, fn, dma):
        self.eng = eng
        self.fn = fn
        self.deps = set()
        self.dma = dma
        self.sem = None
        self.count = 0
        self.needs_inc = False
        self.seq = 0


class Prog:
    ENGS = ("pe", "act", "dve", "pool", "sp")

    def __init__(self, nc, stack):
        self.nc = nc
        self.stack = stack
        self.ops = []
        self.nsem = 0
        self.stores = []

    def new_sem(self, name):
        self.nsem += 1
        return self.stack.enter_context(self.nc.semaphore(f"s{self.nsem}_{name}"))

    def sb(self, name, shape, dtype):
        return self.stack.enter_context(self.nc.sbuf_tensor(name, list(shape), dtype))

    def ps(self, name, shape, dtype=F32):
        return self.stack.enter_context(self.nc.psum_tensor(name, list(shape), dtype))

    def op(self, eng, fn, reads=(), writes=(), dma=False, no_waw=False):
        o = Op(eng, fn, dma)
        for b in reads:
            if b.last_w is not None:
                o.deps.add(b.last_w)
        for b in writes:
            for r in b.readers:
                o.deps.add(r)
            if no_waw:
                for r in b.war:
                    o.deps.add(r)
            elif b.last_w is not None:
                o.deps.add(b.last_w)
        o.deps.discard(o)
        if dma:
            if writes:
                b = writes[0]
                if b.sem_in is None:
                    b.sem_in = self.new_sem("i_" + b.name)
                b.n_in += 16
                o.sem, o.count = b.sem_in, b.n_in
            else:
                b = reads[0]
                if b.sem_out is None:
                    b.sem_out = self.new_sem("o_" + b.name)
                b.n_out += 16
                o.sem, o.count = b.sem_out, b.n_out
                self.stores.append(o)
        for b in reads:
            b.readers.append(o)
        for b in writes:
            if b.readers or not no_waw:
                b.war = [r for r in b.readers if r is not o]
            b.readers = []
            b.last_w = o
        for d in o.deps:
            if not d.dma and (d.eng != eng or dma or (SAME_ENGINE_SYNC and eng != "pe")):
                d.needs_inc = True
        self.ops.append(o)
        return o

    def pe(self, fn, reads=(), writes=()):
        return self.op("pe", fn, reads, writes)

    def act(self, fn, reads=(), writes=()):
        return self.op("act", fn, reads, writes)

    def dve(self, fn, reads=(), writes=()):
        return self.op("dve", fn, reads, writes)

    def pool(self, fn, reads=(), writes=()):
        return self.op("pool", fn, reads, writes)

    def dma(self, fn, reads=(), writes=(), q="sp", no_waw=True):
        return self.op(q, fn, reads, writes, dma=True, no_waw=no_waw)

    def emit(self):
        nc = self.nc
        eng_sem = {e: self.new_sem("eng_" + e) for e in self.ENGS}
        cnt = {e: 0 for e in self.ENGS}
        for o in self.ops:
            if not o.dma and o.needs_inc:
                cnt[o.eng] += 1
                o.seq = cnt[o.eng]
        per = {e: [o for o in self.ops if o.eng == e] for e in self.ENGS}
        final_waits = {}
        for o in self.stores:
            k = id(o.sem)
            if k not in final_waits or final_waits[k][1] < o.count:
                final_waits[k] = (o.sem, o.count)

        def run(engname, engobj):
            waited = {}
            for o in per[engname]:
                need = {}
                for d in o.deps:
                    if d.dma:
                        s, c = d.sem, d.count
                    else:
                        if d.eng == engname and not o.dma and (engname == "pe" or not SAME_ENGINE_SYNC):
                            continue
                        s, c = eng_sem[d.eng], d.seq
                    k = id(s)
                    if k not in need or need[k][1] < c:
                        need[k] = (s, c)
                for k, (s, c) in need.items():
                    if waited.get(k, 0) >= c:
                        continue
                    engobj.wait_ge(s, c)
                    waited[k] = c
                inst = o.fn(engobj)
                if o.dma:
                    inst.then_inc(o.sem, 16)
                elif o.needs_inc:
                    inst.then_inc(eng_sem[engname], 1)
            if engname == "sp":
                for k, (s, c) in final_waits.items():
                    if waited.get(k, 0) < c:
                        engobj.wait_ge(s, c)

        with nc.Block() as block:
            @block.tensor
            def _(e):
                run("pe", e)

            @block.scalar
            def _(e):
                run("act", e)

            @block.vector
            def _(e):
                run("dve", e)

            @block.gpsimd
            def _(e):
                run("pool", e)

            @block.sync
            def _(e):
                run("sp", e)


class T:
    def __init__(self, P, name, shape, dtype, psum=False):
        self.t = P.ps("P_" + name, shape, dtype) if psum else P.sb("S_" + name, shape, dtype)
        self.b = Buf(name)


class View:
    def __init__(self, ap, b):
        self.t = ap
        self.b = b


class Rot:
    def __init__(self, items):
        self.items = items
        self.i = 0

    def next(self):
        x = self.items[self.i % len(self.items)]
        self.i += 1
        return x


class Builder:
    def __init__(self, nc, P, do_sample=True, n_tiles=8):
        self.nc = nc
        self.P = P
        self.do_sample = do_sample
        self.n_tiles = n_tiles
        dt = nc.dram_tensor
        I, O = "ExternalInput", "ExternalOutput"
        self.xp = dt("xp", [2, SEQ, D], F32, kind=I)
        self.xs = dt("xs", [NSEQ_S, D], F32, kind=I)
        if do_sample:
            self.ckv = dt("ckv", [NPOOL, 128, 128], F32, kind=I)
            self.cpe = dt("cpe", [NPOOL, 128, 64], F32, kind=I)
            self.sgla = dt("sgla", [NSEQ_S, 4, 64, 128], F32, kind=I)
            self.ptl = dt("ptl", [128, NSEQ_S * 4], I32, kind=I)
        self.w = {}
        for name, shp in (("f1g", [D, DFF]), ("f1u", [D, DFF]), ("f1d", [DFF, D]), ("win", [D, 2000]),
                          ("waup", [17, 256]), ("wuq", [256, 768]), ("wuk", [128, 512]), ("wuv", [128, 512]),
                          ("wout", [D, D]), ("f2g", [D, DFF]), ("f2u", [D, DFF]), ("f2d", [DFF, D]),
                          ("smallw", [128, 36]), ("consts", [128, 448]), ("rope", [2, 64, SEQ + 16]), ("wfin", [128, D])):
            self.w[name] = dt(name, shp, F32, kind=I)
        self.scr = dt("wscr", [NUNITS, 128, 11 * 256], BF16, kind="Internal")
        self.scrb = Buf("wscr")
        self.units = {}
        self.first_pass = True
        self.yp = dt("yp", [2, SEQ, D], F32, kind=O)
        self.ys = dt("ys", [NSEQ_S, D], F32, kind=O)
        self.kvp = dt("kvp", [2, SEQ, 128], F32, kind=O)
        self.pep = dt("pep", [2, SEQ, 64], F32, kind=O)
        self.glap = dt("glap", [2, 4, 64, 128], F32, kind=O)
        self.kvs = dt("kvs", [NSEQ_S, 128], F32, kind=O)
        self.pes = dt("pes", [NSEQ_S, 64], F32, kind=O)
        self.glas = dt("glas", [NSEQ_S, 4, 64, 128], F32, kind=O)

    def alloc(self):
        P = self.P
        mk = lambda n, s, d=F32: T(P, n, s, d)
        self.cst = mk("cst", [128, 448])
        self.smw = mk("smw", [128, 36])
        self.wfin = mk("wfin", [128, D])
        self.fst = mk("fst", [128, 2])
        self.identb = mk("identb", [128, 128], BF16)
        self.trib = mk("trib", [128, 128], BF16)
        self.on1024 = mk("on1024", [128, 128], BF16)
        self.on256 = mk("on256", [128, 128], BF16)
        self.on128 = mk("on128", [128, 128], BF16)
        self.on1 = mk("on1", [128, 128], BF16)
        self.onf = mk("onf", [128, 128])
        self.waup = mk("waup", [17, 256], BF16)
        self.wuq = mk("wuq", [128, 2, 768], BF16)
        self.wuqr = mk("wuqr", [128, 2, 4, 64], BF16)
        self.wuk = mk("wuk", [128, 512], BF16)
        self.wuv = mk("wuv", [128, 512], BF16)
        self.wukT = mk("wukT", [128, 4, 128], BF16)
        self.xio = Rot([mk(f"xio{i}", [128, D]) for i in range(2)])
        self.xT = mk("xT", [128, KC, NT])
        self.xn = mk("xn", [128, KC, NT], BF16)
        self.h = mk("h", [128, 11, NT], BF16)
        self.rstd = mk("rstd", [128, NT])
        self.sg = Rot([mk(f"sg{i}", [128, NT]) for i in range(2)])
        self.su = Rot([mk(f"su{i}", [128, NT]) for i in range(2)])
        self.wbuf = Rot([mk(f"wb{i}", [128, 11, 256], BF16) for i in range(4)])
        self.dummy = mk("dummy", [128, 8])
        self.arena = mk("arena", [128, 6144], BF16)
        ar = self.arena.t
        self.latT_t = ar[:, 0:2048]
        self.latM_t = ar[:, 2048:4096].rearrange("p (a c) -> p a c", c=128)
        self.kpe_t = ar[0:64, 4096:6144]
        self.bA, self.bB, self.bC = Buf("arA"), Buf("arB"), Buf("arC")
        self.alow = mk("alow", [17, NT], BF16)
        self.Lt = Rot([mk(f"Lt{i}", [128, 256]) for i in range(2)])
        self.enb = Rot([mk(f"enb{i}", [128, 256]) for i in range(2)])
        self.bTs = mk("bTs", [64, 4, NT])
        self.eb = Rot([mk(f"eb{i}", [64, NT]) for i in range(2)])
        self.elast = mk("elast", [64, 4, 4])
        self.qt = [mk(f"qt{h}", [64, NT], BF16) for h in range(4)]
        self.kt = [mk(f"kt{h}", [64, NT], BF16) for h in range(4)]
        self.ktm = mk("ktm", [128, 4, 256], BF16)
        self.vtm = mk("vtm", [128, 4, 512], BF16)
        self.sgT = mk("sgT", [128, 4, NT], BF16)
        self.cq = mk("cq", [128, 2, NT])
        self.cqn = mk("cqn", [128, 2, NT], BF16)
        self.qn = Rot([mk(f"qn{i}", [128, NT], BF16) for i in range(1)])
        self.ql = Rot([mk(f"ql{i}", [128, NT], BF16) for i in range(4)])
        self.qpe = Rot([mk(f"qpe{i}", [64, NT], BF16) for i in range(4)])
        self.ckvf = mk("ckvf", [128, NT])
        self.latf = mk("latf", [128, NT])
        self.kpef = mk("kpef", [64, NT])
        self.cs = mk("cs", [64, 2, NT])
        self.rt = Rot([mk(f"rt{i}", [64, NT]) for i in range(2)])
        self.wkvr = mk("wkvr", [128, KC, 64], BF16)
        self.S = [mk(f"S{h}", [64, 128]) for h in range(4)]
        self.Sb = [mk(f"Sb{h}", [64, 128], BF16) for h in range(4)]
        self.ym = mk("ym", [128, 8, NT], BF16)
        self.pt = Rot([mk(f"pt{i}", [128, NT], BF16) for i in range(3)])
        self.atm = Rot([mk(f"atm{i}", [128, 128], BF16) for i in range(3)])
        self.rl = mk("rl", [128, NT])
        self.ol = mk("ol", [128, NT], BF16)
        self.otok = Rot([mk(f"otok{i}", [128, 4, 128]) for i in range(2)])
        self.otok2 = Rot([mk(f"otk2{i}", [128, 4, 64]) for i in range(2)])
        self.G = Rot([T(P, f"pg{i}", [128, NT], F32, psum=True) for i in range(3)])
        self.O = Rot([T(P, f"po{i}", [128, NT], F32, psum=True) for i in range(2)])
        self.XS = T(P, "pxs", [128, NT], F32, psum=True)
        self.F = Rot(self.G.items + [self.XS])
        xbs = []
        for i in range(2):
            full = T(P, f"pxb{i}", [128, 1024], BF16, psum=True)
            for j in range(2):
                v = View(full.t[:, j * 512:(j + 1) * 512], full.b)
                xbs.append(v)
        self.XB = Rot(xbs)
        if self.do_sample:
            self.idx = mk("idx", [128, NSEQ_S * 4], I32)
            self.ptls = mk("ptls", [128, NSEQ_S * 4], I32)
            self.lt4 = Rot([mk(f"lt4_{i}", [128, 512], BF16) for i in range(2)])
            self.pt4 = Rot([mk(f"pt4_{i}", [128, 256], BF16) for i in range(2)])
            self.pts = Rot([mk(f"pts{i}", [128, 16, 4], BF16) for i in range(2)])
            self.ptsum = Rot([mk(f"ptsum{i}", [128, 4]) for i in range(2)])
            self.QL = mk("QL", [128, NSEQ_S, 4], BF16)
            self.QP = mk("QP", [128, NSEQ_S, 4], BF16)
            self.s0 = Rot([mk(f"s0_{i}", [64, 4, 128]) for i in range(2)])
            self.s1 = mk("s1", [64, 4, 128])
            self.kmask = Rot([mk(f"kmask{i}", [16, 256], BF16) for i in range(2)])
            self.qsf = mk("qsf", [64, 4, 18])
            self.esf = mk("esf", [64, 4, 16])
            self.num = mk("num", [128, NSEQ_S, 4])
            self.den = mk("den", [128, NSEQ_S, 4])
            self.pnew = mk("pnew", [128, 16])
            self.prod = mk("prod", [128, 16])
            self.prod2 = mk("prod2", [64, 16])
            self.tmp16 = mk("tmp16", [128, 16])
            self.gsb = mk("gsb", [128, 64])

    def mm(self, out, ob, lhsT, lb, rhs, rb, start=True, stop=True):
        rd = [lb] if rb is lb else [lb, rb]
        self.P.pe(lambda e: e.matmul(out, lhsT=lhsT, rhs=rhs, start=start, stop=stop), reads=rd, writes=[ob])

    def tr(self, out, ob, in_, ib, ident, idb):
        self.P.pe(lambda e: e.transpose(out, in_, ident), reads=[ib, idb], writes=[ob])

    def actf(self, out, ob, in_, ib, func, scale=1.0, bias=0.0):
        self.P.act(lambda e: e.activation(out=out, in_=in_, func=func, bias=bias, scale=scale), reads=[ib], writes=[ob])

    def acp(self, out, ob, in_, ib):
        self.P.act(lambda e: e.activation(out=out, in_=in_, func=AF.Copy), reads=[ib], writes=[ob])

    def tt(self, out, ob, a, ab, b, bb, op):
        self.P.dve(lambda e: e.tensor_tensor(out=out, in0=a, in1=b, op=op), reads=[ab, bb], writes=[ob])

    def stt(self, out, ob, a, ab, scalar, b, bb, op0, op1, sreads=()):
        self.P.dve(lambda e: e.scalar_tensor_tensor(out=out, in0=a, scalar=scalar, in1=b, op0=op0, op1=op1),
                   reads=[ab, bb] + list(sreads), writes=[ob])

    def ts(self, out, ob, a, ab, s1, op0, s2=None, op1=None, sreads=()):
        if op1 is None:
            self.P.dve(lambda e: e.tensor_scalar(out=out, in0=a, scalar1=s1, scalar2=None, op0=op0),
                       reads=[ab] + list(sreads), writes=[ob])
        else:
            self.P.dve(lambda e: e.tensor_scalar(out=out, in0=a, scalar1=s1, scalar2=s2, op0=op0, op1=op1),
                       reads=[ab] + list(sreads), writes=[ob])

    def cp(self, out, ob, in_, ib):
        self.P.dve(lambda e: e.tensor_copy(out=out, in_=in_), reads=[ib], writes=[ob])

    def memset(self, ap, b, v):
        self.P.dve(lambda e: e.memset(ap, v), writes=[b])

    def recip(self, out, ob, in_, ib):
        self.P.dve(lambda e: e.reciprocal(out=out, in_=in_), reads=[ib], writes=[ob])

    def dma_in(self, out, ob, src, q="sp"):
        self.P.dma(lambda e: e.dma_start(out=out, in_=src), writes=[ob], q=q)

    def dma_out(self, dst, in_, ib, q="sp"):
        self.P.dma(lambda e: e.dma_start(out=dst, in_=in_), reads=[ib], q=q)

    def rstd_from(self, sq_aps, sqb, ones, C, npart=128):
        ps = self.XS
        n = len(sq_aps)
        for i, ap in enumerate(sq_aps):
            self.mm(ps.t[:, 0:C], ps.b, ones.t[0:npart, :], ones.b, ap, sqb, start=(i == 0), stop=(i == n - 1))
        self.actf(self.rstd.t[:, 0:C], self.rstd.b, ps.t[:, 0:C], ps.b, AF.Ln, bias=EPS)
        self.actf(self.rstd.t[:, 0:C], self.rstd.b, self.rstd.t[:, 0:C], self.rstd.b, AF.Exp, scale=-0.5)

    def unit_idx(self, key):
        if key not in self.units:
            assert self.first_pass, key
            self.units[key] = len(self.units)
            assert len(self.units) <= NUNITS
        return self.units[key]

    def wstream(self, key, wt, dst, src_f32, scr_view):
        if self.first_pass:
            self.dma_in(dst, wt.b, src_f32, q="pool")
            self.P.dma(lambda e: e.dma_start(out=scr_view, in_=dst), reads=[wt.b], q="sp")
        else:
            self.P.dma(lambda e: e.dma_start(out=dst, in_=scr_view), writes=[wt.b], q="pool")

    def end_first_pass(self):
        self.first_pass = False
        bufs = [w.b for w in self.wbuf.items]
        self.P.pool(lambda e: e.memset(self.dummy.t[:], 0.0), writes=bufs + [self.dummy.b])

    def wload8(self, wd, c0, ncols):
        wt = self.wbuf.next()
        u = self.unit_idx((wd.name, c0, ncols))
        src = wd.ap()[:, c0:c0 + ncols].rearrange("(kc p) c -> p kc c", p=128)
        scr_view = self.scr.ap()[u, :, 0:KC * ncols].rearrange("p (k c) -> p k c", c=ncols)
        self.wstream(u, wt, wt.t[:, 0:KC, 0:ncols], src, scr_view)
        return wt

    def setup(self):
        w = self.w
        self.dma_in(self.cst.t[:], self.cst.b, w["consts"].ap())
        self.dma_in(self.smw.t[:], self.smw.b, w["smallw"].ap())
        self.dma_in(self.wfin.t[:], self.wfin.b, w["wfin"].ap())
        self.ident = self.cst.t[:, 0:128]
        self.trif = self.cst.t[:, 256:384]
        self.negI = self.cst.t[:, 401:417]
        self.cp(self.identb.t[:], self.identb.b, self.cst.t[:, 0:128], self.cst.b)
        self.cp(self.trib.t[:], self.trib.b, self.cst.t[:, 128:256], self.cst.b)
        for tl, v in ((self.on1024, 1.0 / 1024), (self.on256, 1.0 / 256), (self.on128, 1.0 / 128), (self.on1, 1.0),
                      (self.onf, 1.0), (self.alow, 1.0)):
            self.memset(tl.t[:], tl.b, v)
        self.memset(self.latf.t[:, 0:128], self.latf.b, 0.0)
        self.memset(self.kpef.t[:, 0:128], self.kpef.b, 0.0)
        self.memset(self.xT.t[:, :, 0:128], self.xT.b, 0.0)
        if self.do_sample:
            self.memset(self.qsf.t[:], self.qsf.b, 0.0)
        self.dma_in(self.waup.t[:], self.waup.b, w["waup"].ap(), q="pool")
        self.dma_in(self.wuq.t[:], self.wuq.b, w["wuq"].ap().rearrange("(kc p) c -> p kc c", p=128), q="pool")
        self.dma_in(self.wuk.t[:], self.wuk.b, w["wuk"].ap(), q="pool")
        self.dma_in(self.wuv.t[:], self.wuv.b, w["wuv"].ap(), q="pool")
        for h in range(4):
            b0 = h * 192 + 128
            self.ts(self.wuqr.t[:, :, h, 0:32], self.wuqr.b, self.wuq.t[:, :, b0 + 32:b0 + 64], self.wuq.b, -1.0, ALU.mult)
            self.cp(self.wuqr.t[:, :, h, 32:64], self.wuqr.b, self.wuq.t[:, :, b0:b0 + 32], self.wuq.b)
            xb = self.XB.next()
            self.tr(xb.t[:, 0:128], xb.b, self.wuk.t[:, h * 128:(h + 1) * 128], self.wuk.b, self.identb.t[:], self.identb.b)
            self.cp(self.wukT.t[:, h, :], self.wukT.b, xb.t[:, 0:128], xb.b)

    def norm_x(self, col0, C, inplace=False):
        sq = self.h
        self.actf(sq.t[:, 0:KC, 0:C], sq.b, self.xT.t[:, :, 0:C], self.xT.b, AF.Square)
        self.rstd_from([sq.t[:, kc, 0:C] for kc in range(KC)], sq.b, self.on1024, C)
        for kc in range(KC):
            o, ob = (self.xT.t[:, kc, 0:C], self.xT.b) if inplace else (self.xn.t[:, kc, 0:C], self.xn.b)
            self.stt(o, ob, self.xT.t[:, kc, 0:C], self.xT.b, self.smw.t[:, col0 + kc:col0 + kc + 1],
                     self.rstd.t[:, 0:C], self.rstd.b, ALU.mult, ALU.mult, sreads=[self.smw.b])

    def ffn(self, wg, wu, wdn, col0, C):
        xn = self.xn
        sq = self.h
        self.actf(sq.t[:, 0:KC, 0:C], sq.b, self.xT.t[:, :, 0:C], self.xT.b, AF.Square)
        for kc in range(KC):
            self.ts(xn.t[:, kc, 0:C], xn.b, self.xT.t[:, kc, 0:C], self.xT.b, self.smw.t[:, col0 + kc:col0 + kc + 1], ALU.mult,
                    sreads=[self.smw.b])
        stats_done = False
        for half in range(2):
            f0 = half * 11
            for u in range(6):
                fs = [f for f in (f0 + 2 * u, f0 + 2 * u + 1) if f < f0 + 11]
                wgt = self.wload8(wg, fs[0] * 128, len(fs) * 128)
                wut = self.wload8(wu, fs[0] * 128, len(fs) * 128)
                for i, f in enumerate(fs):
                    pg = self.F.next() if stats_done else self.G.next()
                    pu = self.F.next() if stats_done else self.G.next()
                    for kc in range(KC):
                        self.mm(pg.t[:, 0:C], pg.b, wgt.t[:, kc, i * 128:(i + 1) * 128], wgt.b, xn.t[:, kc, 0:C], xn.b, kc == 0, kc == KC - 1)
                    for kc in range(KC):
                        self.mm(pu.t[:, 0:C], pu.b, wut.t[:, kc, i * 128:(i + 1) * 128], wut.b, xn.t[:, kc, 0:C], xn.b, kc == 0, kc == KC - 1)
                    if not stats_done:
                        self.rstd_from([sq.t[:, kc, 0:C] for kc in range(KC)], sq.b, self.on1024, C)
                        stats_done = True
                    sg = self.sg.next()
                    su = self.su.next()
                    self.tt(sg.t[:, 0:C], sg.b, pg.t[:, 0:C], pg.b, self.rstd.t[:, 0:C], self.rstd.b, ALU.mult)
                    self.actf(sg.t[:, 0:C], sg.b, sg.t[:, 0:C], sg.b, AF.Silu)
                    self.tt(su.t[:, 0:C], su.b, pu.t[:, 0:C], pu.b, self.rstd.t[:, 0:C], self.rstd.b, ALU.mult)
                    self.tt(self.h.t[:, f - f0, 0:C], self.h.b, su.t[:, 0:C], su.b, sg.t[:, 0:C], sg.b, ALU.mult)
            for dcp in range(4):
                wdt = self.wbuf.next()
                src = wdn.ap()[f0 * 128:(f0 + 11) * 128, dcp * 256:(dcp + 1) * 256].rearrange("(f p) c -> p f c", p=128)
                u = self.unit_idx((wdn.name, "down", f0, dcp))
                self.wstream(u, wdt, wdt.t[:], src, self.scr.ap()[u].rearrange("p (f c) -> p f c", c=256))
                for i in range(2):
                    dc = dcp * 2 + i
                    po = self.O.next()
                    for f in range(11):
                        self.mm(po.t[:, 0:C], po.b, wdt.t[:, f, i * 128:(i + 1) * 128], wdt.b, self.h.t[:, f, 0:C], self.h.b, f == 0, f == 10)
                    self.stt(self.xT.t[:, dc, 0:C], self.xT.b, po.t[:, 0:C], po.b, 0.5, self.xT.t[:, dc, 0:C], self.xT.b, ALU.mult, ALU.add)

    def load_x(self, src_rows, nblk, npart):
        for a in range(nblk):
            xi = self.xio.next()
            self.dma_in(xi.t[0:npart, :], xi.b, src_rows(a))
            for g in range(2):
                pg = self.G.next()
                for k in range(4):
                    kc = g * 4 + k
                    self.tr(pg.t[:, k * 128:k * 128 + npart], pg.b, xi.t[0:npart, kc * 128:(kc + 1) * 128], xi.b,
                            self.ident[0:npart, 0:npart], self.cst.b)
                src_v = pg.t[:].rearrange("p (k t) -> p k t", k=4)[:, :, 0:npart]
                dst_v = self.xT.t[:, g * 4:(g + 1) * 4, a * 128:a * 128 + npart]
                if g == 0:
                    self.acp(dst_v, self.xT.b, src_v, pg.b)
                else:
                    self.cp(dst_v, self.xT.b, src_v, pg.b)

    def store_y(self, dst_rows, nblk, npart):
        xT = self.xT
        junk = self.h.t[:, 0:2, :].rearrange("p a c -> p (a c)")
        for a in range(nblk):
            yo = self.xio.next()
            for g in range(2):
                pg = self.G.next()
                for k in range(4):
                    kc = g * 4 + k
                    self.tr(pg.t[:, k * 128:(k + 1) * 128], pg.b, xT.t[:, kc, a * 128:(a + 1) * 128], xT.b, self.ident, self.cst.b)
                if g == 0:
                    self.acp(yo.t[0:npart, 0:512], yo.b, pg.t[0:npart, :], pg.b)
                else:
                    self.cp(yo.t[0:npart, 512:1024], yo.b, pg.t[0:npart, :], pg.b)
            fst = self.fst
            self.P.act(lambda e, yo=yo: e.activation(out=junk[0:npart, :], in_=yo.t[0:npart, :], func=AF.Square,
                                                    accum_out=fst.t[0:npart, 0:1]),
                       reads=[yo.b], writes=[self.h.b, fst.b])
            self.actf(fst.t[0:npart, 1:2], fst.b, fst.t[0:npart, 0:1], fst.b, AF.Ln, scale=1.0 / D, bias=EPS)
            self.actf(fst.t[0:npart, 1:2], fst.b, fst.t[0:npart, 1:2], fst.b, AF.Exp, scale=-0.5)
            self.stt(yo.t[0:npart, :], yo.b, yo.t[0:npart, :], yo.b, fst.t[0:npart, 1:2], self.wfin.t[0:npart, :], self.wfin.b,
                     ALU.mult, ALU.mult, sreads=[fst.b])
            self.dma_out(dst_rows(a), yo.t[0:npart, :], yo.b, q="pool")

    def mixer_proj(self, C, A, npart, pos0, sample):
        win = self.w["win"]
        xn = self.xn
        self.norm_x(8, C)
        self.dma_in(self.cs.t[:, :, 0:C], self.cs.b, self.w["rope"].ap()[:, :, pos0:pos0 + C].rearrange("s r c -> r s c"))
        cum = self.negI[0:npart, 0:npart] if sample else self.trif
        wa = self.wload8(win, C_A, 16)
        pa = self.G.next()
        for kc in range(KC):
            self.mm(pa.t[0:16, 0:C], pa.b, wa.t[:, kc, 0:16], wa.b, xn.t[:, kc, 0:C], xn.b, kc == 0, kc == KC - 1)
        self.cp(self.alow.t[0:16, 0:C], self.alow.b, pa.t[0:16, 0:C], pa.b)
        for u in range(2):
            wg = self.wload8(win, C_G + u * 256, 256)
            for i in range(2):
                pg = self.G.next()
                for kc in range(KC):
                    self.mm(pg.t[:, 0:C], pg.b, wg.t[:, kc, i * 128:(i + 1) * 128], wg.b, xn.t[:, kc, 0:C], xn.b, kc == 0, kc == KC - 1)
                self.actf(self.sgT.t[:, u * 2 + i, 0:C], self.sgT.b, pg.t[:, 0:C], pg.b, AF.Silu)
        wc = self.wload8(win, C_CQ, 256)
        for i in range(2):
            pc = self.G.next()
            for kc in range(KC):
                self.mm(pc.t[:, 0:C], pc.b, wc.t[:, kc, i * 128:(i + 1) * 128], wc.b, xn.t[:, kc, 0:C], xn.b, kc == 0, kc == KC - 1)
            self.cp(self.cq.t[:, i, 0:C], self.cq.b, pc.t[:, 0:C], pc.b)
        sq = self.h
        self.actf(sq.t[:, 0:2, 0:C], sq.b, self.cq.t[:, :, 0:C], self.cq.b, AF.Square)
        self.rstd_from([sq.t[:, i, 0:C] for i in range(2)], sq.b, self.on256, C)
        for i in range(2):
            self.stt(self.cqn.t[:, i, 0:C], self.cqn.b, self.cq.t[:, i, 0:C], self.cq.b, self.smw.t[:, 33 + i:34 + i],
                     self.rstd.t[:, 0:C], self.rstd.b, ALU.mult, ALU.mult, sreads=[self.smw.b])
        wkv = self.wload8(win, C_KV, 192)
        self.ts(self.wkvr.t[:, :, 0:32], self.wkvr.b, wkv.t[:, 0:KC, 160:192], wkv.b, -1.0, ALU.mult)
        self.cp(self.wkvr.t[:, :, 32:64], self.wkvr.b, wkv.t[:, 0:KC, 128:160], wkv.b)
        pc = self.G.next()
        for kc in range(KC):
            self.mm(pc.t[:, 0:C], pc.b, wkv.t[:, kc, 0:128], wkv.b, xn.t[:, kc, 0:C], xn.b, kc == 0, kc == KC - 1)
        self.cp(self.ckvf.t[:, 0:C], self.ckvf.b, pc.t[:, 0:C], pc.b)
        self.actf(sq.t[:, 0, 0:C], sq.b, self.ckvf.t[:, 0:C], self.ckvf.b, AF.Square)
        self.rstd_from([sq.t[:, 0, 0:C]], sq.b, self.on128, C)
        self.stt(self.latf.t[:, 0:C], self.latf.b, self.ckvf.t[:, 0:C], self.ckvf.b, self.smw.t[:, 35:36],
                 self.rstd.t[:, 0:C], self.rstd.b, ALU.mult, ALU.mult, sreads=[self.smw.b])
        p1 = self.G.next()
        for kc in range(KC):
            self.mm(p1.t[0:64, 0:C], p1.b, wkv.t[:, kc, 128:192], wkv.b, xn.t[:, kc, 0:C], xn.b, kc == 0, kc == KC - 1)
        r1 = self.rt.next()
        self.tt(r1.t[:, 0:C], r1.b, p1.t[0:64, 0:C], p1.b, self.cs.t[:, 0, 0:C], self.cs.b, ALU.mult)
        p2 = self.G.next()
        for kc in range(KC):
            self.mm(p2.t[0:64, 0:C], p2.b, self.wkvr.t[:, kc, :], self.wkvr.b, xn.t[:, kc, 0:C], xn.b, kc == 0, kc == KC - 1)
        r2 = self.rt.next()
        self.tt(r2.t[:, 0:C], r2.b, p2.t[0:64, 0:C], p2.b, self.cs.t[:, 1, 0:C], self.cs.b, ALU.mult)
        self.tt(self.kpef.t[:, 0:C], self.kpef.b, r1.t[:, 0:C], r1.b, r2.t[:, 0:C], r2.b, ALU.add)

        wk = self.wload8(win, C_K, 256)
        wv0 = self.wload8(win, C_V, 256)
        wv1 = self.wload8(win, C_V + 256, 256)
        for a in range(A):
            ca = slice(a * 128, a * 128 + npart)
            pz = self.XS
            self.mm(pz.t[0:npart, 0:256], pz.b, self.alow.t[:, ca], self.alow.b, self.waup.t[:], self.waup.b)
            L = self.Lt.next()
            self.actf(L.t[0:npart, :], L.b, pz.t[0:npart, 0:256], pz.b, AF.Exp, scale=-1.0)
            self.actf(L.t[0:npart, :], L.b, L.t[0:npart, :], L.b, AF.Ln, bias=1.0)
            pv = self.G.next()
            for kc in range(KC):
                self.mm(pv.t[0:npart, 0:256], pv.b, xn.t[:, kc, ca], xn.b, wv0.t[:, kc, 0:256], wv0.b, kc == 0, kc == KC - 1)
            for kc in range(KC):
                self.mm(pv.t[0:npart, 256:512], pv.b, xn.t[:, kc, ca], xn.b, wv1.t[:, kc, 0:256], wv1.b, kc == 0, kc == KC - 1)
            self.acp(self.vtm.t[0:npart, a, :], self.vtm.b, pv.t[0:npart, :], pv.b)
            pk = self.G.next()
            for kc in range(KC):
                self.mm(pk.t[0:npart, 0:256], pk.b, xn.t[:, kc, ca], xn.b, wk.t[:, kc, 0:256], wk.b, kc == 0, kc == KC - 1)
            if sample:
                self.cp(self.ktm.t[0:npart, a, :], self.ktm.b, pk.t[0:npart, 0:256], pk.b)
            else:
                pb = self.XS
                self.mm(pb.t[0:npart, 0:256], pb.b, cum, self.cst.b, L.t[0:npart, :], L.b)
                enb = self.enb.next()
                self.actf(enb.t[0:npart, :], enb.b, pb.t[0:npart, 0:256], pb.b, AF.Exp, scale=-1.0)
            pbt = self.G.next()
            for h in range(4):
                self.mm(pbt.t[0:64, h * 128:h * 128 + npart], pbt.b, L.t[0:npart, h * 64:(h + 1) * 64], L.b, cum, self.cst.b)
            self.cp(self.bTs.t[:, :, ca], self.bTs.b, pbt.t[0:64, :].rearrange("p (h t) -> p h t", h=4)[:, :, 0:npart], pbt.b)
            if not sample:
                self.tt(self.ktm.t[0:npart, a, :], self.ktm.b, pk.t[0:npart, 0:256], pk.b, enb.t[0:npart, :], enb.b, ALU.mult)
        wq = self.wload8(win, C_Q, 256)
        for h in range(4):
            ebq = self.eb.next()
            self.actf(ebq.t[:, 0:C], ebq.b, self.bTs.t[:, h, 0:C], self.bTs.b, AF.Exp, scale=1.0)
            if sample:
                self.cp(self.esf.t[:, h, 0:C], self.esf.b, ebq.t[:, 0:C], ebq.b)
            else:
                for a in range(A):
                    self.cp(self.elast.t[:, h, a:a + 1], self.elast.b, ebq.t[:, a * 128 + 127:a * 128 + 128], ebq.b)
            pq = self.G.next()
            for kc in range(KC):
                self.mm(pq.t[0:64, 0:C], pq.b, wq.t[:, kc, h * 64:(h + 1) * 64], wq.b, xn.t[:, kc, 0:C], xn.b, kc == 0, kc == KC - 1)
            if sample:
                self.ts(self.qsf.t[:, h, 0:C], self.qsf.b, pq.t[0:64, 0:C], pq.b, 0.125, ALU.mult)
            else:
                self.stt(self.qt[h].t[:, 0:C], self.qt[h].b, pq.t[0:64, 0:C], pq.b, 0.125, ebq.t[:, 0:C], ebq.b, ALU.mult, ALU.mult)
                ebk = self.eb.next()
                self.actf(ebk.t[:, 0:C], ebk.b, self.bTs.t[:, h, 0:C], self.bTs.b, AF.Exp, scale=-1.0)
                pk = self.G.next()
                for kc in range(KC):
                    self.mm(pk.t[0:64, 0:C], pk.b, wk.t[:, kc, h * 64:(h + 1) * 64], wk.b, xn.t[:, kc, 0:C], xn.b, kc == 0, kc == KC - 1)
                self.tt(self.kt[h].t[:, 0:C], self.kt[h].b, pk.t[0:64, 0:C], pk.b, ebk.t[:, 0:C], ebk.b, ALU.mult)
    def mla_q(self, h, C):
        cqn = self.cqn
        pn = self.G.next()
        for i in range(2):
            self.mm(pn.t[:, 0:C], pn.b, self.wuq.t[:, i, h * 192:h * 192 + 128], self.wuq.b, cqn.t[:, i, 0:C], cqn.b, i == 0, i == 1)
        qn = self.qn.next()
        self.acp(qn.t[:, 0:C], qn.b, pn.t[:, 0:C], pn.b)
        pl = self.G.next()
        self.mm(pl.t[:, 0:C], pl.b, self.wukT.t[:, h, :], self.wukT.b, qn.t[:, 0:C], qn.b)
        ql = self.ql.next()
        self.acp(ql.t[:, 0:C], ql.b, pl.t[:, 0:C], pl.b)
        p1 = self.G.next()
        for i in range(2):
            self.mm(p1.t[0:64, 0:C], p1.b, self.wuq.t[:, i, h * 192 + 128:h * 192 + 192], self.wuq.b, cqn.t[:, i, 0:C], cqn.b, i == 0, i == 1)
        r1 = self.rt.next()
        self.tt(r1.t[:, 0:C], r1.b, p1.t[0:64, 0:C], p1.b, self.cs.t[:, 0, 0:C], self.cs.b, ALU.mult)
        p2 = self.G.next()
        for i in range(2):
            self.mm(p2.t[0:64, 0:C], p2.b, self.wuqr.t[:, i, h, :], self.wuqr.b, cqn.t[:, i, 0:C], cqn.b, i == 0, i == 1)
        r2 = self.rt.next()
        self.tt(r2.t[:, 0:C], r2.b, p2.t[0:64, 0:C], p2.b, self.cs.t[:, 1, 0:C], self.cs.b, ALU.mult)
        qp = self.qpe.next()
        self.tt(qp.t[:, 0:C], qp.b, r1.t[:, 0:C], r1.b, r2.t[:, 0:C], r2.b, ALU.add)
        return ql, qp

    def gla_out(self, h, src, srcb, C):
        sq = self.h
        self.actf(sq.t[:, 0, 0:C], sq.b, src, srcb, AF.Square)
        self.rstd_from([sq.t[:, 0, 0:C]], sq.b, self.on128, C)
        r = self.rl
        self.stt(r.t[:, 0:C], r.b, src, srcb, self.smw.t[:, 32:33], self.rstd.t[:, 0:C], self.rstd.b,
                 ALU.mult, ALU.mult, sreads=[self.smw.b])
        self.tt(self.ym.t[:, h, 0:C], self.ym.b, r.t[:, 0:C], r.b, self.sgT.t[:, h, 0:C], self.sgT.b, ALU.mult)

    def out_proj(self, C):
        wout = self.w["wout"]
        for u in range(4):
            wo = self.wload8(wout, u * 256, 256)
            for i in range(2):
                dc = u * 2 + i
                po = self.O.next()
                for kc in range(KC):
                    self.mm(po.t[:, 0:C], po.b, wo.t[:, kc, i * 128:(i + 1) * 128], wo.b, self.ym.t[:, kc, 0:C], self.ym.b, kc == 0, kc == KC - 1)
                self.tt(self.xT.t[:, dc, 0:C], self.xT.b, po.t[:, 0:C], po.b, self.xT.t[:, dc, 0:C], self.xT.b, ALU.add)

    def prompt_tile(self, ti):
        s, t = ti // 4, ti % 4
        C = NT
        tok0 = t * NT
        w = self.w
        self.load_x(lambda a: self.xp.ap()[s, tok0 + a * 128:tok0 + (a + 1) * 128, :], 4, 128)
        self.ffn(w["f1g"], w["f1u"], w["f1d"], 0, C)
        if t == 0:
            for h in range(4):
                self.memset(self.S[h].t[:], self.S[h].b, 0.0)
                self.memset(self.Sb[h].t[:], self.Sb[h].b, 0.0)
        self.mixer_proj(C, 4, 128, tok0, sample=False)
        self.cp(self.latT_t[:, tok0:tok0 + C], self.bA, self.latf.t[:, 0:C], self.latf.b)
        self.cp(self.kpe_t[:, tok0:tok0 + C], self.bC, self.kpef.t[:, 0:C], self.kpef.b)
        pt_ = self.G.next()
        for a in range(4):
            self.tr(pt_.t[:, a * 128:(a + 1) * 128], pt_.b, self.latf.t[:, a * 128:(a + 1) * 128], self.latf.b, self.ident, self.cst.b)
        ot = self.otok.next()
        self.acp(ot.t[:].rearrange("p a c -> p (a c)"), ot.b, pt_.t[:], pt_.b)
        self.cp(self.latM_t[:, t * 4:(t + 1) * 4, :], self.bB, ot.t[:], ot.b)
        self.dma_out(self.kvp.ap()[s, tok0:tok0 + C, :].rearrange("(a p) c -> p a c", p=128), ot.t[:], ot.b)
        pt2 = self.G.next()
        for a in range(4):
            self.tr(pt2.t[:, a * 64:(a + 1) * 64], pt2.b, self.kpef.t[:, a * 128:(a + 1) * 128], self.kpef.b, self.ident[0:64, 0:64], self.cst.b)
        ot2 = self.otok2.next()
        self.acp(ot2.t[:].rearrange("p a c -> p (a c)"), ot2.b, pt2.t[:, 0:256], pt2.b)
        self.dma_out(self.pep.ap()[s, tok0:tok0 + C, :].rearrange("(a p) c -> p a c", p=128), ot2.t[:], ot2.b)
        gitems = [(a, h) for a in range(4) for h in range(4)]
        pos = {}

        def gla_A(a, h):
            ca = slice(a * 128, (a + 1) * 128)
            pa = self.G.next()
            self.mm(pa.t[:, 0:128], pa.b, self.kt[h].t[:, ca], self.kt[h].b, self.qt[h].t[:, ca], self.qt[h].b)
            am = self.atm.next()
            self.tt(am.t[:], am.b, pa.t[:, 0:128], pa.b, self.trib.t[:], self.trib.b, ALU.mult)
            return am

        def gla_rest(a, h, am):
            ca = slice(a * 128, (a + 1) * 128)
            hc = slice(h * 128, (h + 1) * 128)
            if h == 0:
                pos[a] = self.O.next()
            po = pos[a]
            self.mm(po.t[:, hc], po.b, self.vtm.t[:, a, hc], self.vtm.b, am.t[:], am.b, True, False)
            self.mm(po.t[:, hc], po.b, self.Sb[h].t[:], self.Sb[h].b, self.qt[h].t[:, ca], self.qt[h].b, False, True)
            pS = self.G.next()
            self.mm(pS.t[0:64, 0:128], pS.b, self.ktm.t[:, a, h * 64:(h + 1) * 64], self.ktm.b, self.vtm.t[:, a, hc], self.vtm.b)
            el = self.elast.t[:, h, a:a + 1]
            self.ts(self.S[h].t[:], self.S[h].b, self.S[h].t[:], self.S[h].b, el, ALU.mult, sreads=[self.elast.b])
            self.stt(self.S[h].t[:], self.S[h].b, pS.t[0:64, 0:128], pS.b, el, self.S[h].t[:], self.S[h].b, ALU.mult, ALU.add,
                     sreads=[self.elast.b])
            self.acp(self.Sb[h].t[:], self.Sb[h].b, self.S[h].t[:], self.S[h].b)
            if h == 3:
                sq = self.h
                self.actf(sq.t[:, 0, 0:C], sq.b, po.t[:, 0:C], po.b, AF.Square)
                self.rstd_from([sq.t[:, 0, 0:C]], sq.b, self.on128, C)
                r = self.rl
                self.stt(r.t[:, 0:C], r.b, po.t[:, 0:C], po.b, self.smw.t[:, 32:33], self.rstd.t[:, 0:C], self.rstd.b,
                         ALU.mult, ALU.mult, sreads=[self.smw.b])
                self.tt(self.ym.t[:, 0:4, ca], self.ym.b, r.t[:, 0:C].rearrange("p (h t) -> p h t", h=4), r.b,
                        self.sgT.t[:, 0:4, ca], self.sgT.b, ALU.mult)

        am_next = gla_A(*gitems[0])
        for i, (a, h) in enumerate(gitems):
            am_cur = am_next
            if i + 1 < len(gitems):
                am_next = gla_A(*gitems[i + 1])
            gla_rest(a, h, am_cur)
        if t == 3:
            for h in range(4):
                self.dma_out(self.glap.ap()[s, h], self.S[h].t[:], self.S[h].b)
        qs = [self.mla_q(h, C) for h in range(4)]
        nkb = 4 * t + 4
        items = [(h, kb) for h in range(4) for kb in range(nkb)]
        acc = {}

        def emit_scores(h, kb):
            ql, qp = qs[h]
            r = kb - 4 * t
            c0 = 128 * r if r > 0 else 0
            N = C - c0
            ps = self.G.next()
            ks = slice(kb * 128, (kb + 1) * 128)
            self.mm(ps.t[:, 0:N], ps.b, self.latT_t[:, ks], self.bA, ql.t[:, c0:C], ql.b, True, False)
            self.mm(ps.t[:, 0:N], ps.b, self.kpe_t[:, ks], self.bC, qp.t[:, c0:C], qp.b, False, True)
            return ps, r, c0, N

        def emit_pv(h, kb, info):
            ps, r, c0, N = info
            if kb == 0:
                acc[h] = (self.O.next(), self.O.next())
            po, pl = acc[h]
            pt = self.pt.next()
            self.actf(pt.t[:, 0:N], pt.b, ps.t[:, 0:N], ps.b, AF.Exp, scale=MLA_SCALE)
            if r >= 0:
                self.tt(pt.t[:, 0:128], pt.b, pt.t[:, 0:128], pt.b, self.trib.t[:], self.trib.b, ALU.mult)
            self.mm(po.t[:, c0:C], po.b, self.latM_t[:, kb, :], self.bB, pt.t[:, 0:N], pt.b, kb == 0, kb == nkb - 1)
            self.mm(pl.t[:, c0:C], pl.b, self.on1.t[:], self.on1.b, pt.t[:, 0:N], pt.b, kb == 0, kb == nkb - 1)
            if kb == nkb - 1:
                self.actf(self.rl.t[:, 0:C], self.rl.b, pl.t[:, 0:C], pl.b, AF.Ln)
                self.actf(self.rl.t[:, 0:C], self.rl.b, self.rl.t[:, 0:C], self.rl.b, AF.Exp, scale=-1.0)
                self.tt(self.ol.t[:, 0:C], self.ol.b, po.t[:, 0:C], po.b, self.rl.t[:, 0:C], self.rl.b, ALU.mult)
                py = self.G.next()
                self.mm(py.t[:, 0:C], py.b, self.wuv.t[:, h * 128:(h + 1) * 128], self.wuv.b, self.ol.t[:, 0:C], self.ol.b)
                self.acp(self.ym.t[:, 4 + h, 0:C], self.ym.b, py.t[:, 0:C], py.b)

        info = emit_scores(*items[0])
        for i, (h, kb) in enumerate(items):
            nxt = emit_scores(*items[i + 1]) if i + 1 < len(items) else None
            emit_pv(h, kb, info)
            info = nxt
        self.out_proj(C)
        self.ffn(w["f2g"], w["f2u"], w["f2d"], 16, C)
        self.store_y(lambda a: self.yp.ap()[s, tok0 + a * 128:tok0 + (a + 1) * 128, :], 4, 128)

    def sample_tile(self):
        C = NSEQ_S
        w = self.w
        self.load_x(lambda a: self.xs.ap(), 1, 16)
        self.ffn(w["f1g"], w["f1u"], w["f1d"], 0, C)
        self.mixer_proj(C, 1, 16, SEQ, sample=True)
        pt_ = self.G.next()
        self.tr(pt_.t[:, 0:128], pt_.b, self.latf.t[:, 0:128], self.latf.b, self.ident, self.cst.b)
        ot = self.otok.next()
        self.acp(ot.t[0:16, 0, :], ot.b, pt_.t[0:16, 0:128], pt_.b)
        self.dma_out(self.kvs.ap(), ot.t[0:16, 0, :], ot.b)
        pt2 = self.G.next()
        self.tr(pt2.t[:, 0:64], pt2.b, self.kpef.t[:, 0:128], self.kpef.b, self.ident[0:64, 0:64], self.cst.b)
        ot2 = self.otok2.next()
        self.acp(ot2.t[0:16, 0, :], ot2.b, pt2.t[0:16, 0:64], pt2.b)
        self.dma_out(self.pes.ap(), ot2.t[0:16, 0, :], ot2.b)
        self.decode_attention()
        self.out_proj(C)
        self.ffn(w["f2g"], w["f2u"], w["f2d"], 16, C)
        self.store_y(lambda a: self.ys.ap(), 1, 16)

    def gla_sample_seq(self, b):
        pgl = self.XS
        s0 = self.s0.next()
        self.dma_in(s0.t[:], s0.b, self.sgla.ap()[b].rearrange("h d v -> d h v"))
        km = self.kmask.next()
        self.ts(km.t[:], km.b, self.ktm.t[0:16, 0, :], self.ktm.b, self.cst.t[0:16, b:b + 1], ALU.mult, sreads=[self.cst.b])
        pd = self.G.next()
        for h in range(4):
            self.mm(pd.t[0:64, h * 128:(h + 1) * 128], pd.b, km.t[:, h * 64:(h + 1) * 64], km.b,
                    self.vtm.t[0:16, 0, h * 128:(h + 1) * 128], self.vtm.b)
        s1 = self.s1
        for h in range(4):
            self.stt(s1.t[:, h, :], s1.b, s0.t[:, h, :], s0.b, self.esf.t[:, h, b:b + 1], pd.t[0:64, h * 128:(h + 1) * 128], pd.b,
                     ALU.mult, ALU.add, sreads=[self.esf.b])
        self.dma_out(self.glas.ap()[b].rearrange("h d v -> d h v"), s1.t[:], s1.b)
        for h in range(4):
            self.mm(pgl.t[:, h * 18 + b:h * 18 + b + 2], pgl.b, s1.t[:, h, :], s1.b, self.qsf.t[:, h, b:b + 2], self.qsf.b)

    def gla_sample_finish(self):
        pgl = self.XS
        self.cp(self.gsb.t[:].rearrange("p (h t) -> p h t", h=4), self.gsb.b,
                pgl.t[:, 0:72].rearrange("p (h t) -> p h t", h=4)[:, :, 0:16], pgl.b)
        for h in range(4):
            self.gla_out(h, self.gsb.t[:, h * 16:(h + 1) * 16], self.gsb.b, NSEQ_S)

    def bcreg(self, e):
        if getattr(self, "_bcreg", None) is None:
            self._bcreg = e.alloc_register("gather_bound")
            e.reg_mov(self._bcreg, NPOOL * 8 - 1)
        return self._bcreg

    def decode_attention(self):
        P = self.P
        C = NSEQ_S
        ar = self.arena.t
        gl = [(ar[:, 0:2048].rearrange("p (t c) -> p t c", c=128), ar[:, 0:2048], self.bA),
              (ar[:, 2048:4096].rearrange("p (t c) -> p t c", c=128), ar[:, 2048:4096], self.bB)]
        gp = [(ar[:, 4096:5120].rearrange("p (t c) -> p t c", c=64), ar[:, 4096:5120], self.bC),
              (ar[:, 5120:6144].rearrange("p (t c) -> p t c", c=64), ar[:, 5120:6144], self.bC)]
        self.dma_in(self.ptls.t[:], self.ptls.b, self.ptl.ap())
        self.ts(self.idx.t[:], self.idx.b, self.ptls.t[:], self.ptls.b, 8.0, ALU.mult, self.cst.t[:, 384:385], ALU.add, sreads=[self.cst.b])
        ckv_v = self.ckv.ap().rearrange("n (g t) c -> (n g) (t c)", t=16)
        cpe_v = self.cpe.ap().rearrange("n (g t) c -> (n g) (t c)", t=16)
        for h in range(4):
            ql, qp = self.mla_q(h, C)
            self.cp(self.QL.t[:, :, h], self.QL.b, ql.t[:, 0:C], ql.b)
            self.cp(self.QP.t[0:64, :, h], self.QP.b, qp.t[:, 0:C], qp.b)
        P.dma(lambda e: e.dma_start(out=self.QP.t[64:128, :, :], in_=self.QP.t[0:64, :, :]), writes=[self.QP.b], q="sp", no_waw=False)
        pnum, pden = self.O.next(), self.O.next()
        NB_ = 2
        NG = NSEQ_S * 4

        def gather(gi):
            glv, glf, glb = gl[gi % NB_]
            gpv, gpf, gpb = gp[gi % NB_]
            P.dma(lambda e: e.indirect_dma_start(
                out=glf, out_offset=None, in_=ckv_v,
                in_offset=bass.IndirectOffsetOnAxis(ap=self.idx.t[:, gi:gi + 1], axis=0),
                bounds_check=self.bcreg(e), oob_is_err=False),
                reads=[self.idx.b], writes=[glb], q="pool", no_waw=False)
            P.dma(lambda e: e.indirect_dma_start(
                out=gpf, out_offset=None, in_=cpe_v,
                in_offset=bass.IndirectOffsetOnAxis(ap=self.idx.t[:, gi:gi + 1], axis=0),
                bounds_check=self.bcreg(e), oob_is_err=False),
                reads=[self.idx.b], writes=[gpb], q="pool", no_waw=False)

        stage = {}
        psts = {}
        ptss = {}

        def emit_T(qi):
            gi, quad = qi // 4, qi % 4
            glv, glf, glb = gl[gi % NB_]
            gpv, gpf, gpb = gp[gi % NB_]
            xa, xb = self.XB.next(), self.XB.next()
            for k in range(4):
                self.tr(xa.t[:, k * 128:(k + 1) * 128], xa.b, glv[:, quad * 4 + k, :], glb, self.identb.t[:], self.identb.b)
            for j in range(2):
                c0_ = (quad * 4 + 2 * j) * 64
                self.tr(xb.t[:, j * 128:(j + 1) * 128], xb.b, gpf[:, c0_:c0_ + 128], gpb, self.identb.t[:], self.identb.b)
            lt4 = self.lt4.next()
            self.acp(lt4.t[:], lt4.b, xa.t[:], xa.b)
            pt4 = self.pt4.next()
            self.P.dve(lambda e: e.tensor_copy(out=pt4.t[:], in_=xb.t[:, 0:256]), reads=[xb.b, lt4.b], writes=[pt4.b])
            stage[qi] = (lt4, pt4)

        def emit_S(qi):
            gi, quad = qi // 4, qi % 4
            b, g = gi // 4, gi % 4
            if b not in psts:
                psts[b] = self.G.next()
            pst = psts[b]
            lt4, pt4 = stage.pop(qi)
            for k in range(4):
                kbi = g * 16 + quad * 4 + k
                self.mm(pst.t[:, kbi * 4:(kbi + 1) * 4], pst.b, lt4.t[:, k * 128:(k + 1) * 128], lt4.b, self.QL.t[:, b, :], self.QL.b, True, False)
                hp = slice((k % 2) * 64, (k % 2) * 64 + 64)
                self.mm(pst.t[:, kbi * 4:(kbi + 1) * 4], pst.b, pt4.t[hp, (k // 2) * 128:(k // 2 + 1) * 128], pt4.b,
                        self.QP.t[hp, b, :], self.QP.b, False, True)
            if quad == 3:
                pts = self.pts.next()
                self.actf(pts.t[:].rearrange("p t h -> p (t h)"), pts.b, pst.t[:, g * 64:(g + 1) * 64], pst.b, AF.Exp, scale=MLA_SCALE)
                psm = self.ptsum.next()
                P.dve(lambda e: e.tensor_reduce(out=psm.t[:], in_=pts.t[:].rearrange("p t h -> p h t"),
                                                axis=mybir.AxisListType.X, op=ALU.add),
                      reads=[pts.b], writes=[psm.b])
                ptss[gi] = (pts, psm)

        def emit_PV(gi):
            b, g = gi // 4, gi % 4
            glv, glf, glb = gl[gi % NB_]
            pts, psm = ptss.pop(gi)
            for tt_ in range(16):
                self.mm(pnum.t[:, b * 4:(b + 1) * 4], pnum.b, glv[:, tt_, :], glb, pts.t[:, tt_, :], pts.b,
                        g == 0 and tt_ == 0, g == 3 and tt_ == 15)
            self.mm(pden.t[:, b * 4:(b + 1) * 4], pden.b, self.onf.t[:], self.onf.b, psm.t[:], psm.b, g == 0, g == 3)

        NQ = NG * 4
        gather(0)
        for qi in range(NQ + 1):
            if qi < NQ:
                if qi % 4 == 0 and qi // 4 + 1 < NG:
                    pass
                emit_T(qi)
            if qi >= 1:
                emit_S(qi - 1)
                if (qi - 1) % 4 == 3:
                    gdone = (qi - 1) // 4
                    emit_PV(gdone)
                    if gdone + 2 < NG:
                        gather(gdone + 2)
            if qi == 0 and NG > 1:
                gather(1)
            if qi % 16 == 6 and qi < NQ:
                self.gla_sample_seq(qi // 16)
        self.gla_sample_finish()
        self.cp(self.num.t[:].rearrange("p b h -> p (b h)"), self.num.b, pnum.t[:, 0:64], pnum.b)
        self.cp(self.den.t[:].rearrange("p b h -> p (b h)"), self.den.b, pden.t[:, 0:64], pden.b)
        for h in range(4):
            self.tt(self.prod.t[:], self.prod.b, self.QL.t[:, :, h], self.QL.b, self.latf.t[:, 0:C], self.latf.b, ALU.mult)
            self.tt(self.prod2.t[:], self.prod2.b, self.QP.t[0:64, :, h], self.QP.b, self.kpef.t[:, 0:C], self.kpef.b, ALU.mult)
            psn = self.G.next()
            self.mm(psn.t[:, 0:C], psn.b, self.onf.t[:], self.onf.b, self.prod.t[:], self.prod.b, True, False)
            self.mm(psn.t[:, 0:C], psn.b, self.onf.t[0:64, :], self.onf.b, self.prod2.t[:], self.prod2.b, False, True)
            self.actf(self.pnew.t[:], self.pnew.b, psn.t[:, 0:C], psn.b, AF.Exp, scale=MLA_SCALE)
            self.tt(self.tmp16.t[:], self.tmp16.b, self.pnew.t[:], self.pnew.b, self.latf.t[:, 0:C], self.latf.b, ALU.mult)
            self.tt(self.num.t[:, :, h], self.num.b, self.num.t[:, :, h], self.num.b, self.tmp16.t[:], self.tmp16.b, ALU.add)
            self.tt(self.den.t[:, :, h], self.den.b, self.den.t[:, :, h], self.den.b, self.pnew.t[:], self.pnew.b, ALU.add)
            self.recip(self.tmp16.t[:], self.tmp16.b, self.den.t[:, :, h], self.den.b)
            self.tt(self.ol.t[:, 0:C], self.ol.b, self.num.t[:, :, h], self.num.b, self.tmp16.t[:], self.tmp16.b, ALU.mult)
            py = self.G.next()
            self.mm(py.t[:, 0:C], py.b, self.wuv.t[:, h * 128:(h + 1) * 128], self.wuv.b, self.ol.t[:, 0:C], self.ol.b)
            self.acp(self.ym.t[:, 4 + h, 0:C], self.ym.b, py.t[:, 0:C], py.b)


def build_program(do_sample=True, n_tiles=8):
    nc = bass.Bass("TRN2", target_bir_lowering=False)
    with ExitStack() as st:
        P = Prog(nc, st)
        B = Builder(nc, P, do_sample=do_sample, n_tiles=n_tiles)
        B.alloc()
        build_program.sbuf_left = nc.sbuf_bytes_remaining
        B.setup()
        for ti in range(n_tiles):
            B.prompt_tile(ti)
            if B.first_pass:
                B.end_first_pass()
        if do_sample:
            B.sample_tile()
        build_program.n_ops = len(P.ops)
        P.emit()
    return nc


def _consts():
    c = np.zeros((128, 448), np.float32)
    p = np.arange(128)
    c[:, 0:128] = np.eye(128, dtype=np.float32)
    tri = (p[:, None] <= p[None, :]).astype(np.float32)
    c[:, 128:256] = tri
    c[:, 256:384] = tri * np.float32(-1.0 / 16.0)
    c[:, 384] = (p % 8).astype(np.float32)
    c[0:16, 401:417] = np.eye(16, dtype=np.float32) * np.float32(-1.0 / 16.0)
    return c


def _rope_tables():
    half = 32
    inv_freq = np.power(np.float32(10000.0), -np.arange(half, dtype=np.float32) / np.float32(half)).astype(np.float32)
    pos = np.concatenate([np.arange(SEQ, dtype=np.float32), np.full(16, 8192.0, np.float32)])
    ang = (pos[None, :] * inv_freq[:, None]).astype(np.float32)
    cos = np.cos(ang).astype(np.float32)
    sin = np.sin(ang).astype(np.float32)
    return np.stack([np.concatenate([cos, cos], 0), np.concatenate([sin, sin], 0)], 0)


_NC_CACHE = {}


def kernel(x_prompt, x_sample, cache_kv, cache_pe, state_gla, page_table,
           ffn1_norm_w, ffn1_w_gate, ffn1_w_up, ffn1_w_down, mix_norm_w, w_in,
           gla_w_a_up, gla_b_a, gla_norm_w, mla_q_norm_w, mla_w_uq, mla_kv_norm_w,
           mla_w_uk, mla_w_uv, w_out, ffn2_norm_w, ffn2_w_gate, ffn2_w_up, ffn2_w_down,
           final_norm_w, _do_sample=True, _n_tiles=8, _trace=False):
    f = lambda a: np.ascontiguousarray(np.asarray(a, dtype=np.float32))
    key = (_do_sample, _n_tiles)
    if key not in _NC_CACHE:
        _NC_CACHE[key] = build_program(_do_sample, _n_tiles)
    nc = _NC_CACHE[key]
    col = lambda v: np.asarray(v, np.float32).reshape(-1, 128).T
    smallw = np.zeros((128, 36), np.float32)
    smallw[:, 0:8] = col(ffn1_norm_w[0])
    smallw[:, 8:16] = col(mix_norm_w[0])
    smallw[:, 16:24] = col(ffn2_norm_w[0])
    smallw[:, 24:32] = col(final_norm_w)
    smallw[:, 32:33] = col(gla_norm_w[0])
    smallw[:, 33:35] = col(mla_q_norm_w[0])
    smallw[:, 35:36] = col(mla_kv_norm_w[0])
    shared = {
        "f1g": f(ffn1_w_gate[0]), "f1u": f(ffn1_w_up[0]), "f1d": f(ffn1_w_down[0]), "win": f(w_in[0]),
        "waup": f(np.concatenate([np.asarray(gla_w_a_up[0]), np.asarray(gla_b_a[0])[None, :]], 0)),
        "wuq": f(mla_w_uq[0]), "wuk": f(np.asarray(mla_w_uk[0]).reshape(128, 512)),
        "wuv": f(np.asarray(mla_w_uv[0]).reshape(128, 512)), "wout": f(w_out[0]),
        "f2g": f(ffn2_w_gate[0]), "f2u": f(ffn2_w_up[0]), "f2d": f(ffn2_w_down[0]),
        "smallw": smallw, "consts": _consts(), "rope": _rope_tables(),
        "wfin": np.ascontiguousarray(np.broadcast_to(np.asarray(final_norm_w, np.float32).reshape(1, D), (128, D))),
    }
    xp = np.asarray(x_prompt, np.float32)
    xs = np.asarray(x_sample, np.float32)
    sg = np.asarray(state_gla, np.float32)
    ptab = np.asarray(page_table, np.int32)
    pidx = np.arange(128) // 8
    ckv_full = f(cache_kv[0]) if _do_sample else None
    cpe_full = f(cache_pe[0]) if _do_sample else None
    in_maps = []
    for c in range(NCORES):
        m = dict(shared)
        m["xp"] = np.ascontiguousarray(xp[2 * c:2 * c + 2])
        m["xs"] = np.ascontiguousarray(xs[16 * c:16 * c + 16, 0, :])
        if not _do_sample:
            in_maps.append(m)
            continue
        m["ckv"] = ckv_full
        m["cpe"] = cpe_full
        m["sgla"] = np.ascontiguousarray(sg[0, 16 * c:16 * c + 16])
        pt_c = ptab[16 * c:16 * c + 16].reshape(16, 4, 16)
        m["ptl"] = np.ascontiguousarray(pt_c[:, :, pidx].transpose(2, 0, 1).reshape(128, 64)).astype(np.int32)
        in_maps.append(m)
    res = run_bass_kernel_spmd(nc, in_maps, core_ids=list(range(NCORES)), trace=_trace)
    R = res.results
    cat = lambda k: np.concatenate([np.asarray(r[k]) for r in R], axis=0)
    y_prompt = cat("yp")
    y_sample = cat("ys")[:, None, :]
    outs = (y_prompt, y_sample, cat("kvp")[None], cat("pep")[None], cat("glap")[None],
            cat("kvs")[None, :, None, :], cat("pes")[None, :, None, :], cat("glas")[None])
    outs = tuple(np.ascontiguousarray(o, dtype=np.float32) for o in outs)
    if _trace:
        kernel.last_exec_ns = res.exec_time_ns
    return outs
```

```python
import math
import numpy as np
from contextlib import ExitStack
import concourse.bass as bass
import concourse.mybir as mybir
from concourse.bass_utils import run_bass_kernel_spmd

F32 = mybir.dt.float32
BF16 = mybir.dt.bfloat16
I32 = mybir.dt.int32
AF = mybir.ActivationFunctionType
ALU = mybir.AluOpType

NCORES = 8
D = 1024
KC = 8
DFF = 2816
SEQ = 2048
NT = 512
NPOOL = 10240
NSEQ_S = 16
EPS = 1e-6
MLA_SCALE = 1.0 / math.sqrt(192.0)
NUNITS = 80
SAME_ENGINE_SYNC = True

C_Q, C_K, C_V, C_G, C_A, C_CQ, C_KV = 0, 256, 512, 1024, 1536, 1552, 1808


class Buf:
    __slots__ = ("name", "last_w", "readers", "war", "sem_in", "n_in", "sem_out", "n_out")

    def __init__(self, name):
        self.name = name
        self.last_w = None
        self.readers = []
        self.war = []
        self.sem_in = None
        self.n_in = 0
        self.sem_out = None
        self.n_out = 0


class Op:
    __slots__ = ("eng", "fn", "deps", "dma", "sem", "count", "needs_inc", "seq")

    def __init__(self, eng, fn, dma):
        self.eng = eng
        self.fn = fn
        self.deps = set()
        self.dma = dma
        self.sem = None
        self.count = 0
        self.needs_inc = False
        self.seq = 0


class Prog:
    ENGS = ("pe", "act", "dve", "pool", "sp")

    def __init__(self, nc, stack):
        self.nc = nc
        self.stack = stack
        self.ops = []
        self.nsem = 0
        self.stores = []

    def new_sem(self, name):
        self.nsem += 1
        return self.stack.enter_context(self.nc.semaphore(f"s{self.nsem}_{name}"))

    def sb(self, name, shape, dtype):
        return self.stack.enter_context(self.nc.sbuf_tensor(name, list(shape), dtype))

    def ps(self, name, shape, dtype=F32):
        return self.stack.enter_context(self.nc.psum_tensor(name, list(shape), dtype))

    def op(self, eng, fn, reads=(), writes=(), dma=False, no_waw=False):
        o = Op(eng, fn, dma)
        for b in reads:
            if b.last_w is not None:
                o.deps.add(b.last_w)
        for b in writes:
            for r in b.readers:
                o.deps.add(r)
            if no_waw:
                for r in b.war:
                    o.deps.add(r)
            elif b.last_w is not None:
                o.deps.add(b.last_w)
        o.deps.discard(o)
        if dma:
            if writes:
                b = writes[0]
                if b.sem_in is None:
                    b.sem_in = self.new_sem("i_" + b.name)
                b.n_in += 16
                o.sem, o.count = b.sem_in, b.n_in
            else:
                b = reads[0]
                if b.sem_out is None:
                    b.sem_out = self.new_sem("o_" + b.name)
                b.n_out += 16
                o.sem, o.count = b.sem_out, b.n_out
                self.stores.append(o)
        for b in reads:
            b.readers.append(o)
        for b in writes:
            if b.readers or not no_waw:
                b.war = [r for r in b.readers if r is not o]
            b.readers = []
            b.last_w = o
        for d in o.deps:
            if not d.dma and (d.eng != eng or dma or (SAME_ENGINE_SYNC and eng != "pe")):
                d.needs_inc = True
        self.ops.append(o)
        return o

    def pe(self, fn, reads=(), writes=()):
        return self.op("pe", fn, reads, writes)

    def act(self, fn, reads=(), writes=()):
        return self.op("act", fn, reads, writes)

    def dve(self, fn, reads=(), writes=()):
        return self.op("dve", fn, reads, writes)

    def pool(self, fn, reads=(), writes=()):
        return self.op("pool", fn, reads, writes)

    def dma(self, fn, reads=(), writes=(), q="sp", no_waw=True):
        return self.op(q, fn, reads, writes, dma=True, no_waw=no_waw)

    def emit(self):
        nc = self.nc
        eng_sem = {e: self.new_sem("eng_" + e) for e in self.ENGS}
        cnt = {e: 0 for e in self.ENGS}
        for o in self.ops:
            if not o.dma and o.needs_inc:
                cnt[o.eng] += 1
                o.seq = cnt[o.eng]
        per = {e: [o for o in self.ops if o.eng == e] for e in self.ENGS}
        final_waits = {}
        for o in self.stores:
            k = id(o.sem)
            if k not in final_waits or final_waits[k][1] < o.count:
                final_waits[k] = (o.sem, o.count)

        def run(engname, engobj):
            waited = {}
            for o in per[engname]:
                need = {}
                for d in o.deps:
                    if d.dma:
                        s, c = d.sem, d.count
                    else:
                        if d.eng == engname and not o.dma and (engname == "pe" or not SAME_ENGINE_SYNC):
                            continue
                        s, c = eng_sem[d.eng], d.seq
                    k = id(s)
                    if k not in need or need[k][1] < c:
                        need[k] = (s, c)
                for k, (s, c) in need.items():
                    if waited.get(k, 0) >= c:
                        continue
                    engobj.wait_ge(s, c)
                    waited[k] = c
                inst = o.fn(engobj)
                if o.dma:
                    inst.then_inc(o.sem, 16)
                elif o.needs_inc:
                    inst.then_inc(eng_sem[engname], 1)
            if engname == "sp":
                for k, (s, c) in final_waits.items():
                    if waited.get(k, 0) < c:
                        engobj.wait_ge(s, c)

        with nc.Block() as block:
            @block.tensor
            def _(e):
                run("pe", e)

            @block.scalar
            def _(e):
                run("act", e)

            @block.vector
            def _(e):
                run("dve", e)

            @block.gpsimd
            def _(e):
                run("pool", e)

            @block.sync
            def _(e):
                run("sp", e)


class T:
    def __init__(self, P, name, shape, dtype, psum=False):
        self.t = P.ps("P_" + name, shape, dtype) if psum else P.sb("S_" + name, shape, dtype)
        self.b = Buf(name)


class View:
    def __init__(self, ap, b):
        self.t = ap
        self.b = b


class Rot:
    def __init__(self, items):
        self.items = items
        self.i = 0

    def next(self):
        x = self.items[self.i % len(self.items)]
        self.i += 1
        return x


class Builder:
    def __init__(self, nc, P, do_sample=True, n_tiles=8):
        self.nc = nc
        self.P = P
        self.do_sample = do_sample
        self.n_tiles = n_tiles
        dt = nc.dram_tensor
        I, O = "ExternalInput", "ExternalOutput"
        self.xp = dt("xp", [2, SEQ, D], F32, kind=I)
        self.xs = dt("xs", [NSEQ_S, D], F32, kind=I)
        if do_sample:
            self.ckv = dt("ckv", [NPOOL, 128, 128], F32, kind=I)
            self.cpe = dt("cpe", [NPOOL, 128, 64], F32, kind=I)
            self.sgla = dt("sgla", [NSEQ_S, 4, 64, 128], F32, kind=I)
            self.ptl = dt("ptl", [128, NSEQ_S * 4], I32, kind=I)
        self.w = {}
        for name, shp in (("f1g", [D, DFF]), ("f1u", [D, DFF]), ("f1d", [DFF, D]), ("win", [D, 2000]),
                          ("waup", [17, 256]), ("wuq", [256, 768]), ("wuk", [128, 512]), ("wuv", [128, 512]),
                          ("wout", [D, D]), ("f2g", [D, DFF]), ("f2u", [D, DFF]), ("f2d", [DFF, D]),
                          ("smallw", [128, 36]), ("consts", [128, 448]), ("rope", [2, 64, SEQ + 16]), ("wfin", [128, D])):
            self.w[name] = dt(name, shp, F32, kind=I)
        self.scr = dt("wscr", [NUNITS, 128, 11 * 256], BF16, kind="Internal")
        self.scrb = Buf("wscr")
        self.units = {}
        self.first_pass = True
        self.yp = dt("yp", [2, SEQ, D], F32, kind=O)
        self.ys = dt("ys", [NSEQ_S, D], F32, kind=O)
        self.kvp = dt("kvp", [2, SEQ, 128], F32, kind=O)
        self.pep = dt("pep", [2, SEQ, 64], F32, kind=O)
        self.glap = dt("glap", [2, 4, 64, 128], F32, kind=O)
        self.kvs = dt("kvs", [NSEQ_S, 128], F32, kind=O)
        self.pes = dt("pes", [NSEQ_S, 64], F32, kind=O)
        self.glas = dt("glas", [NSEQ_S, 4, 64, 128], F32, kind=O)

    def alloc(self):
        P = self.P
        mk = lambda n, s, d=F32: T(P, n, s, d)
        self.cst = mk("cst", [128, 448])
        self.smw = mk("smw", [128, 36])
        self.wfin = mk("wfin", [128, D])
        self.fst = mk("fst", [128, 2])
        self.identb = mk("identb", [128, 128], BF16)
        self.trib = mk("trib", [128, 128], BF16)
        self.on1024 = mk("on1024", [128, 128], BF16)
        self.on256 = mk("on256", [128, 128], BF16)
        self.on128 = mk("on128", [128, 128], BF16)
        self.on1 = mk("on1", [128, 128], BF16)
        self.onf = mk("onf", [128, 128])
        self.waup = mk("waup", [17, 256], BF16)
        self.wuq = mk("wuq", [128, 2, 768], BF16)
        self.wuqr = mk("wuqr", [128, 2, 4, 64], BF16)
        self.wuk = mk("wuk", [128, 512], BF16)
        self.wuv = mk("wuv", [128, 512], BF16)
        self.wukT = mk("wukT", [128, 4, 128], BF16)
        self.xio = Rot([mk(f"xio{i}", [128, D]) for i in range(2)])
        self.xT = mk("xT", [128, KC, NT])
        self.xn = mk("xn", [128, KC, NT], BF16)
        self.h = mk("h", [128, 11, NT], BF16)
        self.rstd = mk("rstd", [128, NT])
        self.sg = Rot([mk(f"sg{i}", [128, NT]) for i in range(2)])
        self.su = Rot([mk(f"su{i}", [128, NT]) for i in range(2)])
        self.wbuf = Rot([mk(f"wb{i}", [128, 11, 256], BF16) for i in range(4)])
        self.dummy = mk("dummy", [128, 8])
        self.arena = mk("arena", [128, 6144], BF16)
        ar = self.arena.t
        self.latT_t = ar[:, 0:2048]
        self.latM_t = ar[:, 2048:4096].rearrange("p (a c) -> p a c", c=128)
        self.kpe_t = ar[0:64, 4096:6144]
        self.bA, self.bB, self.bC = Buf("arA"), Buf("arB"), Buf("arC")
        self.alow = mk("alow", [17, NT], BF16)
        self.Lt = Rot([mk(f"Lt{i}", [128, 256]) for i in range(2)])
        self.enb = Rot([mk(f"enb{i}", [128, 256]) for i in range(2)])
        self.bTs = mk("bTs", [64, 4, NT])
        self.eb = Rot([mk(f"eb{i}", [64, NT]) for i in range(2)])
        self.elast = mk("elast", [64, 4, 4])
        self.qt = [mk(f"qt{h}", [64, NT], BF16) for h in range(4)]
        self.kt = [mk(f"kt{h}", [64, NT], BF16) for h in range(4)]
        self.ktm = mk("ktm", [128, 4, 256], BF16)
        self.vtm = mk("vtm", [128, 4, 512], BF16)
        self.sgT = mk("sgT", [128, 4, NT], BF16)
        self.cq = mk("cq", [128, 2, NT])
        self.cqn = mk("cqn", [128, 2, NT], BF16)
        self.qn = Rot([mk(f"qn{i}", [128, NT], BF16) for i in range(1)])
        self.ql = Rot([mk(f"ql{i}", [128, NT], BF16) for i in range(4)])
        self.qpe = Rot([mk(f"qpe{i}", [64, NT], BF16) for i in range(4)])
        self.ckvf = mk("ckvf", [128, NT])
        self.latf = mk("latf", [128, NT])
        self.kpef = mk("kpef", [64, NT])
        self.cs = mk("cs", [64, 2, NT])
        self.rt = Rot([mk(f"rt{i}", [64, NT]) for i in range(2)])
        self.wkvr = mk("wkvr", [128, KC, 64], BF16)
        self.S = [mk(f"S{h}", [64, 128]) for h in range(4)]
        self.Sb = [mk(f"Sb{h}", [64, 128], BF16) for h in range(4)]
        self.ym = mk("ym", [128, 8, NT], BF16)
        self.pt = Rot([mk(f"pt{i}", [128, NT], BF16) for i in range(3)])
        self.atm = Rot([mk(f"atm{i}", [128, 128], BF16) for i in range(3)])
        self.rl = mk("rl", [128, NT])
        self.ol = mk("ol", [128, NT], BF16)
        self.otok = Rot([mk(f"otok{i}", [128, 4, 128]) for i in range(2)])
        self.otok2 = Rot([mk(f"otk2{i}", [128, 4, 64]) for i in range(2)])
        self.G = Rot([T(P, f"pg{i}", [128, NT], F32, psum=True) for i in range(3)])
        self.O = Rot([T(P, f"po{i}", [128, NT], F32, psum=True) for i in range(2)])
        self.XS = T(P, "pxs", [128, NT], F32, psum=True)
        self.F = Rot(self.G.items + [self.XS])
        xbs = []
        for i in range(2):
            full = T(P, f"pxb{i}", [128, 1024], BF16, psum=True)
            for j in range(2):
                v = View(full.t[:, j * 512:(j + 1) * 512], full.b)
                xbs.append(v)
        self.XB = Rot(xbs)
        if self.do_sample:
            self.idx = mk("idx", [128, NSEQ_S * 4], I32)
            self.ptls = mk("ptls", [128, NSEQ_S * 4], I32)
            self.lt4 = Rot([mk(f"lt4_{i}", [128, 512], BF16) for i in range(2)])
            self.pt4 = Rot([mk(f"pt4_{i}", [128, 256], BF16) for i in range(2)])
            self.pts = Rot([mk(f"pts{i}", [128, 16, 4], BF16) for i in range(2)])
            self.ptsum = Rot([mk(f"ptsum{i}", [128, 4]) for i in range(2)])
            self.QL = mk("QL", [128, NSEQ_S, 4], BF16)
            self.QP = mk("QP", [128, NSEQ_S, 4], BF16)
            self.s0 = Rot([mk(f"s0_{i}", [64, 4, 128]) for i in range(2)])
            self.s1 = mk("s1", [64, 4, 128])
            self.kmask = Rot([mk(f"kmask{i}", [16, 256], BF16) for i in range(2)])
            self.qsf = mk("qsf", [64, 4, 18])
            self.esf = mk("esf", [64, 4, 16])
            self.num = mk("num", [128, NSEQ_S, 4])
            self.den = mk("den", [128, NSEQ_S, 4])
            self.pnew = mk("pnew", [128, 16])
            self.prod = mk("prod", [128, 16])
            self.prod2 = mk("prod2", [64, 16])
            self.tmp16 = mk("tmp16", [128, 16])
            self.gsb = mk("gsb", [128, 64])

    def mm(self, out, ob, lhsT, lb, rhs, rb, start=True, stop=True):
        rd = [lb] if rb is lb else [lb, rb]
        self.P.pe(lambda e: e.matmul(out, lhsT=lhsT, rhs=rhs, start=start, stop=stop), reads=rd, writes=[ob])

    def tr(self, out, ob, in_, ib, ident, idb):
        self.P.pe(lambda e: e.transpose(out, in_, ident), reads=[ib, idb], writes=[ob])

    def actf(self, out, ob, in_, ib, func, scale=1.0, bias=0.0):
        self.P.act(lambda e: e.activation(out=out, in_=in_, func=func, bias=bias, scale=scale), reads=[ib], writes=[ob])

    def acp(self, out, ob, in_, ib):
        self.P.act(lambda e: e.activation(out=out, in_=in_, func=AF.Copy), reads=[ib], writes=[ob])

    def tt(self, out, ob, a, ab, b, bb, op):
        self.P.dve(lambda e: e.tensor_tensor(out=out, in0=a, in1=b, op=op), reads=[ab, bb], writes=[ob])

    def stt(self, out, ob, a, ab, scalar, b, bb, op0, op1, sreads=()):
        self.P.dve(lambda e: e.scalar_tensor_tensor(out=out, in0=a, scalar=scalar, in1=b, op0=op0, op1=op1),
                   reads=[ab, bb] + list(sreads), writes=[ob])

    def ts(self, out, ob, a, ab, s1, op0, s2=None, op1=None, sreads=()):
        if op1 is None:
            self.P.dve(lambda e: e.tensor_scalar(out=out, in0=a, scalar1=s1, scalar2=None, op0=op0),
                       reads=[ab] + list(sreads), writes=[ob])
        else:
            self.P.dve(lambda e: e.tensor_scalar(out=out, in0=a, scalar1=s1, scalar2=s2, op0=op0, op1=op1),
                       reads=[ab] + list(sreads), writes=[ob])

    def cp(self, out, ob, in_, ib):
        self.P.dve(lambda e: e.tensor_copy(out=out, in_=in_), reads=[ib], writes=[ob])

    def memset(self, ap, b, v):
        self.P.dve(lambda e: e.memset(ap, v), writes=[b])

    def recip(self, out, ob, in_, ib):
        self.P.dve(lambda e: e.reciprocal(out=out, in_=in_), reads=[ib], writes=[ob])

    def dma_in(self, out, ob, src, q="sp"):
        self.P.dma(lambda e: e.dma_start(out=out, in_=src), writes=[ob], q=q)

    def dma_out(self, dst, in_, ib, q="sp"):
        self.P.dma(lambda e: e.dma_start(out=dst, in_=in_), reads=[ib], q=q)

    def rstd_from(self, sq_aps, sqb, ones, C, npart=128):
        ps = self.XS
        n = len(sq_aps)
        for i, ap in enumerate(sq_aps):
            self.mm(ps.t[:, 0:C], ps.b, ones.t[0:npart, :], ones.b, ap, sqb, start=(i == 0), stop=(i == n - 1))
        self.actf(self.rstd.t[:, 0:C], self.rstd.b, ps.t[:, 0:C], ps.b, AF.Ln, bias=EPS)
        self.actf(self.rstd.t[:, 0:C], self.rstd.b, self.rstd.t[:, 0:C], self.rstd.b, AF.Exp, scale=-0.5)

    def unit_idx(self, key):
        if key not in self.units:
            assert self.first_pass, key
            self.units[key] = len(self.units)
            assert len(self.units) <= NUNITS
        return self.units[key]

    def wstream(self, key, wt, dst, src_f32, scr_view):
        if self.first_pass:
            self.dma_in(dst, wt.b, src_f32, q="pool")
            self.P.dma(lambda e: e.dma_start(out=scr_view, in_=dst), reads=[wt.b], q="sp")
        else:
            self.P.dma(lambda e: e.dma_start(out=dst, in_=scr_view), writes=[wt.b], q="pool")

    def end_first_pass(self):
        self.first_pass = False
        bufs = [w.b for w in self.wbuf.items]
        self.P.pool(lambda e: e.memset(self.dummy.t[:], 0.0), writes=bufs + [self.dummy.b])

    def wload8(self, wd, c0, ncols):
        wt = self.wbuf.next()
        u = self.unit_idx((wd.name, c0, ncols))
        src = wd.ap()[:, c0:c0 + ncols].rearrange("(kc p) c -> p kc c", p=128)
        scr_view = self.scr.ap()[u, :, 0:KC * ncols].rearrange("p (k c) -> p k c", c=ncols)
        self.wstream(u, wt, wt.t[:, 0:KC, 0:ncols], src, scr_view)
        return wt

    def setup(self):
        w = self.w
        self.dma_in(self.cst.t[:], self.cst.b, w["consts"].ap())
        self.dma_in(self.smw.t[:], self.smw.b, w["smallw"].ap())
        self.dma_in(self.wfin.t[:], self.wfin.b, w["wfin"].ap())
        self.ident = self.cst.t[:, 0:128]
        self.trif = self.cst.t[:, 256:384]
        self.negI = self.cst.t[:, 401:417]
        self.cp(self.identb.t[:], self.identb.b, self.cst.t[:, 0:128], self.cst.b)
        self.cp(self.trib.t[:], self.trib.b, self.cst.t[:, 128:256], self.cst.b)
        for tl, v in ((self.on1024, 1.0 / 1024), (self.on256, 1.0 / 256), (self.on128, 1.0 / 128), (self.on1, 1.0),
                      (self.onf, 1.0), (self.alow, 1.0)):
            self.memset(tl.t[:], tl.b, v)
        self.memset(self.latf.t[:, 0:128], self.latf.b, 0.0)
        self.memset(self.kpef.t[:, 0:128], self.kpef.b, 0.0)
        self.memset(self.xT.t[:, :, 0:128], self.xT.b, 0.0)
        if self.do_sample:
            self.memset(self.qsf.t[:], self.qsf.b, 0.0)
        self.dma_in(self.waup.t[:], self.waup.b, w["waup"].ap(), q="pool")
        self.dma_in(self.wuq.t[:], self.wuq.b, w["wuq"].ap().rearrange("(kc p) c -> p kc c", p=128), q="pool")
        self.dma_in(self.wuk.t[:], self.wuk.b, w["wuk"].ap(), q="pool")
        self.dma_in(self.wuv.t[:], self.wuv.b, w["wuv"].ap(), q="pool")
        for h in range(4):
            b0 = h * 192 + 128
            self.ts(self.wuqr.t[:, :, h, 0:32], self.wuqr.b, self.wuq.t[:, :, b0 + 32:b0 + 64], self.wuq.b, -1.0, ALU.mult)
            self.cp(self.wuqr.t[:, :, h, 32:64], self.wuqr.b, self.wuq.t[:, :, b0:b0 + 32], self.wuq.b)
            xb = self.XB.next()
            self.tr(xb.t[:, 0:128], xb.b, self.wuk.t[:, h * 128:(h + 1) * 128], self.wuk.b, self.identb.t[:], self.identb.b)
            self.cp(self.wukT.t[:, h, :], self.wukT.b, xb.t[:, 0:128], xb.b)

    def norm_x(self, col0, C, inplace=False):
        sq = self.h
        self.actf(sq.t[:, 0:KC, 0:C], sq.b, self.xT.t[:, :, 0:C], self.xT.b, AF.Square)
        self.rstd_from([sq.t[:, kc, 0:C] for kc in range(KC)], sq.b, self.on1024, C)
        for kc in range(KC):
            o, ob = (self.xT.t[:, kc, 0:C], self.xT.b) if inplace else (self.xn.t[:, kc, 0:C], self.xn.b)
            self.stt(o, ob, self.xT.t[:, kc, 0:C], self.xT.b, self.smw.t[:, col0 + kc:col0 + kc + 1],
                     self.rstd.t[:, 0:C], self.rstd.b, ALU.mult, ALU.mult, sreads=[self.smw.b])

    def ffn(self, wg, wu, wdn, col0, C):
        xn = self.xn
        sq = self.h
        self.actf(sq.t[:, 0:KC, 0:C], sq.b, self.xT.t[:, :, 0:C], self.xT.b, AF.Square)
        for kc in range(KC):
            self.ts(xn.t[:, kc, 0:C], xn.b, self.xT.t[:, kc, 0:C], self.xT.b, self.smw.t[:, col0 + kc:col0 + kc + 1], ALU.mult,
                    sreads=[self.smw.b])
        stats_done = False
        for half in range(2):
            f0 = half * 11
            for u in range(6):
                fs = [f for f in (f0 + 2 * u, f0 + 2 * u + 1) if f < f0 + 11]
                wgt = self.wload8(wg, fs[0] * 128, len(fs) * 128)
                wut = self.wload8(wu, fs[0] * 128, len(fs) * 128)
                for i, f in enumerate(fs):
                    pg = self.F.next() if stats_done else self.G.next()
                    pu = self.F.next() if stats_done else self.G.next()
                    for kc in range(KC):
                        self.mm(pg.t[:, 0:C], pg.b, wgt.t[:, kc, i * 128:(i + 1) * 128], wgt.b, xn.t[:, kc, 0:C], xn.b, kc == 0, kc == KC - 1)
                    for kc in range(KC):
                        self.mm(pu.t[:, 0:C], pu.b, wut.t[:, kc, i * 128:(i + 1) * 128], wut.b, xn.t[:, kc, 0:C], xn.b, kc == 0, kc == KC - 1)
                    if not stats_done:
                        self.rstd_from([sq.t[:, kc, 0:C] for kc in range(KC)], sq.b, self.on1024, C)
                        stats_done = True
                    sg = self.sg.next()
                    su = self.su.next()
                    self.tt(sg.t[:, 0:C], sg.b, pg.t[:, 0:C], pg.b, self.rstd.t[:, 0:C], self.rstd.b, ALU.mult)
                    self.actf(sg.t[:, 0:C], sg.b, sg.t[:, 0:C], sg.b, AF.Silu)
                    self.tt(su.t[:, 0:C], su.b, pu.t[:, 0:C], pu.b, self.rstd.t[:, 0:C], self.rstd.b, ALU.mult)
                    self.tt(self.h.t[:, f - f0, 0:C], self.h.b, su.t[:, 0:C], su.b, sg.t[:, 0:C], sg.b, ALU.mult)
            for dcp in range(4):
                wdt = self.wbuf.next()
                src = wdn.ap()[f0 * 128:(f0 + 11) * 128, dcp * 256:(dcp + 1) * 256].rearrange("(f p) c -> p f c", p=128)
                u = self.unit_idx((wdn.name, "down", f0, dcp))
                self.wstream(u, wdt, wdt.t[:], src, self.scr.ap()[u].rearrange("p (f c) -> p f c", c=256))
                for i in range(2):
                    dc = dcp * 2 + i
                    po = self.O.next()
                    for f in range(11):
                        self.mm(po.t[:, 0:C], po.b, wdt.t[:, f, i * 128:(i + 1) * 128], wdt.b, self.h.t[:, f, 0:C], self.h.b, f == 0, f == 10)
                    self.stt(self.xT.t[:, dc, 0:C], self.xT.b, po.t[:, 0:C], po.b, 0.5, self.xT.t[:, dc, 0:C], self.xT.b, ALU.mult, ALU.add)

    def load_x(self, src_rows, nblk, npart):
        for a in range(nblk):
            xi = self.xio.next()
            self.dma_in(xi.t[0:npart, :], xi.b, src_rows(a))
            for g in range(2):
                pg = self.G.next()
                for k in range(4):
                    kc = g * 4 + k
                    self.tr(pg.t[:, k * 128:k * 128 + npart], pg.b, xi.t[0:npart, kc * 128:(kc + 1) * 128], xi.b,
                            self.ident[0:npart, 0:npart], self.cst.b)
                src_v = pg.t[:].rearrange("p (k t) -> p k t", k=4)[:, :, 0:npart]
                dst_v = self.xT.t[:, g * 4:(g + 1) * 4, a * 128:a * 128 + npart]
                if g == 0:
                    self.acp(dst_v, self.xT.b, src_v, pg.b)
                else:
                    self.cp(dst_v, self.xT.b, src_v, pg.b)

    def store_y(self, dst_rows, nblk, npart):
        xT = self.xT
        junk = self.h.t[:, 0:2, :].rearrange("p a c -> p (a c)")
        for a in range(nblk):
            yo = self.xio.next()
            for g in range(2):
                pg = self.G.next()
                for k in range(4):
                    kc = g * 4 + k
                    self.tr(pg.t[:, k * 128:(k + 1) * 128], pg.b, xT.t[:, kc, a * 128:(a + 1) * 128], xT.b, self.ident, self.cst.b)
                if g == 0:
                    self.acp(yo.t[0:npart, 0:512], yo.b, pg.t[0:npart, :], pg.b)
                else:
                    self.cp(yo.t[0:npart, 512:1024], yo.b, pg.t[0:npart, :], pg.b)
            fst = self.fst
            self.P.act(lambda e, yo=yo: e.activation(out=junk[0:npart, :], in_=yo.t[0:npart, :], func=AF.Square,
                                                    accum_out=fst.t[0:npart, 0:1]),
                       reads=[yo.b], writes=[self.h.b, fst.b])
            self.actf(fst.t[0:npart, 1:2], fst.b, fst.t[0:npart, 0:1], fst.b, AF.Ln, scale=1.0 / D, bias=EPS)
            self.actf(fst.t[0:npart, 1:2], fst.b, fst.t[0:npart, 1:2], fst.b, AF.Exp, scale=-0.5)
            self.stt(yo.t[0:npart, :], yo.b, yo.t[0:npart, :], yo.b, fst.t[0:npart, 1:2], self.wfin.t[0:npart, :], self.wfin.b,
                     ALU.mult, ALU.mult, sreads=[fst.b])
            self.dma_out(dst_rows(a), yo.t[0:npart, :], yo.b, q="pool")

    def mixer_proj(self, C, A, npart, pos0, sample):
        win = self.w["win"]
        xn = self.xn
        self.norm_x(8, C)
        self.dma_in(self.cs.t[:, :, 0:C], self.cs.b, self.w["rope"].ap()[:, :, pos0:pos0 + C].rearrange("s r c -> r s c"))
        cum = self.negI[0:npart, 0:npart] if sample else self.trif
        wa = self.wload8(win, C_A, 16)
        pa = self.G.next()
        for kc in range(KC):
            self.mm(pa.t[0:16, 0:C], pa.b, wa.t[:, kc, 0:16], wa.b, xn.t[:, kc, 0:C], xn.b, kc == 0, kc == KC - 1)
        self.cp(self.alow.t[0:16, 0:C], self.alow.b, pa.t[0:16, 0:C], pa.b)
        for u in range(2):
            wg = self.wload8(win, C_G + u * 256, 256)
            for i in range(2):
                pg = self.G.next()
                for kc in range(KC):
                    self.mm(pg.t[:, 0:C], pg.b, wg.t[:, kc, i * 128:(i + 1) * 128], wg.b, xn.t[:, kc, 0:C], xn.b, kc == 0, kc == KC - 1)
                self.actf(self.sgT.t[:, u * 2 + i, 0:C], self.sgT.b, pg.t[:, 0:C], pg.b, AF.Silu)
        wc = self.wload8(win, C_CQ, 256)
        for i in range(2):
            pc = self.G.next()
            for kc in range(KC):
                self.mm(pc.t[:, 0:C], pc.b, wc.t[:, kc, i * 128:(i + 1) * 128], wc.b, xn.t[:, kc, 0:C], xn.b, kc == 0, kc == KC - 1)
            self.cp(self.cq.t[:, i, 0:C], self.cq.b, pc.t[:, 0:C], pc.b)
        sq = self.h
        self.actf(sq.t[:, 0:2, 0:C], sq.b, self.cq.t[:, :, 0:C], self.cq.b, AF.Square)
        self.rstd_from([sq.t[:, i, 0:C] for i in range(2)], sq.b, self.on256, C)
        for i in range(2):
            self.stt(self.cqn.t[:, i, 0:C], self.cqn.b, self.cq.t[:, i, 0:C], self.cq.b, self.smw.t[:, 33 + i:34 + i],
                     self.rstd.t[:, 0:C], self.rstd.b, ALU.mult, ALU.mult, sreads=[self.smw.b])
        wkv = self.wload8(win, C_KV, 192)
        self.ts(self.wkvr.t[:, :, 0:32], self.wkvr.b, wkv.t[:, 0:KC, 160:192], wkv.b, -1.0, ALU.mult)
        self.cp(self.wkvr.t[:, :, 32:64], self.wkvr.b, wkv.t[:, 0:KC, 128:160], wkv.b)
        pc = self.G.next()
        for kc in range(KC):
            self.mm(pc.t[:, 0:C], pc.b, wkv.t[:, kc, 0:128], wkv.b, xn.t[:, kc, 0:C], xn.b, kc == 0, kc == KC - 1)
        self.cp(self.ckvf.t[:, 0:C], self.ckvf.b, pc.t[:, 0:C], pc.b)
        self.actf(sq.t[:, 0, 0:C], sq.b, self.ckvf.t[:, 0:C], self.ckvf.b, AF.Square)
        self.rstd_from([sq.t[:, 0, 0:C]], sq.b, self.on128, C)
        self.stt(self.latf.t[:, 0:C], self.latf.b, self.ckvf.t[:, 0:C], self.ckvf.b, self.smw.t[:, 35:36],
                 self.rstd.t[:, 0:C], self.rstd.b, ALU.mult, ALU.mult, sreads=[self.smw.b])
        p1 = self.G.next()
        for kc in range(KC):
            self.mm(p1.t[0:64, 0:C], p1.b, wkv.t[:, kc, 128:192], wkv.b, xn.t[:, kc, 0:C], xn.b, kc == 0, kc == KC - 1)
        r1 = self.rt.next()
        self.tt(r1.t[:, 0:C], r1.b, p1.t[0:64, 0:C], p1.b, self.cs.t[:, 0, 0:C], self.cs.b, ALU.mult)
        p2 = self.G.next()
        for kc in range(KC):
            self.mm(p2.t[0:64, 0:C], p2.b, self.wkvr.t[:, kc, :], self.wkvr.b, xn.t[:, kc, 0:C], xn.b, kc == 0, kc == KC - 1)
        r2 = self.rt.next()
        self.tt(r2.t[:, 0:C], r2.b, p2.t[0:64, 0:C], p2.b, self.cs.t[:, 1, 0:C], self.cs.b, ALU.mult)
        self.tt(self.kpef.t[:, 0:C], self.kpef.b, r1.t[:, 0:C], r1.b, r2.t[:, 0:C], r2.b, ALU.add)

        wk = self.wload8(win, C_K, 256)
        wv0 = self.wload8(win, C_V, 256)
        wv1 = self.wload8(win, C_V + 256, 256)
        for a in range(A):
            ca = slice(a * 128, a * 128 + npart)
            pz = self.XS
            self.mm(pz.t[0:npart, 0:256], pz.b, self.alow.t[:, ca], self.alow.b, self.waup.t[:], self.waup.b)
            L = self.Lt.next()
            self.actf(L.t[0:npart, :], L.b, pz.t[0:npart, 0:256], pz.b, AF.Exp, scale=-1.0)
            self.actf(L.t[0:npart, :], L.b, L.t[0:npart, :], L.b, AF.Ln, bias=1.0)
            pv = self.G.next()
            for kc in range(KC):
                self.mm(pv.t[0:npart, 0:256], pv.b, xn.t[:, kc, ca], xn.b, wv0.t[:, kc, 0:256], wv0.b, kc == 0, kc == KC - 1)
            for kc in range(KC):
                self.mm(pv.t[0:npart, 256:512], pv.b, xn.t[:, kc, ca], xn.b, wv1.t[:, kc, 0:256], wv1.b, kc == 0, kc == KC - 1)
            self.acp(self.vtm.t[0:npart, a, :], self.vtm.b, pv.t[0:npart, :], pv.b)
            pk = self.G.next()
            for kc in range(KC):
                self.mm(pk.t[0:npart, 0:256], pk.b, xn.t[:, kc, ca], xn.b, wk.t[:, kc, 0:256], wk.b, kc == 0, kc == KC - 1)
            if sample:
                self.cp(self.ktm.t[0:npart, a, :], self.ktm.b, pk.t[0:npart, 0:256], pk.b)
            else:
                pb = self.XS
                self.mm(pb.t[0:npart, 0:256], pb.b, cum, self.cst.b, L.t[0:npart, :], L.b)
                enb = self.enb.next()
                self.actf(enb.t[0:npart, :], enb.b, pb.t[0:npart, 0:256], pb.b, AF.Exp, scale=-1.0)
            pbt = self.G.next()
            for h in range(4):
                self.mm(pbt.t[0:64, h * 128:h * 128 + npart], pbt.b, L.t[0:npart, h * 64:(h + 1) * 64], L.b, cum, self.cst.b)
            self.cp(self.bTs.t[:, :, ca], self.bTs.b, pbt.t[0:64, :].rearrange("p (h t) -> p h t", h=4)[:, :, 0:npart], pbt.b)
            if not sample:
                self.tt(self.ktm.t[0:npart, a, :], self.ktm.b, pk.t[0:npart, 0:256], pk.b, enb.t[0:npart, :], enb.b, ALU.mult)
        wq = self.wload8(win, C_Q, 256)
        for h in range(4):
            ebq = self.eb.next()
            self.actf(ebq.t[:, 0:C], ebq.b, self.bTs.t[:, h, 0:C], self.bTs.b, AF.Exp, scale=1.0)
            if sample:
                self.cp(self.esf.t[:, h, 0:C], self.esf.b, ebq.t[:, 0:C], ebq.b)
            else:
                for a in range(A):
                    self.cp(self.elast.t[:, h, a:a + 1], self.elast.b, ebq.t[:, a * 128 + 127:a * 128 + 128], ebq.b)
            pq = self.G.next()
            for kc in range(KC):
                self.mm(pq.t[0:64, 0:C], pq.b, wq.t[:, kc, h * 64:(h + 1) * 64], wq.b, xn.t[:, kc, 0:C], xn.b, kc == 0, kc == KC - 1)
            if sample:
                self.ts(self.qsf.t[:, h, 0:C], self.qsf.b, pq.t[0:64, 0:C], pq.b, 0.125, ALU.mult)
            else:
                self.stt(self.qt[h].t[:, 0:C], self.qt[h].b, pq.t[0:64, 0:C], pq.b, 0.125, ebq.t[:, 0:C], ebq.b, ALU.mult, ALU.mult)
                ebk = self.eb.next()
                self.actf(ebk.t[:, 0:C], ebk.b, self.bTs.t[:, h, 0:C], self.bTs.b, AF.Exp, scale=-1.0)
                pk = self.G.next()
                for kc in range(KC):
                    self.mm(pk.t[0:64, 0:C], pk.b, wk.t[:, kc, h * 64:(h + 1) * 64], wk.b, xn.t[:, kc, 0:C], xn.b, kc == 0, kc == KC - 1)
                self.tt(self.kt[h].t[:, 0:C], self.kt[h].b, pk.t[0:64, 0:C], pk.b, ebk.t[:, 0:C], ebk.b, ALU.mult)
    def mla_q(self, h, C):
        cqn = self.cqn
        pn = self.G.next()
        for i in range(2):
            self.mm(pn.t[:, 0:C], pn.b, self.wuq.t[:, i, h * 192:h * 192 + 128], self.wuq.b, cqn.t[:, i, 0:C], cqn.b, i == 0, i == 1)
        qn = self.qn.next()
        self.acp(qn.t[:, 0:C], qn.b, pn.t[:, 0:C], pn.b)
        pl = self.G.next()
        self.mm(pl.t[:, 0:C], pl.b, self.wukT.t[:, h, :], self.wukT.b, qn.t[:, 0:C], qn.b)
        ql = self.ql.next()
        self.acp(ql.t[:, 0:C], ql.b, pl.t[:, 0:C], pl.b)
        p1 = self.G.next()
        for i in range(2):
            self.mm(p1.t[0:64, 0:C], p1.b, self.wuq.t[:, i, h * 192 + 128:h * 192 + 192], self.wuq.b, cqn.t[:, i, 0:C], cqn.b, i == 0, i == 1)
        r1 = self.rt.next()
        self.tt(r1.t[:, 0:C], r1.b, p1.t[0:64, 0:C], p1.b, self.cs.t[:, 0, 0:C], self.cs.b, ALU.mult)
        p2 = self.G.next()
        for i in range(2):
            self.mm(p2.t[0:64, 0:C], p2.b, self.wuqr.t[:, i, h, :], self.wuqr.b, cqn.t[:, i, 0:C], cqn.b, i == 0, i == 1)
        r2 = self.rt.next()
        self.tt(r2.t[:, 0:C], r2.b, p2.t[0:64, 0:C], p2.b, self.cs.t[:, 1, 0:C], self.cs.b, ALU.mult)
        qp = self.qpe.next()
        self.tt(qp.t[:, 0:C], qp.b, r1.t[:, 0:C], r1.b, r2.t[:, 0:C], r2.b, ALU.add)
        return ql, qp

    def gla_out(self, h, src, srcb, C):
        sq = self.h
        self.actf(sq.t[:, 0, 0:C], sq.b, src, srcb, AF.Square)
        self.rstd_from([sq.t[:, 0, 0:C]], sq.b, self.on128, C)
        r = self.rl
        self.stt(r.t[:, 0:C], r.b, src, srcb, self.smw.t[:, 32:33], self.rstd.t[:, 0:C], self.rstd.b,
                 ALU.mult, ALU.mult, sreads=[self.smw.b])
        self.tt(self.ym.t[:, h, 0:C], self.ym.b, r.t[:, 0:C], r.b, self.sgT.t[:, h, 0:C], self.sgT.b, ALU.mult)

    def out_proj(self, C):
        wout = self.w["wout"]
        for u in range(4):
            wo = self.wload8(wout, u * 256, 256)
            for i in range(2):
                dc = u * 2 + i
                po = self.O.next()
                for kc in range(KC):
                    self.mm(po.t[:, 0:C], po.b, wo.t[:, kc, i * 128:(i + 1) * 128], wo.b, self.ym.t[:, kc, 0:C], self.ym.b, kc == 0, kc == KC - 1)
                self.tt(self.xT.t[:, dc, 0:C], self.xT.b, po.t[:, 0:C], po.b, self.xT.t[:, dc, 0:C], self.xT.b, ALU.add)

    def prompt_tile(self, ti):
        s, t = ti // 4, ti % 4
        C = NT
        tok0 = t * NT
        w = self.w
        self.load_x(lambda a: self.xp.ap()[s, tok0 + a * 128:tok0 + (a + 1) * 128, :], 4, 128)
        self.ffn(w["f1g"], w["f1u"], w["f1d"], 0, C)
        if t == 0:
            for h in range(4):
                self.memset(self.S[h].t[:], self.S[h].b, 0.0)
                self.memset(self.Sb[h].t[:], self.Sb[h].b, 0.0)
        self.mixer_proj(C, 4, 128, tok0, sample=False)
        self.cp(self.latT_t[:, tok0:tok0 + C], self.bA, self.latf.t[:, 0:C], self.latf.b)
        self.cp(self.kpe_t[:, tok0:tok0 + C], self.bC, self.kpef.t[:, 0:C], self.kpef.b)
        pt_ = self.G.next()
        for a in range(4):
            self.tr(pt_.t[:, a * 128:(a + 1) * 128], pt_.b, self.latf.t[:, a * 128:(a + 1) * 128], self.latf.b, self.ident, self.cst.b)
        ot = self.otok.next()
        self.acp(ot.t[:].rearrange("p a c -> p (a c)"), ot.b, pt_.t[:], pt_.b)
        self.cp(self.latM_t[:, t * 4:(t + 1) * 4, :], self.bB, ot.t[:], ot.b)
        self.dma_out(self.kvp.ap()[s, tok0:tok0 + C, :].rearrange("(a p) c -> p a c", p=128), ot.t[:], ot.b)
        pt2 = self.G.next()
        for a in range(4):
            self.tr(pt2.t[:, a * 64:(a + 1) * 64], pt2.b, self.kpef.t[:, a * 128:(a + 1) * 128], self.kpef.b, self.ident[0:64, 0:64], self.cst.b)
        ot2 = self.otok2.next()
        self.acp(ot2.t[:].rearrange("p a c -> p (a c)"), ot2.b, pt2.t[:, 0:256], pt2.b)
        self.dma_out(self.pep.ap()[s, tok0:tok0 + C, :].rearrange("(a p) c -> p a c", p=128), ot2.t[:], ot2.b)
        gitems = [(a, h) for a in range(4) for h in range(4)]
        pos = {}

        def gla_A(a, h):
            ca = slice(a * 128, (a + 1) * 128)
            pa = self.G.next()
            self.mm(pa.t[:, 0:128], pa.b, self.kt[h].t[:, ca], self.kt[h].b, self.qt[h].t[:, ca], self.qt[h].b)
            am = self.atm.next()
            self.tt(am.t[:], am.b, pa.t[:, 0:128], pa.b, self.trib.t[:], self.trib.b, ALU.mult)
            return am

        def gla_rest(a, h, am):
            ca = slice(a * 128, (a + 1) * 128)
            hc = slice(h * 128, (h + 1) * 128)
            if h == 0:
                pos[a] = self.O.next()
            po = pos[a]
            self.mm(po.t[:, hc], po.b, self.vtm.t[:, a, hc], self.vtm.b, am.t[:], am.b, True, False)
            self.mm(po.t[:, hc], po.b, self.Sb[h].t[:], self.Sb[h].b, self.qt[h].t[:, ca], self.qt[h].b, False, True)
            pS = self.G.next()
            self.mm(pS.t[0:64, 0:128], pS.b, self.ktm.t[:, a, h * 64:(h + 1) * 64], self.ktm.b, self.vtm.t[:, a, hc], self.vtm.b)
            el = self.elast.t[:, h, a:a + 1]
            self.ts(self.S[h].t[:], self.S[h].b, self.S[h].t[:], self.S[h].b, el, ALU.mult, sreads=[self.elast.b])
            self.stt(self.S[h].t[:], self.S[h].b, pS.t[0:64, 0:128], pS.b, el, self.S[h].t[:], self.S[h].b, ALU.mult, ALU.add,
                     sreads=[self.elast.b])
            self.acp(self.Sb[h].t[:], self.Sb[h].b, self.S[h].t[:], self.S[h].b)

        def gla_norm(a):
            ca = slice(a * 128, (a + 1) * 128)
            po = pos[a]
            sq = self.h
            self.actf(sq.t[:, 0, 0:C], sq.b, po.t[:, 0:C], po.b, AF.Square)
            self.rstd_from([sq.t[:, 0, 0:C]], sq.b, self.on128, C)
            r = self.rl
            self.stt(r.t[:, 0:C], r.b, po.t[:, 0:C], po.b, self.smw.t[:, 32:33], self.rstd.t[:, 0:C], self.rstd.b,
                     ALU.mult, ALU.mult, sreads=[self.smw.b])
            self.tt(self.ym.t[:, 0:4, ca], self.ym.b, r.t[:, 0:C].rearrange("p (h t) -> p h t", h=4), r.b,
                    self.sgT.t[:, 0:4, ca], self.sgT.b, ALU.mult)

        am_next = gla_A(*gitems[0])
        for i, (a, h) in enumerate(gitems):
            am_cur = am_next
            if i + 1 < len(gitems):
                am_next = gla_A(*gitems[i + 1])
            gla_rest(a, h, am_cur)
            if h == 0 and a >= 1:
                gla_norm(a - 1)
        gla_norm(3)
        if t == 3:
            for h in range(4):
                self.dma_out(self.glap.ap()[s, h], self.S[h].t[:], self.S[h].b)
        qs = [self.mla_q(h, C) for h in range(4)]
        nkb = 4 * t + 4
        items = [(h, kb) for h in range(4) for kb in range(nkb)]
        acc = {}

        def emit_scores(h, kb):
            ql, qp = qs[h]
            r = kb - 4 * t
            c0 = 128 * r if r > 0 else 0
            N = C - c0
            ps = self.G.next()
            ks = slice(kb * 128, (kb + 1) * 128)
            self.mm(ps.t[:, 0:N], ps.b, self.latT_t[:, ks], self.bA, ql.t[:, c0:C], ql.b, True, False)
            self.mm(ps.t[:, 0:N], ps.b, self.kpe_t[:, ks], self.bC, qp.t[:, c0:C], qp.b, False, True)
            return ps, r, c0, N

        def emit_pv(h, kb, info):
            ps, r, c0, N = info
            if kb == 0:
                acc[h] = (self.O.next(), self.O.next())
            po, pl = acc[h]
            pt = self.pt.next()
            self.actf(pt.t[:, 0:N], pt.b, ps.t[:, 0:N], ps.b, AF.Exp, scale=MLA_SCALE)
            if r >= 0:
                self.tt(pt.t[:, 0:128], pt.b, pt.t[:, 0:128], pt.b, self.trib.t[:], self.trib.b, ALU.mult)
            self.mm(po.t[:, c0:C], po.b, self.latM_t[:, kb, :], self.bB, pt.t[:, 0:N], pt.b, kb == 0, kb == nkb - 1)
            self.mm(pl.t[:, c0:C], pl.b, self.on1.t[:], self.on1.b, pt.t[:, 0:N], pt.b, kb == 0, kb == nkb - 1)
            if kb == nkb - 1:
                self.actf(self.rl.t[:, 0:C], self.rl.b, pl.t[:, 0:C], pl.b, AF.Ln)
                self.actf(self.rl.t[:, 0:C], self.rl.b, self.rl.t[:, 0:C], self.rl.b, AF.Exp, scale=-1.0)
                self.tt(self.ol.t[:, 0:C], self.ol.b, po.t[:, 0:C], po.b, self.rl.t[:, 0:C], self.rl.b, ALU.mult)
                py = self.G.next()
                self.mm(py.t[:, 0:C], py.b, self.wuv.t[:, h * 128:(h + 1) * 128], self.wuv.b, self.ol.t[:, 0:C], self.ol.b)
                self.acp(self.ym.t[:, 4 + h, 0:C], self.ym.b, py.t[:, 0:C], py.b)

        info = emit_scores(*items[0])
        for i, (h, kb) in enumerate(items):
            nxt = emit_scores(*items[i + 1]) if i + 1 < len(items) else None
            emit_pv(h, kb, info)
            info = nxt
        self.out_proj(C)
        self.ffn(w["f2g"], w["f2u"], w["f2d"], 16, C)
        self.store_y(lambda a: self.yp.ap()[s, tok0 + a * 128:tok0 + (a + 1) * 128, :], 4, 128)

    def sample_tile(self):
        C = NSEQ_S
        w = self.w
        self.load_x(lambda a: self.xs.ap(), 1, 16)
        self.ffn(w["f1g"], w["f1u"], w["f1d"], 0, C)
        self.mixer_proj(C, 1, 16, SEQ, sample=True)
        pt_ = self.G.next()
        self.tr(pt_.t[:, 0:128], pt_.b, self.latf.t[:, 0:128], self.latf.b, self.ident, self.cst.b)
        ot = self.otok.next()
        self.acp(ot.t[0:16, 0, :], ot.b, pt_.t[0:16, 0:128], pt_.b)
        self.dma_out(self.kvs.ap(), ot.t[0:16, 0, :], ot.b)
        pt2 = self.G.next()
        self.tr(pt2.t[:, 0:64], pt2.b, self.kpef.t[:, 0:128], self.kpef.b, self.ident[0:64, 0:64], self.cst.b)
        ot2 = self.otok2.next()
        self.acp(ot2.t[0:16, 0, :], ot2.b, pt2.t[0:16, 0:64], pt2.b)
        self.dma_out(self.pes.ap(), ot2.t[0:16, 0, :], ot2.b)
        self.decode_attention()
        self.out_proj(C)
        self.ffn(w["f2g"], w["f2u"], w["f2d"], 16, C)
        self.store_y(lambda a: self.ys.ap(), 1, 16)

    def gla_sample_seq(self, b):
        pgl = self.XS
        s0 = self.s0.next()
        self.dma_in(s0.t[:], s0.b, self.sgla.ap()[b].rearrange("h d v -> d h v"))
        km = self.kmask.next()
        self.ts(km.t[:], km.b, self.ktm.t[0:16, 0, :], self.ktm.b, self.cst.t[0:16, b:b + 1], ALU.mult, sreads=[self.cst.b])
        pd = self.G.next()
        for h in range(4):
            self.mm(pd.t[0:64, h * 128:(h + 1) * 128], pd.b, km.t[:, h * 64:(h + 1) * 64], km.b,
                    self.vtm.t[0:16, 0, h * 128:(h + 1) * 128], self.vtm.b)
        s1 = self.s1
        for h in range(4):
            self.stt(s1.t[:, h, :], s1.b, s0.t[:, h, :], s0.b, self.esf.t[:, h, b:b + 1], pd.t[0:64, h * 128:(h + 1) * 128], pd.b,
                     ALU.mult, ALU.add, sreads=[self.esf.b])
        self.dma_out(self.glas.ap()[b].rearrange("h d v -> d h v"), s1.t[:], s1.b)
        for h in range(4):
            self.mm(pgl.t[:, h * 18 + b:h * 18 + b + 2], pgl.b, s1.t[:, h, :], s1.b, self.qsf.t[:, h, b:b + 2], self.qsf.b)

    def gla_sample_finish(self):
        pgl = self.XS
        self.cp(self.gsb.t[:].rearrange("p (h t) -> p h t", h=4), self.gsb.b,
                pgl.t[:, 0:72].rearrange("p (h t) -> p h t", h=4)[:, :, 0:16], pgl.b)
        for h in range(4):
            self.gla_out(h, self.gsb.t[:, h * 16:(h + 1) * 16], self.gsb.b, NSEQ_S)

    def bcreg(self, e):
        if getattr(self, "_bcreg", None) is None:
            self._bcreg = e.alloc_register("gather_bound")
            e.reg_mov(self._bcreg, NPOOL * 8 - 1)
        return self._bcreg

    def decode_attention(self):
        P = self.P
        C = NSEQ_S
        ar = self.arena.t
        gl = [(ar[:, 0:2048].rearrange("p (t c) -> p t c", c=128), ar[:, 0:2048], self.bA),
              (ar[:, 2048:4096].rearrange("p (t c) -> p t c", c=128), ar[:, 2048:4096], self.bB)]
        gp = [(ar[:, 4096:5120].rearrange("p (t c) -> p t c", c=64), ar[:, 4096:5120], self.bC),
              (ar[:, 5120:6144].rearrange("p (t c) -> p t c", c=64), ar[:, 5120:6144], self.bC)]
        self.dma_in(self.ptls.t[:], self.ptls.b, self.ptl.ap())
        self.ts(self.idx.t[:], self.idx.b, self.ptls.t[:], self.ptls.b, 8.0, ALU.mult, self.cst.t[:, 384:385], ALU.add, sreads=[self.cst.b])
        ckv_v = self.ckv.ap().rearrange("n (g t) c -> (n g) (t c)", t=16)
        cpe_v = self.cpe.ap().rearrange("n (g t) c -> (n g) (t c)", t=16)
        for h in range(4):
            ql, qp = self.mla_q(h, C)
            self.cp(self.QL.t[:, :, h], self.QL.b, ql.t[:, 0:C], ql.b)
            self.cp(self.QP.t[0:64, :, h], self.QP.b, qp.t[:, 0:C], qp.b)
        P.dma(lambda e: e.dma_start(out=self.QP.t[64:128, :, :], in_=self.QP.t[0:64, :, :]), writes=[self.QP.b], q="sp", no_waw=False)
        pnum, pden = self.O.next(), self.O.next()
        NB_ = 2
        NG = NSEQ_S * 4

        def gather(gi):
            glv, glf, glb = gl[gi % NB_]
            gpv, gpf, gpb = gp[gi % NB_]
            P.dma(lambda e: e.indirect_dma_start(
                out=glf, out_offset=None, in_=ckv_v,
                in_offset=bass.IndirectOffsetOnAxis(ap=self.idx.t[:, gi:gi + 1], axis=0),
                bounds_check=self.bcreg(e), oob_is_err=False),
                reads=[self.idx.b], writes=[glb], q="pool", no_waw=False)
            P.dma(lambda e: e.indirect_dma_start(
                out=gpf, out_offset=None, in_=cpe_v,
                in_offset=bass.IndirectOffsetOnAxis(ap=self.idx.t[:, gi:gi + 1], axis=0),
                bounds_check=self.bcreg(e), oob_is_err=False),
                reads=[self.idx.b], writes=[gpb], q="pool", no_waw=False)

        stage = {}
        psts = {}
        ptss = {}

        def emit_T(qi):
            gi, quad = qi // 4, qi % 4
            glv, glf, glb = gl[gi % NB_]
            gpv, gpf, gpb = gp[gi % NB_]
            xa, xb = self.XB.next(), self.XB.next()
            for k in range(4):
                self.tr(xa.t[:, k * 128:(k + 1) * 128], xa.b, glv[:, quad * 4 + k, :], glb, self.identb.t[:], self.identb.b)
            for j in range(2):
                c0_ = (quad * 4 + 2 * j) * 64
                self.tr(xb.t[:, j * 128:(j + 1) * 128], xb.b, gpf[:, c0_:c0_ + 128], gpb, self.identb.t[:], self.identb.b)
            lt4 = self.lt4.next()
            self.acp(lt4.t[:], lt4.b, xa.t[:], xa.b)
            pt4 = self.pt4.next()
            self.P.dve(lambda e: e.tensor_copy(out=pt4.t[:], in_=xb.t[:, 0:256]), reads=[xb.b, lt4.b], writes=[pt4.b])
            stage[qi] = (lt4, pt4)

        def emit_S(qi):
            gi, quad = qi // 4, qi % 4
            b, g = gi // 4, gi % 4
            if b not in psts:
                psts[b] = self.G.next()
            pst = psts[b]
            lt4, pt4 = stage.pop(qi)
            for k in range(4):
                kbi = g * 16 + quad * 4 + k
                self.mm(pst.t[:, kbi * 4:(kbi + 1) * 4], pst.b, lt4.t[:, k * 128:(k + 1) * 128], lt4.b, self.QL.t[:, b, :], self.QL.b, True, False)
                hp = slice((k % 2) * 64, (k % 2) * 64 + 64)
                self.mm(pst.t[:, kbi * 4:(kbi + 1) * 4], pst.b, pt4.t[hp, (k // 2) * 128:(k // 2 + 1) * 128], pt4.b,
                        self.QP.t[hp, b, :], self.QP.b, False, True)
            if quad == 3:
                pts = self.pts.next()
                self.actf(pts.t[:].rearrange("p t h -> p (t h)"), pts.b, pst.t[:, g * 64:(g + 1) * 64], pst.b, AF.Exp, scale=MLA_SCALE)
                psm = self.ptsum.next()
                P.dve(lambda e: e.tensor_reduce(out=psm.t[:], in_=pts.t[:].rearrange("p t h -> p h t"),
                                                axis=mybir.AxisListType.X, op=ALU.add),
                      reads=[pts.b], writes=[psm.b])
                ptss[gi] = (pts, psm)

        def emit_PV(gi):
            b, g = gi // 4, gi % 4
            glv, glf, glb = gl[gi % NB_]
            pts, psm = ptss.pop(gi)
            for tt_ in range(16):
                self.mm(pnum.t[:, b * 4:(b + 1) * 4], pnum.b, glv[:, tt_, :], glb, pts.t[:, tt_, :], pts.b,
                        g == 0 and tt_ == 0, g == 3 and tt_ == 15)
            self.mm(pden.t[:, b * 4:(b + 1) * 4], pden.b, self.onf.t[:], self.onf.b, psm.t[:], psm.b, g == 0, g == 3)

        NQ = NG * 4
        gather(0)
        for qi in range(NQ + 1):
            if qi < NQ:
                if qi % 4 == 0 and qi // 4 + 1 < NG:
                    pass
                emit_T(qi)
            if qi >= 1:
                emit_S(qi - 1)
            if qi >= 2 and (qi - 2) % 4 == 3:
                gdone = (qi - 2) // 4
                emit_PV(gdone)
                if gdone + 2 < NG:
                    gather(gdone + 2)
            if qi == 0 and NG > 1:
                gather(1)
            if qi % 16 == 6 and qi < NQ:
                self.gla_sample_seq(qi // 16)
        emit_PV(NG - 1)
        self.gla_sample_finish()
        self.cp(self.num.t[:].rearrange("p b h -> p (b h)"), self.num.b, pnum.t[:, 0:64], pnum.b)
        self.cp(self.den.t[:].rearrange("p b h -> p (b h)"), self.den.b, pden.t[:, 0:64], pden.b)
        for h in range(4):
            self.tt(self.prod.t[:], self.prod.b, self.QL.t[:, :, h], self.QL.b, self.latf.t[:, 0:C], self.latf.b, ALU.mult)
            self.tt(self.prod2.t[:], self.prod2.b, self.QP.t[0:64, :, h], self.QP.b, self.kpef.t[:, 0:C], self.kpef.b, ALU.mult)
            psn = self.G.next()
            self.mm(psn.t[:, 0:C], psn.b, self.onf.t[:], self.onf.b, self.prod.t[:], self.prod.b, True, False)
            self.mm(psn.t[:, 0:C], psn.b, self.onf.t[0:64, :], self.onf.b, self.prod2.t[:], self.prod2.b, False, True)
            self.actf(self.pnew.t[:], self.pnew.b, psn.t[:, 0:C], psn.b, AF.Exp, scale=MLA_SCALE)
            self.tt(self.tmp16.t[:], self.tmp16.b, self.pnew.t[:], self.pnew.b, self.latf.t[:, 0:C], self.latf.b, ALU.mult)
            self.tt(self.num.t[:, :, h], self.num.b, self.num.t[:, :, h], self.num.b, self.tmp16.t[:], self.tmp16.b, ALU.add)
            self.tt(self.den.t[:, :, h], self.den.b, self.den.t[:, :, h], self.den.b, self.pnew.t[:], self.pnew.b, ALU.add)
            self.recip(self.tmp16.t[:], self.tmp16.b, self.den.t[:, :, h], self.den.b)
            self.tt(self.ol.t[:, 0:C], self.ol.b, self.num.t[:, :, h], self.num.b, self.tmp16.t[:], self.tmp16.b, ALU.mult)
            py = self.G.next()
            self.mm(py.t[:, 0:C], py.b, self.wuv.t[:, h * 128:(h + 1) * 128], self.wuv.b, self.ol.t[:, 0:C], self.ol.b)
            self.acp(self.ym.t[:, 4 + h, 0:C], self.ym.b, py.t[:, 0:C], py.b)


def build_program(do_sample=True, n_tiles=8):
    nc = bass.Bass("TRN2", target_bir_lowering=False)
    with ExitStack() as st:
        P = Prog(nc, st)
        B = Builder(nc, P, do_sample=do_sample, n_tiles=n_tiles)
        B.alloc()
        build_program.sbuf_left = nc.sbuf_bytes_remaining
        B.setup()
        for ti in range(n_tiles):
            B.prompt_tile(ti)
            if B.first_pass:
                B.end_first_pass()
        if do_sample:
            B.sample_tile()
        build_program.n_ops = len(P.ops)
        P.emit()
    return nc


def _consts():
    c = np.zeros((128, 448), np.float32)
    p = np.arange(128)
    c[:, 0:128] = np.eye(128, dtype=np.float32)
    tri = (p[:, None] <= p[None, :]).astype(np.float32)
    c[:, 128:256] = tri
    c[:, 256:384] = tri * np.float32(-1.0 / 16.0)
    c[:, 384] = (p % 8).astype(np.float32)
    c[0:16, 401:417] = np.eye(16, dtype=np.float32) * np.float32(-1.0 / 16.0)
    return c


def _rope_tables():
    half = 32
    inv_freq = np.power(np.float32(10000.0), -np.arange(half, dtype=np.float32) / np.float32(half)).astype(np.float32)
    pos = np.concatenate([np.arange(SEQ, dtype=np.float32), np.full(16, 8192.0, np.float32)])
    ang = (pos[None, :] * inv_freq[:, None]).astype(np.float32)
    cos = np.cos(ang).astype(np.float32)
    sin = np.sin(ang).astype(np.float32)
    return np.stack([np.concatenate([cos, cos], 0), np.concatenate([sin, sin], 0)], 0)


_NC_CACHE = {}


def kernel(x_prompt, x_sample, cache_kv, cache_pe, state_gla, page_table,
           ffn1_norm_w, ffn1_w_gate, ffn1_w_up, ffn1_w_down, mix_norm_w, w_in,
           gla_w_a_up, gla_b_a, gla_norm_w, mla_q_norm_w, mla_w_uq, mla_kv_norm_w,
           mla_w_uk, mla_w_uv, w_out, ffn2_norm_w, ffn2_w_gate, ffn2_w_up, ffn2_w_down,
           final_norm_w, _do_sample=True, _n_tiles=8, _trace=False):
    f = lambda a: np.ascontiguousarray(np.asarray(a, dtype=np.float32))
    key = (_do_sample, _n_tiles)
    if key not in _NC_CACHE:
        _NC_CACHE[key] = build_program(_do_sample, _n_tiles)
    nc = _NC_CACHE[key]
    col = lambda v: np.asarray(v, np.float32).reshape(-1, 128).T
    smallw = np.zeros((128, 36), np.float32)
    smallw[:, 0:8] = col(ffn1_norm_w[0])
    smallw[:, 8:16] = col(mix_norm_w[0])
    smallw[:, 16:24] = col(ffn2_norm_w[0])
    smallw[:, 24:32] = col(final_norm_w)
    smallw[:, 32:33] = col(gla_norm_w[0])
    smallw[:, 33:35] = col(mla_q_norm_w[0])
    smallw[:, 35:36] = col(mla_kv_norm_w[0])
    shared = {
        "f1g": f(ffn1_w_gate[0]), "f1u": f(ffn1_w_up[0]), "f1d": f(ffn1_w_down[0]), "win": f(w_in[0]),
        "waup": f(np.concatenate([np.asarray(gla_w_a_up[0]), np.asarray(gla_b_a[0])[None, :]], 0)),
        "wuq": f(mla_w_uq[0]), "wuk": f(np.asarray(mla_w_uk[0]).reshape(128, 512)),
        "wuv": f(np.asarray(mla_w_uv[0]).reshape(128, 512)), "wout": f(w_out[0]),
        "f2g": f(ffn2_w_gate[0]), "f2u": f(ffn2_w_up[0]), "f2d": f(ffn2_w_down[0]),
        "smallw": smallw, "consts": _consts(), "rope": _rope_tables(),
        "wfin": np.ascontiguousarray(np.broadcast_to(np.asarray(final_norm_w, np.float32).reshape(1, D), (128, D))),
    }
    xp = np.asarray(x_prompt, np.float32)
    xs = np.asarray(x_sample, np.float32)
    sg = np.asarray(state_gla, np.float32)
    ptab = np.asarray(page_table, np.int32)
    pidx = np.arange(128) // 8
    ckv_full = f(cache_kv[0]) if _do_sample else None
    cpe_full = f(cache_pe[0]) if _do_sample else None
    in_maps = []
    for c in range(NCORES):
        m = dict(shared)
        m["xp"] = np.ascontiguousarray(xp[2 * c:2 * c + 2])
        m["xs"] = np.ascontiguousarray(xs[16 * c:16 * c + 16, 0, :])
        if not _do_sample:
            in_maps.append(m)
            continue
        m["ckv"] = ckv_full
        m["cpe"] = cpe_full
        m["sgla"] = np.ascontiguousarray(sg[0, 16 * c:16 * c + 16])
        pt_c = ptab[16 * c:16 * c + 16].reshape(16, 4, 16)
        m["ptl"] = np.ascontiguousarray(pt_c[:, :, pidx].transpose(2, 0, 1).reshape(128, 64)).astype(np.int32)
        in_maps.append(m)
    res = run_bass_kernel_spmd(nc, in_maps, core_ids=list(range(NCORES)), trace=_trace)
    R = res.results
    cat = lambda k: np.concatenate([np.asarray(r[k]) for r in R], axis=0)
    y_prompt = cat("yp")
    y_sample = cat("ys")[:, None, :]
    outs = (y_prompt, y_sample, cat("kvp")[None], cat("pep")[None], cat("glap")[None],
            cat("kvs")[None, :, None, :], cat("pes")[None, :, None, :], cat("glas")[None])
    outs = tuple(np.ascontiguousarray(o, dtype=np.float32) for o in outs)
    if _trace:
        kernel.last_exec_ns = res.exec_time_ns
    return outs
```
